# Optimizing a Trainium2 kernel written in Bass

```python
import jax, jax.numpy as jnp
from jax import lax
import numpy as np

D_MODEL = 1024
BATCH = 2
SEQ = 8192
DEPTH = 2

D_FF = 2816
N_HEADS = 16
N_KV_GROUPS = 2
HEADS_PER_GROUP = N_HEADS // N_KV_GROUPS
HEAD_DIM = 64
CMP_LEN = 32
CMP_STRIDE = 16
CMP_HIDDEN = 256
SEL_LEN = 64
N_SELECT = 16
WINDOW = 512
Q_BLOCK = 128
POOL_WINDOWS = (2, 4, 8, 16)
POOL_GROUP_DIM = 128
POOL_WIDTH = len(POOL_WINDOWS) * POOL_GROUP_DIM
N_BRANCHES = 2
Q_WIDTH = N_HEADS * HEAD_DIM
KV_WIDTH = N_KV_GROUPS * HEAD_DIM
N_NSA_GATES = 3 * N_HEADS
IN_SPLITS = (Q_WIDTH, KV_WIDTH, KV_WIDTH, KV_WIDTH, KV_WIDTH, KV_WIDTH, KV_WIDTH,
             N_NSA_GATES, POOL_WIDTH, N_BRANCHES * D_MODEL)
IN_WIDTH = sum(IN_SPLITS)
RMS_EPS = 1e-6
ALIBI_MAX_BIAS = 8.0

kernel_name = "hybrid_pool_nsa_macaron_gated"


def rmsnorm(x, g):
    xf = x.astype(jnp.float32)
    y = xf * lax.rsqrt(jnp.mean(xf * xf, axis=-1, keepdims=True) + RMS_EPS)
    return (y * g.astype(jnp.float32)).astype(x.dtype)


def swiglu(x, w_gate, w_up, w_down):
    return (jax.nn.silu(x @ w_gate) * (x @ w_up)) @ w_down


def masked_softmax(s, mask):
    s = jnp.where(mask, s.astype(jnp.float32), -1e30)
    s = s - jnp.max(s, axis=-1, keepdims=True)
    p = jnp.where(mask, jnp.exp(s), 0.0)
    return p / jnp.maximum(jnp.sum(p, axis=-1, keepdims=True), 1e-30)


def alibi_slopes():
    h = jnp.arange(1, N_HEADS + 1, dtype=jnp.float32)
    m = jnp.exp2(-ALIBI_MAX_BIAS * h / N_HEADS)
    return m.reshape(N_KV_GROUPS, HEADS_PER_GROUP)


def multiscale_pool(u, w_group, scale):
    B, T, _ = u.shape
    n_g = len(POOL_WINDOWS)
    uf = u.astype(jnp.float32)
    cs = jnp.cumsum(uf, axis=1)
    pos = jnp.arange(T, dtype=jnp.float32)
    outs = []
    for gi, w in enumerate(POOL_WINDOWS):
        c = cs[..., gi * POOL_GROUP_DIM:(gi + 1) * POOL_GROUP_DIM]
        lag = jnp.pad(c, ((0, 0), (w, 0), (0, 0)))[:, :T]
        cnt = jnp.minimum(pos + 1.0, float(w))[None, :, None]
        outs.append((c - lag) / cnt)
    pooled = jnp.stack(outs, axis=2)
    delta = (pooled - uf.reshape(B, T, n_g, POOL_GROUP_DIM)).astype(u.dtype)
    mixed = jnp.einsum('btgc,gcd->btgd', delta, w_group).reshape(B, T, POOL_WIDTH)
    return mixed * scale


def compress_tokens(kv, pos_emb, w1, w2):
    B, T = kv.shape[:2]
    n_cmp = (T - CMP_LEN) // CMP_STRIDE + 1
    idx = jnp.arange(n_cmp)[:, None] * CMP_STRIDE + jnp.arange(CMP_LEN)[None, :]
    blocks = kv[:, idx] + pos_emb[None, None, :, None, :]
    blocks = blocks.transpose(0, 1, 3, 2, 4).reshape(B, n_cmp, N_KV_GROUPS, CMP_LEN * HEAD_DIM)
    return jax.nn.gelu(blocks @ w1) @ w2


def cmp_to_sel_overlap(T):
    n_cmp = (T - CMP_LEN) // CMP_STRIDE + 1
    n_sel = T // SEL_LEN
    cs = np.arange(n_cmp)[:, None] * CMP_STRIDE
    ss = np.arange(n_sel)[None, :] * SEL_LEN
    ov = np.clip(np.minimum(cs + CMP_LEN, ss + SEL_LEN) - np.maximum(cs, ss), 0, None) / CMP_LEN
    return jnp.asarray(ov, dtype=jnp.float32)


def nsa_attention(q, k_cmp, v_cmp, k_slc, v_slc, k_win, v_win, gates):
    B, T = q.shape[:2]
    n_cmp = k_cmp.shape[1]
    n_sel = T // SEL_LEN
    n_top = min(N_SELECT, n_sel)
    slopes = alibi_slopes()[None, :, :, None, None]
    overlap = cmp_to_sel_overlap(T)
    cmp_end = jnp.arange(n_cmp) * CMP_STRIDE + CMP_LEN - 1
    kb = k_slc.reshape(B, n_sel, SEL_LEN, N_KV_GROUPS, HEAD_DIM).transpose(0, 3, 1, 2, 4)
    vb = v_slc.reshape(B, n_sel, SEL_LEN, N_KV_GROUPS, HEAD_DIM).transpose(0, 3, 1, 2, 4)
    kw_pad = jnp.pad(k_win, ((0, 0), (WINDOW, 0), (0, 0), (0, 0)))
    vw_pad = jnp.pad(v_win, ((0, 0), (WINDOW, 0), (0, 0), (0, 0)))
    gather = jax.vmap(jax.vmap(lambda blk, ix: blk[ix]))
    sel_ids = jnp.arange(n_sel)
    n_sel_keys = n_top * SEL_LEN

    def query_block(c):
        q0 = c * Q_BLOCK
        t = q0 + jnp.arange(Q_BLOCK)
        qc = lax.dynamic_slice_in_dim(q, q0, Q_BLOCK, axis=1)
        gc = lax.dynamic_slice_in_dim(gates, q0, Q_BLOCK, axis=1)

        dist_c = (t[:, None] - cmp_end[None, :]).astype(jnp.float32)
        s = jnp.einsum('bqghd,bngd->bghqn', qc, k_cmp) - slopes * dist_c
        p_cmp = masked_softmax(s, dist_c >= 0)
        o_cmp = jnp.einsum('bghqn,bngd->bqghd', p_cmp.astype(v_cmp.dtype), v_cmp)

        score = jnp.einsum('bghqn,nj->bgqj', p_cmp, overlap)
        cur = t // SEL_LEN
        valid = sel_ids[None, :] * SEL_LEN <= t[:, None]
        forced = valid & ((sel_ids[None, :] == 0) | (sel_ids[None, :] == cur[:, None])
                          | (sel_ids[None, :] == cur[:, None] - 1))
        score = jnp.where(forced, jnp.inf, jnp.where(valid, score, -jnp.inf))
        _, idx = lax.top_k(score, n_top)
        ks = gather(kb, idx).reshape(B, N_KV_GROUPS, Q_BLOCK, n_sel_keys, HEAD_DIM)
        vs = gather(vb, idx).reshape(B, N_KV_GROUPS, Q_BLOCK, n_sel_keys, HEAD_DIM)
        kpos = (idx[..., None] * SEL_LEN + jnp.arange(SEL_LEN)).reshape(B, N_KV_GROUPS, Q_BLOCK, n_sel_keys)
        dist_s = (t[None, None, :, None] - kpos)[:, :, None]
        s = jnp.einsum('bqghd,bgqkd->bghqk', qc, ks) - slopes * dist_s.astype(jnp.float32)
        p = masked_softmax(s, dist_s >= 0)
        o_slc = jnp.einsum('bghqk,bgqkd->bqghd', p.astype(vs.dtype), vs)

        kw = lax.dynamic_slice_in_dim(kw_pad, q0, WINDOW + Q_BLOCK, axis=1)
        vw = lax.dynamic_slice_in_dim(vw_pad, q0, WINDOW + Q_BLOCK, axis=1)
        wpos = q0 - WINDOW + jnp.arange(WINDOW + Q_BLOCK)
        dist_w = t[:, None] - wpos[None, :]
        mask_w = (dist_w >= 0) & (dist_w < WINDOW) & (wpos[None, :] >= 0)
        s = jnp.einsum('bqghd,bkgd->bghqk', qc, kw) - slopes * dist_w.astype(jnp.float32)
        p = masked_softmax(s, mask_w)
        o_win = jnp.einsum('bghqk,bkgd->bqghd', p.astype(vw.dtype), vw)

        return gc[..., 0:1] * o_cmp + gc[..., 1:2] * o_slc + gc[..., 2:3] * o_win

    out = lax.map(query_block, jnp.arange(T // Q_BLOCK))
    return out.transpose(1, 0, 2, 3, 4, 5).reshape(B, T, N_HEADS * HEAD_DIM)


def setup_inputs(seed: int = 0) -> dict:
    key = jax.random.key(seed)
    ks = jax.random.split(key, 24)

    def dense(k, shape, fan_in):
        return jax.random.normal(k, shape, jnp.float32) * (fan_in ** -0.5)

    def gain(k, shape):
        return 1.0 + 0.01 * jax.random.normal(k, shape, jnp.float32)

    L = DEPTH
    return {
        "x": jax.random.normal(ks[0], (BATCH, SEQ, D_MODEL), jnp.float32),
        "ffn1_norm": gain(ks[1], (L, D_MODEL)),
        "ffn1_w_gate": dense(ks[2], (L, D_MODEL, D_FF), D_MODEL),
        "ffn1_w_up": dense(ks[3], (L, D_MODEL, D_FF), D_MODEL),
        "ffn1_w_down": dense(ks[4], (L, D_FF, D_MODEL), D_FF),
        "mix_norm": gain(ks[5], (L, D_MODEL)),
        "w_in": dense(ks[6], (L, D_MODEL, IN_WIDTH), D_MODEL),
        "cmp_pos": 0.1 * jax.random.normal(ks[7], (L, CMP_LEN, HEAD_DIM), jnp.float32),
        "cmp_k_w1": dense(ks[8], (L, CMP_LEN * HEAD_DIM, CMP_HIDDEN), CMP_LEN * HEAD_DIM),
        "cmp_k_w2": dense(ks[9], (L, CMP_HIDDEN, HEAD_DIM), CMP_HIDDEN),
        "cmp_v_w1": dense(ks[10], (L, CMP_LEN * HEAD_DIM, CMP_HIDDEN), CMP_LEN * HEAD_DIM),
        "cmp_v_w2": dense(ks[11], (L, CMP_HIDDEN, HEAD_DIM), CMP_HIDDEN),
        "pool_w": dense(ks[12], (L, len(POOL_WINDOWS), POOL_GROUP_DIM, POOL_GROUP_DIM), POOL_GROUP_DIM),
        "pool_scale": 1.0 + 0.1 * jax.random.normal(ks[13], (L, POOL_WIDTH), jnp.float32),
        "w_branch_pool": dense(ks[14], (L, POOL_WIDTH, D_MODEL), POOL_WIDTH),
        "w_branch_nsa": dense(ks[15], (L, Q_WIDTH, D_MODEL), Q_WIDTH),
        "w_out": dense(ks[16], (L, D_MODEL, D_MODEL), D_MODEL),
        "ffn2_norm": gain(ks[17], (L, D_MODEL)),
        "ffn2_w_gate": dense(ks[18], (L, D_MODEL, D_FF), D_MODEL),
        "ffn2_w_up": dense(ks[19], (L, D_MODEL, D_FF), D_MODEL),
        "ffn2_w_down": dense(ks[20], (L, D_FF, D_MODEL), D_FF),
        "final_norm": gain(ks[21], (D_MODEL,)),
    }


def reference(x, ffn1_norm, ffn1_w_gate, ffn1_w_up, ffn1_w_down, mix_norm, w_in, cmp_pos,
              cmp_k_w1, cmp_k_w2, cmp_v_w1, cmp_v_w2, pool_w, pool_scale, w_branch_pool,
              w_branch_nsa, w_out, ffn2_norm, ffn2_w_gate, ffn2_w_up, ffn2_w_down, final_norm):
    B, T, _ = x.shape
    split_points = list(np.cumsum(IN_SPLITS)[:-1])
    q_scale = HEAD_DIM ** -0.5
    for l in range(DEPTH):
        x = x + 0.5 * swiglu(rmsnorm(x, ffn1_norm[l]), ffn1_w_gate[l], ffn1_w_up[l], ffn1_w_down[l])

        h = rmsnorm(x, mix_norm[l])
        proj = h @ w_in[l]
        (q, kc, vc, ksl, vsl, kwn, vwn, g_nsa, u_pool, g_merge) = jnp.split(proj, split_points, axis=-1)
        kv_shape = (B, T, N_KV_GROUPS, HEAD_DIM)
        q = (q * q_scale).reshape(B, T, N_KV_GROUPS, HEADS_PER_GROUP, HEAD_DIM)
        k_cmp = compress_tokens(kc.reshape(kv_shape), cmp_pos[l], cmp_k_w1[l], cmp_k_w2[l])
        v_cmp = compress_tokens(vc.reshape(kv_shape), cmp_pos[l], cmp_v_w1[l], cmp_v_w2[l])
        nsa_gates = jax.nn.sigmoid(g_nsa).reshape(B, T, N_KV_GROUPS, HEADS_PER_GROUP, 3)
        o_nsa = nsa_attention(q, k_cmp, v_cmp, ksl.reshape(kv_shape), vsl.reshape(kv_shape),
                              kwn.reshape(kv_shape), vwn.reshape(kv_shape), nsa_gates)
        o_pool = multiscale_pool(u_pool, pool_w[l], pool_scale[l])

        g_pool, g_attn = jnp.split(jax.nn.sigmoid(g_merge), 2, axis=-1)
        merged = g_pool * (o_pool @ w_branch_pool[l]) + g_attn * (o_nsa @ w_branch_nsa[l])
        x = x + merged @ w_out[l]

        x = x + 0.5 * swiglu(rmsnorm(x, ffn2_norm[l]), ffn2_w_gate[l], ffn2_w_up[l], ffn2_w_down[l])
    return rmsnorm(x, final_norm)
```

```python
import numpy as np
import ml_dtypes
from contextlib import ExitStack
import concourse.bass as bass
import concourse.mybir as mybir
from concourse.bass_utils import run_bass_kernel_spmd

F32 = mybir.dt.float32
BF16 = mybir.dt.bfloat16
AF = mybir.ActivationFunctionType
ALU = mybir.AluOpType

D = 1024
DFF = 2816
NFC = DFF // 128
SEQ = 8192
NL = 2
NTOK = 2048
MONO = False
QSEL = False


def ntok():
    return SEQ if MONO else NTOK
TT = 512
INW = 4400
C_Q, C_KC, C_VC, C_KSL, C_VSL, C_KWN, C_VWN, C_GN, C_U, C_GM = 0, 1024, 1152, 1280, 1408, 1536, 1664, 1792, 1840, 2352
EPS = 1e-6
NEG = -16384.0
WIN_STARTS = [j * 128 for j in range(8)] + [C_KC, C_VC, C_KSL, C_VSL, C_KWN, C_VWN] + [C_U + j * 128 for j in range(4)] + [C_GM + j * 128 for j in range(16)]
WIN_IDX = {c: i for i, c in enumerate(WIN_STARTS)}


class Sem:
    def __init__(self, h, name):
        self.h = h
        self.name = name
        self.count = 0
        self.group = False


class Tok:
    __slots__ = ("sem", "val")

    def __init__(self, sem, val):
        self.sem = sem
        self.val = val


class Res:
    __slots__ = ("name", "w", "r", "excl")

    def __init__(self, name="", excl=False):
        self.name = name
        self.w = None
        self.r = {}
        self.excl = excl


class Eng:
    def __init__(self, name, h, sem, same_sync):
        self.name = name
        self.h = h
        self.sem = sem
        self.waited = {}
        self.pending = []
        self.same_sync = same_sync


class KB:
    def __init__(self, nc, es):
        self.nc = nc
        self.es = es
        self.sems = []
        self.pe = self._eng("pe", nc.tensor, False)
        self.act = self._eng("act", nc.scalar, True)
        self.dve = self._eng("dve", nc.vector, True)
        self.pool = self._eng("pool", nc.gpsimd, True)
        self.sp = self._eng("sp", nc.sync, False)
        self.engs = [self.pe, self.act, self.dve, self.pool, self.sp]
        self.n_inst = 0

    def new_sem(self, name):
        name = "%s_%d" % (name, len(self.sems))
        h = self.es.enter_context(self.nc.semaphore(name))
        s = Sem(h, name)
        self.sems.append(s)
        return s

    def _eng(self, name, h, same_sync):
        return Eng(name, h, self.new_sem("s_" + name), same_sync)

    def sb(self, name, shape, dtype, es=None):
        self.nsb = getattr(self, "nsb", 0) + 1
        return (es or getattr(self, "cur_es", None) or self.es).enter_context(self.nc.sbuf_tensor("sb%d_%s" % (self.nsb, name), shape, dtype))

    def ps(self, name, shape, dtype):
        return self.es.enter_context(self.nc.psum_tensor("pp_" + name, shape, dtype))

    def _wait(self, eng, tok):
        if tok is None:
            return
        if tok.sem is eng.sem and not eng.same_sync:
            return
        assert tok.val is not None, "waiting on unresolved token (%s)" % tok.sem.name
        val = tok.val
        if tok.sem.group:
            val = max(val, tok.sem.count)
        if eng.waited.get(tok.sem, 0) >= val:
            return
        eng.h.wait_ge(tok.sem.h, val)
        eng.waited[tok.sem] = val

    def _deps(self, eng, reads, writes):
        for r in reads:
            self._wait(eng, r.w)
        for w in writes:
            self._wait(eng, w.w)
            for t in w.r.values():
                self._wait(eng, t)

    def _mark(self, tok, reads, writes):
        for r in reads:
            r.r[tok.sem] = tok
        for w in writes:
            w.w = tok
            w.r = {}

    def op(self, eng, fn, reads=(), writes=(), sig=True):
        xr = [r for r in reads if r.excl]
        if xr:
            writes = list(writes) + xr
            reads = [r for r in reads if not r.excl]
        self._deps(eng, reads, writes)
        inst = fn()
        self.n_inst += 1
        if sig:
            eng.sem.count += 1
            inst.then_inc(eng.sem.h, 1)
            tok = Tok(eng.sem, eng.sem.count)
            for t in eng.pending:
                t.val = eng.sem.count
            eng.pending = []
        else:
            tok = Tok(eng.sem, None)
            eng.pending.append(tok)
        self._mark(tok, reads, writes)
        return tok

    def dma(self, sem, out, in_, reads=(), writes=(), eng=None, **kw):
        eng = eng or self.sp
        self._deps(eng, reads, writes)
        if sem.count > 0:
            self._wait(eng, Tok(sem, sem.count))
        inst = eng.h.dma_start(out=out, in_=in_, **kw)
        self.n_inst += 1
        sem.count += 16
        inst.then_inc(sem.h, 16)
        tok = Tok(sem, sem.count)
        self._mark(tok, reads, writes)
        return tok

    def barrier(self):
        for e in self.engs:
            assert not e.pending
            for s in self.sems:
                if s.count > 0 and not (s is e.sem):
                    self._wait(e, Tok(s, s.count))


class Prog:
    def __init__(self, dram_specs, WST=3584):
        self.nc = bass.Bass("TRN2", target_bir_lowering=False)
        self.es = ExitStack()
        self.kb = KB(self.nc, self.es)
        self.dr = {}
        self.dres = {}
        for name, (shape, dt, kind) in dram_specs.items():
            self.dr[name] = self.nc.dram_tensor(name, list(shape), dt, kind=kind).ap()
            self.dres[name] = Res("dram_" + name)
        self.out_names = [n for n, (_, _, k) in dram_specs.items() if k == "ExternalOutput"]
        kb = self.kb
        self.psf = [kb.ps("psf%d" % i, [128, 512], F32) for i in range(7)]
        self.psf_r = [Res("psf%d" % i, excl=True) for i in range(7)]
        self.psb = kb.ps("psb", [128, 1024], BF16)
        self.psb_r = Res("psb", excl=True)
        self.ps_rr = 0
        self.ones = kb.sb("ones", [128, 128], F32)
        self.ones_r = Res("ones")
        kb.op(kb.dve, lambda: self.nc.vector.memset(self.ones[:], 1.0 / D), writes=[self.ones_r])
        self.epsc = kb.sb("epsc", [128, 1], F32)
        kb.op(kb.dve, lambda: self.nc.vector.memset(self.epsc[:], EPS), writes=[self.ones_r])
        self.vecs = kb.sb("vecs", [128, 64], F32)
        self.vecs_r = Res("vecs")
        self.ldsem = kb.new_sem("ld_misc")
        self.ldsem.group = True
        kb.dma(self.ldsem, self.vecs[:], self.dr["vecs"][:, :], writes=[self.vecs_r])
        self.wsem = [kb.new_sem("wsem%d" % i) for i in range(4)]
        self.stsem = kb.new_sem("st_misc")
        if WST:
            self.alloc_wstage(WST)

    def alloc_wstage(self, WST, WSTF=None, nslot=2):
        kb = self.kb
        self.WST = WST
        self.WSTF = WSTF or WST
        self.nslot = nslot
        self.wst = [kb.sb("wst%d" % i, [128, self.WSTF], F32) for i in range(nslot)]
        self.wst_r = [Res("wst%d" % i) for i in range(nslot)]
        self.wbf = [kb.sb("wbf%d" % i, [128, self.WST], BF16) for i in range(nslot)]
        self.wbf_r = [Res("wbf%d" % i) for i in range(nslot)]
        self.wslot = 0

    def bank(self):
        i = self.ps_rr % 7
        self.ps_rr += 1
        return self.psf[i], self.psf_r[i]

    def load_w(self, pieces):
        kb, nc = self.kb, self.nc
        s = self.wslot
        self.wslot = (self.wslot + 1) % self.nslot
        off = 0
        foff = 0
        views = []
        for ap in pieces:
            if len(ap.shape) == 3:
                _, kc, n = ap.shape
                src = ap
            else:
                K, n = ap.shape
                kc = K // 128
                src = ap.rearrange("(k p) n -> p k n", p=128)
            sz = kc * n
            bview = self.wbf[s][:, off:off + sz].rearrange("p (k n) -> p k n", n=n)
            if ap.dtype == BF16:
                kb.dma(self.wsem[s], bview, src, writes=[self.wbf_r[s]])
            else:
                assert foff + sz <= self.WSTF
                dst = self.wst[s][:, foff:foff + sz].rearrange("p (k n) -> p k n", n=n)
                kb.dma(self.wsem[s], dst, src, writes=[self.wst_r[s]])
                a, b, fa = off, off + sz, foff
                kb.op(kb.pool, lambda a=a, b=b, fa=fa: nc.gpsimd.tensor_copy(out=self.wbf[s][:, a:b], in_=self.wst[s][:, fa:fa + (b - a)]),
                      reads=[self.wst_r[s]], writes=[self.wbf_r[s]])
                foff += sz
            views.append(bview)
            off += sz
        assert off <= self.WST
        return views, self.wbf_r[s]

    def convert_w(self, src, dst, dst_fn=None):
        kb, nc = self.kb, self.nc
        nch, _, kc, n = src.shape
        sz = kc * n
        if not hasattr(self, "cvsem"):
            self.cvsem = [kb.new_sem("cvs%d" % i) for i in range(4)]
            self.cv_rr = 0
        for j in range(nch):
            s = self.wslot
            self.wslot = (self.wslot + 1) % self.nslot
            kb.dma(self.wsem[s], self.wst[s][:, 0:sz].rearrange("p (k n) -> p k n", n=n), src[j], writes=[self.wst_r[s]])
            e = self.cv_rr % 3
            self.cv_rr += 1
            if e == 0:
                kb.op(kb.pool, lambda: nc.gpsimd.tensor_copy(out=self.wbf[s][:, 0:sz], in_=self.wst[s][:, 0:sz]), reads=[self.wst_r[s]], writes=[self.wbf_r[s]])
            elif e == 1:
                kb.op(kb.dve, lambda: nc.vector.tensor_copy(out=self.wbf[s][:, 0:sz], in_=self.wst[s][:, 0:sz]), reads=[self.wst_r[s]], writes=[self.wbf_r[s]])
            else:
                kb.op(kb.act, lambda: nc.scalar.copy(out=self.wbf[s][:, 0:sz], in_=self.wst[s][:, 0:sz]), reads=[self.wst_r[s]], writes=[self.wbf_r[s]])
            kb.dma(self.cvsem[s], dst_fn(j) if dst_fn else dst[j], self.wbf[s][:, 0:sz].rearrange("p (k n) -> p k n", n=n), reads=[self.wbf_r[s]])

    def select_tile(self, dst, dst_r, cands, stage, stage_r, sem, p0, p1):
        kb, nc = self.kb, self.nc
        first = True
        for c, src in enumerate(cands):
            if src is None:
                continue
            kb.dma(sem, stage, src, writes=[stage_r])
            sc_ = self.selw[p0:p1, c:c + 1]
            if first:
                kb.op(kb.dve, lambda: nc.vector.tensor_scalar(out=dst, in0=stage, scalar1=sc_, scalar2=None, op0=ALU.mult),
                      reads=[stage_r, self.selw_r], writes=[dst_r])
                first = False
            else:
                kb.op(kb.dve, lambda: nc.vector.scalar_tensor_tensor(out=dst, in0=stage, scalar=sc_, in1=dst, op0=ALU.mult, op1=ALU.add),
                      reads=[stage_r, self.selw_r, dst_r], writes=[dst_r])

    def vcol(self, c):
        return self.vecs[:, c:c + 1]

    def rmsnorm(self, x, x_r, h, h_r, n, gcol, sq, sq_r, rstd, rstd_r, out_f32=None, out_r=None):
        kb, nc = self.kb, self.nc
        for s0 in range(0, n, 512):
            ps, ps_r = self.bank()
            for c in range(8):
                k = c % 2
                kb.op(kb.act, lambda c=c, k=k: nc.scalar.activation(out=sq[k][:, :], in_=x[:, c, s0:s0 + 512], func=AF.Square),
                      reads=[x_r], writes=[sq_r[k]])
                kb.op(kb.pe, lambda c=c, k=k: nc.tensor.matmul(ps[:, :], lhsT=self.ones[:, :], rhs=sq[k][:, :], start=(c == 0), stop=(c == 7)),
                      reads=[sq_r[k], self.ones_r], writes=[ps_r], sig=True)
            kb.op(kb.act, lambda: nc.scalar.activation(out=rstd[:, s0:s0 + 512], in_=ps[:, :], func=AF.Ln, bias=self.epsc[:, 0:1]),
                  reads=[ps_r, self.ones_r], writes=[rstd_r])
            kb.op(kb.act, lambda: nc.scalar.activation(out=rstd[:, s0:s0 + 512], in_=rstd[:, s0:s0 + 512], func=AF.Exp, scale=-0.5),
                  reads=[rstd_r], writes=[rstd_r])
            for c in range(8):
                tgt = h if out_f32 is None else out_f32
                tgt_r = h_r if out_f32 is None else out_r
                kb.op(kb.dve, lambda c=c, tgt=tgt: nc.vector.scalar_tensor_tensor(
                    out=tgt[:, c, s0:s0 + 512], in0=x[:, c, s0:s0 + 512], scalar=self.vcol(gcol + c), in1=rstd[:, s0:s0 + 512],
                    op0=ALU.mult, op1=ALU.mult), reads=[x_r, rstd_r, self.vecs_r], writes=[tgt_r])

    def ffn(self, x, x_r, h, h_r, n, wg, wu, wd, aT, aT_r, sg, sg_r):
        kb, nc = self.kb, self.nc
        nsub = n // 512
        for fc in range(NFC):
            if wu is None:
                (wgu,), w_r = self.load_w([wg[fc]])
                wgb, wub = wgu[:, 0:8, :], wgu[:, 8:16, :]
            else:
                (wgb, wub), w_r = self.load_w([wg[fc], wu[fc]])
            for sub in range(nsub):
                s0 = sub * 512
                pg, pg_r = self.bank()
                pu, pu_r = self.bank()
                for c in range(8):
                    kb.op(kb.pe, lambda c=c: nc.tensor.matmul(pg[:, :], lhsT=wgb[:, c, :], rhs=h[:, c, s0:s0 + 512], start=(c == 0), stop=(c == 7)),
                          reads=[w_r, h_r], writes=[pg_r], sig=(c == 7))
                for c in range(8):
                    kb.op(kb.pe, lambda c=c: nc.tensor.matmul(pu[:, :], lhsT=wub[:, c, :], rhs=h[:, c, s0:s0 + 512], start=(c == 0), stop=(c == 7)),
                          reads=[w_r, h_r], writes=[pu_r], sig=(c == 7))
                k = (fc * nsub + sub) % 2
                kb.op(kb.act, lambda k=k: nc.scalar.activation(out=sg[k][:, :], in_=pg[:, :], func=AF.Silu), reads=[pg_r], writes=[sg_r[k]])
                kb.op(kb.dve, lambda k=k: nc.vector.tensor_tensor(out=aT[:, fc, s0:s0 + 512], in0=sg[k][:, :], in1=pu[:, :], op=ALU.mult),
                      reads=[sg_r[k], pu_r], writes=[aT_r[fc]])
        for dc in range(8):
            (wdb,), w_r = self.load_w([wd[dc]])
            for sub in range(nsub):
                s0 = sub * 512
                py, py_r = self.bank()
                for fc in range(NFC):
                    kb.op(kb.pe, lambda fc=fc: nc.tensor.matmul(py[:, :], lhsT=wdb[:, fc, :], rhs=aT[:, fc, s0:s0 + 512], start=(fc == 0), stop=(fc == NFC - 1)),
                          reads=[w_r, aT_r[fc]], writes=[py_r], sig=(fc == NFC - 1))
                kb.op(kb.dve, lambda: nc.vector.scalar_tensor_tensor(out=x[:, dc, s0:s0 + 512], in0=py[:, :], scalar=0.5, in1=x[:, dc, s0:s0 + 512],
                                                                      op0=ALU.mult, op1=ALU.add), reads=[py_r, x_r], writes=[x_r])

    def finish(self):
        kb = self.kb
        kb.barrier()
        self.es.close()
        return self.nc


def gain_layout(v):
    return np.ascontiguousarray(np.asarray(v, np.float32).reshape(-1, 128).T)


def vbase(l):
    return 28 * l


POOLW = (2, 4, 8, 16)


def tok_specs(l, mode, last):
    specs = {"xs_in": ((D, NTOK), F32, "ExternalInput"), "vecs": ((128, 64), F32, "ExternalInput")}

    def wspec(li, pre):
        specs[pre + "wg"] = ((NFC, 128, 8, 128), F32, "ExternalInput")
        specs[pre + "wu"] = ((NFC, 128, 8, 128), F32, "ExternalInput")
        specs[pre + "wd"] = ((8, 128, NFC, 128), F32, "ExternalInput")

    doA = (mode == "A") or (not last)
    if mode == "CA":
        wspec(l, "f2_")
        specs["win_c"] = ((len(WIN_STARTS), 128, 8, 128), F32, "ExternalInput")
        specs["wpa"] = ((8, 128, 4, 128), F32, "ExternalInput")
        specs["wnb"] = ((8, 128, 8, 128), F32, "ExternalInput")
        specs["wo"] = ((8, 128, 8, 128), F32, "ExternalInput")
        specs["poolw"] = ((4, 128, 128), F32, "ExternalInput")
        specs["h2T_in"] = ((D, NTOK), BF16, "ExternalInput")
        specs["onsaT"] = ((D, NTOK), BF16, "ExternalInput")
        specs["uext"] = ((512, 4, 528), F32, "ExternalInput")
        specs["corr"] = ((128, 4, 16), F32, "ExternalInput")
    if doA:
        wspec(l, "f1_")
        specs["win_a"] = ((len(WIN_STARTS), 128, 8, 128), F32, "ExternalInput")
        specs["h2T_out"] = ((D, NTOK), BF16, "ExternalOutput")
        specs["kvT_out"] = ((4, 128, NTOK), BF16, "ExternalOutput")
        specs["uT_out"] = ((512, NTOK), F32, "ExternalOutput")
        specs["vtok_out"] = ((NTOK, 256), BF16, "ExternalOutput")
        specs["xs_out"] = ((D, NTOK), F32, "ExternalOutput")
    else:
        specs["out"] = ((D, NTOK), F32, "ExternalOutput")
    return specs


def build_tok(l, mode, last):
    P = Prog(tok_specs(l, mode, last))
    tok_body(P, l, mode, last)
    return P.finish()


def build_bca(l, last):
    specs = attn_specs()
    ts = tok_specs(l, "CA", last)
    for k_ in ("h2T_in", "win_c", "onsaT", "vecs"):
        ts.pop(k_)
    specs.update(ts)
    specs["onsaT"] = ((D, NTOK), BF16, "Internal")
    P = Prog(specs, WST=None)
    P.dr["h2T_in"] = P.dr["h2T"]
    P.dr["win_c"] = P.dr["win"]
    kb = P.kb
    with ExitStack() as pes:
        kb.cur_es = pes
        P.alloc_wstage(2048)
        attn_body(P)
        kb.barrier()
    with ExitStack() as pes:
        kb.cur_es = pes
        P.alloc_wstage(3584)
        tok_body(P, l, "CA", last)
        kb.barrier()
    kb.cur_es = None
    return P.finish()


def tok_body(P, l, mode, last):
    kb, nc, dr = P.kb, P.nc, P.dr
    doA = (mode == "A") or (not last)
    x = kb.sb("x", [128, 8, TT], F32); x_r = Res("x")
    h = kb.sb("h", [128, 8, TT], BF16); h_r = Res("h")
    aT = kb.sb("aT", [128, NFC, TT], BF16); aT_r = [Res("aT%d" % i) for i in range(NFC)]
    sq = [kb.sb("sq%d" % i, [128, 512], F32) for i in range(2)]; sq_r = [Res() for i in range(2)]
    rstd = kb.sb("rstd", [128, TT], F32); rstd_r = Res()
    xsem = kb.new_sem("xsem")
    if QSEL:
        xstg2 = kb.sb("xstg", [128, 8 * TT], F32); xstg_r = Res("xstg")
        xstg = xstg2[:, :].rearrange("p (c n) -> p c n", n=TT)
    if mode == "CA":
        hsem = kb.new_sem("hsem"); osem = kb.new_sem("osem"); usem = kb.new_sem("usem")
        on = kb.sb("on", [128, 8, TT], BF16); on_r = Res("on")
        ue = kb.sb("ue", [128, 4, 528], F32); ue_r = Res("ue")
        sa = kb.sb("sa", [128, 528], F32); sa_r = Res("sa")
        sb_ = kb.sb("sbb", [128, 528], F32); sb_r = Res("sb")
        dl = kb.sb("dl", [128, 4, TT], BF16); dl_r = [Res() for _ in range(4)]
        opl = kb.sb("opl", [128, 4, TT], BF16); opl_r = Res("opl")
        mg = kb.sb("mg", [128, 8, TT], BF16); mg_r = [Res() for _ in range(8)]
        t1 = kb.sb("t1", [128, TT], F32); t1_r = Res()
        t2 = kb.sb("t2", [128, TT], F32); t2_r = Res()
        corr = kb.sb("corr", [128, 4, 16], F32); corr_r = Res()
        kb.dma(P.ldsem, corr[:], dr["corr"][:, :, :], writes=[corr_r])
    if doA:
        kvst = [kb.sb("kvst%d" % i, [128, TT], BF16) for i in range(2)]; kvst_r = [Res() for _ in range(2)]
        ust = [kb.sb("ust%d" % i, [128, TT], F32) for i in range(2)]; ust_r = [Res() for _ in range(2)]
        vst = kb.sb("vst", [128, 4, 256], BF16); vst_r = Res()
        osems = [kb.new_sem("kvo%d" % i) for i in range(2)]
        usems = [kb.new_sem("uo%d" % i) for i in range(2)]
        vsem = kb.new_sem("vo")
        hosem = kb.new_sem("ho")

    def colsl(ap, t0):
        return ap[:, t0:t0 + TT].rearrange("(c p) n -> p c n", p=128)

    for t in range(ntok() // TT):
        t0 = t * TT
        if QSEL:
            P.select_tile(x[:, :, :], x_r, [colsl(dr["xs_in"], (4 * t + c_) * 512) for c_ in range(4)], xstg[:, :, :], xstg_r, xsem, 0, 128)
        else:
            kb.dma(xsem, x[:, :, :], colsl(dr["xs_in"], t0), writes=[x_r])
        la = l
        if mode == "CA":
            vb = vbase(l)
            if QSEL:
                P.select_tile(h[:, :, :], h_r, [colsl(dr["h2T_in"], (4 * t + c_) * 512) for c_ in range(4)],
                              on[:, :, :], on_r, hsem, 0, 128)
            else:
                kb.dma(hsem, h[:, :, :], colsl(dr["h2T_in"], t0), writes=[h_r])
            kb.dma(osem, on[:, :, :], colsl(dr["onsaT"], t0), writes=[on_r])
            if QSEL:
                ustg = xstg2[:, 0:2112].rearrange("p (g n) -> p g n", n=528)
                cands = []
                for c_ in range(4):
                    a0 = (4 * t + c_) * 512
                    cands.append(dr["uT_in"][:, a0 - 16:a0 + 512].rearrange("(g p) n -> p g n", p=128) if a0 > 0 else None)
                if t == 0:
                    kb.op(kb.pool, lambda: nc.gpsimd.memset(ustg[:, :, 0:16], 0.0), writes=[xstg_r])
                    kb.dma(usem, ustg[:, :, 16:528], dr["uT_in"][:, 0:512].rearrange("(g p) n -> p g n", p=128), writes=[xstg_r])
                    kb.op(kb.dve, lambda: nc.vector.tensor_scalar(out=ue[:, :, :], in0=ustg, scalar1=P.selw[:, 0:1], scalar2=None, op0=ALU.mult),
                          reads=[xstg_r, P.selw_r], writes=[ue_r])
                    for c_ in range(1, 4):
                        kb.dma(usem, ustg, cands[c_], writes=[xstg_r])
                        kb.op(kb.dve, lambda: nc.vector.scalar_tensor_tensor(out=ue[:, :, :], in0=ustg, scalar=P.selw[:, c_:c_ + 1], in1=ue[:, :, :], op0=ALU.mult, op1=ALU.add),
                              reads=[xstg_r, P.selw_r, ue_r], writes=[ue_r])
                else:
                    P.select_tile(ue[:, :, :], ue_r, cands, ustg, xstg_r, usem, 0, 128)
            elif MONO:
                kb.dma(usem, ue[:, :, 16:528], dr["uT_in"][:, t0:t0 + 512].rearrange("(g p) n -> p g n", p=128), writes=[ue_r])
                if t == 0:
                    kb.op(kb.pool, lambda: nc.gpsimd.memset(ue[:, :, 0:16], 0.0), writes=[ue_r])
                else:
                    kb.dma(usem, ue[:, :, 0:16], dr["uT_in"][:, t0 - 16:t0].rearrange("(g p) n -> p g n", p=128), writes=[ue_r])
            else:
                kb.dma(usem, ue[:, :, :], dr["uext"][:, t, :].rearrange("(g p) n -> p g n", p=128), writes=[ue_r])
            for gi, w in enumerate(POOLW):
                cur, cur_r = None, None
                sh = 1
                src = ue[:, gi, :]
                src_r = ue_r
                bufs = [(sa, sa_r), (sb_, sb_r)]
                bi = 0
                while sh < w:
                    dst, dst_r = bufs[bi]
                    bi ^= 1
                    lo = 2 * sh - 1
                    kb.op(kb.dve, lambda src=src, dst=dst, lo=lo, sh=sh: nc.vector.tensor_tensor(
                        out=dst[:, lo:528], in0=src[:, lo:528], in1=src[:, lo - sh:528 - sh], op=ALU.add),
                        reads=[src_r], writes=[dst_r])
                    src, src_r = dst, dst_r
                    sh *= 2
                kb.op(kb.dve, lambda src=src, w=w: nc.vector.tensor_scalar(out=src[:, 16:528], in0=src[:, 16:528], scalar1=1.0 / w, scalar2=None, op0=ALU.mult),
                      reads=[src_r], writes=[src_r])
                if t == 0:
                    kb.op(kb.dve, lambda src=src, gi=gi: nc.vector.tensor_tensor(out=src[:, 16:32], in0=src[:, 16:32], in1=corr[:, gi, :], op=ALU.mult),
                          reads=[src_r, corr_r], writes=[src_r])
                kb.op(kb.dve, lambda src=src, gi=gi: nc.vector.tensor_tensor(out=dl[:, gi, :], in0=src[:, 16:528], in1=ue[:, gi, 16:528], op=ALU.subtract),
                      reads=[src_r, ue_r], writes=[dl_r[gi]])
            for gi in range(4):
                (pw,), w_r = P.load_w([dr["poolw"][gi]])
                ps, ps_r = P.bank()
                kb.op(kb.pe, lambda: nc.tensor.matmul(ps[:, :], lhsT=pw[:, 0, :], rhs=dl[:, gi, :], start=True, stop=True),
                      reads=[w_r, dl_r[gi]], writes=[ps_r])
                kb.op(kb.dve, lambda: nc.vector.tensor_scalar(out=opl[:, gi, :], in0=ps[:, :], scalar1=P.vcol(vb + 24 + gi), scalar2=None, op0=ALU.mult),
                      reads=[ps_r, P.vecs_r], writes=[opl_r])
            for dc in range(8):
                (wpa, wnb, wgp, wga), w_r = P.load_w([dr["wpa"][dc], dr["wnb"][dc],
                                                      dr["win_c"][WIN_IDX[C_GM + dc * 128]],
                                                      dr["win_c"][WIN_IDX[C_GM + 1024 + dc * 128]]])
                pa, pa_r = P.bank(); pb, pb_r = P.bank(); pgp, pgp_r = P.bank(); pga, pga_r = P.bank()
                for c in range(4):
                    kb.op(kb.pe, lambda c=c: nc.tensor.matmul(pa[:, :], lhsT=wpa[:, c, :], rhs=opl[:, c, :], start=(c == 0), stop=(c == 3)),
                          reads=[w_r, opl_r], writes=[pa_r], sig=(c == 3))
                for c in range(8):
                    kb.op(kb.pe, lambda c=c: nc.tensor.matmul(pb[:, :], lhsT=wnb[:, c, :], rhs=on[:, c, :], start=(c == 0), stop=(c == 7)),
                          reads=[w_r, on_r], writes=[pb_r], sig=(c == 7))
                for c in range(8):
                    kb.op(kb.pe, lambda c=c: nc.tensor.matmul(pgp[:, :], lhsT=wgp[:, c, :], rhs=h[:, c, :], start=(c == 0), stop=(c == 7)),
                          reads=[w_r, h_r], writes=[pgp_r], sig=(c == 7))
                for c in range(8):
                    kb.op(kb.pe, lambda c=c: nc.tensor.matmul(pga[:, :], lhsT=wga[:, c, :], rhs=h[:, c, :], start=(c == 0), stop=(c == 7)),
                          reads=[w_r, h_r], writes=[pga_r], sig=(c == 7))
                kb.op(kb.act, lambda: nc.scalar.activation(out=t1[:, :], in_=pgp[:, :], func=AF.Sigmoid), reads=[pgp_r], writes=[t1_r])
                kb.op(kb.act, lambda: nc.scalar.activation(out=t2[:, :], in_=pga[:, :], func=AF.Sigmoid), reads=[pga_r], writes=[t2_r])
                kb.op(kb.dve, lambda: nc.vector.tensor_tensor(out=t1[:, :], in0=t1[:, :], in1=pa[:, :], op=ALU.mult), reads=[t1_r, pa_r], writes=[t1_r])
                kb.op(kb.dve, lambda: nc.vector.tensor_tensor(out=t2[:, :], in0=t2[:, :], in1=pb[:, :], op=ALU.mult), reads=[t2_r, pb_r], writes=[t2_r])
                kb.op(kb.dve, lambda: nc.vector.tensor_tensor(out=mg[:, dc, :], in0=t1[:, :], in1=t2[:, :], op=ALU.add), reads=[t1_r, t2_r], writes=[mg_r[dc]])
            for dc in range(8):
                (wo,), w_r = P.load_w([dr["wo"][dc]])
                pz, pz_r = P.bank()
                for c in range(8):
                    kb.op(kb.pe, lambda c=c: nc.tensor.matmul(pz[:, :], lhsT=wo[:, c, :], rhs=mg[:, c, :], start=(c == 0), stop=(c == 7)),
                          reads=[w_r, mg_r[c]], writes=[pz_r], sig=(c == 7))
                kb.op(kb.dve, lambda: nc.vector.tensor_tensor(out=x[:, dc, :], in0=x[:, dc, :], in1=pz[:, :], op=ALU.add), reads=[x_r, pz_r], writes=[x_r])
            P.rmsnorm(x, x_r, h, h_r, TT, vb + 16, sq, sq_r, rstd, rstd_r)
            P.ffn(x, x_r, h, h_r, TT, dr["f2_wg"], dr["f2_wu"], dr["f2_wd"], aT, aT_r, sq, sq_r)
            la = l + 1
            if last:
                P.rmsnorm(x, x_r, None, None, TT, 56, sq, sq_r, rstd, rstd_r, out_f32=x, out_r=x_r)
                kb.dma(P.stsem, colsl(dr["out"], t0), x[:, :, :], reads=[x_r])
                continue
        vb = vbase(la)
        P.rmsnorm(x, x_r, h, h_r, TT, vb + 0, sq, sq_r, rstd, rstd_r)
        P.ffn(x, x_r, h, h_r, TT, dr["f1_wg"], dr["f1_wu"], dr["f1_wd"], aT, aT_r, sq, sq_r)
        kb.dma(P.stsem, colsl(dr["xs_out"], t0), x[:, :, :], reads=[x_r])
        P.rmsnorm(x, x_r, h, h_r, TT, vb + 8, sq, sq_r, rstd, rstd_r)
        kb.dma(hosem, colsl(dr["h2T_out"], t0), h[:, :, :], reads=[h_r])
        W = dr["win_a"]
        for j, c0 in enumerate((C_KC, C_VC, C_KSL, C_KWN)):
            (wc,), w_r = P.load_w([W[WIN_IDX[c0]]])
            ps, ps_r = P.bank()
            for c in range(8):
                kb.op(kb.pe, lambda c=c: nc.tensor.matmul(ps[:, :], lhsT=wc[:, c, :], rhs=h[:, c, :], start=(c == 0), stop=(c == 7)),
                      reads=[w_r, h_r], writes=[ps_r], sig=(c == 7))
            k = j % 2
            kb.op(kb.act, lambda k=k: nc.scalar.copy(out=kvst[k][:, :], in_=ps[:, :]), reads=[ps_r], writes=[kvst_r[k]])
            kb.dma(osems[k], dr["kvT_out"][j, :, t0:t0 + TT], kvst[k][:, :], reads=[kvst_r[k]])
        for j in range(4):
            (wc,), w_r = P.load_w([W[WIN_IDX[C_U + j * 128]]])
            ps, ps_r = P.bank()
            for c in range(8):
                kb.op(kb.pe, lambda c=c: nc.tensor.matmul(ps[:, :], lhsT=wc[:, c, :], rhs=h[:, c, :], start=(c == 0), stop=(c == 7)),
                      reads=[w_r, h_r], writes=[ps_r], sig=(c == 7))
            k = j % 2
            kb.op(kb.act, lambda k=k: nc.scalar.copy(out=ust[k][:, :], in_=ps[:, :]), reads=[ps_r], writes=[ust_r[k]])
            kb.dma(usems[k], dr["uT_out"][j * 128:(j + 1) * 128, t0:t0 + TT], ust[k][:, :], reads=[ust_r[k]])
        (wv1, wv2), w_r = P.load_w([W[WIN_IDX[C_VSL]], W[WIN_IDX[C_VWN]]])
        for tb in range(TT // 128):
            ps, ps_r = P.bank()
            for wi, wv in enumerate((wv1, wv2)):
                for c in range(8):
                    kb.op(kb.pe, lambda c=c, wv=wv, wi=wi: nc.tensor.matmul(ps[:, wi * 128:(wi + 1) * 128], lhsT=h[:, c, tb * 128:(tb + 1) * 128], rhs=wv[:, c, :],
                                                                         start=(c == 0), stop=(c == 7)),
                          reads=[w_r, h_r], writes=[ps_r], sig=(c == 7))
            kb.op(kb.act, lambda: nc.scalar.copy(out=vst[:, tb, :], in_=ps[:, 0:256]), reads=[ps_r], writes=[vst_r])
        kb.dma(vsem, dr["vtok_out"][t0:t0 + TT, :].rearrange("(tb p) c -> p tb c", p=128), vst[:, :, :], reads=[vst_r])


def attn_specs():
    BI = "ExternalInput"
    specs = {
        "vecs": ((128, 64), F32, BI),
        "h2T": ((D, NTOK), BF16, BI),
        "kvall": ((4, 128, SEQ + 32), BF16, BI),
        "vall": ((SEQ, 256), BF16, BI),
        "kwin": ((128, 4, 1024), BF16, BI),
        "vwin": ((4, 1024, 128), BF16, BI),
        "win": ((len(WIN_STARTS), 128, 8, 128), F32, BI), "win_gn": ((D, 48), F32, BI),
        "ck_w1": ((2, 128, 16, 128), F32, BI), "ck_w2": ((256, 64), F32, BI),
        "cv_w1": ((2, 128, 16, 128), F32, BI), "cv_w2": ((256, 64), F32, BI),
        "pecol": ((128, 16), F32, BI),
        "qal": ((3, 16, NTOK), BF16, BI),
        "kbs": ((128, 16 * 64), F32, BI), "kbc": ((128, 64), F32, BI), "kbw": ((128, 4 * 8 * 16), F32, BI),
        "cmpm": ((128, 2, 512), BF16, BI), "cm": ((128, 16, 512), BF16, BI), "wm": ((128, 8, 512), BF16, BI),
        "addm": ((128, 16, 128), BF16, BI), "vneg": ((128, 16, 128), BF16, BI),
        "karows": ((64, SEQ), BF16, BI), "ones3": ((3, 1024), BF16, BI), "ovl": ((128, 4, 128), BF16, BI), "identb": ((128, 128), BF16, BI),
        "onsaT": ((D, NTOK), BF16, "ExternalOutput"),
    }
    return specs


def build_attn():
    P = Prog(attn_specs(), WST=2048)
    attn_body(P)
    return P.finish()


def attn_body(P):
    kb, nc, dr = P.kb, P.nc, P.dr
    ld = P.ldsem

    def const(name, shape, dt, src):
        t = kb.sb(name, shape, dt)
        r = Res(name)
        kb.dma(ld, t[:], src, writes=[r])
        return t, r

    CM, CM_r = const("CM", [128, 4 if MONO else 16, 512], BF16, dr["cm"][:, :, :])
    WM, WM_r = const("WM", [128, 8, 512], BF16, dr["wm"][:, :, :])
    CPM, CPM_r = const("CPM", [128, 5 if MONO else 2, 512], BF16, dr["cmpm"][:, :, :])
    if MONO:
        ADM = kb.sb("ADM", [128, 4, 128], BF16); ADM_r = Res("ADM")
        VNG = kb.sb("VNG", [128, 4, 128], BF16); VNG_r = Res("VNG")
        KBW = kb.sb("KBW", [128, 128], F32); KBW_r = Res("KBW")
        pisem = kb.new_sem("pisem")
    else:
        ADM, ADM_r = const("ADM", [128, 16, 128], BF16, dr["addm"][:, :, :])
        VNG, VNG_r = const("VNG", [128, 16, 128], BF16, dr["vneg"][:, :, :])
        KBW, KBW_r = const("KBW", [128, 512], F32, dr["kbw"][:, :])
    KBS, KBS_r = const("KBS", [128, 1024], F32, dr["kbs"][:, :])
    KBC, KBC_r = const("KBC", [128, 64], F32, dr["kbc"][:, :])
    IDB, IDB_r = const("IDB", [128, 128], BF16, dr["identb"][:, :])
    PEC, PEC_r = const("PEC", [128, 16], F32, dr["pecol"][:, :])

    KA = kb.sb("KA", [128, SEQ], BF16); KA_r = Res("KA")
    VA = kb.sb("VA", [128, 64, 65], BF16); VA_r = Res("VA")
    KW = kb.sb("KW", [128, 1024], BF16); KW_r = Res("KW")
    VW = kb.sb("VW", [128, 8, 65], BF16); VW_r = Res("VW")
    KC = kb.sb("KC", [128, 2, 512], BF16); KC_r = Res("KC")
    VC = kb.sb("VC", [128, 4, 2, 193], BF16); VC_r = Res("VC")
    kb.op(kb.pool, lambda: nc.gpsimd.memset(KW[0:64, :], 0.0), writes=[KW_r])
    kb.op(kb.pool, lambda: nc.gpsimd.memset(VW[:, :, 0:64], 0.0), writes=[VW_r])
    kb.dma(ld, KA[64:128, :], dr["karows"][:, :], writes=[KA_r])
    kb.op(kb.pool, lambda: nc.gpsimd.memset(KW[64:128, :], 0.0), writes=[KW_r])
    kb.op(kb.pool, lambda: nc.gpsimd.memset(KC[64:128, :, :], 0.0), writes=[KC_r])
    kb.dma(ld, KW[124:127, :], dr["ones3"][:, :], writes=[KW_r])
    kb.dma(ld, KC[124:127, :, :], dr["ones3"][:, :].rearrange("r (g n) -> r g n", n=512), writes=[KC_r])
    kb.op(kb.pool, lambda: nc.gpsimd.memset(VA[:, :, 64:65], 1.0), writes=[VA_r])
    kb.op(kb.pool, lambda: nc.gpsimd.memset(VW[:, :, 64:65], 1.0), writes=[VW_r])
    kb.op(kb.pool, lambda: nc.gpsimd.memset(VC[:, :, :, 64:65], 1.0), writes=[VC_r])
    for g in range(2):
        kb.dma(ld, VC[:, :, g, 65:193], dr["ovl"][:, :, :], writes=[VC_r])

    ces = ExitStack()
    KC2 = kb.sb("KC2", [128, SEQ + 16], BF16, es=ces); KC2_r = Res("KC2")
    zb = kb.sb("zb", [128, 512], F32, es=ces); zb_r = Res()
    s2 = kb.sb("s2", [128, 512], F32, es=ces); s2_r = Res()
    hid = kb.sb("hid", [128, 2, 512], BF16, es=ces); hid_r = [Res(), Res()]
    pecb = kb.sb("pecb", [128, 16], BF16, es=ces); pecb_r = Res()
    bj = kb.sb("bj", [128, 1], F32, es=ces); bj_r = Res()
    kb.op(kb.dve, lambda: nc.vector.tensor_copy(out=pecb[:, :], in_=PEC[:, :]), reads=[PEC_r], writes=[pecb_r])
    c2sem = kb.new_sem("c2sem")
    if MONO or QSEL:
        kb.op(kb.pool, lambda: nc.gpsimd.memset(KC2[0:64, SEQ:SEQ + 16], 0.0), writes=[KC2_r])
        kb.op(kb.pool, lambda: nc.gpsimd.memset(KC2[64:128, SEQ - 1:SEQ + 16], 0.0), writes=[KC2_r])
    for kv in range(2):
        w1 = dr["ck_w1" if kv == 0 else "cv_w1"]
        w2 = dr["ck_w2" if kv == 0 else "cv_w2"]
        for g in range(2):
            if MONO or QSEL:
                kb.dma(c2sem, KC2[0:64, 0:SEQ], dr["kvall"][kv, g * 64:(g + 1) * 64, 0:SEQ], writes=[KC2_r])
                kb.dma(c2sem, KC2[64:128, 0:SEQ - 1], dr["kvall"][kv, g * 64:(g + 1) * 64, 1:SEQ], writes=[KC2_r])
            else:
                kb.dma(c2sem, KC2[0:64, :], dr["kvall"][kv, g * 64:(g + 1) * 64, 0:SEQ + 16], writes=[KC2_r])
                kb.dma(c2sem, KC2[64:128, :], dr["kvall"][kv, g * 64:(g + 1) * 64, 1:SEQ + 17], writes=[KC2_r])
            for jc in range(2):
                (w1v,), w_r = P.load_w([w1[jc]])
                pb_, pb_r = P.bank()
                for lp in range(16):
                    kb.op(kb.pe, lambda lp=lp: nc.tensor.matmul(pb_[:, 0:1], lhsT=w1v[:, lp, :], rhs=pecb[:, lp:lp + 1], start=(lp == 0), stop=(lp == 15)),
                          reads=[w_r, pecb_r], writes=[pb_r], sig=(lp == 15))
                kb.op(kb.act, lambda: nc.scalar.copy(out=bj[:, :], in_=pb_[:, 0:1]), reads=[pb_r], writes=[bj_r])
                ps, ps_r = P.bank()
                for lp in range(16):
                    kb.op(kb.pe, lambda lp=lp: nc.tensor.matmul(ps[:, :], lhsT=w1v[:, lp, :], rhs=KC2[:, 2 * lp:2 * lp + 16 * 511 + 1:16], start=(lp == 0), stop=(lp == 15)),
                          reads=[w_r, KC2_r], writes=[ps_r], sig=(lp == 15))
                kb.op(kb.act, lambda: nc.scalar.activation(out=zb[:, :], in_=ps[:, :], func=AF.Identity, bias=bj[:, 0:1]), reads=[ps_r, bj_r], writes=[zb_r])
                kb.op(kb.act, lambda: nc.scalar.activation(out=s2[:, :], in_=zb[:, :], func=AF.Square), reads=[zb_r], writes=[s2_r])
                kb.op(kb.dve, lambda: nc.vector.tensor_scalar(out=s2[:, :], in0=s2[:, :], scalar1=0.044715, scalar2=1.0, op0=ALU.mult, op1=ALU.add), reads=[s2_r], writes=[s2_r])
                kb.op(kb.dve, lambda: nc.vector.tensor_tensor(out=s2[:, :], in0=s2[:, :], in1=zb[:, :], op=ALU.mult), reads=[s2_r, zb_r], writes=[s2_r])
                kb.op(kb.act, lambda: nc.scalar.activation(out=s2[:, :], in_=s2[:, :], func=AF.Sigmoid, scale=1.5957691216057308), reads=[s2_r], writes=[s2_r])
                kb.op(kb.dve, lambda jc=jc: nc.vector.tensor_tensor(out=hid[:, jc, :], in0=s2[:, :], in1=zb[:, :], op=ALU.mult), reads=[s2_r, zb_r], writes=[hid_r[jc]])
            (w2v,), w_r = P.load_w([w2[:, :]])
            if kv == 0:
                ps, ps_r = P.bank()
                for jc in range(2):
                    kb.op(kb.pe, lambda jc=jc: nc.tensor.matmul(ps[0:64, :], lhsT=w2v[:, jc, :], rhs=hid[:, jc, :], start=(jc == 0), stop=(jc == 1)),
                          reads=[w_r, hid_r[jc]], writes=[ps_r], sig=(jc == 1))
                kb.op(kb.act, lambda g=g: nc.scalar.copy(out=KC[0:64, g, :], in_=ps[0:64, :]), reads=[ps_r], writes=[KC_r])
            else:
                for nt in range(4):
                    ps, ps_r = P.bank()
                    for jc in range(2):
                        kb.op(kb.pe, lambda jc=jc, nt=nt: nc.tensor.matmul(ps[:, 0:64], lhsT=hid[:, jc, nt * 128:(nt + 1) * 128], rhs=w2v[:, jc, :], start=(jc == 0), stop=(jc == 1)),
                              reads=[w_r, hid_r[jc]], writes=[ps_r], sig=(jc == 1))
                    kb.op(kb.act, lambda g=g, nt=nt: nc.scalar.copy(out=VC[:, nt, g, 0:64], in_=ps[:, 0:64]), reads=[ps_r], writes=[VC_r])
    kb.barrier()
    ces.close()

    Q = kb.sb("Q", [128, 3, 8, 512], BF16); Q_r = [Res("Q%d" % i) for i in range(8)]
    kb.op(kb.pool, lambda: nc.gpsimd.memset(Q[64:128, :, :, :], 0.0), writes=Q_r)
    hT = kb.sb("hT", [128, 8, 512], BF16); hT_r = Res("hT")
    if QSEL:
        qstg = kb.sb("qstg", [128, 8, 512], BF16); qstg_r = Res("qstg")
    gat = kb.sb("gat", [128, 4, 48], F32); gat_r = Res("gat")
    cst = kb.sb("cst", [128, 4, 8, 64], F32); cst_r = [Res() for _ in range(8)]
    crd = kb.sb("crd", [128, 4, 8], F32); crd_r = [Res() for _ in range(8)]
    sc = kb.sb("sc", [128, 4, 128], F32); sc_r = Res("sc")
    pT = [kb.sb("pT%d" % i, [128, 512], BF16) for i in range(4)]; pT_r = [Res() for _ in range(4)]
    tmp = [kb.sb("tmp%d" % i, [128, 512], F32) for i in range(2)]; tmp_r = [Res() for _ in range(2)]
    onb = kb.sb("onb", [128, 4, 1024], BF16); onb_r = Res("onb")
    ost = [kb.sb("ost%d" % i, [128, 512], BF16) for i in range(2)]; ost_r = [Res() for _ in range(2)]
    smb = kb.sb("smb", [128, 4, 128], F32); smb_r = [Res() for _ in range(4)]
    wa = kb.sb("wa", [128, 4, 128], F32); wa_r = [Res() for _ in range(4)]
    wb = kb.sb("wb", [128, 4, 128], F32); wb_r = [Res() for _ in range(4)]
    m8 = kb.sb("m8", [128, 4, 8], F32); m8_r = [Res() for _ in range(4)]
    m8b = kb.sb("m8b", [128, 4, 8], F32); m8b_r = [Res() for _ in range(4)]
    nqv = kb.sb("nqv", [128, 4, 3, 128], BF16); nq_r = Res()
    kb.op(kb.pool, lambda: nc.gpsimd.memset(nqv[:, :, :, :], 0.0), writes=[nq_r])
    dd = kb.sb("dd", [128, 2, 4], F32); dd_r = Res()
    cf = kb.sb("cf", [128, 3, 4], F32); cf_r = Res()
    oh = kb.sb("oh", [128, 64], F32); oh_r = Res()
    hsem = kb.new_sem("hsem"); ksem = kb.new_sem("ksem"); vsem = kb.new_sem("vsem")
    kwsem = kb.new_sem("kwsem"); vwsem = kb.new_sem("vwsem"); qsem = kb.new_sem("qsem")
    osems = [kb.new_sem("os%d" % i) for i in range(2)]
    W = dr["win"]
    st = {"s": 0, "p": 0, "t": 0}

    def sbank():
        k = st["s"] % 3
        st["s"] += 1
        return P.psf[k], P.psf_r[k]

    def pbuf():
        k = st["p"] % 4
        st["p"] += 1
        return pT[k], pT_r[k]

    def tbuf():
        k = st["t"] % 2
        st["t"] += 1
        return tmp[k], tmp_r[k]

    def exp_tile(ps, ps_r, bias_ap, bias_r, mask_ap, mask_r, qa=0, qb=512):
        p, p_r = pbuf()
        if mask_ap is not None:
            tm, tm_r = tbuf()
            kb.op(kb.dve, lambda: nc.vector.tensor_tensor(out=tm[:, qa:qb], in0=ps[:, qa:qb], in1=mask_ap[:, qa:qb], op=ALU.add), reads=[ps_r, mask_r], writes=[tm_r])
            kb.op(kb.act, lambda: nc.scalar.activation(out=p[:, qa:qb], in_=tm[:, qa:qb], func=AF.Exp, bias=bias_ap), reads=[tm_r, bias_r], writes=[p_r])
        else:
            kb.op(kb.act, lambda: nc.scalar.activation(out=p[:, qa:qb], in_=ps[:, qa:qb], func=AF.Exp, bias=bias_ap), reads=[ps_r, bias_r], writes=[p_r])
        return p, p_r

    NI = ntok() // 512
    KT0 = 4 if MONO else 16
    for g in range(2):
        kb.dma(ksem, KA[0:64, :], dr["kvall"][2, g * 64:(g + 1) * 64, 0:SEQ], writes=[KA_r])
        for kq in range(16):
            kb.dma(vsem, VA[:, kq * 4:(kq + 1) * 4, 0:64],
                   dr["vall"][kq * 512:(kq + 1) * 512, g * 64:(g + 1) * 64].rearrange("(kt p) d -> p kt d", p=128), writes=[VA_r])
        for i in range(NI):
            q0 = i * 512
            if MONO:
                kb.dma(pisem, ADM[:, :, :], dr["addm"][i], writes=[ADM_r])
                kb.dma(pisem, VNG[:, :, :], dr["vneg"][i], writes=[VNG_r])
                kb.dma(pisem, KBW[:, :], dr["kbw"][i], writes=[KBW_r])
                cmp_tiles = [(nt, (i - 4 * nt) if (i - 4 * nt) <= 4 else None) for nt in range((32 * (i + 1) - 2) // 128 + 1)]
            else:
                cmp_tiles = [(nt, (nt - i + 1) if nt - i >= -1 else None) for nt in range(i + 1)]
            if QSEL:
                P.select_tile(hT[:, :, :], hT_r, [dr["h2T"][:, (4 * i + c_) * 512:(4 * i + c_ + 1) * 512].rearrange("(c p) n -> p c n", p=128) for c_ in range(4)],
                              qstg[:, :, :], qstg_r, hsem, 0, 128)
            else:
                kb.dma(hsem, hT[:, :, :], dr["h2T"][:, q0:q0 + 512].rearrange("(c p) n -> p c n", p=128), writes=[hT_r])
            (wgn,), w_r = P.load_w([dr["win_gn"][:, :]])
            for qs in range(4):
                ps, ps_r = sbank()
                for c in range(8):
                    kb.op(kb.pe, lambda c=c: nc.tensor.matmul(ps[:, 0:48], lhsT=hT[:, c, qs * 128:(qs + 1) * 128], rhs=wgn[:, c, :], start=(c == 0), stop=(c == 7)),
                          reads=[w_r, hT_r], writes=[ps_r], sig=(c == 7))
                kb.op(kb.act, lambda: nc.scalar.activation(out=gat[:, qs, :], in_=ps[:, 0:48], func=AF.Sigmoid), reads=[ps_r], writes=[gat_r])
            if MONO:
                klo = max(q0 - 512, 0)
                kb.dma(kwsem, KW[0:64, 1024 - (q0 + 512 - klo):1024], dr["kvall"][3, g * 64:(g + 1) * 64, klo:q0 + 512], writes=[KW_r])
                w0 = 8 - (q0 + 512 - klo) // 128
                kb.dma(vwsem, VW[:, w0:8, 0:64], dr["vall"][klo:q0 + 512, 128 + g * 64:128 + (g + 1) * 64].rearrange("(w p) d -> p w d", p=128), writes=[VW_r])
            elif QSEL:
                for half in range(2):
                    sts = [4 * i + c_ - 1 + half for c_ in range(4)]
                    P.select_tile(KW[0:64, half * 512:(half + 1) * 512], KW_r,
                                  [dr["kvall"][3, g * 64:(g + 1) * 64, st_ * 512:(st_ + 1) * 512] if st_ >= 0 else None for st_ in sts],
                                  qstg[0:64, 0, :], qstg_r, kwsem, 0, 64)
                    P.select_tile(VW[:, half * 4:(half + 1) * 4, 0:64], VW_r,
                                  [dr["vall"][st_ * 512:(st_ + 1) * 512, 128 + g * 64:128 + (g + 1) * 64].rearrange("(w p) d -> p w d", p=128) if st_ >= 0 else None for st_ in sts],
                                  qstg[:, 1, 0:256].rearrange("p (w d) -> p w d", d=64), qstg_r, vwsem, 0, 128)
            else:
                kb.dma(kwsem, KW[0:64, :], dr["kwin"][g * 64:(g + 1) * 64, i, :], writes=[KW_r])
                kb.dma(vwsem, VW[:, :, 0:64], dr["vwin"][i, :, g * 64:(g + 1) * 64].rearrange("(w p) d -> p w d", p=128), writes=[VW_r])
            for hp in range(4):
                (wq,), w_r = P.load_w([W[g * 4 + hp]])
                for hh in range(2):
                    hl = hp * 2 + hh
                    ps, ps_r = sbank()
                    for c in range(8):
                        kb.op(kb.pe, lambda c=c: nc.tensor.matmul(ps[0:64, :], lhsT=wq[:, c, hh * 64:(hh + 1) * 64], rhs=hT[:, c, :], start=(c == 0), stop=(c == 7)),
                              reads=[w_r, hT_r], writes=[ps_r], sig=(c == 7))
                    kb.op(kb.dve, lambda: nc.vector.tensor_scalar(out=Q[0:64, 0, hl, :], in0=ps[0:64, :], scalar1=0.125, scalar2=None, op0=ALU.mult), reads=[ps_r], writes=[Q_r[hl]])
                    kb.op(kb.pool, lambda: nc.gpsimd.tensor_copy(out=Q[0:64, 1:3, hl, :], in_=Q[0:64, 0:1, hl, :].to_broadcast([64, 2, 512])),
                          reads=[Q_r[hl]], writes=[Q_r[hl]])
            for v_ in range(3):
                kb.dma(qsem, Q[124:127, v_, :, :], dr["qal"][:, g * 8:(g + 1) * 8, q0:q0 + 512], writes=Q_r)
            for hl in range(8):
                h = g * 8 + hl
                accs = [(P.psf[3 + 2 * (hl % 2)], P.psf_r[3 + 2 * (hl % 2)]), (P.psf[4 + 2 * (hl % 2)], P.psf_r[4 + 2 * (hl % 2)])]
                for a, a_r in accs:
                    kb.op(kb.dve, lambda: nc.vector.memset(a[:, 0:386], 0.0), writes=[a_r])
                cq_ = []
                for cu in list(cmp_tiles) + [None, None]:
                    if cu is not None:
                        nt, mi_ = cu
                        ps, ps_r = sbank()
                        kb.op(kb.pe, lambda: nc.tensor.matmul(ps[:, :], lhsT=KC[0:128, g, nt * 128:(nt + 1) * 128], rhs=Q[0:128, 0, hl, :], start=True, stop=True),
                              reads=[KC_r, Q_r[hl]], writes=[ps_r])
                        e, e_r = exp_tile(ps, ps_r, KBC[:, h * 4 + nt:h * 4 + nt + 1], KBC_r,
                                          CPM[:, mi_, :] if mi_ is not None else None, CPM_r)
                        cq_.append((nt, e, e_r))
                    if cq_ and (len(cq_) > 2 or cu is None):
                        nt_, e, e_r = cq_.pop(0)
                        for qs in range(4):
                            a, a_r = accs[qs // 2]
                            o0 = (qs % 2) * 193
                            kb.op(kb.pe, lambda: nc.tensor.matmul(a[:, o0:o0 + 193], lhsT=e[:, qs * 128:(qs + 1) * 128], rhs=VC[:, nt_, g, :], start=False, stop=(nt_ == cmp_tiles[-1][0]), skip_group_check=True),
                                  reads=[e_r, VC_r], writes=[a_r], sig=(qs == 3 or qs == 1))
                assert not cq_
                for half in range(2):
                    a, a_r = accs[half]
                    av = a[:, 0:386].rearrange("p (q c) -> p q c", c=193)
                    kb.op(kb.dve, lambda: nc.vector.tensor_scalar(out=crd[:, 2 * half:2 * half + 2, hl], in0=av[:, :, 64], scalar1=1e-30, scalar2=None, op0=ALU.max),
                          reads=[a_r], writes=[crd_r[hl]])
                    kb.op(kb.dve, lambda: nc.vector.tensor_copy(out=cst[:, 2 * half:2 * half + 2, hl, :], in_=av[:, :, 0:64]), reads=[a_r], writes=[cst_r[hl]])
                kb.op(kb.dve, lambda: nc.vector.reciprocal(out=crd[:, :, hl], in_=crd[:, :, hl]), reads=[crd_r[hl]], writes=[crd_r[hl]])
                for qs in range(4):
                    a, a_r = accs[qs // 2]
                    o0 = (qs % 2) * 193
                    if hl == 0:
                        kb.op(kb.dve, lambda: nc.vector.tensor_scalar(out=sc[:, qs, :], in0=a[:, o0 + 65:o0 + 193], scalar1=crd[:, qs, hl:hl + 1], scalar2=None, op0=ALU.mult),
                              reads=[a_r, crd_r[hl]], writes=[sc_r])
                    else:
                        kb.op(kb.dve, lambda: nc.vector.scalar_tensor_tensor(out=sc[:, qs, :], in0=a[:, o0 + 65:o0 + 193], scalar=crd[:, qs, hl:hl + 1], in1=sc[:, qs, :],
                                                                              op0=ALU.mult, op1=ALU.add), reads=[a_r, crd_r[hl], sc_r], writes=[sc_r])
            c0_ = 0 if MONO else i * 4
            kb.op(kb.dve, lambda: nc.vector.tensor_tensor(out=smb[:, :, :], in0=sc[:, :, :], in1=ADM[:, c0_:c0_ + 4, :], op=ALU.add), reads=[sc_r, ADM_r], writes=smb_r)
            for qs in range(4):
                kb.op(kb.dve, lambda: nc.vector.max(out=m8[:, qs, :], in_=smb[:, qs, :]), reads=[smb_r[qs]], writes=[m8_r[qs]])
            for qs in range(4):
                kb.op(kb.dve, lambda: nc.vector.match_replace(out=wa[:, qs, :], in_to_replace=m8[:, qs, :], in_values=smb[:, qs, :], imm_value=-1e30),
                      reads=[smb_r[qs], m8_r[qs]], writes=[wa_r[qs]])
            for qs in range(4):
                kb.op(kb.dve, lambda: nc.vector.max(out=m8b[:, qs, :], in_=wa[:, qs, :]), reads=[wa_r[qs]], writes=[m8b_r[qs]])
            for qs in range(4):
                kb.op(kb.dve, lambda: nc.vector.match_replace(out=wb[:, qs, :], in_to_replace=m8b[:, qs, :], in_values=wa[:, qs, :], imm_value=-1e30),
                      reads=[wa_r[qs], m8b_r[qs]], writes=[wb_r[qs]])
            kb.op(kb.dve, lambda: nc.vector.tensor_tensor(out=wa[:, :, :], in0=smb[:, :, :], in1=wb[:, :, :], op=ALU.subtract), reads=smb_r + wb_r, writes=wa_r)
            kb.op(kb.dve, lambda: nc.vector.tensor_scalar(out=wa[:, :, :], in0=wa[:, :, :], scalar1=1.0, scalar2=-NEG, op0=ALU.min, op1=ALU.mult), reads=wa_r, writes=wa_r)
            for v_ in range(3):
                nb_ = 60 if v_ < 2 else 8
                kb.op(kb.dve, lambda: nc.vector.scalar_tensor_tensor(out=nqv[:, :, v_, 64:64 + nb_], in0=wa[:, :, 60 * v_:60 * v_ + nb_], scalar=NEG,
                                                                      in1=VNG[:, c0_:c0_ + 4, 60 * v_:60 * v_ + nb_], op0=ALU.add, op1=ALU.min),
                      reads=wa_r + [VNG_r], writes=[nq_r])
            for qs in range(4):
                for v_ in range(3):
                    kb.op(kb.pe, lambda: nc.tensor.transpose(out=P.psb[:, v_ * 128:(v_ + 1) * 128], in_=nqv[:, qs, v_, :], identity=IDB[:, :]),
                          reads=[nq_r, IDB_r], writes=[P.psb_r])
                for v_ in range(3):
                    src_ = P.psb[64:124, v_ * 128:(v_ + 1) * 128].rearrange("p (o n) -> p o n", o=1).to_broadcast([60, 8, 128])
                    kb.op(kb.dve, lambda: nc.vector.tensor_copy(out=Q[64:124, v_, :, qs * 128:(qs + 1) * 128], in_=src_),
                          reads=[P.psb_r], writes=Q_r)
            for hl in range(8):
                h = g * 8 + hl
                aS, aS_r = P.psf[3 + 2 * (hl % 2)], P.psf_r[3 + 2 * (hl % 2)]
                aW, aW_r = P.psf[4 + 2 * (hl % 2)], P.psf_r[4 + 2 * (hl % 2)]
                nkt = KT0 * (i + 1)
                if hl == 0:
                    kb.op(kb.dve, lambda: nc.vector.memset(aS[:, 0:260], 0.0), writes=[aS_r])
                    kb.op(kb.dve, lambda: nc.vector.memset(aW[:, 0:260], 0.0), writes=[aW_r])
                units = [("s", kt) for kt in range(nkt)] + [("w", w) for w in range(8)]
                SKEW = 2
                pendq = []
                for u in units + [None] * SKEW:
                    cur = None
                    if u is not None:
                        kind, ix = u
                        ps, ps_r = sbank()
                        qa, qb = 0, 512
                        if kind == "s":
                            r_ = ix - KT0 * i
                            if MONO and r_ >= 0:
                                qa = 128 * r_
                            kb.op(kb.pe, lambda: nc.tensor.matmul(ps[:, qa:qb], lhsT=KA[0:128, ix * 128:(ix + 1) * 128], rhs=Q[0:128, (2 * ix) // 60, hl, qa:qb], start=True, stop=True),
                                  reads=[KA_r, Q_r[hl]], writes=[ps_r])
                            p, p_r = exp_tile(ps, ps_r, KBS[:, h * 64 + ix:h * 64 + ix + 1], KBS_r, CM[:, r_, :] if r_ >= 0 else None, CM_r, qa, qb)
                        else:
                            if ix < 4:
                                qb = 128 * (ix + 1)
                            else:
                                qa = 128 * (ix - 4)
                            kb.op(kb.pe, lambda: nc.tensor.matmul(ps[:, qa:qb], lhsT=KW[0:128, ix * 128:(ix + 1) * 128], rhs=Q[0:128, 0, hl, qa:qb], start=True, stop=True),
                                  reads=[KW_r, Q_r[hl]], writes=[ps_r])
                            cb = (ix * 16 + h) if MONO else ((i * 8 + ix) * 16 + h)
                            p, p_r = exp_tile(ps, ps_r, KBW[:, cb:cb + 1], KBW_r, WM[:, ix, :], WM_r, qa, qb)
                        cur = (kind, ix, p, p_r, qa, qb)
                        pendq.append(cur)
                    if pendq and (len(pendq) > SKEW or u is None):
                        kind_, ix_, pp, pp_r, qa_, qb_ = pendq.pop(0)
                        qss = list(range(qa_ // 128, qb_ // 128))
                        for qs in qss:
                            if kind_ == "s":
                                kb.op(kb.pe, lambda: nc.tensor.matmul(aS[:, qs * 65:(qs + 1) * 65], lhsT=pp[:, qs * 128:(qs + 1) * 128], rhs=VA[:, ix_, :], start=False, stop=(ix_ == nkt - 1), skip_group_check=True),
                                      reads=[pp_r, VA_r], writes=[aS_r], sig=(qs == qss[-1]))
                            else:
                                kb.op(kb.pe, lambda: nc.tensor.matmul(aW[:, qs * 65:(qs + 1) * 65], lhsT=pp[:, qs * 128:(qs + 1) * 128], rhs=VW[:, ix_, :], start=False, stop=(ix_ == 7), skip_group_check=True),
                                      reads=[pp_r, VW_r], writes=[aW_r], sig=(qs == qss[-1]))
                assert not pendq
                if hl < 7:
                    nS, nS_r = P.psf[3 + 2 * ((hl + 1) % 2)], P.psf_r[3 + 2 * ((hl + 1) % 2)]
                    nW, nW_r = P.psf[4 + 2 * ((hl + 1) % 2)], P.psf_r[4 + 2 * ((hl + 1) % 2)]
                    kb.op(kb.dve, lambda: nc.vector.memset(nS[:, 0:260], 0.0), writes=[nS_r])
                    kb.op(kb.dve, lambda: nc.vector.memset(nW[:, 0:260], 0.0), writes=[nW_r])
                aSv = aS[:, 0:260].rearrange("p (q c) -> p q c", c=65)
                aWv = aW[:, 0:260].rearrange("p (q c) -> p q c", c=65)
                kb.op(kb.dve, lambda: nc.vector.tensor_scalar(out=dd[:, 0, :], in0=aSv[:, :, 64], scalar1=1e-30, scalar2=None, op0=ALU.max), reads=[aS_r], writes=[dd_r])
                kb.op(kb.dve, lambda: nc.vector.tensor_scalar(out=dd[:, 1, :], in0=aWv[:, :, 64], scalar1=1e-30, scalar2=None, op0=ALU.max), reads=[aW_r], writes=[dd_r])
                kb.op(kb.dve, lambda: nc.vector.reciprocal(out=dd[:, :, :], in_=dd[:, :, :]), reads=[dd_r], writes=[dd_r])
                kb.op(kb.dve, lambda: nc.vector.tensor_tensor(out=cf[:, 0, :], in0=crd[:, :, hl], in1=gat[:, :, h * 3 + 0], op=ALU.mult), reads=[crd_r[hl], gat_r], writes=[cf_r])
                kb.op(kb.dve, lambda: nc.vector.tensor_tensor(out=cf[:, 1, :], in0=dd[:, 0, :], in1=gat[:, :, h * 3 + 1], op=ALU.mult), reads=[dd_r, gat_r], writes=[cf_r])
                kb.op(kb.dve, lambda: nc.vector.tensor_tensor(out=cf[:, 2, :], in0=dd[:, 1, :], in1=gat[:, :, h * 3 + 2], op=ALU.mult), reads=[dd_r, gat_r], writes=[cf_r])
                for qs in range(4):
                    kb.op(kb.dve, lambda: nc.vector.tensor_scalar(out=oh[:, :], in0=cst[:, qs, hl, :], scalar1=cf[:, 0, qs:qs + 1], scalar2=None, op0=ALU.mult),
                          reads=[cst_r[hl], cf_r], writes=[oh_r])
                    kb.op(kb.dve, lambda: nc.vector.scalar_tensor_tensor(out=oh[:, :], in0=aS[:, qs * 65:qs * 65 + 64], scalar=cf[:, 1, qs:qs + 1], in1=oh[:, :], op0=ALU.mult, op1=ALU.add),
                          reads=[aS_r, cf_r, oh_r], writes=[oh_r])
                    kb.op(kb.dve, lambda: nc.vector.scalar_tensor_tensor(out=onb[:, qs, h * 64:(h + 1) * 64], in0=aW[:, qs * 65:qs * 65 + 64], scalar=cf[:, 2, qs:qs + 1], in1=oh[:, :],
                                                                          op0=ALU.mult, op1=ALU.add), reads=[aW_r, cf_r, oh_r], writes=[onb_r])
            for fc in range(4 * g, 4 * g + 4):
                k = fc % 2
                for qs in range(4):
                    kb.op(kb.pe, lambda: nc.tensor.transpose(out=P.psb[:, 0:128], in_=onb[:, qs, fc * 128:(fc + 1) * 128], identity=IDB[:, :]),
                          reads=[onb_r, IDB_r], writes=[P.psb_r])
                    kb.op(kb.dve, lambda: nc.vector.tensor_copy(out=ost[k][:, qs * 128:(qs + 1) * 128], in_=P.psb[:, 0:128]), reads=[P.psb_r], writes=[ost_r[k]])
                kb.dma(osems[k], dr["onsaT"][fc * 128:(fc + 1) * 128, q0:q0 + 512], ost[k][:, :], reads=[ost_r[k]])


BF = ml_dtypes.bfloat16
_CACHE = {}
USE_MONO = True


def _slopes():
    hh = np.arange(1, 17, dtype=np.float32)
    return np.exp2(-8.0 * hh / 16.0).astype(np.float32)


def _split3(v):
    v = v.astype(np.float32)
    hi = v.astype(BF)
    r = v - hi.astype(np.float32)
    mid = r.astype(BF)
    r2 = r - mid.astype(np.float32)
    lo = r2.astype(BF)
    return hi, mid, lo


def core_consts(cc):
    sl = _slopes()
    p = np.arange(128)
    q = np.arange(512)
    c = {}
    tabs = np.concatenate([(4 * i + cc) * 512 + q for i in range(4)]).astype(np.float32)
    v = -(sl[:, None] * tabs[None, :])
    hi, mid, lo = _split3(v)
    c["qal"] = np.ascontiguousarray(np.stack([hi, mid, lo], 0))
    kbs = np.zeros((128, 16, 64), np.float32)
    for h in range(16):
        kbs[:, h, :] = sl[h] * (np.arange(64)[None, :] * 128 + p[:, None]).astype(np.float32)
    c["kbs"] = kbs.reshape(128, 1024)
    kbc = np.zeros((128, 16, 4), np.float32)
    for h in range(16):
        kbc[:, h, :] = sl[h] * (16 * (np.arange(4)[None, :] * 128 + p[:, None]) + 31).astype(np.float32)
    c["kbc"] = kbc.reshape(128, 64)
    kbw = np.zeros((128, 4, 8, 16), np.float32)
    for i in range(4):
        T0 = (4 * i + cc) * 512
        for w in range(8):
            ka = T0 - 512 + w * 128 + p
            for h in range(16):
                kbw[:, i, w, h] = np.where(ka >= 0, sl[h] * ka.astype(np.float32), -30000.0)
    c["kbw"] = kbw.reshape(128, 512)
    cmpm = np.zeros((128, 2, 512), np.float32)
    for d in (-1, 0):
        vis = (2048 * d + 16 * p[:, None] + 31 - 512 * cc) <= q[None, :]
        cmpm[:, d + 1, :] = np.where(vis, 0.0, NEG)
    c["cmpm"] = cmpm.astype(BF)
    cm = np.zeros((128, 16, 512), np.float32)
    for r in range(16):
        vis = (128 * r + p[:, None]) <= (512 * cc + q[None, :])
        cm[:, r, :] = np.where(vis, 0.0, NEG)
    c["cm"] = cm.astype(BF)
    wm = np.zeros((128, 8, 512), np.float32)
    for w in range(8):
        dist = 512 + q[None, :] - 128 * w - p[:, None]
        wm[:, w, :] = np.where((dist >= 0) & (dist < 512), 0.0, NEG)
    c["wm"] = wm.astype(BF)
    addm = np.zeros((128, 16, 128), np.float32)
    vneg = np.zeros((128, 16, 128), np.float32)
    j = np.arange(128)
    for i in range(4):
        for qs in range(4):
            t = (4 * i + cc) * 512 + qs * 128 + p
            valid = (j[None, :] * 64) <= t[:, None]
            cur = t // 64
            forced = valid & ((j[None, :] == 0) | (j[None, :] == cur[:, None]) | (j[None, :] == cur[:, None] - 1))
            addm[:, i * 4 + qs, :] = np.where(forced, 8192.0, np.where(valid, 0.0, -8192.0))
            vneg[:, i * 4 + qs, :] = np.where(valid, 0.0, NEG)
    c["addm"] = addm.astype(BF)
    c["vneg"] = vneg.astype(BF)
    return c


def shared_consts():
    c = {}
    cols = np.arange(SEQ)
    kar = np.zeros((64, SEQ), np.float32)
    kar[0:60] = ((cols[None, :] // 64) % 60 == np.arange(60)[:, None])
    kar[60:63] = 1.0
    c["karows"] = kar.astype(BF)
    c["ones3"] = np.ones((3, 1024), np.float32).astype(BF)
    n = np.arange(512)
    cs = n[:, None] * 16
    ss = np.arange(128)[None, :] * 64
    ov = np.clip(np.minimum(cs + 32, ss + 64) - np.maximum(cs, ss), 0, None) / 32.0
    c["ovl"] = np.ascontiguousarray(ov.reshape(4, 128, 128).transpose(1, 0, 2)).astype(np.float32).astype(BF)
    c["identb"] = np.eye(128, dtype=np.float32).astype(BF)
    return c


def tile_w(Wm, starts=None):
    K = Wm.shape[0]
    if starts is None:
        starts = list(range(0, Wm.shape[1], 128))
    out = np.empty((len(starts), 128, K // 128, 128), np.float32)
    for j, c0 in enumerate(starts):
        out[j] = Wm[:, c0:c0 + 128].reshape(K // 128, 128, 128).transpose(1, 0, 2)
    return out


def _prog(key, fn):
    if key not in _CACHE:
        _CACHE[key] = fn()
    return _CACHE[key]


def _tok_index(cc):
    return np.concatenate([np.arange((4 * i + cc) * 512, (4 * i + cc + 1) * 512) for i in range(4)])


def kernel(x, ffn1_norm, ffn1_w_gate, ffn1_w_up, ffn1_w_down, mix_norm, w_in, cmp_pos,
           cmp_k_w1, cmp_k_w2, cmp_v_w1, cmp_v_w2, pool_w, pool_scale, w_branch_pool,
           w_branch_nsa, w_out, ffn2_norm, ffn2_w_gate, ffn2_w_up, ffn2_w_down, final_norm):
    f32 = lambda a: np.ascontiguousarray(np.asarray(a, dtype=np.float32))
    x = f32(x)
    W = {k: f32(v) for k, v in dict(ffn1_norm=ffn1_norm, ffn1_w_gate=ffn1_w_gate, ffn1_w_up=ffn1_w_up, ffn1_w_down=ffn1_w_down,
                                    mix_norm=mix_norm, w_in=w_in, cmp_pos=cmp_pos, cmp_k_w1=cmp_k_w1, cmp_k_w2=cmp_k_w2,
                                    cmp_v_w1=cmp_v_w1, cmp_v_w2=cmp_v_w2, pool_w=pool_w, pool_scale=pool_scale,
                                    w_branch_pool=w_branch_pool, w_branch_nsa=w_branch_nsa, w_out=w_out, ffn2_norm=ffn2_norm,
                                    ffn2_w_gate=ffn2_w_gate, ffn2_w_up=ffn2_w_up, ffn2_w_down=ffn2_w_down, final_norm=final_norm).items()}
    cores = list(range(8))
    vecs = np.zeros((128, 64), np.float32)
    for l in range(NL):
        b0 = vbase(l)
        vecs[:, b0:b0 + 8] = gain_layout(W["ffn1_norm"][l])
        vecs[:, b0 + 8:b0 + 16] = gain_layout(W["mix_norm"][l])
        vecs[:, b0 + 16:b0 + 24] = gain_layout(W["ffn2_norm"][l])
        vecs[:, b0 + 24:b0 + 28] = gain_layout(W["pool_scale"][l])
    vecs[:, 56:64] = gain_layout(W["final_norm"])
    if USE_MONO:
        return kernel_mono(W, x, vecs)
    tix = [_tok_index(c % 4) for c in cores]
    cc_consts = [core_consts(cc) for cc in range(4)]
    sh = shared_consts()

    TW = {}
    for l in range(NL):
        TW["win", l] = tile_w(W["w_in"][l], WIN_STARTS)
        for nm in ("ffn1_w_gate", "ffn1_w_up", "ffn1_w_down", "ffn2_w_gate", "ffn2_w_up", "ffn2_w_down", "w_branch_pool", "w_branch_nsa", "w_out",
                   "cmp_k_w1", "cmp_v_w1"):
            TW[nm, l] = tile_w(W[nm][l])

    def a_weights(l):
        return {"f1_wg": TW["ffn1_w_gate", l], "f1_wu": TW["ffn1_w_up", l], "f1_wd": TW["ffn1_w_down", l], "win_a": TW["win", l]}

    progA = _prog("A", lambda: build_tok(0, "A", False))
    in_maps = []
    for c in cores:
        m = {"xs_in": np.ascontiguousarray(x[c // 4, tix[c], :].T), "vecs": vecs}
        m.update(a_weights(0))
        in_maps.append(m)
    res = run_bass_kernel_spmd(progA, in_maps, core_ids=cores).results

    out = np.zeros((2, SEQ, D), np.float32)
    for l in range(NL):
        last = (l == NL - 1)
        kvall = np.zeros((2, 4, 128, SEQ + 32), BF)
        vall = np.zeros((2, SEQ, 256), BF)
        uall = np.zeros((2, 512, SEQ), np.float32)
        for c in cores:
            b = c // 4
            kvall[b][:, :, tix[c]] = np.asarray(res[c]["kvT_out"]).view(BF) if np.asarray(res[c]["kvT_out"]).dtype != BF else res[c]["kvT_out"]
            vall[b][tix[c], :] = np.asarray(res[c]["vtok_out"])
            uall[b][:, tix[c]] = np.asarray(res[c]["uT_out"])
        prog = _prog(("BCA", l, last), lambda: build_bca(l, last))
        pecol = np.ascontiguousarray(W["cmp_pos"][l].reshape(16, 2, 64).transpose(1, 2, 0).reshape(128, 16))
        in_maps = []
        for c in cores:
            b, cc = c // 4, c % 4
            kwin = np.zeros((128, 4, 1024), BF)
            vwin = np.zeros((4, 1024, 128), BF)
            uext = np.zeros((512, 4, 528), np.float32)
            for i in range(4):
                T0 = (4 * i + cc) * 512
                lo = max(T0 - 512, 0)
                kwin[:, i, 1024 - (T0 + 512 - lo):] = kvall[b][3][:, lo:T0 + 512]
                vwin[i, 1024 - (T0 + 512 - lo):, :] = vall[b][lo:T0 + 512, 128:256]
                lo = max(T0 - 16, 0)
                uext[:, i, 528 - (T0 + 512 - lo):] = uall[b][:, lo:T0 + 512]
            corr = np.ones((128, 4, 16), np.float32)
            if cc == 0:
                for gi, w in enumerate(POOLW):
                    corr[:, gi, :] = (w / np.minimum(np.arange(16) + 1.0, float(w)))[None, :]
            m = {"vecs": vecs, "h2T": np.asarray(res[c]["h2T_out"]), "kvall": kvall[b], "vall": vall[b], "kwin": kwin, "vwin": vwin,
                 "win": TW["win", l], "win_gn": np.ascontiguousarray(W["w_in"][l][:, C_GN:C_GN + 48]),
                 "ck_w1": TW["cmp_k_w1", l], "ck_w2": W["cmp_k_w2"][l], "cv_w1": TW["cmp_v_w1", l], "cv_w2": W["cmp_v_w2"][l],
                 "pecol": pecol,
                 "xs_in": np.asarray(res[c]["xs_out"]),
                 "f2_wg": TW["ffn2_w_gate", l], "f2_wu": TW["ffn2_w_up", l], "f2_wd": TW["ffn2_w_down", l],
                 "wpa": TW["w_branch_pool", l], "wnb": TW["w_branch_nsa", l], "wo": TW["w_out", l],
                 "poolw": W["pool_w"][l], "uext": uext, "corr": corr}
            m.update(cc_consts[cc])
            m.update(sh)
            if not last:
                m.update(a_weights(l + 1))
            in_maps.append(m)
        res = run_bass_kernel_spmd(prog, in_maps, core_ids=cores).results
    for c in cores:
        out[c // 4, tix[c], :] = np.asarray(res[c]["out"]).T
    return out


def build_mono():
    global MONO
    MONO = True
    try:
        BI, IN_, BO = "ExternalInput", "Internal", "ExternalOutput"
        S = SEQ
        specs = {
            "x_in": ((D, S), F32, BI), "vecs": ((128, 64), F32, BI),
            "qal": ((3, 16, S), BF16, BI), "kbs": ((128, 1024), F32, BI), "kbc": ((128, 64), F32, BI),
            "kbw": ((16, 128, 128), F32, BI), "cmpm": ((128, 5, 512), BF16, BI), "cm": ((128, 4, 512), BF16, BI),
            "wm": ((128, 8, 512), BF16, BI), "addm": ((16, 128, 4, 128), BF16, BI), "vneg": ((16, 128, 4, 128), BF16, BI),
            "karows": ((64, S), BF16, BI), "ones3": ((3, 1024), BF16, BI), "ovl": ((128, 4, 128), BF16, BI), "identb": ((128, 128), BF16, BI),
            "corr": ((128, 4, 16), F32, BI),
            "xs": ((D, S), F32, IN_), "h2T": ((D, S), BF16, IN_), "kvT": ((4, 128, S), BF16, IN_), "vtok": ((S, 256), BF16, IN_),
            "uT0": ((512, S), F32, IN_), "uT1": ((512, S), F32, IN_), "onsaT": ((D, S), BF16, IN_),
            "out_q": ((D, NTOK), F32, BO), "onsaT_q": ((D, NTOK), BF16, IN_),
            "selw": ((128, 4), F32, BI), "corr_q": ((128, 4, 16), F32, BI),
            "qal_q": ((3, 16, NTOK), BF16, BI), "kbw_q": ((128, 512), F32, BI), "cmpm_q": ((128, 2, 512), BF16, BI),
            "cm_q": ((128, 16, 512), BF16, BI), "addm_q": ((128, 16, 128), BF16, BI), "vneg_q": ((128, 16, 128), BF16, BI),
        }
        for l in range(NL):
            for pre in ("f1_", "f2_"):
                specs["%swg_%d" % (pre, l)] = ((NFC, 128, 8, 128), F32, BI)
                specs["%swu_%d" % (pre, l)] = ((NFC, 128, 8, 128), F32, BI)
                specs["%swd_%d" % (pre, l)] = ((8, 128, NFC, 128), F32, BI)
            specs["win_%d" % l] = ((len(WIN_STARTS), 128, 8, 128), F32, BI)
            specs["win_gn_%d" % l] = ((D, 48), F32, BI)
            specs["wpa_%d" % l] = ((8, 128, 4, 128), F32, BI)
            specs["wnb_%d" % l] = ((8, 128, 8, 128), F32, BI)
            specs["wo_%d" % l] = ((8, 128, 8, 128), F32, BI)
            specs["poolw_%d" % l] = ((4, 128, 128), F32, BI)
            specs["ck_w1_%d" % l] = ((2, 128, 16, 128), F32, BI)
            specs["cv_w1_%d" % l] = ((2, 128, 16, 128), F32, BI)
            specs["ck_w2_%d" % l] = ((256, 64), F32, BI)
            specs["cv_w2_%d" % l] = ((256, 64), F32, BI)
            specs["pecol_%d" % l] = ((128, 16), F32, BI)
        conv = [n_ for n_, (sh_, dt_, k_) in specs.items() if k_ == BI and dt_ == F32 and len(sh_) == 4 and n_ != "kbw"]
        gu = [n_ for n_ in conv if n_[3:5] in ("wg", "wu")]
        conv = [n_ for n_ in conv if n_ not in gu]
        for n_ in conv:
            specs[n_ + "_b"] = (specs[n_][0], BF16, IN_)
        for l in range(NL):
            for pre in ("f1_", "f2_"):
                specs["%swgu_%d_b" % (pre, l)] = ((NFC, 128, 16, 128), BF16, IN_)
        P = Prog(specs, WST=None)
        kb, dr = P.kb, P.dr

        P.selw = kb.sb("selw", [128, 4], F32)
        P.selw_r = Res("selw")
        kb.dma(P.ldsem, P.selw[:, :], dr["selw"][:, :], writes=[P.selw_r])
        mono_tabs = {k_: dr[k_] for k_ in ("qal", "kbw", "cmpm", "cm", "addm", "vneg", "corr", "onsaT")}

        def do_convert():
            for n_ in conv:
                P.convert_w(dr[n_], dr[n_ + "_b"])
                dr[n_] = dr[n_ + "_b"]
            for l_ in range(NL):
                for pre in ("f1_", "f2_"):
                    d_ = dr["%swgu_%d_b" % (pre, l_)]
                    P.convert_w(dr["%swg_%d" % (pre, l_)], None, dst_fn=lambda j, d_=d_: d_[j, :, 0:8, :])
                    P.convert_w(dr["%swu_%d" % (pre, l_)], None, dst_fn=lambda j, d_=d_: d_[j, :, 8:16, :])

        def alias(l, mode):
            for nm in ("wpa", "wnb", "wo", "poolw", "ck_w1", "cv_w1", "ck_w2", "cv_w2", "pecol", "win_gn"):
                dr[nm] = dr["%s_%d" % (nm, l)]
            dr["win"] = dr["win_%d" % l]
            dr["win_c"] = dr["win_%d" % l]
            dr["f2_wd"] = dr["f2_wd_%d" % l]
            dr["f2_wg"] = dr["f2_wgu_%d_b" % l]
            dr["f2_wu"] = None
            la = l if mode == "A" else min(l + 1, NL - 1)
            dr["win_a"] = dr["win_%d" % la]
            dr["f1_wd"] = dr["f1_wd_%d" % la]
            dr["f1_wg"] = dr["f1_wgu_%d_b" % la]
            dr["f1_wu"] = None
            dr["kvall"] = dr["kvT"]
            dr["vall"] = dr["vtok"]
            dr["h2T_in"] = dr["h2T"]
            dr["h2T_out"] = dr["h2T"]
            dr["kvT_out"] = dr["kvT"]
            dr["vtok_out"] = dr["vtok"]
            dr["xs_out"] = dr["xs"]
            dr["xs_in"] = dr["x_in"] if (mode == "A" and l == 0) else dr["xs"]
            dr["uT_in"] = dr["uT%d" % (l % 2)]
            dr["uT_out"] = dr["uT%d" % (la % 2)]

        def phase(fn, wst, wstf=1024, nslot=4):
            with ExitStack() as pes:
                kb.cur_es = pes
                P.alloc_wstage(wst, wstf, nslot)
                fn()
                kb.barrier()
            kb.cur_es = None

        phase(do_convert, 3584, 3584, 2)
        alias(0, "A")
        phase(lambda: tok_body(P, 0, "A", False), 3584)
        global QSEL
        for l in range(NL):
            last = (l == NL - 1)
            alias(l, "CA")
            if last:
                MONO, QSEL = False, True
                for k_ in ("qal", "kbw", "cmpm", "cm", "addm", "vneg", "corr", "onsaT"):
                    dr[k_] = dr[k_ + "_q"]
                dr["out"] = dr["out_q"]
            phase(lambda: attn_body(P), 2048, 1024, 2)
            phase(lambda: tok_body(P, l, "CA", last), 3584)
        return P.finish()
    finally:
        MONO = False
        QSEL = False


def mono_consts():
    sl = _slopes()
    p = np.arange(128)
    q = np.arange(512)
    c = {}
    tabs = np.arange(SEQ).astype(np.float32)
    hi, mid, lo = _split3(-(sl[:, None] * tabs[None, :]))
    c["qal"] = np.ascontiguousarray(np.stack([hi, mid, lo], 0))
    kbs = np.zeros((128, 16, 64), np.float32)
    kbc = np.zeros((128, 16, 4), np.float32)
    for h in range(16):
        kbs[:, h, :] = sl[h] * (np.arange(64)[None, :] * 128 + p[:, None]).astype(np.float32)
        kbc[:, h, :] = sl[h] * (16 * (np.arange(4)[None, :] * 128 + p[:, None]) + 31).astype(np.float32)
    c["kbs"] = kbs.reshape(128, 1024)
    c["kbc"] = kbc.reshape(128, 64)
    kbw = np.zeros((16, 128, 8, 16), np.float32)
    for i in range(16):
        for w in range(8):
            ka = 512 * (i - 1) + w * 128 + p
            for h in range(16):
                kbw[i, :, w, h] = np.where(ka >= 0, sl[h] * ka.astype(np.float32), -30000.0)
    c["kbw"] = kbw.reshape(16, 128, 128)
    cmpm = np.zeros((128, 5, 512), np.float32)
    for d in range(5):
        cmpm[:, d, :] = np.where((16 * p[:, None] + 31 - 512 * d) <= q[None, :], 0.0, NEG)
    c["cmpm"] = cmpm.astype(BF)
    cm = np.zeros((128, 4, 512), np.float32)
    for r in range(4):
        cm[:, r, :] = np.where((128 * r + p[:, None]) <= q[None, :], 0.0, NEG)
    c["cm"] = cm.astype(BF)
    wm = np.zeros((128, 8, 512), np.float32)
    for w in range(8):
        dist = 512 + q[None, :] - 128 * w - p[:, None]
        wm[:, w, :] = np.where((dist >= 0) & (dist < 512), 0.0, NEG)
    c["wm"] = wm.astype(BF)
    addm = np.zeros((16, 128, 4, 128), np.float32)
    vneg = np.zeros((16, 128, 4, 128), np.float32)
    j = np.arange(128)
    for i in range(16):
        for qs in range(4):
            t = i * 512 + qs * 128 + p
            valid = (j[None, :] * 64) <= t[:, None]
            cur = t // 64
            forced = valid & ((j[None, :] == 0) | (j[None, :] == cur[:, None]) | (j[None, :] == cur[:, None] - 1))
            addm[i, :, qs, :] = np.where(forced, 8192.0, np.where(valid, 0.0, -8192.0))
            vneg[i, :, qs, :] = np.where(valid, 0.0, NEG)
    c["addm"] = addm.astype(BF)
    c["vneg"] = vneg.astype(BF)
    corr = np.ones((128, 4, 16), np.float32)
    for gi, w in enumerate(POOLW):
        corr[:, gi, :] = (w / np.minimum(np.arange(16) + 1.0, float(w)))[None, :]
    c["corr"] = corr
    c.update(shared_consts())
    return c


def kernel_mono(W, x, vecs):
    prog = _prog("MONO", build_mono)
    base = {"vecs": vecs}
    base.update(mono_consts())
    for l in range(NL):
        base["win_%d" % l] = tile_w(W["w_in"][l], WIN_STARTS)
        base["win_gn_%d" % l] = np.ascontiguousarray(W["w_in"][l][:, C_GN:C_GN + 48])
        for pre, a in (("f1_", "ffn1"), ("f2_", "ffn2")):
            base["%swg_%d" % (pre, l)] = tile_w(W[a + "_w_gate"][l])
            base["%swu_%d" % (pre, l)] = tile_w(W[a + "_w_up"][l])
            base["%swd_%d" % (pre, l)] = tile_w(W[a + "_w_down"][l])
        base["wpa_%d" % l] = tile_w(W["w_branch_pool"][l])
        base["wnb_%d" % l] = tile_w(W["w_branch_nsa"][l])
        base["wo_%d" % l] = tile_w(W["w_out"][l])
        base["poolw_%d" % l] = W["pool_w"][l]
        base["ck_w1_%d" % l] = tile_w(W["cmp_k_w1"][l])
        base["cv_w1_%d" % l] = tile_w(W["cmp_v_w1"][l])
        base["ck_w2_%d" % l] = W["cmp_k_w2"][l]
        base["cv_w2_%d" % l] = W["cmp_v_w2"][l]
        base["pecol_%d" % l] = np.ascontiguousarray(W["cmp_pos"][l].reshape(16, 2, 64).transpose(1, 2, 0).reshape(128, 16))
    cores = list(range(8))
    in_maps = []
    xT = [np.ascontiguousarray(x[b].T) for b in range(2)]
    for c in cores:
        b, cc = c % 2, c // 2
        m = dict(base)
        m["x_in"] = xT[b]
        cq = core_consts(cc)
        for k_ in ("qal", "kbw", "cmpm", "cm", "addm", "vneg"):
            m[k_ + "_q"] = cq[k_]
        selw = np.zeros((128, 4), np.float32)
        selw[:, cc] = 1.0
        m["selw"] = selw
        corr = np.ones((128, 4, 16), np.float32)
        if cc == 0:
            corr = base["corr"]
        m["corr_q"] = corr
        in_maps.append(m)
    res = run_bass_kernel_spmd(prog, in_maps, core_ids=cores).results
    out = np.zeros((2, SEQ, D), np.float32)
    for c in cores:
        out[c % 2, _tok_index(c // 2), :] = np.asarray(res[c]["out_q"]).T
    return out
```

```python
import numpy as np
import ml_dtypes
from contextlib import ExitStack
import concourse.bass as bass
import concourse.mybir as mybir
from concourse.bass_utils import run_bass_kernel_spmd

F32 = mybir.dt.float32
BF16 = mybir.dt.bfloat16
AF = mybir.ActivationFunctionType
ALU = mybir.AluOpType

D = 1024
DFF = 2816
NFC = DFF // 128
SEQ = 8192
NL = 2
NTOK = 2048
MONO = False
QSEL = False


def ntok():
    return SEQ if MONO else NTOK
TT = 512
INW = 4400
C_Q, C_KC, C_VC, C_KSL, C_VSL, C_KWN, C_VWN, C_GN, C_U, C_GM = 0, 1024, 1152, 1280, 1408, 1536, 1664, 1792, 1840, 2352
EPS = 1e-6
NEG = -16384.0
WIN_STARTS = [j * 128 for j in range(8)] + [C_KC, C_VC, C_KSL, C_VSL, C_KWN, C_VWN] + [C_U + j * 128 for j in range(4)] + [C_GM + j * 128 for j in range(16)]
WIN_IDX = {c: i for i, c in enumerate(WIN_STARTS)}


class Sem:
    def __init__(self, h, name):
        self.h = h
        self.name = name
        self.count = 0
        self.group = False


class Tok:
    __slots__ = ("sem", "val")

    def __init__(self, sem, val):
        self.sem = sem
        self.val = val


class Res:
    __slots__ = ("name", "w", "r", "excl")

    def __init__(self, name="", excl=False):
        self.name = name
        self.w = None
        self.r = {}
        self.excl = excl


class Eng:
    def __init__(self, name, h, sem, same_sync):
        self.name = name
        self.h = h
        self.sem = sem
        self.waited = {}
        self.pending = []
        self.same_sync = same_sync


class KB:
    def __init__(self, nc, es):
        self.nc = nc
        self.es = es
        self.sems = []
        self.pe = self._eng("pe", nc.tensor, False)
        self.act = self._eng("act", nc.scalar, True)
        self.dve = self._eng("dve", nc.vector, True)
        self.pool = self._eng("pool", nc.gpsimd, True)
        self.sp = self._eng("sp", nc.sync, False)
        self.engs = [self.pe, self.act, self.dve, self.pool, self.sp]
        self.n_inst = 0

    def new_sem(self, name):
        name = "%s_%d" % (name, len(self.sems))
        h = self.es.enter_context(self.nc.semaphore(name))
        s = Sem(h, name)
        self.sems.append(s)
        return s

    def _eng(self, name, h, same_sync):
        return Eng(name, h, self.new_sem("s_" + name), same_sync)

    def sb(self, name, shape, dtype, es=None):
        self.nsb = getattr(self, "nsb", 0) + 1
        return (es or getattr(self, "cur_es", None) or self.es).enter_context(self.nc.sbuf_tensor("sb%d_%s" % (self.nsb, name), shape, dtype))

    def ps(self, name, shape, dtype):
        return self.es.enter_context(self.nc.psum_tensor("pp_" + name, shape, dtype))

    def _wait(self, eng, tok):
        if tok is None:
            return
        if tok.sem is eng.sem and not eng.same_sync:
            return
        assert tok.val is not None, "waiting on unresolved token (%s)" % tok.sem.name
        val = tok.val
        if tok.sem.group:
            val = max(val, tok.sem.count)
        if eng.waited.get(tok.sem, 0) >= val:
            return
        eng.h.wait_ge(tok.sem.h, val)
        eng.waited[tok.sem] = val

    def _deps(self, eng, reads, writes):
        for r in reads:
            self._wait(eng, r.w)
        for w in writes:
            self._wait(eng, w.w)
            for t in w.r.values():
                self._wait(eng, t)

    def _mark(self, tok, reads, writes):
        for r in reads:
            r.r[tok.sem] = tok
        for w in writes:
            w.w = tok
            w.r = {}

    def op(self, eng, fn, reads=(), writes=(), sig=True):
        xr = [r for r in reads if r.excl]
        if xr:
            writes = list(writes) + xr
            reads = [r for r in reads if not r.excl]
        self._deps(eng, reads, writes)
        inst = fn()
        self.n_inst += 1
        if sig:
            eng.sem.count += 1
            inst.then_inc(eng.sem.h, 1)
            tok = Tok(eng.sem, eng.sem.count)
            for t in eng.pending:
                t.val = eng.sem.count
            eng.pending = []
        else:
            tok = Tok(eng.sem, None)
            eng.pending.append(tok)
        self._mark(tok, reads, writes)
        return tok

    def dma(self, sem, out, in_, reads=(), writes=(), eng=None, **kw):
        eng = eng or self.sp
        self._deps(eng, reads, writes)
        if sem.count > 0:
            self._wait(eng, Tok(sem, sem.count))
        inst = eng.h.dma_start(out=out, in_=in_, **kw)
        self.n_inst += 1
        sem.count += 16
        inst.then_inc(sem.h, 16)
        tok = Tok(sem, sem.count)
        self._mark(tok, reads, writes)
        return tok

    def barrier(self):
        for e in self.engs:
            assert not e.pending
            for s in self.sems:
                if s.count > 0 and not (s is e.sem):
                    self._wait(e, Tok(s, s.count))


class Prog:
    def __init__(self, dram_specs, WST=3584):
        self.nc = bass.Bass("TRN2", target_bir_lowering=False)
        self.es = ExitStack()
        self.kb = KB(self.nc, self.es)
        self.dr = {}
        self.dres = {}
        for name, (shape, dt, kind) in dram_specs.items():
            self.dr[name] = self.nc.dram_tensor(name, list(shape), dt, kind=kind).ap()
            self.dres[name] = Res("dram_" + name)
        self.out_names = [n for n, (_, _, k) in dram_specs.items() if k == "ExternalOutput"]
        kb = self.kb
        self.psf = [kb.ps("psf%d" % i, [128, 512], F32) for i in range(7)]
        self.psf_r = [Res("psf%d" % i, excl=True) for i in range(7)]
        self.psb = kb.ps("psb", [128, 1024], BF16)
        self.psb_r = Res("psb", excl=True)
        self.ps_rr = 0
        self.ones = kb.sb("ones", [128, 128], F32)
        self.ones_r = Res("ones")
        kb.op(kb.dve, lambda: self.nc.vector.memset(self.ones[:], 1.0 / D), writes=[self.ones_r])
        self.epsc = kb.sb("epsc", [128, 1], F32)
        kb.op(kb.dve, lambda: self.nc.vector.memset(self.epsc[:], EPS), writes=[self.ones_r])
        self.vecs = kb.sb("vecs", [128, 64], F32)
        self.vecs_r = Res("vecs")
        self.ldsem = kb.new_sem("ld_misc")
        self.ldsem.group = True
        kb.dma(self.ldsem, self.vecs[:], self.dr["vecs"][:, :], writes=[self.vecs_r])
        self.wsem = [kb.new_sem("wsem%d" % i) for i in range(4)]
        self.stsem = kb.new_sem("st_misc")
        if WST:
            self.alloc_wstage(WST)

    def alloc_wstage(self, WST, WSTF=None, nslot=2):
        kb = self.kb
        self.WST = WST
        self.WSTF = WSTF or WST
        self.nslot = nslot
        self.wst = [kb.sb("wst%d" % i, [128, self.WSTF], F32) for i in range(nslot)]
        self.wst_r = [Res("wst%d" % i) for i in range(nslot)]
        self.wbf = [kb.sb("wbf%d" % i, [128, self.WST], BF16) for i in range(nslot)]
        self.wbf_r = [Res("wbf%d" % i) for i in range(nslot)]
        self.wslot = 0

    def bank(self):
        i = self.ps_rr % 7
        self.ps_rr += 1
        return self.psf[i], self.psf_r[i]

    def load_w(self, pieces):
        kb, nc = self.kb, self.nc
        s = self.wslot
        self.wslot = (self.wslot + 1) % self.nslot
        off = 0
        foff = 0
        views = []
        for ap in pieces:
            if len(ap.shape) == 3:
                _, kc, n = ap.shape
                src = ap
            else:
                K, n = ap.shape
                kc = K // 128
                src = ap.rearrange("(k p) n -> p k n", p=128)
            sz = kc * n
            bview = self.wbf[s][:, off:off + sz].rearrange("p (k n) -> p k n", n=n)
            if ap.dtype == BF16:
                kb.dma(self.wsem[s], bview, src, writes=[self.wbf_r[s]])
            else:
                assert foff + sz <= self.WSTF
                dst = self.wst[s][:, foff:foff + sz].rearrange("p (k n) -> p k n", n=n)
                kb.dma(self.wsem[s], dst, src, writes=[self.wst_r[s]])
                a, b, fa = off, off + sz, foff
                kb.op(kb.pool, lambda a=a, b=b, fa=fa: nc.gpsimd.tensor_copy(out=self.wbf[s][:, a:b], in_=self.wst[s][:, fa:fa + (b - a)]),
                      reads=[self.wst_r[s]], writes=[self.wbf_r[s]])
                foff += sz
            views.append(bview)
            off += sz
        assert off <= self.WST
        return views, self.wbf_r[s]

    def convert_w(self, src, dst, dst_fn=None):
        kb, nc = self.kb, self.nc
        nch, _, kc, n = src.shape
        sz = kc * n
        if not hasattr(self, "cvsem"):
            self.cvsem = [kb.new_sem("cvs%d" % i) for i in range(4)]
            self.cv_rr = 0
        for j in range(nch):
            s = self.wslot
            self.wslot = (self.wslot + 1) % self.nslot
            kb.dma(self.wsem[s], self.wst[s][:, 0:sz].rearrange("p (k n) -> p k n", n=n), src[j], writes=[self.wst_r[s]])
            e = self.cv_rr % 3
            self.cv_rr += 1
            if e == 0:
                kb.op(kb.pool, lambda: nc.gpsimd.tensor_copy(out=self.wbf[s][:, 0:sz], in_=self.wst[s][:, 0:sz]), reads=[self.wst_r[s]], writes=[self.wbf_r[s]])
            elif e == 1:
                kb.op(kb.dve, lambda: nc.vector.tensor_copy(out=self.wbf[s][:, 0:sz], in_=self.wst[s][:, 0:sz]), reads=[self.wst_r[s]], writes=[self.wbf_r[s]])
            else:
                kb.op(kb.act, lambda: nc.scalar.copy(out=self.wbf[s][:, 0:sz], in_=self.wst[s][:, 0:sz]), reads=[self.wst_r[s]], writes=[self.wbf_r[s]])
            kb.dma(self.cvsem[s], dst_fn(j) if dst_fn else dst[j], self.wbf[s][:, 0:sz].rearrange("p (k n) -> p k n", n=n), reads=[self.wbf_r[s]])

    def select_tile(self, dst, dst_r, cands, stage, stage_r, sem, p0, p1):
        kb, nc = self.kb, self.nc
        first = True
        for c, src in enumerate(cands):
            if src is None:
                continue
            kb.dma(sem, stage, src, writes=[stage_r])
            sc_ = self.selw[p0:p1, c:c + 1]
            if first:
                kb.op(kb.dve, lambda: nc.vector.tensor_scalar(out=dst, in0=stage, scalar1=sc_, scalar2=None, op0=ALU.mult),
                      reads=[stage_r, self.selw_r], writes=[dst_r])
                first = False
            else:
                kb.op(kb.dve, lambda: nc.vector.scalar_tensor_tensor(out=dst, in0=stage, scalar=sc_, in1=dst, op0=ALU.mult, op1=ALU.add),
                      reads=[stage_r, self.selw_r, dst_r], writes=[dst_r])

    def vcol(self, c):
        return self.vecs[:, c:c + 1]

    def rmsnorm(self, x, x_r, h, h_r, n, gcol, sq, sq_r, rstd, rstd_r, out_f32=None, out_r=None):
        kb, nc = self.kb, self.nc
        for s0 in range(0, n, 512):
            ps, ps_r = self.bank()
            for c in range(8):
                k = c % 2
                kb.op(kb.act, lambda c=c, k=k: nc.scalar.activation(out=sq[k][:, :], in_=x[:, c, s0:s0 + 512], func=AF.Square),
                      reads=[x_r], writes=[sq_r[k]])
                kb.op(kb.pe, lambda c=c, k=k: nc.tensor.matmul(ps[:, :], lhsT=self.ones[:, :], rhs=sq[k][:, :], start=(c == 0), stop=(c == 7)),
                      reads=[sq_r[k], self.ones_r], writes=[ps_r], sig=True)
            kb.op(kb.act, lambda: nc.scalar.activation(out=rstd[:, s0:s0 + 512], in_=ps[:, :], func=AF.Ln, bias=self.epsc[:, 0:1]),
                  reads=[ps_r, self.ones_r], writes=[rstd_r])
            kb.op(kb.act, lambda: nc.scalar.activation(out=rstd[:, s0:s0 + 512], in_=rstd[:, s0:s0 + 512], func=AF.Exp, scale=-0.5),
                  reads=[rstd_r], writes=[rstd_r])
            for c in range(8):
                tgt = h if out_f32 is None else out_f32
                tgt_r = h_r if out_f32 is None else out_r
                kb.op(kb.dve, lambda c=c, tgt=tgt: nc.vector.scalar_tensor_tensor(
                    out=tgt[:, c, s0:s0 + 512], in0=x[:, c, s0:s0 + 512], scalar=self.vcol(gcol + c), in1=rstd[:, s0:s0 + 512],
                    op0=ALU.mult, op1=ALU.mult), reads=[x_r, rstd_r, self.vecs_r], writes=[tgt_r])

    def ffn(self, x, x_r, h, h_r, n, wg, wu, wd, aT, aT_r, sg, sg_r):
        kb, nc = self.kb, self.nc
        nsub = n // 512
        for fc in range(NFC):
            if wu is None:
                (wgu,), w_r = self.load_w([wg[fc]])
                wgb, wub = wgu[:, 0:8, :], wgu[:, 8:16, :]
            else:
                (wgb, wub), w_r = self.load_w([wg[fc], wu[fc]])
            for sub in range(nsub):
                s0 = sub * 512
                pg, pg_r = self.bank()
                pu, pu_r = self.bank()
                for c in range(8):
                    kb.op(kb.pe, lambda c=c: nc.tensor.matmul(pg[:, :], lhsT=wgb[:, c, :], rhs=h[:, c, s0:s0 + 512], start=(c == 0), stop=(c == 7)),
                          reads=[w_r, h_r], writes=[pg_r], sig=(c == 7))
                for c in range(8):
                    kb.op(kb.pe, lambda c=c: nc.tensor.matmul(pu[:, :], lhsT=wub[:, c, :], rhs=h[:, c, s0:s0 + 512], start=(c == 0), stop=(c == 7)),
                          reads=[w_r, h_r], writes=[pu_r], sig=(c == 7))
                k = (fc * nsub + sub) % 2
                kb.op(kb.act, lambda k=k: nc.scalar.activation(out=sg[k][:, :], in_=pg[:, :], func=AF.Silu), reads=[pg_r], writes=[sg_r[k]])
                kb.op(kb.dve, lambda k=k: nc.vector.tensor_tensor(out=aT[:, fc, s0:s0 + 512], in0=sg[k][:, :], in1=pu[:, :], op=ALU.mult),
                      reads=[sg_r[k], pu_r], writes=[aT_r[fc]])
        for dc in range(8):
            (wdb,), w_r = self.load_w([wd[dc]])
            for sub in range(nsub):
                s0 = sub * 512
                py, py_r = self.bank()
                for fc in range(NFC):
                    kb.op(kb.pe, lambda fc=fc: nc.tensor.matmul(py[:, :], lhsT=wdb[:, fc, :], rhs=aT[:, fc, s0:s0 + 512], start=(fc == 0), stop=(fc == NFC - 1)),
                          reads=[w_r, aT_r[fc]], writes=[py_r], sig=(fc == NFC - 1))
                kb.op(kb.dve, lambda: nc.vector.scalar_tensor_tensor(out=x[:, dc, s0:s0 + 512], in0=py[:, :], scalar=0.5, in1=x[:, dc, s0:s0 + 512],
                                                                      op0=ALU.mult, op1=ALU.add), reads=[py_r, x_r], writes=[x_r])

    def finish(self):
        kb = self.kb
        kb.barrier()
        self.es.close()
        return self.nc


def gain_layout(v):
    return np.ascontiguousarray(np.asarray(v, np.float32).reshape(-1, 128).T)


def vbase(l):
    return 28 * l


POOLW = (2, 4, 8, 16)


def tok_specs(l, mode, last):
    specs = {"xs_in": ((D, NTOK), F32, "ExternalInput"), "vecs": ((128, 64), F32, "ExternalInput")}

    def wspec(li, pre):
        specs[pre + "wg"] = ((NFC, 128, 8, 128), F32, "ExternalInput")
        specs[pre + "wu"] = ((NFC, 128, 8, 128), F32, "ExternalInput")
        specs[pre + "wd"] = ((8, 128, NFC, 128), F32, "ExternalInput")

    doA = (mode == "A") or (not last)
    if mode == "CA":
        wspec(l, "f2_")
        specs["win_c"] = ((len(WIN_STARTS), 128, 8, 128), F32, "ExternalInput")
        specs["wpa"] = ((8, 128, 4, 128), F32, "ExternalInput")
        specs["wnb"] = ((8, 128, 8, 128), F32, "ExternalInput")
        specs["wo"] = ((8, 128, 8, 128), F32, "ExternalInput")
        specs["poolw"] = ((4, 128, 128), F32, "ExternalInput")
        specs["h2T_in"] = ((D, NTOK), BF16, "ExternalInput")
        specs["onsaT"] = ((D, NTOK), BF16, "ExternalInput")
        specs["uext"] = ((512, 4, 528), F32, "ExternalInput")
        specs["corr"] = ((128, 4, 16), F32, "ExternalInput")
    if doA:
        wspec(l, "f1_")
        specs["win_a"] = ((len(WIN_STARTS), 128, 8, 128), F32, "ExternalInput")
        specs["h2T_out"] = ((D, NTOK), BF16, "ExternalOutput")
        specs["kvT_out"] = ((4, 128, NTOK), BF16, "ExternalOutput")
        specs["uT_out"] = ((512, NTOK), F32, "ExternalOutput")
        specs["vtok_out"] = ((NTOK, 256), BF16, "ExternalOutput")
        specs["xs_out"] = ((D, NTOK), F32, "ExternalOutput")
    else:
        specs["out"] = ((D, NTOK), F32, "ExternalOutput")
    return specs


def build_tok(l, mode, last):
    P = Prog(tok_specs(l, mode, last))
    tok_body(P, l, mode, last)
    return P.finish()


def build_bca(l, last):
    specs = attn_specs()
    ts = tok_specs(l, "CA", last)
    for k_ in ("h2T_in", "win_c", "onsaT", "vecs"):
        ts.pop(k_)
    specs.update(ts)
    specs["onsaT"] = ((D, NTOK), BF16, "Internal")
    P = Prog(specs, WST=None)
    P.dr["h2T_in"] = P.dr["h2T"]
    P.dr["win_c"] = P.dr["win"]
    kb = P.kb
    with ExitStack() as pes:
        kb.cur_es = pes
        P.alloc_wstage(2048)
        attn_body(P)
        kb.barrier()
    with ExitStack() as pes:
        kb.cur_es = pes
        P.alloc_wstage(3584)
        tok_body(P, l, "CA", last)
        kb.barrier()
    kb.cur_es = None
    return P.finish()


def tok_body(P, l, mode, last):
    kb, nc, dr = P.kb, P.nc, P.dr
    stq = kb.act if (MONO or QSEL) else kb.sp
    doA = (mode == "A") or (not last)
    x = kb.sb("x", [128, 8, TT], F32); x_r = Res("x")
    h = kb.sb("h", [128, 8, TT], BF16); h_r = Res("h")
    aT = kb.sb("aT", [128, NFC, TT], BF16); aT_r = [Res("aT%d" % i) for i in range(NFC)]
    sq = [kb.sb("sq%d" % i, [128, 512], F32) for i in range(2)]; sq_r = [Res() for i in range(2)]
    rstd = kb.sb("rstd", [128, TT], F32); rstd_r = Res()
    xsem = kb.new_sem("xsem")
    if QSEL:
        xstg2 = kb.sb("xstg", [128, 8 * TT], F32); xstg_r = Res("xstg")
        xstg = xstg2[:, :].rearrange("p (c n) -> p c n", n=TT)
    if mode == "CA":
        hsem = kb.new_sem("hsem"); osem = kb.new_sem("osem"); usem = kb.new_sem("usem")
        on = kb.sb("on", [128, 8, TT], BF16); on_r = Res("on")
        ue = kb.sb("ue", [128, 4, 528], F32); ue_r = Res("ue")
        sa = kb.sb("sa", [128, 528], F32); sa_r = Res("sa")
        sb_ = kb.sb("sbb", [128, 528], F32); sb_r = Res("sb")
        dl = kb.sb("dl", [128, 4, TT], BF16); dl_r = [Res() for _ in range(4)]
        opl = kb.sb("opl", [128, 4, TT], BF16); opl_r = Res("opl")
        mg = kb.sb("mg", [128, 8, TT], BF16); mg_r = [Res() for _ in range(8)]
        t1 = kb.sb("t1", [128, TT], F32); t1_r = Res()
        t2 = kb.sb("t2", [128, TT], F32); t2_r = Res()
        corr = kb.sb("corr", [128, 4, 16], F32); corr_r = Res()
        kb.dma(P.ldsem, corr[:], dr["corr"][:, :, :], writes=[corr_r])
    if doA:
        kvst = [kb.sb("kvst%d" % i, [128, TT], BF16) for i in range(2)]; kvst_r = [Res() for _ in range(2)]
        ust = [kb.sb("ust%d" % i, [128, TT], F32) for i in range(2)]; ust_r = [Res() for _ in range(2)]
        vst = kb.sb("vst", [128, 4, 256], BF16); vst_r = Res()
        osems = [kb.new_sem("kvo%d" % i) for i in range(2)]
        usems = [kb.new_sem("uo%d" % i) for i in range(2)]
        vsem = kb.new_sem("vo")
        hosem = kb.new_sem("ho")

    def colsl(ap, t0):
        return ap[:, t0:t0 + TT].rearrange("(c p) n -> p c n", p=128)

    for t in range(ntok() // TT):
        t0 = t * TT
        if QSEL:
            P.select_tile(x[:, :, :], x_r, [colsl(dr["xs_in"], (4 * t + c_) * 512) for c_ in range(4)], xstg[:, :, :], xstg_r, xsem, 0, 128)
        else:
            kb.dma(xsem, x[:, :, :], colsl(dr["xs_in"], t0), writes=[x_r])
        la = l
        if mode == "CA":
            vb = vbase(l)
            if QSEL:
                P.select_tile(h[:, :, :], h_r, [colsl(dr["h2T_in"], (4 * t + c_) * 512) for c_ in range(4)],
                              on[:, :, :], on_r, hsem, 0, 128)
            else:
                kb.dma(hsem, h[:, :, :], colsl(dr["h2T_in"], t0), writes=[h_r])
            kb.dma(osem, on[:, :, :], colsl(dr["onsaT"], t0), writes=[on_r])
            if QSEL:
                ustg = xstg2[:, 0:2112].rearrange("p (g n) -> p g n", n=528)
                cands = []
                for c_ in range(4):
                    a0 = (4 * t + c_) * 512
                    cands.append(dr["uT_in"][:, a0 - 16:a0 + 512].rearrange("(g p) n -> p g n", p=128) if a0 > 0 else None)
                if t == 0:
                    kb.op(kb.pool, lambda: nc.gpsimd.memset(ustg[:, :, 0:16], 0.0), writes=[xstg_r])
                    kb.dma(usem, ustg[:, :, 16:528], dr["uT_in"][:, 0:512].rearrange("(g p) n -> p g n", p=128), writes=[xstg_r])
                    kb.op(kb.dve, lambda: nc.vector.tensor_scalar(out=ue[:, :, :], in0=ustg, scalar1=P.selw[:, 0:1], scalar2=None, op0=ALU.mult),
                          reads=[xstg_r, P.selw_r], writes=[ue_r])
                    for c_ in range(1, 4):
                        kb.dma(usem, ustg, cands[c_], writes=[xstg_r])
                        kb.op(kb.dve, lambda: nc.vector.scalar_tensor_tensor(out=ue[:, :, :], in0=ustg, scalar=P.selw[:, c_:c_ + 1], in1=ue[:, :, :], op0=ALU.mult, op1=ALU.add),
                              reads=[xstg_r, P.selw_r, ue_r], writes=[ue_r])
                else:
                    P.select_tile(ue[:, :, :], ue_r, cands, ustg, xstg_r, usem, 0, 128)
            elif MONO:
                kb.dma(usem, ue[:, :, 16:528], dr["uT_in"][:, t0:t0 + 512].rearrange("(g p) n -> p g n", p=128), writes=[ue_r])
                if t == 0:
                    kb.op(kb.pool, lambda: nc.gpsimd.memset(ue[:, :, 0:16], 0.0), writes=[ue_r])
                else:
                    kb.dma(usem, ue[:, :, 0:16], dr["uT_in"][:, t0 - 16:t0].rearrange("(g p) n -> p g n", p=128), writes=[ue_r])
            else:
                kb.dma(usem, ue[:, :, :], dr["uext"][:, t, :].rearrange("(g p) n -> p g n", p=128), writes=[ue_r])
            for gi, w in enumerate(POOLW):
                cur, cur_r = None, None
                sh = 1
                src = ue[:, gi, :]
                src_r = ue_r
                bufs = [(sa, sa_r), (sb_, sb_r)]
                bi = 0
                while sh < w:
                    dst, dst_r = bufs[bi]
                    bi ^= 1
                    lo = 2 * sh - 1
                    kb.op(kb.dve, lambda src=src, dst=dst, lo=lo, sh=sh: nc.vector.tensor_tensor(
                        out=dst[:, lo:528], in0=src[:, lo:528], in1=src[:, lo - sh:528 - sh], op=ALU.add),
                        reads=[src_r], writes=[dst_r])
                    src, src_r = dst, dst_r
                    sh *= 2
                kb.op(kb.dve, lambda src=src, w=w: nc.vector.tensor_scalar(out=src[:, 16:528], in0=src[:, 16:528], scalar1=1.0 / w, scalar2=None, op0=ALU.mult),
                      reads=[src_r], writes=[src_r])
                if t == 0:
                    kb.op(kb.dve, lambda src=src, gi=gi: nc.vector.tensor_tensor(out=src[:, 16:32], in0=src[:, 16:32], in1=corr[:, gi, :], op=ALU.mult),
                          reads=[src_r, corr_r], writes=[src_r])
                kb.op(kb.dve, lambda src=src, gi=gi: nc.vector.tensor_tensor(out=dl[:, gi, :], in0=src[:, 16:528], in1=ue[:, gi, 16:528], op=ALU.subtract),
                      reads=[src_r, ue_r], writes=[dl_r[gi]])
            for gi in range(4):
                (pw,), w_r = P.load_w([dr["poolw"][gi]])
                ps, ps_r = P.bank()
                kb.op(kb.pe, lambda: nc.tensor.matmul(ps[:, :], lhsT=pw[:, 0, :], rhs=dl[:, gi, :], start=True, stop=True),
                      reads=[w_r, dl_r[gi]], writes=[ps_r])
                kb.op(kb.dve, lambda: nc.vector.tensor_scalar(out=opl[:, gi, :], in0=ps[:, :], scalar1=P.vcol(vb + 24 + gi), scalar2=None, op0=ALU.mult),
                      reads=[ps_r, P.vecs_r], writes=[opl_r])
            for dc in range(8):
                (wpa, wnb, wgp, wga), w_r = P.load_w([dr["wpa"][dc], dr["wnb"][dc],
                                                      dr["win_c"][WIN_IDX[C_GM + dc * 128]],
                                                      dr["win_c"][WIN_IDX[C_GM + 1024 + dc * 128]]])
                pa, pa_r = P.bank(); pb, pb_r = P.bank(); pgp, pgp_r = P.bank(); pga, pga_r = P.bank()
                for c in range(4):
                    kb.op(kb.pe, lambda c=c: nc.tensor.matmul(pa[:, :], lhsT=wpa[:, c, :], rhs=opl[:, c, :], start=(c == 0), stop=(c == 3)),
                          reads=[w_r, opl_r], writes=[pa_r], sig=(c == 3))
                for c in range(8):
                    kb.op(kb.pe, lambda c=c: nc.tensor.matmul(pb[:, :], lhsT=wnb[:, c, :], rhs=on[:, c, :], start=(c == 0), stop=(c == 7)),
                          reads=[w_r, on_r], writes=[pb_r], sig=(c == 7))
                for c in range(8):
                    kb.op(kb.pe, lambda c=c: nc.tensor.matmul(pgp[:, :], lhsT=wgp[:, c, :], rhs=h[:, c, :], start=(c == 0), stop=(c == 7)),
                          reads=[w_r, h_r], writes=[pgp_r], sig=(c == 7))
                for c in range(8):
                    kb.op(kb.pe, lambda c=c: nc.tensor.matmul(pga[:, :], lhsT=wga[:, c, :], rhs=h[:, c, :], start=(c == 0), stop=(c == 7)),
                          reads=[w_r, h_r], writes=[pga_r], sig=(c == 7))
                kb.op(kb.act, lambda: nc.scalar.activation(out=t1[:, :], in_=pgp[:, :], func=AF.Sigmoid), reads=[pgp_r], writes=[t1_r])
                kb.op(kb.act, lambda: nc.scalar.activation(out=t2[:, :], in_=pga[:, :], func=AF.Sigmoid), reads=[pga_r], writes=[t2_r])
                kb.op(kb.dve, lambda: nc.vector.tensor_tensor(out=t1[:, :], in0=t1[:, :], in1=pa[:, :], op=ALU.mult), reads=[t1_r, pa_r], writes=[t1_r])
                kb.op(kb.dve, lambda: nc.vector.tensor_tensor(out=t2[:, :], in0=t2[:, :], in1=pb[:, :], op=ALU.mult), reads=[t2_r, pb_r], writes=[t2_r])
                kb.op(kb.dve, lambda: nc.vector.tensor_tensor(out=mg[:, dc, :], in0=t1[:, :], in1=t2[:, :], op=ALU.add), reads=[t1_r, t2_r], writes=[mg_r[dc]])
            for dc in range(8):
                (wo,), w_r = P.load_w([dr["wo"][dc]])
                pz, pz_r = P.bank()
                for c in range(8):
                    kb.op(kb.pe, lambda c=c: nc.tensor.matmul(pz[:, :], lhsT=wo[:, c, :], rhs=mg[:, c, :], start=(c == 0), stop=(c == 7)),
                          reads=[w_r, mg_r[c]], writes=[pz_r], sig=(c == 7))
                kb.op(kb.dve, lambda: nc.vector.tensor_tensor(out=x[:, dc, :], in0=x[:, dc, :], in1=pz[:, :], op=ALU.add), reads=[x_r, pz_r], writes=[x_r])
            P.rmsnorm(x, x_r, h, h_r, TT, vb + 16, sq, sq_r, rstd, rstd_r)
            P.ffn(x, x_r, h, h_r, TT, dr["f2_wg"], dr["f2_wu"], dr["f2_wd"], aT, aT_r, sq, sq_r)
            la = l + 1
            if last:
                P.rmsnorm(x, x_r, None, None, TT, 56, sq, sq_r, rstd, rstd_r, out_f32=x, out_r=x_r)
                kb.dma(P.stsem, colsl(dr["out"], t0), x[:, :, :], reads=[x_r], eng=stq)
                continue
        vb = vbase(la)
        P.rmsnorm(x, x_r, h, h_r, TT, vb + 0, sq, sq_r, rstd, rstd_r)
        P.ffn(x, x_r, h, h_r, TT, dr["f1_wg"], dr["f1_wu"], dr["f1_wd"], aT, aT_r, sq, sq_r)
        kb.dma(P.stsem, colsl(dr["xs_out"], t0), x[:, :, :], reads=[x_r], eng=stq)
        P.rmsnorm(x, x_r, h, h_r, TT, vb + 8, sq, sq_r, rstd, rstd_r)
        kb.dma(hosem, colsl(dr["h2T_out"], t0), h[:, :, :], reads=[h_r], eng=stq)
        W = dr["win_a"]
        for j, c0 in enumerate((C_KC, C_VC, C_KSL, C_KWN)):
            (wc,), w_r = P.load_w([W[WIN_IDX[c0]]])
            ps, ps_r = P.bank()
            for c in range(8):
                kb.op(kb.pe, lambda c=c: nc.tensor.matmul(ps[:, :], lhsT=wc[:, c, :], rhs=h[:, c, :], start=(c == 0), stop=(c == 7)),
                      reads=[w_r, h_r], writes=[ps_r], sig=(c == 7))
            k = j % 2
            kb.op(kb.act, lambda k=k: nc.scalar.copy(out=kvst[k][:, :], in_=ps[:, :]), reads=[ps_r], writes=[kvst_r[k]])
            kb.dma(osems[k], dr["kvT_out"][j, :, t0:t0 + TT], kvst[k][:, :], reads=[kvst_r[k]], eng=stq)
        for j in range(4):
            (wc,), w_r = P.load_w([W[WIN_IDX[C_U + j * 128]]])
            ps, ps_r = P.bank()
            for c in range(8):
                kb.op(kb.pe, lambda c=c: nc.tensor.matmul(ps[:, :], lhsT=wc[:, c, :], rhs=h[:, c, :], start=(c == 0), stop=(c == 7)),
                      reads=[w_r, h_r], writes=[ps_r], sig=(c == 7))
            k = j % 2
            kb.op(kb.act, lambda k=k: nc.scalar.copy(out=ust[k][:, :], in_=ps[:, :]), reads=[ps_r], writes=[ust_r[k]])
            kb.dma(usems[k], dr["uT_out"][j * 128:(j + 1) * 128, t0:t0 + TT], ust[k][:, :], reads=[ust_r[k]], eng=stq)
        (wv1, wv2), w_r = P.load_w([W[WIN_IDX[C_VSL]], W[WIN_IDX[C_VWN]]])
        for tb in range(TT // 128):
            ps, ps_r = P.bank()
            for wi, wv in enumerate((wv1, wv2)):
                for c in range(8):
                    kb.op(kb.pe, lambda c=c, wv=wv, wi=wi: nc.tensor.matmul(ps[:, wi * 128:(wi + 1) * 128], lhsT=h[:, c, tb * 128:(tb + 1) * 128], rhs=wv[:, c, :],
                                                                         start=(c == 0), stop=(c == 7)),
                          reads=[w_r, h_r], writes=[ps_r], sig=(c == 7))
            kb.op(kb.act, lambda: nc.scalar.copy(out=vst[:, tb, :], in_=ps[:, 0:256]), reads=[ps_r], writes=[vst_r])
        kb.dma(vsem, dr["vtok_out"][t0:t0 + TT, :].rearrange("(tb p) c -> p tb c", p=128), vst[:, :, :], reads=[vst_r], eng=stq)


def attn_specs():
    BI = "ExternalInput"
    specs = {
        "vecs": ((128, 64), F32, BI),
        "h2T": ((D, NTOK), BF16, BI),
        "kvall": ((4, 128, SEQ + 32), BF16, BI),
        "vall": ((SEQ, 256), BF16, BI),
        "kwin": ((128, 4, 1024), BF16, BI),
        "vwin": ((4, 1024, 128), BF16, BI),
        "win": ((len(WIN_STARTS), 128, 8, 128), F32, BI), "win_gn": ((D, 48), F32, BI),
        "ck_w1": ((2, 128, 16, 128), F32, BI), "ck_w2": ((256, 64), F32, BI),
        "cv_w1": ((2, 128, 16, 128), F32, BI), "cv_w2": ((256, 64), F32, BI),
        "pecol": ((128, 16), F32, BI),
        "qal": ((3, 16, NTOK), BF16, BI),
        "kbs": ((128, 16 * 64), F32, BI), "kbc": ((128, 64), F32, BI), "kbw": ((128, 4 * 8 * 16), F32, BI),
        "cmpm": ((128, 2, 512), BF16, BI), "cm": ((128, 16, 512), BF16, BI), "wm": ((128, 8, 512), BF16, BI),
        "addm": ((128, 16, 128), BF16, BI), "vneg": ((128, 16, 128), BF16, BI),
        "karows": ((64, SEQ), BF16, BI), "ones3": ((3, 1024), BF16, BI), "ovl": ((128, 4, 128), BF16, BI), "identb": ((128, 128), BF16, BI),
        "onsaT": ((D, NTOK), BF16, "ExternalOutput"),
    }
    return specs


def build_attn():
    P = Prog(attn_specs(), WST=2048)
    attn_body(P)
    return P.finish()


def attn_body(P):
    kb, nc, dr = P.kb, P.nc, P.dr
    ld = P.ldsem

    def const(name, shape, dt, src):
        t = kb.sb(name, shape, dt)
        r = Res(name)
        kb.dma(ld, t[:], src, writes=[r])
        return t, r

    CM, CM_r = const("CM", [128, 4 if MONO else 16, 512], BF16, dr["cm"][:, :, :])
    WM, WM_r = const("WM", [128, 8, 512], BF16, dr["wm"][:, :, :])
    CPM, CPM_r = const("CPM", [128, 5 if MONO else 2, 512], BF16, dr["cmpm"][:, :, :])
    if MONO:
        ADM = kb.sb("ADM", [128, 4, 128], BF16); ADM_r = Res("ADM")
        VNG = kb.sb("VNG", [128, 4, 128], BF16); VNG_r = Res("VNG")
        KBW = kb.sb("KBW", [128, 128], F32); KBW_r = Res("KBW")
        pisem = kb.new_sem("pisem")
    else:
        ADM, ADM_r = const("ADM", [128, 16, 128], BF16, dr["addm"][:, :, :])
        VNG, VNG_r = const("VNG", [128, 16, 128], BF16, dr["vneg"][:, :, :])
        KBW, KBW_r = const("KBW", [128, 512], F32, dr["kbw"][:, :])
    KBS, KBS_r = const("KBS", [128, 1024], F32, dr["kbs"][:, :])
    KBC, KBC_r = const("KBC", [128, 64], F32, dr["kbc"][:, :])
    IDB, IDB_r = const("IDB", [128, 128], BF16, dr["identb"][:, :])
    PEC, PEC_r = const("PEC", [128, 16], F32, dr["pecol"][:, :])

    KA = kb.sb("KA", [128, SEQ], BF16); KA_r = Res("KA")
    VA = kb.sb("VA", [128, 64, 65], BF16); VA_r = Res("VA")
    KW = kb.sb("KW", [128, 1024], BF16); KW_r = Res("KW")
    VW = kb.sb("VW", [128, 8, 65], BF16); VW_r = Res("VW")
    KC = kb.sb("KC", [128, 2, 512], BF16); KC_r = Res("KC")
    VC = kb.sb("VC", [128, 4, 2, 193], BF16); VC_r = Res("VC")
    kb.op(kb.pool, lambda: nc.gpsimd.memset(KW[0:64, :], 0.0), writes=[KW_r])
    kb.op(kb.pool, lambda: nc.gpsimd.memset(VW[:, :, 0:64], 0.0), writes=[VW_r])
    kb.dma(ld, KA[64:128, :], dr["karows"][:, :], writes=[KA_r])
    kb.op(kb.pool, lambda: nc.gpsimd.memset(KW[64:128, :], 0.0), writes=[KW_r])
    kb.op(kb.pool, lambda: nc.gpsimd.memset(KC[64:128, :, :], 0.0), writes=[KC_r])
    kb.dma(ld, KW[124:127, :], dr["ones3"][:, :], writes=[KW_r])
    kb.dma(ld, KC[124:127, :, :], dr["ones3"][:, :].rearrange("r (g n) -> r g n", n=512), writes=[KC_r])
    kb.op(kb.pool, lambda: nc.gpsimd.memset(VA[:, :, 64:65], 1.0), writes=[VA_r])
    kb.op(kb.pool, lambda: nc.gpsimd.memset(VW[:, :, 64:65], 1.0), writes=[VW_r])
    kb.op(kb.pool, lambda: nc.gpsimd.memset(VC[:, :, :, 64:65], 1.0), writes=[VC_r])
    for g in range(2):
        kb.dma(ld, VC[:, :, g, 65:193], dr["ovl"][:, :, :], writes=[VC_r])

    ces = ExitStack()
    KC2 = kb.sb("KC2", [128, SEQ + 16], BF16, es=ces); KC2_r = Res("KC2")
    zb = kb.sb("zb", [128, 512], F32, es=ces); zb_r = Res()
    s2 = kb.sb("s2", [128, 512], F32, es=ces); s2_r = Res()
    hid = kb.sb("hid", [128, 2, 512], BF16, es=ces); hid_r = [Res(), Res()]
    pecb = kb.sb("pecb", [128, 16], BF16, es=ces); pecb_r = Res()
    bj = kb.sb("bj", [128, 1], F32, es=ces); bj_r = Res()
    kb.op(kb.dve, lambda: nc.vector.tensor_copy(out=pecb[:, :], in_=PEC[:, :]), reads=[PEC_r], writes=[pecb_r])
    c2sem = kb.new_sem("c2sem")
    if MONO or QSEL:
        kb.op(kb.pool, lambda: nc.gpsimd.memset(KC2[0:64, SEQ:SEQ + 16], 0.0), writes=[KC2_r])
        kb.op(kb.pool, lambda: nc.gpsimd.memset(KC2[64:128, SEQ - 1:SEQ + 16], 0.0), writes=[KC2_r])
    for kv in range(2):
        w1 = dr["ck_w1" if kv == 0 else "cv_w1"]
        w2 = dr["ck_w2" if kv == 0 else "cv_w2"]
        for g in range(2):
            if MONO or QSEL:
                kb.dma(c2sem, KC2[0:64, 0:SEQ], dr["kvall"][kv, g * 64:(g + 1) * 64, 0:SEQ], writes=[KC2_r])
                kb.dma(c2sem, KC2[64:128, 0:SEQ - 1], dr["kvall"][kv, g * 64:(g + 1) * 64, 1:SEQ], writes=[KC2_r])
            else:
                kb.dma(c2sem, KC2[0:64, :], dr["kvall"][kv, g * 64:(g + 1) * 64, 0:SEQ + 16], writes=[KC2_r])
                kb.dma(c2sem, KC2[64:128, :], dr["kvall"][kv, g * 64:(g + 1) * 64, 1:SEQ + 17], writes=[KC2_r])
            for jc in range(2):
                (w1v,), w_r = P.load_w([w1[jc]])
                pb_, pb_r = P.bank()
                for lp in range(16):
                    kb.op(kb.pe, lambda lp=lp: nc.tensor.matmul(pb_[:, 0:1], lhsT=w1v[:, lp, :], rhs=pecb[:, lp:lp + 1], start=(lp == 0), stop=(lp == 15)),
                          reads=[w_r, pecb_r], writes=[pb_r], sig=(lp == 15))
                kb.op(kb.act, lambda: nc.scalar.copy(out=bj[:, :], in_=pb_[:, 0:1]), reads=[pb_r], writes=[bj_r])
                ps, ps_r = P.bank()
                for lp in range(16):
                    kb.op(kb.pe, lambda lp=lp: nc.tensor.matmul(ps[:, :], lhsT=w1v[:, lp, :], rhs=KC2[:, 2 * lp:2 * lp + 16 * 511 + 1:16], start=(lp == 0), stop=(lp == 15)),
                          reads=[w_r, KC2_r], writes=[ps_r], sig=(lp == 15))
                kb.op(kb.act, lambda: nc.scalar.activation(out=zb[:, :], in_=ps[:, :], func=AF.Identity, bias=bj[:, 0:1]), reads=[ps_r, bj_r], writes=[zb_r])
                kb.op(kb.act, lambda: nc.scalar.activation(out=s2[:, :], in_=zb[:, :], func=AF.Square), reads=[zb_r], writes=[s2_r])
                kb.op(kb.dve, lambda: nc.vector.tensor_scalar(out=s2[:, :], in0=s2[:, :], scalar1=0.044715, scalar2=1.0, op0=ALU.mult, op1=ALU.add), reads=[s2_r], writes=[s2_r])
                kb.op(kb.dve, lambda: nc.vector.tensor_tensor(out=s2[:, :], in0=s2[:, :], in1=zb[:, :], op=ALU.mult), reads=[s2_r, zb_r], writes=[s2_r])
                kb.op(kb.act, lambda: nc.scalar.activation(out=s2[:, :], in_=s2[:, :], func=AF.Sigmoid, scale=1.5957691216057308), reads=[s2_r], writes=[s2_r])
                kb.op(kb.dve, lambda jc=jc: nc.vector.tensor_tensor(out=hid[:, jc, :], in0=s2[:, :], in1=zb[:, :], op=ALU.mult), reads=[s2_r, zb_r], writes=[hid_r[jc]])
            (w2v,), w_r = P.load_w([w2[:, :]])
            if kv == 0:
                ps, ps_r = P.bank()
                for jc in range(2):
                    kb.op(kb.pe, lambda jc=jc: nc.tensor.matmul(ps[0:64, :], lhsT=w2v[:, jc, :], rhs=hid[:, jc, :], start=(jc == 0), stop=(jc == 1)),
                          reads=[w_r, hid_r[jc]], writes=[ps_r], sig=(jc == 1))
                kb.op(kb.act, lambda g=g: nc.scalar.copy(out=KC[0:64, g, :], in_=ps[0:64, :]), reads=[ps_r], writes=[KC_r])
            else:
                for nt in range(4):
                    ps, ps_r = P.bank()
                    for jc in range(2):
                        kb.op(kb.pe, lambda jc=jc, nt=nt: nc.tensor.matmul(ps[:, 0:64], lhsT=hid[:, jc, nt * 128:(nt + 1) * 128], rhs=w2v[:, jc, :], start=(jc == 0), stop=(jc == 1)),
                              reads=[w_r, hid_r[jc]], writes=[ps_r], sig=(jc == 1))
                    kb.op(kb.act, lambda g=g, nt=nt: nc.scalar.copy(out=VC[:, nt, g, 0:64], in_=ps[:, 0:64]), reads=[ps_r], writes=[VC_r])
    kb.barrier()
    ces.close()

    Q = kb.sb("Q", [128, 3, 8, 512], BF16); Q_r = [Res("Q%d" % i) for i in range(8)]
    kb.op(kb.pool, lambda: nc.gpsimd.memset(Q[64:128, :, :, :], 0.0), writes=Q_r)
    hT = kb.sb("hT", [128, 8, 512], BF16); hT_r = Res("hT")
    if QSEL:
        qstg = kb.sb("qstg", [128, 8, 512], BF16); qstg_r = Res("qstg")
    gat = kb.sb("gat", [128, 4, 48], F32); gat_r = Res("gat")
    cst = kb.sb("cst", [128, 4, 8, 64], F32); cst_r = [Res() for _ in range(8)]
    crd = kb.sb("crd", [128, 4, 8], F32); crd_r = [Res() for _ in range(8)]
    sc = kb.sb("sc", [128, 4, 128], F32); sc_r = Res("sc")
    pT = [kb.sb("pT%d" % i, [128, 512], BF16) for i in range(4)]; pT_r = [Res() for _ in range(4)]
    tmp = [kb.sb("tmp%d" % i, [128, 512], F32) for i in range(2)]; tmp_r = [Res() for _ in range(2)]
    onb = kb.sb("onb", [128, 4, 1024], BF16); onb_r = Res("onb")
    ost = [kb.sb("ost%d" % i, [128, 512], BF16) for i in range(2)]; ost_r = [Res() for _ in range(2)]
    smb = kb.sb("smb", [128, 4, 128], F32); smb_r = [Res() for _ in range(4)]
    wa = kb.sb("wa", [128, 4, 128], F32); wa_r = [Res() for _ in range(4)]
    wb = kb.sb("wb", [128, 4, 128], F32); wb_r = [Res() for _ in range(4)]
    m8 = kb.sb("m8", [128, 4, 8], F32); m8_r = [Res() for _ in range(4)]
    m8b = kb.sb("m8b", [128, 4, 8], F32); m8b_r = [Res() for _ in range(4)]
    nqv = kb.sb("nqv", [128, 4, 3, 128], BF16); nq_r = Res()
    kb.op(kb.pool, lambda: nc.gpsimd.memset(nqv[:, :, :, :], 0.0), writes=[nq_r])
    dd = kb.sb("dd", [128, 2, 4], F32); dd_r = Res()
    cf = kb.sb("cf", [128, 3, 4], F32); cf_r = Res()
    oh = kb.sb("oh", [128, 64], F32); oh_r = Res()
    hsem = kb.new_sem("hsem"); ksem = kb.new_sem("ksem"); vsem = kb.new_sem("vsem")
    kwsem = kb.new_sem("kwsem"); vwsem = kb.new_sem("vwsem"); qsem = kb.new_sem("qsem")
    osems = [kb.new_sem("os%d" % i) for i in range(2)]
    W = dr["win"]
    st = {"s": 0, "p": 0, "t": 0}

    def sbank():
        k = st["s"] % 3
        st["s"] += 1
        return P.psf[k], P.psf_r[k]

    def pbuf():
        k = st["p"] % 4
        st["p"] += 1
        return pT[k], pT_r[k]

    def tbuf():
        k = st["t"] % 2
        st["t"] += 1
        return tmp[k], tmp_r[k]

    def exp_tile(ps, ps_r, bias_ap, bias_r, mask_ap, mask_r, qa=0, qb=512):
        p, p_r = pbuf()
        if mask_ap is not None:
            tm, tm_r = tbuf()
            kb.op(kb.dve, lambda: nc.vector.tensor_tensor(out=tm[:, qa:qb], in0=ps[:, qa:qb], in1=mask_ap[:, qa:qb], op=ALU.add), reads=[ps_r, mask_r], writes=[tm_r])
            kb.op(kb.act, lambda: nc.scalar.activation(out=p[:, qa:qb], in_=tm[:, qa:qb], func=AF.Exp, bias=bias_ap), reads=[tm_r, bias_r], writes=[p_r])
        else:
            kb.op(kb.act, lambda: nc.scalar.activation(out=p[:, qa:qb], in_=ps[:, qa:qb], func=AF.Exp, bias=bias_ap), reads=[ps_r, bias_r], writes=[p_r])
        return p, p_r

    NI = ntok() // 512
    KT0 = 4 if MONO else 16
    for g in range(2):
        kb.dma(ksem, KA[0:64, :], dr["kvall"][2, g * 64:(g + 1) * 64, 0:SEQ], writes=[KA_r])
        for kq in range(16):
            kb.dma(vsem, VA[:, kq * 4:(kq + 1) * 4, 0:64],
                   dr["vall"][kq * 512:(kq + 1) * 512, g * 64:(g + 1) * 64].rearrange("(kt p) d -> p kt d", p=128), writes=[VA_r])
        for i in range(NI):
            q0 = i * 512
            if MONO:
                kb.dma(pisem, ADM[:, :, :], dr["addm"][i], writes=[ADM_r])
                kb.dma(pisem, VNG[:, :, :], dr["vneg"][i], writes=[VNG_r])
                kb.dma(pisem, KBW[:, :], dr["kbw"][i], writes=[KBW_r])
                cmp_tiles = [(nt, (i - 4 * nt) if (i - 4 * nt) <= 4 else None) for nt in range((32 * (i + 1) - 2) // 128 + 1)]
            else:
                cmp_tiles = [(nt, (nt - i + 1) if nt - i >= -1 else None) for nt in range(i + 1)]
            if QSEL:
                P.select_tile(hT[:, :, :], hT_r, [dr["h2T"][:, (4 * i + c_) * 512:(4 * i + c_ + 1) * 512].rearrange("(c p) n -> p c n", p=128) for c_ in range(4)],
                              qstg[:, :, :], qstg_r, hsem, 0, 128)
            else:
                kb.dma(hsem, hT[:, :, :], dr["h2T"][:, q0:q0 + 512].rearrange("(c p) n -> p c n", p=128), writes=[hT_r])
            (wgn,), w_r = P.load_w([dr["win_gn"][:, :]])
            for qs in range(4):
                ps, ps_r = sbank()
                for c in range(8):
                    kb.op(kb.pe, lambda c=c: nc.tensor.matmul(ps[:, 0:48], lhsT=hT[:, c, qs * 128:(qs + 1) * 128], rhs=wgn[:, c, :], start=(c == 0), stop=(c == 7)),
                          reads=[w_r, hT_r], writes=[ps_r], sig=(c == 7))
                kb.op(kb.act, lambda: nc.scalar.activation(out=gat[:, qs, :], in_=ps[:, 0:48], func=AF.Sigmoid), reads=[ps_r], writes=[gat_r])
            if MONO:
                klo = max(q0 - 512, 0)
                kb.dma(kwsem, KW[0:64, 1024 - (q0 + 512 - klo):1024], dr["kvall"][3, g * 64:(g + 1) * 64, klo:q0 + 512], writes=[KW_r])
                w0 = 8 - (q0 + 512 - klo) // 128
                kb.dma(vwsem, VW[:, w0:8, 0:64], dr["vall"][klo:q0 + 512, 128 + g * 64:128 + (g + 1) * 64].rearrange("(w p) d -> p w d", p=128), writes=[VW_r])
            elif QSEL:
                for half in range(2):
                    sts = [4 * i + c_ - 1 + half for c_ in range(4)]
                    P.select_tile(KW[0:64, half * 512:(half + 1) * 512], KW_r,
                                  [dr["kvall"][3, g * 64:(g + 1) * 64, st_ * 512:(st_ + 1) * 512] if st_ >= 0 else None for st_ in sts],
                                  qstg[0:64, 0, :], qstg_r, kwsem, 0, 64)
                    P.select_tile(VW[:, half * 4:(half + 1) * 4, 0:64], VW_r,
                                  [dr["vall"][st_ * 512:(st_ + 1) * 512, 128 + g * 64:128 + (g + 1) * 64].rearrange("(w p) d -> p w d", p=128) if st_ >= 0 else None for st_ in sts],
                                  qstg[:, 1, 0:256].rearrange("p (w d) -> p w d", d=64), qstg_r, vwsem, 0, 128)
            else:
                kb.dma(kwsem, KW[0:64, :], dr["kwin"][g * 64:(g + 1) * 64, i, :], writes=[KW_r])
                kb.dma(vwsem, VW[:, :, 0:64], dr["vwin"][i, :, g * 64:(g + 1) * 64].rearrange("(w p) d -> p w d", p=128), writes=[VW_r])
            for hp in range(4):
                (wq,), w_r = P.load_w([W[g * 4 + hp]])
                for hh in range(2):
                    hl = hp * 2 + hh
                    ps, ps_r = sbank()
                    for c in range(8):
                        kb.op(kb.pe, lambda c=c: nc.tensor.matmul(ps[0:64, :], lhsT=wq[:, c, hh * 64:(hh + 1) * 64], rhs=hT[:, c, :], start=(c == 0), stop=(c == 7)),
                              reads=[w_r, hT_r], writes=[ps_r], sig=(c == 7))
                    kb.op(kb.dve, lambda: nc.vector.tensor_scalar(out=Q[0:64, 0, hl, :], in0=ps[0:64, :], scalar1=0.125, scalar2=None, op0=ALU.mult), reads=[ps_r], writes=[Q_r[hl]])
                    kb.op(kb.pool, lambda: nc.gpsimd.tensor_copy(out=Q[0:64, 1:3, hl, :], in_=Q[0:64, 0:1, hl, :].to_broadcast([64, 2, 512])),
                          reads=[Q_r[hl]], writes=[Q_r[hl]])
            for v_ in range(3):
                kb.dma(qsem, Q[124:127, v_, :, :], dr["qal"][:, g * 8:(g + 1) * 8, q0:q0 + 512], writes=Q_r)
            for hl in range(8):
                h = g * 8 + hl
                accs = [(P.psf[3 + 2 * (hl % 2)], P.psf_r[3 + 2 * (hl % 2)]), (P.psf[4 + 2 * (hl % 2)], P.psf_r[4 + 2 * (hl % 2)])]
                for a, a_r in accs:
                    kb.op(kb.dve, lambda: nc.vector.memset(a[:, 0:386], 0.0), writes=[a_r])
                cq_ = []
                for cu in list(cmp_tiles) + [None, None]:
                    if cu is not None:
                        nt, mi_ = cu
                        ps, ps_r = sbank()
                        kb.op(kb.pe, lambda: nc.tensor.matmul(ps[:, :], lhsT=KC[0:128, g, nt * 128:(nt + 1) * 128], rhs=Q[0:128, 0, hl, :], start=True, stop=True),
                              reads=[KC_r, Q_r[hl]], writes=[ps_r])
                        e, e_r = exp_tile(ps, ps_r, KBC[:, h * 4 + nt:h * 4 + nt + 1], KBC_r,
                                          CPM[:, mi_, :] if mi_ is not None else None, CPM_r)
                        cq_.append((nt, e, e_r))
                    if cq_ and (len(cq_) > 2 or cu is None):
                        nt_, e, e_r = cq_.pop(0)
                        for qs in range(4):
                            a, a_r = accs[qs // 2]
                            o0 = (qs % 2) * 193
                            kb.op(kb.pe, lambda: nc.tensor.matmul(a[:, o0:o0 + 193], lhsT=e[:, qs * 128:(qs + 1) * 128], rhs=VC[:, nt_, g, :], start=False, stop=(nt_ == cmp_tiles[-1][0]), skip_group_check=True),
                                  reads=[e_r, VC_r], writes=[a_r], sig=(qs == 3 or qs == 1))
                assert not cq_
                for half in range(2):
                    a, a_r = accs[half]
                    av = a[:, 0:386].rearrange("p (q c) -> p q c", c=193)
                    kb.op(kb.dve, lambda: nc.vector.tensor_scalar(out=crd[:, 2 * half:2 * half + 2, hl], in0=av[:, :, 64], scalar1=1e-30, scalar2=None, op0=ALU.max),
                          reads=[a_r], writes=[crd_r[hl]])
                    kb.op(kb.dve, lambda: nc.vector.tensor_copy(out=cst[:, 2 * half:2 * half + 2, hl, :], in_=av[:, :, 0:64]), reads=[a_r], writes=[cst_r[hl]])
                kb.op(kb.dve, lambda: nc.vector.reciprocal(out=crd[:, :, hl], in_=crd[:, :, hl]), reads=[crd_r[hl]], writes=[crd_r[hl]])
                for qs in range(4):
                    a, a_r = accs[qs // 2]
                    o0 = (qs % 2) * 193
                    if hl == 0:
                        kb.op(kb.dve, lambda: nc.vector.tensor_scalar(out=sc[:, qs, :], in0=a[:, o0 + 65:o0 + 193], scalar1=crd[:, qs, hl:hl + 1], scalar2=None, op0=ALU.mult),
                              reads=[a_r, crd_r[hl]], writes=[sc_r])
                    else:
                        kb.op(kb.dve, lambda: nc.vector.scalar_tensor_tensor(out=sc[:, qs, :], in0=a[:, o0 + 65:o0 + 193], scalar=crd[:, qs, hl:hl + 1], in1=sc[:, qs, :],
                                                                              op0=ALU.mult, op1=ALU.add), reads=[a_r, crd_r[hl], sc_r], writes=[sc_r])
            c0_ = 0 if MONO else i * 4
            kb.op(kb.dve, lambda: nc.vector.tensor_tensor(out=smb[:, :, :], in0=sc[:, :, :], in1=ADM[:, c0_:c0_ + 4, :], op=ALU.add), reads=[sc_r, ADM_r], writes=smb_r)
            for qs in range(4):
                kb.op(kb.dve, lambda: nc.vector.max(out=m8[:, qs, :], in_=smb[:, qs, :]), reads=[smb_r[qs]], writes=[m8_r[qs]])
            for qs in range(4):
                kb.op(kb.dve, lambda: nc.vector.match_replace(out=wa[:, qs, :], in_to_replace=m8[:, qs, :], in_values=smb[:, qs, :], imm_value=-1e30),
                      reads=[smb_r[qs], m8_r[qs]], writes=[wa_r[qs]])
            for qs in range(4):
                kb.op(kb.dve, lambda: nc.vector.max(out=m8b[:, qs, :], in_=wa[:, qs, :]), reads=[wa_r[qs]], writes=[m8b_r[qs]])
            for qs in range(4):
                kb.op(kb.dve, lambda: nc.vector.match_replace(out=wb[:, qs, :], in_to_replace=m8b[:, qs, :], in_values=wa[:, qs, :], imm_value=-1e30),
                      reads=[wa_r[qs], m8b_r[qs]], writes=[wb_r[qs]])
            kb.op(kb.dve, lambda: nc.vector.tensor_tensor(out=wa[:, :, :], in0=smb[:, :, :], in1=wb[:, :, :], op=ALU.subtract), reads=smb_r + wb_r, writes=wa_r)
            kb.op(kb.dve, lambda: nc.vector.tensor_scalar(out=wa[:, :, :], in0=wa[:, :, :], scalar1=1.0, scalar2=-NEG, op0=ALU.min, op1=ALU.mult), reads=wa_r, writes=wa_r)
            for v_ in range(3):
                nb_ = 60 if v_ < 2 else 8
                kb.op(kb.dve, lambda: nc.vector.scalar_tensor_tensor(out=nqv[:, :, v_, 64:64 + nb_], in0=wa[:, :, 60 * v_:60 * v_ + nb_], scalar=NEG,
                                                                      in1=VNG[:, c0_:c0_ + 4, 60 * v_:60 * v_ + nb_], op0=ALU.add, op1=ALU.min),
                      reads=wa_r + [VNG_r], writes=[nq_r])
            for qs in range(4):
                for v_ in range(3):
                    kb.op(kb.pe, lambda: nc.tensor.transpose(out=P.psb[:, v_ * 128:(v_ + 1) * 128], in_=nqv[:, qs, v_, :], identity=IDB[:, :]),
                          reads=[nq_r, IDB_r], writes=[P.psb_r])
                for v_ in range(3):
                    src_ = P.psb[64:124, v_ * 128:(v_ + 1) * 128].rearrange("p (o n) -> p o n", o=1).to_broadcast([60, 8, 128])
                    kb.op(kb.dve, lambda: nc.vector.tensor_copy(out=Q[64:124, v_, :, qs * 128:(qs + 1) * 128], in_=src_),
                          reads=[P.psb_r], writes=Q_r)
            for hl in range(8):
                h = g * 8 + hl
                aS, aS_r = P.psf[3 + 2 * (hl % 2)], P.psf_r[3 + 2 * (hl % 2)]
                aW, aW_r = P.psf[4 + 2 * (hl % 2)], P.psf_r[4 + 2 * (hl % 2)]
                nkt = KT0 * (i + 1)
                if hl == 0:
                    kb.op(kb.dve, lambda: nc.vector.memset(aS[:, 0:260], 0.0), writes=[aS_r])
                    kb.op(kb.dve, lambda: nc.vector.memset(aW[:, 0:260], 0.0), writes=[aW_r])
                units = [("s", kt) for kt in range(nkt)] + [("w", w) for w in range(8)]
                SKEW = 2
                pendq = []
                for u in units + [None] * SKEW:
                    cur = None
                    if u is not None:
                        kind, ix = u
                        ps, ps_r = sbank()
                        qa, qb = 0, 512
                        if kind == "s":
                            r_ = ix - KT0 * i
                            if MONO and r_ >= 0:
                                qa = 128 * r_
                            kb.op(kb.pe, lambda: nc.tensor.matmul(ps[:, qa:qb], lhsT=KA[0:128, ix * 128:(ix + 1) * 128], rhs=Q[0:128, (2 * ix) // 60, hl, qa:qb], start=True, stop=True),
                                  reads=[KA_r, Q_r[hl]], writes=[ps_r])
                            p, p_r = exp_tile(ps, ps_r, KBS[:, h * 64 + ix:h * 64 + ix + 1], KBS_r, CM[:, r_, :] if r_ >= 0 else None, CM_r, qa, qb)
                        else:
                            if ix < 4:
                                qb = 128 * (ix + 1)
                            else:
                                qa = 128 * (ix - 4)
                            kb.op(kb.pe, lambda: nc.tensor.matmul(ps[:, qa:qb], lhsT=KW[0:128, ix * 128:(ix + 1) * 128], rhs=Q[0:128, 0, hl, qa:qb], start=True, stop=True),
                                  reads=[KW_r, Q_r[hl]], writes=[ps_r])
                            cb = (ix * 16 + h) if MONO else ((i * 8 + ix) * 16 + h)
                            p, p_r = exp_tile(ps, ps_r, KBW[:, cb:cb + 1], KBW_r, WM[:, ix, :], WM_r, qa, qb)
                        cur = (kind, ix, p, p_r, qa, qb)
                        pendq.append(cur)
                    if pendq and (len(pendq) > SKEW or u is None):
                        kind_, ix_, pp, pp_r, qa_, qb_ = pendq.pop(0)
                        qss = list(range(qa_ // 128, qb_ // 128))
                        for qs in qss:
                            if kind_ == "s":
                                kb.op(kb.pe, lambda: nc.tensor.matmul(aS[:, qs * 65:(qs + 1) * 65], lhsT=pp[:, qs * 128:(qs + 1) * 128], rhs=VA[:, ix_, :], start=False, stop=(ix_ == nkt - 1), skip_group_check=True),
                                      reads=[pp_r, VA_r], writes=[aS_r], sig=(qs == qss[-1]))
                            else:
                                kb.op(kb.pe, lambda: nc.tensor.matmul(aW[:, qs * 65:(qs + 1) * 65], lhsT=pp[:, qs * 128:(qs + 1) * 128], rhs=VW[:, ix_, :], start=False, stop=(ix_ == 7), skip_group_check=True),
                                      reads=[pp_r, VW_r], writes=[aW_r], sig=(qs == qss[-1]))
                assert not pendq
                if hl < 7:
                    nS, nS_r = P.psf[3 + 2 * ((hl + 1) % 2)], P.psf_r[3 + 2 * ((hl + 1) % 2)]
                    nW, nW_r = P.psf[4 + 2 * ((hl + 1) % 2)], P.psf_r[4 + 2 * ((hl + 1) % 2)]
                    kb.op(kb.dve, lambda: nc.vector.memset(nS[:, 0:260], 0.0), writes=[nS_r])
                    kb.op(kb.dve, lambda: nc.vector.memset(nW[:, 0:260], 0.0), writes=[nW_r])
                aSv = aS[:, 0:260].rearrange("p (q c) -> p q c", c=65)
                aWv = aW[:, 0:260].rearrange("p (q c) -> p q c", c=65)
                kb.op(kb.dve, lambda: nc.vector.tensor_scalar(out=dd[:, 0, :], in0=aSv[:, :, 64], scalar1=1e-30, scalar2=None, op0=ALU.max), reads=[aS_r], writes=[dd_r])
                kb.op(kb.dve, lambda: nc.vector.tensor_scalar(out=dd[:, 1, :], in0=aWv[:, :, 64], scalar1=1e-30, scalar2=None, op0=ALU.max), reads=[aW_r], writes=[dd_r])
                kb.op(kb.dve, lambda: nc.vector.reciprocal(out=dd[:, :, :], in_=dd[:, :, :]), reads=[dd_r], writes=[dd_r])
                kb.op(kb.dve, lambda: nc.vector.tensor_tensor(out=cf[:, 0, :], in0=crd[:, :, hl], in1=gat[:, :, h * 3 + 0], op=ALU.mult), reads=[crd_r[hl], gat_r], writes=[cf_r])
                kb.op(kb.dve, lambda: nc.vector.tensor_tensor(out=cf[:, 1, :], in0=dd[:, 0, :], in1=gat[:, :, h * 3 + 1], op=ALU.mult), reads=[dd_r, gat_r], writes=[cf_r])
                kb.op(kb.dve, lambda: nc.vector.tensor_tensor(out=cf[:, 2, :], in0=dd[:, 1, :], in1=gat[:, :, h * 3 + 2], op=ALU.mult), reads=[dd_r, gat_r], writes=[cf_r])
                for qs in range(4):
                    kb.op(kb.dve, lambda: nc.vector.tensor_scalar(out=oh[:, :], in0=cst[:, qs, hl, :], scalar1=cf[:, 0, qs:qs + 1], scalar2=None, op0=ALU.mult),
                          reads=[cst_r[hl], cf_r], writes=[oh_r])
                    kb.op(kb.dve, lambda: nc.vector.scalar_tensor_tensor(out=oh[:, :], in0=aS[:, qs * 65:qs * 65 + 64], scalar=cf[:, 1, qs:qs + 1], in1=oh[:, :], op0=ALU.mult, op1=ALU.add),
                          reads=[aS_r, cf_r, oh_r], writes=[oh_r])
                    kb.op(kb.dve, lambda: nc.vector.scalar_tensor_tensor(out=onb[:, qs, h * 64:(h + 1) * 64], in0=aW[:, qs * 65:qs * 65 + 64], scalar=cf[:, 2, qs:qs + 1], in1=oh[:, :],
                                                                          op0=ALU.mult, op1=ALU.add), reads=[aW_r, cf_r, oh_r], writes=[onb_r])
            for fc in range(4 * g, 4 * g + 4):
                k = fc % 2
                for qs in range(4):
                    kb.op(kb.pe, lambda: nc.tensor.transpose(out=P.psb[:, 0:128], in_=onb[:, qs, fc * 128:(fc + 1) * 128], identity=IDB[:, :]),
                          reads=[onb_r, IDB_r], writes=[P.psb_r])
                    kb.op(kb.dve, lambda: nc.vector.tensor_copy(out=ost[k][:, qs * 128:(qs + 1) * 128], in_=P.psb[:, 0:128]), reads=[P.psb_r], writes=[ost_r[k]])
                kb.dma(osems[k], dr["onsaT"][fc * 128:(fc + 1) * 128, q0:q0 + 512], ost[k][:, :], reads=[ost_r[k]])


BF = ml_dtypes.bfloat16
_CACHE = {}
USE_MONO = True


def _slopes():
    hh = np.arange(1, 17, dtype=np.float32)
    return np.exp2(-8.0 * hh / 16.0).astype(np.float32)


def _split3(v):
    v = v.astype(np.float32)
    hi = v.astype(BF)
    r = v - hi.astype(np.float32)
    mid = r.astype(BF)
    r2 = r - mid.astype(np.float32)
    lo = r2.astype(BF)
    return hi, mid, lo


def core_consts(cc):
    sl = _slopes()
    p = np.arange(128)
    q = np.arange(512)
    c = {}
    tabs = np.concatenate([(4 * i + cc) * 512 + q for i in range(4)]).astype(np.float32)
    v = -(sl[:, None] * tabs[None, :])
    hi, mid, lo = _split3(v)
    c["qal"] = np.ascontiguousarray(np.stack([hi, mid, lo], 0))
    kbs = np.zeros((128, 16, 64), np.float32)
    for h in range(16):
        kbs[:, h, :] = sl[h] * (np.arange(64)[None, :] * 128 + p[:, None]).astype(np.float32)
    c["kbs"] = kbs.reshape(128, 1024)
    kbc = np.zeros((128, 16, 4), np.float32)
    for h in range(16):
        kbc[:, h, :] = sl[h] * (16 * (np.arange(4)[None, :] * 128 + p[:, None]) + 31).astype(np.float32)
    c["kbc"] = kbc.reshape(128, 64)
    kbw = np.zeros((128, 4, 8, 16), np.float32)
    for i in range(4):
        T0 = (4 * i + cc) * 512
        for w in range(8):
            ka = T0 - 512 + w * 128 + p
            for h in range(16):
                kbw[:, i, w, h] = np.where(ka >= 0, sl[h] * ka.astype(np.float32), -30000.0)
    c["kbw"] = kbw.reshape(128, 512)
    cmpm = np.zeros((128, 2, 512), np.float32)
    for d in (-1, 0):
        vis = (2048 * d + 16 * p[:, None] + 31 - 512 * cc) <= q[None, :]
        cmpm[:, d + 1, :] = np.where(vis, 0.0, NEG)
    c["cmpm"] = cmpm.astype(BF)
    cm = np.zeros((128, 16, 512), np.float32)
    for r in range(16):
        vis = (128 * r + p[:, None]) <= (512 * cc + q[None, :])
        cm[:, r, :] = np.where(vis, 0.0, NEG)
    c["cm"] = cm.astype(BF)
    wm = np.zeros((128, 8, 512), np.float32)
    for w in range(8):
        dist = 512 + q[None, :] - 128 * w - p[:, None]
        wm[:, w, :] = np.where((dist >= 0) & (dist < 512), 0.0, NEG)
    c["wm"] = wm.astype(BF)
    addm = np.zeros((128, 16, 128), np.float32)
    vneg = np.zeros((128, 16, 128), np.float32)
    j = np.arange(128)
    for i in range(4):
        for qs in range(4):
            t = (4 * i + cc) * 512 + qs * 128 + p
            valid = (j[None, :] * 64) <= t[:, None]
            cur = t // 64
            forced = valid & ((j[None, :] == 0) | (j[None, :] == cur[:, None]) | (j[None, :] == cur[:, None] - 1))
            addm[:, i * 4 + qs, :] = np.where(forced, 8192.0, np.where(valid, 0.0, -8192.0))
            vneg[:, i * 4 + qs, :] = np.where(valid, 0.0, NEG)
    c["addm"] = addm.astype(BF)
    c["vneg"] = vneg.astype(BF)
    return c


def shared_consts():
    c = {}
    cols = np.arange(SEQ)
    kar = np.zeros((64, SEQ), np.float32)
    kar[0:60] = ((cols[None, :] // 64) % 60 == np.arange(60)[:, None])
    kar[60:63] = 1.0
    c["karows"] = kar.astype(BF)
    c["ones3"] = np.ones((3, 1024), np.float32).astype(BF)
    n = np.arange(512)
    cs = n[:, None] * 16
    ss = np.arange(128)[None, :] * 64
    ov = np.clip(np.minimum(cs + 32, ss + 64) - np.maximum(cs, ss), 0, None) / 32.0
    c["ovl"] = np.ascontiguousarray(ov.reshape(4, 128, 128).transpose(1, 0, 2)).astype(np.float32).astype(BF)
    c["identb"] = np.eye(128, dtype=np.float32).astype(BF)
    return c


def tile_w(Wm, starts=None):
    K = Wm.shape[0]
    if starts is None:
        starts = list(range(0, Wm.shape[1], 128))
    out = np.empty((len(starts), 128, K // 128, 128), np.float32)
    for j, c0 in enumerate(starts):
        out[j] = Wm[:, c0:c0 + 128].reshape(K // 128, 128, 128).transpose(1, 0, 2)
    return out


def _prog(key, fn):
    if key not in _CACHE:
        _CACHE[key] = fn()
    return _CACHE[key]


def _tok_index(cc):
    return np.concatenate([np.arange((4 * i + cc) * 512, (4 * i + cc + 1) * 512) for i in range(4)])


def kernel(x, ffn1_norm, ffn1_w_gate, ffn1_w_up, ffn1_w_down, mix_norm, w_in, cmp_pos,
           cmp_k_w1, cmp_k_w2, cmp_v_w1, cmp_v_w2, pool_w, pool_scale, w_branch_pool,
           w_branch_nsa, w_out, ffn2_norm, ffn2_w_gate, ffn2_w_up, ffn2_w_down, final_norm):
    f32 = lambda a: np.ascontiguousarray(np.asarray(a, dtype=np.float32))
    x = f32(x)
    W = {k: f32(v) for k, v in dict(ffn1_norm=ffn1_norm, ffn1_w_gate=ffn1_w_gate, ffn1_w_up=ffn1_w_up, ffn1_w_down=ffn1_w_down,
                                    mix_norm=mix_norm, w_in=w_in, cmp_pos=cmp_pos, cmp_k_w1=cmp_k_w1, cmp_k_w2=cmp_k_w2,
                                    cmp_v_w1=cmp_v_w1, cmp_v_w2=cmp_v_w2, pool_w=pool_w, pool_scale=pool_scale,
                                    w_branch_pool=w_branch_pool, w_branch_nsa=w_branch_nsa, w_out=w_out, ffn2_norm=ffn2_norm,
                                    ffn2_w_gate=ffn2_w_gate, ffn2_w_up=ffn2_w_up, ffn2_w_down=ffn2_w_down, final_norm=final_norm).items()}
    cores = list(range(8))
    vecs = np.zeros((128, 64), np.float32)
    for l in range(NL):
        b0 = vbase(l)
        vecs[:, b0:b0 + 8] = gain_layout(W["ffn1_norm"][l])
        vecs[:, b0 + 8:b0 + 16] = gain_layout(W["mix_norm"][l])
        vecs[:, b0 + 16:b0 + 24] = gain_layout(W["ffn2_norm"][l])
        vecs[:, b0 + 24:b0 + 28] = gain_layout(W["pool_scale"][l])
    vecs[:, 56:64] = gain_layout(W["final_norm"])
    if USE_MONO:
        return kernel_mono(W, x, vecs)
    tix = [_tok_index(c % 4) for c in cores]
    cc_consts = [core_consts(cc) for cc in range(4)]
    sh = shared_consts()

    TW = {}
    for l in range(NL):
        TW["win", l] = tile_w(W["w_in"][l], WIN_STARTS)
        for nm in ("ffn1_w_gate", "ffn1_w_up", "ffn1_w_down", "ffn2_w_gate", "ffn2_w_up", "ffn2_w_down", "w_branch_pool", "w_branch_nsa", "w_out",
                   "cmp_k_w1", "cmp_v_w1"):
            TW[nm, l] = tile_w(W[nm][l])

    def a_weights(l):
        return {"f1_wg": TW["ffn1_w_gate", l], "f1_wu": TW["ffn1_w_up", l], "f1_wd": TW["ffn1_w_down", l], "win_a": TW["win", l]}

    progA = _prog("A", lambda: build_tok(0, "A", False))
    in_maps = []
    for c in cores:
        m = {"xs_in": np.ascontiguousarray(x[c // 4, tix[c], :].T), "vecs": vecs}
        m.update(a_weights(0))
        in_maps.append(m)
    res = run_bass_kernel_spmd(progA, in_maps, core_ids=cores).results

    out = np.zeros((2, SEQ, D), np.float32)
    for l in range(NL):
        last = (l == NL - 1)
        kvall = np.zeros((2, 4, 128, SEQ + 32), BF)
        vall = np.zeros((2, SEQ, 256), BF)
        uall = np.zeros((2, 512, SEQ), np.float32)
        for c in cores:
            b = c // 4
            kvall[b][:, :, tix[c]] = np.asarray(res[c]["kvT_out"]).view(BF) if np.asarray(res[c]["kvT_out"]).dtype != BF else res[c]["kvT_out"]
            vall[b][tix[c], :] = np.asarray(res[c]["vtok_out"])
            uall[b][:, tix[c]] = np.asarray(res[c]["uT_out"])
        prog = _prog(("BCA", l, last), lambda: build_bca(l, last))
        pecol = np.ascontiguousarray(W["cmp_pos"][l].reshape(16, 2, 64).transpose(1, 2, 0).reshape(128, 16))
        in_maps = []
        for c in cores:
            b, cc = c // 4, c % 4
            kwin = np.zeros((128, 4, 1024), BF)
            vwin = np.zeros((4, 1024, 128), BF)
            uext = np.zeros((512, 4, 528), np.float32)
            for i in range(4):
                T0 = (4 * i + cc) * 512
                lo = max(T0 - 512, 0)
                kwin[:, i, 1024 - (T0 + 512 - lo):] = kvall[b][3][:, lo:T0 + 512]
                vwin[i, 1024 - (T0 + 512 - lo):, :] = vall[b][lo:T0 + 512, 128:256]
                lo = max(T0 - 16, 0)
                uext[:, i, 528 - (T0 + 512 - lo):] = uall[b][:, lo:T0 + 512]
            corr = np.ones((128, 4, 16), np.float32)
            if cc == 0:
                for gi, w in enumerate(POOLW):
                    corr[:, gi, :] = (w / np.minimum(np.arange(16) + 1.0, float(w)))[None, :]
            m = {"vecs": vecs, "h2T": np.asarray(res[c]["h2T_out"]), "kvall": kvall[b], "vall": vall[b], "kwin": kwin, "vwin": vwin,
                 "win": TW["win", l], "win_gn": np.ascontiguousarray(W["w_in"][l][:, C_GN:C_GN + 48]),
                 "ck_w1": TW["cmp_k_w1", l], "ck_w2": W["cmp_k_w2"][l], "cv_w1": TW["cmp_v_w1", l], "cv_w2": W["cmp_v_w2"][l],
                 "pecol": pecol,
                 "xs_in": np.asarray(res[c]["xs_out"]),
                 "f2_wg": TW["ffn2_w_gate", l], "f2_wu": TW["ffn2_w_up", l], "f2_wd": TW["ffn2_w_down", l],
                 "wpa": TW["w_branch_pool", l], "wnb": TW["w_branch_nsa", l], "wo": TW["w_out", l],
                 "poolw": W["pool_w"][l], "uext": uext, "corr": corr}
            m.update(cc_consts[cc])
            m.update(sh)
            if not last:
                m.update(a_weights(l + 1))
            in_maps.append(m)
        res = run_bass_kernel_spmd(prog, in_maps, core_ids=cores).results
    for c in cores:
        out[c // 4, tix[c], :] = np.asarray(res[c]["out"]).T
    return out


def build_mono():
    global MONO
    MONO = True
    try:
        BI, IN_, BO = "ExternalInput", "Internal", "ExternalOutput"
        S = SEQ
        specs = {
            "x_in": ((D, S), F32, BI), "vecs": ((128, 64), F32, BI),
            "qal": ((3, 16, S), BF16, BI), "kbs": ((128, 1024), F32, BI), "kbc": ((128, 64), F32, BI),
            "kbw": ((16, 128, 128), F32, BI), "cmpm": ((128, 5, 512), BF16, BI), "cm": ((128, 4, 512), BF16, BI),
            "wm": ((128, 8, 512), BF16, BI), "addm": ((16, 128, 4, 128), BF16, BI), "vneg": ((16, 128, 4, 128), BF16, BI),
            "karows": ((64, S), BF16, BI), "ones3": ((3, 1024), BF16, BI), "ovl": ((128, 4, 128), BF16, BI), "identb": ((128, 128), BF16, BI),
            "corr": ((128, 4, 16), F32, BI),
            "xs": ((D, S), F32, IN_), "h2T": ((D, S), BF16, IN_), "kvT": ((4, 128, S), BF16, IN_), "vtok": ((S, 256), BF16, IN_),
            "uT0": ((512, S), F32, IN_), "uT1": ((512, S), F32, IN_), "onsaT": ((D, S), BF16, IN_),
            "out_q": ((D, NTOK), F32, BO), "onsaT_q": ((D, NTOK), BF16, IN_),
            "selw": ((128, 4), F32, BI), "corr_q": ((128, 4, 16), F32, BI),
            "qal_q": ((3, 16, NTOK), BF16, BI), "kbw_q": ((128, 512), F32, BI), "cmpm_q": ((128, 2, 512), BF16, BI),
            "cm_q": ((128, 16, 512), BF16, BI), "addm_q": ((128, 16, 128), BF16, BI), "vneg_q": ((128, 16, 128), BF16, BI),
        }
        for l in range(NL):
            for pre in ("f1_", "f2_"):
                specs["%swg_%d" % (pre, l)] = ((NFC, 128, 8, 128), F32, BI)
                specs["%swu_%d" % (pre, l)] = ((NFC, 128, 8, 128), F32, BI)
                specs["%swd_%d" % (pre, l)] = ((8, 128, NFC, 128), F32, BI)
            specs["win_%d" % l] = ((len(WIN_STARTS), 128, 8, 128), F32, BI)
            specs["win_gn_%d" % l] = ((D, 48), F32, BI)
            specs["wpa_%d" % l] = ((8, 128, 4, 128), F32, BI)
            specs["wnb_%d" % l] = ((8, 128, 8, 128), F32, BI)
            specs["wo_%d" % l] = ((8, 128, 8, 128), F32, BI)
            specs["poolw_%d" % l] = ((4, 128, 128), F32, BI)
            specs["ck_w1_%d" % l] = ((2, 128, 16, 128), F32, BI)
            specs["cv_w1_%d" % l] = ((2, 128, 16, 128), F32, BI)
            specs["ck_w2_%d" % l] = ((256, 64), F32, BI)
            specs["cv_w2_%d" % l] = ((256, 64), F32, BI)
            specs["pecol_%d" % l] = ((128, 16), F32, BI)
        conv = [n_ for n_, (sh_, dt_, k_) in specs.items() if k_ == BI and dt_ == F32 and len(sh_) == 4 and n_ != "kbw"]
        gu = [n_ for n_ in conv if n_[3:5] in ("wg", "wu")]
        conv = [n_ for n_ in conv if n_ not in gu]
        for n_ in conv:
            specs[n_ + "_b"] = (specs[n_][0], BF16, IN_)
        for l in range(NL):
            for pre in ("f1_", "f2_"):
                specs["%swgu_%d_b" % (pre, l)] = ((NFC, 128, 16, 128), BF16, IN_)
        P = Prog(specs, WST=None)
        kb, dr = P.kb, P.dr

        P.selw = kb.sb("selw", [128, 4], F32)
        P.selw_r = Res("selw")
        kb.dma(P.ldsem, P.selw[:, :], dr["selw"][:, :], writes=[P.selw_r])
        mono_tabs = {k_: dr[k_] for k_ in ("qal", "kbw", "cmpm", "cm", "addm", "vneg", "corr", "onsaT")}

        def do_convert():
            for n_ in conv:
                P.convert_w(dr[n_], dr[n_ + "_b"])
                dr[n_] = dr[n_ + "_b"]
            for l_ in range(NL):
                for pre in ("f1_", "f2_"):
                    d_ = dr["%swgu_%d_b" % (pre, l_)]
                    P.convert_w(dr["%swg_%d" % (pre, l_)], None, dst_fn=lambda j, d_=d_: d_[j, :, 0:8, :])
                    P.convert_w(dr["%swu_%d" % (pre, l_)], None, dst_fn=lambda j, d_=d_: d_[j, :, 8:16, :])

        def alias(l, mode):
            for nm in ("wpa", "wnb", "wo", "poolw", "ck_w1", "cv_w1", "ck_w2", "cv_w2", "pecol", "win_gn"):
                dr[nm] = dr["%s_%d" % (nm, l)]
            dr["win"] = dr["win_%d" % l]
            dr["win_c"] = dr["win_%d" % l]
            dr["f2_wd"] = dr["f2_wd_%d" % l]
            dr["f2_wg"] = dr["f2_wgu_%d_b" % l]
            dr["f2_wu"] = None
            la = l if mode == "A" else min(l + 1, NL - 1)
            dr["win_a"] = dr["win_%d" % la]
            dr["f1_wd"] = dr["f1_wd_%d" % la]
            dr["f1_wg"] = dr["f1_wgu_%d_b" % la]
            dr["f1_wu"] = None
            dr["kvall"] = dr["kvT"]
            dr["vall"] = dr["vtok"]
            dr["h2T_in"] = dr["h2T"]
            dr["h2T_out"] = dr["h2T"]
            dr["kvT_out"] = dr["kvT"]
            dr["vtok_out"] = dr["vtok"]
            dr["xs_out"] = dr["xs"]
            dr["xs_in"] = dr["x_in"] if (mode == "A" and l == 0) else dr["xs"]
            dr["uT_in"] = dr["uT%d" % (l % 2)]
            dr["uT_out"] = dr["uT%d" % (la % 2)]

        def phase(fn, wst, wstf=1024, nslot=4):
            with ExitStack() as pes:
                kb.cur_es = pes
                P.alloc_wstage(wst, wstf, nslot)
                fn()
                kb.barrier()
            kb.cur_es = None

        phase(do_convert, 3584, 3584, 2)
        alias(0, "A")
        phase(lambda: tok_body(P, 0, "A", False), 3584)
        global QSEL
        for l in range(NL):
            last = (l == NL - 1)
            alias(l, "CA")
            if last:
                MONO, QSEL = False, True
                for k_ in ("qal", "kbw", "cmpm", "cm", "addm", "vneg", "corr", "onsaT"):
                    dr[k_] = dr[k_ + "_q"]
                dr["out"] = dr["out_q"]
            phase(lambda: attn_body(P), 2048, 1024, 2)
            phase(lambda: tok_body(P, l, "CA", last), 3584)
        return P.finish()
    finally:
        MONO = False
        QSEL = False


def mono_consts():
    sl = _slopes()
    p = np.arange(128)
    q = np.arange(512)
    c = {}
    tabs = np.arange(SEQ).astype(np.float32)
    hi, mid, lo = _split3(-(sl[:, None] * tabs[None, :]))
    c["qal"] = np.ascontiguousarray(np.stack([hi, mid, lo], 0))
    kbs = np.zeros((128, 16, 64), np.float32)
    kbc = np.zeros((128, 16, 4), np.float32)
    for h in range(16):
        kbs[:, h, :] = sl[h] * (np.arange(64)[None, :] * 128 + p[:, None]).astype(np.float32)
        kbc[:, h, :] = sl[h] * (16 * (np.arange(4)[None, :] * 128 + p[:, None]) + 31).astype(np.float32)
    c["kbs"] = kbs.reshape(128, 1024)
    c["kbc"] = kbc.reshape(128, 64)
    kbw = np.zeros((16, 128, 8, 16), np.float32)
    for i in range(16):
        for w in range(8):
            ka = 512 * (i - 1) + w * 128 + p
            for h in range(16):
                kbw[i, :, w, h] = np.where(ka >= 0, sl[h] * ka.astype(np.float32), -30000.0)
    c["kbw"] = kbw.reshape(16, 128, 128)
    cmpm = np.zeros((128, 5, 512), np.float32)
    for d in range(5):
        cmpm[:, d, :] = np.where((16 * p[:, None] + 31 - 512 * d) <= q[None, :], 0.0, NEG)
    c["cmpm"] = cmpm.astype(BF)
    cm = np.zeros((128, 4, 512), np.float32)
    for r in range(4):
        cm[:, r, :] = np.where((128 * r + p[:, None]) <= q[None, :], 0.0, NEG)
    c["cm"] = cm.astype(BF)
    wm = np.zeros((128, 8, 512), np.float32)
    for w in range(8):
        dist = 512 + q[None, :] - 128 * w - p[:, None]
        wm[:, w, :] = np.where((dist >= 0) & (dist < 512), 0.0, NEG)
    c["wm"] = wm.astype(BF)
    addm = np.zeros((16, 128, 4, 128), np.float32)
    vneg = np.zeros((16, 128, 4, 128), np.float32)
    j = np.arange(128)
    for i in range(16):
        for qs in range(4):
            t = i * 512 + qs * 128 + p
            valid = (j[None, :] * 64) <= t[:, None]
            cur = t // 64
            forced = valid & ((j[None, :] == 0) | (j[None, :] == cur[:, None]) | (j[None, :] == cur[:, None] - 1))
            addm[i, :, qs, :] = np.where(forced, 8192.0, np.where(valid, 0.0, -8192.0))
            vneg[i, :, qs, :] = np.where(valid, 0.0, NEG)
    c["addm"] = addm.astype(BF)
    c["vneg"] = vneg.astype(BF)
    corr = np.ones((128, 4, 16), np.float32)
    for gi, w in enumerate(POOLW):
        corr[:, gi, :] = (w / np.minimum(np.arange(16) + 1.0, float(w)))[None, :]
    c["corr"] = corr
    c.update(shared_consts())
    return c


def kernel_mono(W, x, vecs):
    prog = _prog("MONO", build_mono)
    base = {"vecs": vecs}
    base.update(mono_consts())
    for l in range(NL):
        base["win_%d" % l] = tile_w(W["w_in"][l], WIN_STARTS)
        base["win_gn_%d" % l] = np.ascontiguousarray(W["w_in"][l][:, C_GN:C_GN + 48])
        for pre, a in (("f1_", "ffn1"), ("f2_", "ffn2")):
            base["%swg_%d" % (pre, l)] = tile_w(W[a + "_w_gate"][l])
            base["%swu_%d" % (pre, l)] = tile_w(W[a + "_w_up"][l])
            base["%swd_%d" % (pre, l)] = tile_w(W[a + "_w_down"][l])
        base["wpa_%d" % l] = tile_w(W["w_branch_pool"][l])
        base["wnb_%d" % l] = tile_w(W["w_branch_nsa"][l])
        base["wo_%d" % l] = tile_w(W["w_out"][l])
        base["poolw_%d" % l] = W["pool_w"][l]
        base["ck_w1_%d" % l] = tile_w(W["cmp_k_w1"][l])
        base["cv_w1_%d" % l] = tile_w(W["cmp_v_w1"][l])
        base["ck_w2_%d" % l] = W["cmp_k_w2"][l]
        base["cv_w2_%d" % l] = W["cmp_v_w2"][l]
        base["pecol_%d" % l] = np.ascontiguousarray(W["cmp_pos"][l].reshape(16, 2, 64).transpose(1, 2, 0).reshape(128, 16))
    cores = list(range(8))
    in_maps = []
    xT = [np.ascontiguousarray(x[b].T) for b in range(2)]
    for c in cores:
        b, cc = c % 2, c // 2
        m = dict(base)
        m["x_in"] = xT[b]
        cq = core_consts(cc)
        for k_ in ("qal", "kbw", "cmpm", "cm", "addm", "vneg"):
            m[k_ + "_q"] = cq[k_]
        selw = np.zeros((128, 4), np.float32)
        selw[:, cc] = 1.0
        m["selw"] = selw
        corr = np.ones((128, 4, 16), np.float32)
        if cc == 0:
            corr = base["corr"]
        m["corr_q"] = corr
        in_maps.append(m)
    res = run_bass_kernel_spmd(prog, in_maps, core_ids=cores).results
    out = np.zeros((2, SEQ, D), np.float32)
    for c in cores:
        out[c % 2, _tok_index(c // 2), :] = np.asarray(res[c]["out_q"]).T
    return out
```

```python
import numpy as np
import ml_dtypes
from contextlib import ExitStack
import concourse.bass as bass
import concourse.mybir as mybir
from concourse.bass_utils import run_bass_kernel_spmd

F32 = mybir.dt.float32
BF16 = mybir.dt.bfloat16
AF = mybir.ActivationFunctionType
ALU = mybir.AluOpType

D = 1024
DFF = 2816
NFC = DFF // 128
SEQ = 8192
NL = 2
NTOK = 2048
MONO = False
QSEL = False


def ntok():
    return SEQ if MONO else NTOK
TT = 512
INW = 4400
C_Q, C_KC, C_VC, C_KSL, C_VSL, C_KWN, C_VWN, C_GN, C_U, C_GM = 0, 1024, 1152, 1280, 1408, 1536, 1664, 1792, 1840, 2352
EPS = 1e-6
NEG = -16384.0
WIN_STARTS = [j * 128 for j in range(8)] + [C_KC, C_VC, C_KSL, C_VSL, C_KWN, C_VWN] + [C_U + j * 128 for j in range(4)] + [C_GM + j * 128 for j in range(16)]
WIN_IDX = {c: i for i, c in enumerate(WIN_STARTS)}


class Sem:
    def __init__(self, h, name):
        self.h = h
        self.name = name
        self.count = 0
        self.group = False


class Tok:
    __slots__ = ("sem", "val")

    def __init__(self, sem, val):
        self.sem = sem
        self.val = val


class Res:
    __slots__ = ("name", "w", "r", "excl")

    def __init__(self, name="", excl=False):
        self.name = name
        self.w = None
        self.r = {}
        self.excl = excl


class Eng:
    def __init__(self, name, h, sem, same_sync):
        self.name = name
        self.h = h
        self.sem = sem
        self.waited = {}
        self.pending = []
        self.same_sync = same_sync


class KB:
    def __init__(self, nc, es):
        self.nc = nc
        self.es = es
        self.sems = []
        self.pe = self._eng("pe", nc.tensor, False)
        self.act = self._eng("act", nc.scalar, True)
        self.dve = self._eng("dve", nc.vector, True)
        self.pool = self._eng("pool", nc.gpsimd, True)
        self.sp = self._eng("sp", nc.sync, False)
        self.engs = [self.pe, self.act, self.dve, self.pool, self.sp]
        self.n_inst = 0

    def new_sem(self, name):
        name = "%s_%d" % (name, len(self.sems))
        h = self.es.enter_context(self.nc.semaphore(name))
        s = Sem(h, name)
        self.sems.append(s)
        return s

    def _eng(self, name, h, same_sync):
        return Eng(name, h, self.new_sem("s_" + name), same_sync)

    def sb(self, name, shape, dtype, es=None):
        self.nsb = getattr(self, "nsb", 0) + 1
        return (es or getattr(self, "cur_es", None) or self.es).enter_context(self.nc.sbuf_tensor("sb%d_%s" % (self.nsb, name), shape, dtype))

    def ps(self, name, shape, dtype):
        return self.es.enter_context(self.nc.psum_tensor("pp_" + name, shape, dtype))

    def _wait(self, eng, tok):
        if tok is None:
            return
        if tok.sem is eng.sem and not eng.same_sync:
            return
        assert tok.val is not None, "waiting on unresolved token (%s)" % tok.sem.name
        val = tok.val
        if tok.sem.group:
            val = max(val, tok.sem.count)
        if eng.waited.get(tok.sem, 0) >= val:
            return
        eng.h.wait_ge(tok.sem.h, val)
        eng.waited[tok.sem] = val

    def _deps(self, eng, reads, writes):
        for r in reads:
            self._wait(eng, r.w)
        for w in writes:
            self._wait(eng, w.w)
            for t in w.r.values():
                self._wait(eng, t)

    def _mark(self, tok, reads, writes):
        for r in reads:
            r.r[tok.sem] = tok
        for w in writes:
            w.w = tok
            w.r = {}

    def op(self, eng, fn, reads=(), writes=(), sig=True):
        xr = [r for r in reads if r.excl]
        if xr:
            writes = list(writes) + xr
            reads = [r for r in reads if not r.excl]
        self._deps(eng, reads, writes)
        inst = fn()
        self.n_inst += 1
        if sig:
            eng.sem.count += 1
            inst.then_inc(eng.sem.h, 1)
            tok = Tok(eng.sem, eng.sem.count)
            for t in eng.pending:
                t.val = eng.sem.count
            eng.pending = []
        else:
            tok = Tok(eng.sem, None)
            eng.pending.append(tok)
        self._mark(tok, reads, writes)
        return tok

    def dma(self, sem, out, in_, reads=(), writes=(), eng=None, **kw):
        eng = eng or self.sp
        self._deps(eng, reads, writes)
        if sem.count > 0:
            self._wait(eng, Tok(sem, sem.count))
        inst = eng.h.dma_start(out=out, in_=in_, **kw)
        self.n_inst += 1
        sem.count += 16
        inst.then_inc(sem.h, 16)
        tok = Tok(sem, sem.count)
        self._mark(tok, reads, writes)
        return tok

    def barrier(self):
        for e in self.engs:
            assert not e.pending
            for s in self.sems:
                if s.count > 0 and not (s is e.sem):
                    self._wait(e, Tok(s, s.count))


class Prog:
    def __init__(self, dram_specs, WST=3584):
        self.nc = bass.Bass("TRN2", target_bir_lowering=False)
        self.es = ExitStack()
        self.kb = KB(self.nc, self.es)
        self.dr = {}
        self.dres = {}
        for name, (shape, dt, kind) in dram_specs.items():
            self.dr[name] = self.nc.dram_tensor(name, list(shape), dt, kind=kind).ap()
            self.dres[name] = Res("dram_" + name)
        self.out_names = [n for n, (_, _, k) in dram_specs.items() if k == "ExternalOutput"]
        kb = self.kb
        self.psf = [kb.ps("psf%d" % i, [128, 512], F32) for i in range(7)]
        self.psf_r = [Res("psf%d" % i, excl=True) for i in range(7)]
        self.psb = kb.ps("psb", [128, 1024], BF16)
        self.psb_r = Res("psb", excl=True)
        self.ps_rr = 0
        self.ones = kb.sb("ones", [128, 128], F32)
        self.ones_r = Res("ones")
        kb.op(kb.dve, lambda: self.nc.vector.memset(self.ones[:], 1.0 / D), writes=[self.ones_r])
        self.epsc = kb.sb("epsc", [128, 1], F32)
        kb.op(kb.dve, lambda: self.nc.vector.memset(self.epsc[:], EPS), writes=[self.ones_r])
        self.vecs = kb.sb("vecs", [128, 64], F32)
        self.vecs_r = Res("vecs")
        self.ldsem = kb.new_sem("ld_misc")
        self.ldsem.group = True
        kb.dma(self.ldsem, self.vecs[:], self.dr["vecs"][:, :], writes=[self.vecs_r])
        self.wsem = [kb.new_sem("wsem%d" % i) for i in range(4)]
        self.stsem = kb.new_sem("st_misc")
        if WST:
            self.alloc_wstage(WST)

    def alloc_wstage(self, WST, WSTF=None, nslot=2):
        kb = self.kb
        self.WST = WST
        self.WSTF = WSTF or WST
        self.nslot = nslot
        self.wst = [kb.sb("wst%d" % i, [128, self.WSTF], F32) for i in range(nslot)]
        self.wst_r = [Res("wst%d" % i) for i in range(nslot)]
        self.wbf = [kb.sb("wbf%d" % i, [128, self.WST], BF16) for i in range(nslot)]
        self.wbf_r = [Res("wbf%d" % i) for i in range(nslot)]
        self.wslot = 0

    def bank(self):
        i = self.ps_rr % 7
        self.ps_rr += 1
        return self.psf[i], self.psf_r[i]

    def load_w(self, pieces):
        kb, nc = self.kb, self.nc
        s = self.wslot
        self.wslot = (self.wslot + 1) % self.nslot
        off = 0
        foff = 0
        views = []
        for ap in pieces:
            if len(ap.shape) == 3:
                _, kc, n = ap.shape
                src = ap
            else:
                K, n = ap.shape
                kc = K // 128
                src = ap.rearrange("(k p) n -> p k n", p=128)
            sz = kc * n
            bview = self.wbf[s][:, off:off + sz].rearrange("p (k n) -> p k n", n=n)
            if ap.dtype == BF16:
                kb.dma(self.wsem[s], bview, src, writes=[self.wbf_r[s]])
            else:
                assert foff + sz <= self.WSTF
                dst = self.wst[s][:, foff:foff + sz].rearrange("p (k n) -> p k n", n=n)
                kb.dma(self.wsem[s], dst, src, writes=[self.wst_r[s]])
                a, b, fa = off, off + sz, foff
                kb.op(kb.pool, lambda a=a, b=b, fa=fa: nc.gpsimd.tensor_copy(out=self.wbf[s][:, a:b], in_=self.wst[s][:, fa:fa + (b - a)]),
                      reads=[self.wst_r[s]], writes=[self.wbf_r[s]])
                foff += sz
            views.append(bview)
            off += sz
        assert off <= self.WST
        return views, self.wbf_r[s]

    def convert_w(self, src, dst, dst_fn=None):
        kb, nc = self.kb, self.nc
        nch, _, kc, n = src.shape
        sz = kc * n
        if not hasattr(self, "cvsem"):
            self.cvsem = [kb.new_sem("cvs%d" % i) for i in range(4)]
            self.cv_rr = 0
        for j in range(nch):
            s = self.wslot
            self.wslot = (self.wslot + 1) % self.nslot
            kb.dma(self.wsem[s], self.wst[s][:, 0:sz].rearrange("p (k n) -> p k n", n=n), src[j], writes=[self.wst_r[s]])
            e = self.cv_rr % 3
            self.cv_rr += 1
            if e == 0:
                kb.op(kb.pool, lambda: nc.gpsimd.tensor_copy(out=self.wbf[s][:, 0:sz], in_=self.wst[s][:, 0:sz]), reads=[self.wst_r[s]], writes=[self.wbf_r[s]])
            elif e == 1:
                kb.op(kb.dve, lambda: nc.vector.tensor_copy(out=self.wbf[s][:, 0:sz], in_=self.wst[s][:, 0:sz]), reads=[self.wst_r[s]], writes=[self.wbf_r[s]])
            else:
                kb.op(kb.act, lambda: nc.scalar.copy(out=self.wbf[s][:, 0:sz], in_=self.wst[s][:, 0:sz]), reads=[self.wst_r[s]], writes=[self.wbf_r[s]])
            kb.dma(self.cvsem[s], dst_fn(j) if dst_fn else dst[j], self.wbf[s][:, 0:sz].rearrange("p (k n) -> p k n", n=n), reads=[self.wbf_r[s]])

    def select_tile(self, dst, dst_r, cands, stage, stage_r, sem, p0, p1):
        kb, nc = self.kb, self.nc
        first = True
        for c, src in enumerate(cands):
            if src is None:
                continue
            kb.dma(sem, stage, src, writes=[stage_r])
            sc_ = self.selw[p0:p1, c:c + 1]
            if first:
                kb.op(kb.dve, lambda: nc.vector.tensor_scalar(out=dst, in0=stage, scalar1=sc_, scalar2=None, op0=ALU.mult),
                      reads=[stage_r, self.selw_r], writes=[dst_r])
                first = False
            else:
                kb.op(kb.dve, lambda: nc.vector.scalar_tensor_tensor(out=dst, in0=stage, scalar=sc_, in1=dst, op0=ALU.mult, op1=ALU.add),
                      reads=[stage_r, self.selw_r, dst_r], writes=[dst_r])

    def vcol(self, c):
        return self.vecs[:, c:c + 1]

    def rmsnorm(self, x, x_r, h, h_r, n, gcol, sq, sq_r, rstd, rstd_r, out_f32=None, out_r=None):
        kb, nc = self.kb, self.nc
        for s0 in range(0, n, 512):
            ps, ps_r = self.bank()
            for c in range(8):
                k = c % 2
                kb.op(kb.act, lambda c=c, k=k: nc.scalar.activation(out=sq[k][:, :], in_=x[:, c, s0:s0 + 512], func=AF.Square),
                      reads=[x_r], writes=[sq_r[k]])
                kb.op(kb.pe, lambda c=c, k=k: nc.tensor.matmul(ps[:, :], lhsT=self.ones[:, :], rhs=sq[k][:, :], start=(c == 0), stop=(c == 7)),
                      reads=[sq_r[k], self.ones_r], writes=[ps_r], sig=True)
            kb.op(kb.act, lambda: nc.scalar.activation(out=rstd[:, s0:s0 + 512], in_=ps[:, :], func=AF.Ln, bias=self.epsc[:, 0:1]),
                  reads=[ps_r, self.ones_r], writes=[rstd_r])
            kb.op(kb.act, lambda: nc.scalar.activation(out=rstd[:, s0:s0 + 512], in_=rstd[:, s0:s0 + 512], func=AF.Exp, scale=-0.5),
                  reads=[rstd_r], writes=[rstd_r])
            for c in range(8):
                tgt = h if out_f32 is None else out_f32
                tgt_r = h_r if out_f32 is None else out_r
                kb.op(kb.dve, lambda c=c, tgt=tgt: nc.vector.scalar_tensor_tensor(
                    out=tgt[:, c, s0:s0 + 512], in0=x[:, c, s0:s0 + 512], scalar=self.vcol(gcol + c), in1=rstd[:, s0:s0 + 512],
                    op0=ALU.mult, op1=ALU.mult), reads=[x_r, rstd_r, self.vecs_r], writes=[tgt_r])

    def ffn(self, x, x_r, h, h_r, n, wg, wu, wd, aT, aT_r, sg, sg_r):
        kb, nc = self.kb, self.nc
        nsub = n // 512
        for fc in range(NFC):
            if wu is None:
                (wgu,), w_r = self.load_w([wg[fc]])
                wgb, wub = wgu[:, 0:8, :], wgu[:, 8:16, :]
            else:
                (wgb, wub), w_r = self.load_w([wg[fc], wu[fc]])
            for sub in range(nsub):
                s0 = sub * 512
                pg, pg_r = self.bank()
                pu, pu_r = self.bank()
                for c in range(8):
                    kb.op(kb.pe, lambda c=c: nc.tensor.matmul(pg[:, :], lhsT=wgb[:, c, :], rhs=h[:, c, s0:s0 + 512], start=(c == 0), stop=(c == 7)),
                          reads=[w_r, h_r], writes=[pg_r], sig=(c == 7))
                for c in range(8):
                    kb.op(kb.pe, lambda c=c: nc.tensor.matmul(pu[:, :], lhsT=wub[:, c, :], rhs=h[:, c, s0:s0 + 512], start=(c == 0), stop=(c == 7)),
                          reads=[w_r, h_r], writes=[pu_r], sig=(c == 7))
                k = (fc * nsub + sub) % 2
                kb.op(kb.act, lambda k=k: nc.scalar.activation(out=sg[k][:, :], in_=pg[:, :], func=AF.Silu), reads=[pg_r], writes=[sg_r[k]])
                kb.op(kb.dve, lambda k=k: nc.vector.tensor_tensor(out=aT[:, fc, s0:s0 + 512], in0=sg[k][:, :], in1=pu[:, :], op=ALU.mult),
                      reads=[sg_r[k], pu_r], writes=[aT_r[fc]])
        for dc in range(8):
            (wdb,), w_r = self.load_w([wd[dc]])
            for sub in range(nsub):
                s0 = sub * 512
                py, py_r = self.bank()
                for fc in range(NFC):
                    kb.op(kb.pe, lambda fc=fc: nc.tensor.matmul(py[:, :], lhsT=wdb[:, fc, :], rhs=aT[:, fc, s0:s0 + 512], start=(fc == 0), stop=(fc == NFC - 1)),
                          reads=[w_r, aT_r[fc]], writes=[py_r], sig=(fc == NFC - 1))
                kb.op(kb.dve, lambda: nc.vector.scalar_tensor_tensor(out=x[:, dc, s0:s0 + 512], in0=py[:, :], scalar=0.5, in1=x[:, dc, s0:s0 + 512],
                                                                      op0=ALU.mult, op1=ALU.add), reads=[py_r, x_r], writes=[x_r])

    def finish(self):
        kb = self.kb
        kb.barrier()
        self.es.close()
        return self.nc


def gain_layout(v):
    return np.ascontiguousarray(np.asarray(v, np.float32).reshape(-1, 128).T)


def vbase(l):
    return 28 * l


POOLW = (2, 4, 8, 16)


def tok_specs(l, mode, last):
    specs = {"xs_in": ((D, NTOK), F32, "ExternalInput"), "vecs": ((128, 64), F32, "ExternalInput")}

    def wspec(li, pre):
        specs[pre + "wg"] = ((NFC, 128, 8, 128), F32, "ExternalInput")
        specs[pre + "wu"] = ((NFC, 128, 8, 128), F32, "ExternalInput")
        specs[pre + "wd"] = ((8, 128, NFC, 128), F32, "ExternalInput")

    doA = (mode == "A") or (not last)
    if mode == "CA":
        wspec(l, "f2_")
        specs["win_c"] = ((len(WIN_STARTS), 128, 8, 128), F32, "ExternalInput")
        specs["wpa"] = ((8, 128, 4, 128), F32, "ExternalInput")
        specs["wnb"] = ((8, 128, 8, 128), F32, "ExternalInput")
        specs["wo"] = ((8, 128, 8, 128), F32, "ExternalInput")
        specs["poolw"] = ((4, 128, 128), F32, "ExternalInput")
        specs["h2T_in"] = ((D, NTOK), BF16, "ExternalInput")
        specs["onsaT"] = ((D, NTOK), BF16, "ExternalInput")
        specs["uext"] = ((512, 4, 528), F32, "ExternalInput")
        specs["corr"] = ((128, 4, 16), F32, "ExternalInput")
    if doA:
        wspec(l, "f1_")
        specs["win_a"] = ((len(WIN_STARTS), 128, 8, 128), F32, "ExternalInput")
        specs["h2T_out"] = ((D, NTOK), BF16, "ExternalOutput")
        specs["kvT_out"] = ((4, 128, NTOK), BF16, "ExternalOutput")
        specs["uT_out"] = ((512, NTOK), F32, "ExternalOutput")
        specs["vtok_out"] = ((NTOK, 256), BF16, "ExternalOutput")
        specs["xs_out"] = ((D, NTOK), F32, "ExternalOutput")
    else:
        specs["out"] = ((D, NTOK), F32, "ExternalOutput")
    return specs


def build_tok(l, mode, last):
    P = Prog(tok_specs(l, mode, last))
    tok_body(P, l, mode, last)
    return P.finish()


def build_bca(l, last):
    specs = attn_specs()
    ts = tok_specs(l, "CA", last)
    for k_ in ("h2T_in", "win_c", "onsaT", "vecs"):
        ts.pop(k_)
    specs.update(ts)
    specs["onsaT"] = ((D, NTOK), BF16, "Internal")
    P = Prog(specs, WST=None)
    P.dr["h2T_in"] = P.dr["h2T"]
    P.dr["win_c"] = P.dr["win"]
    kb = P.kb
    with ExitStack() as pes:
        kb.cur_es = pes
        P.alloc_wstage(2048)
        attn_body(P)
        kb.barrier()
    with ExitStack() as pes:
        kb.cur_es = pes
        P.alloc_wstage(3584)
        tok_body(P, l, "CA", last)
        kb.barrier()
    kb.cur_es = None
    return P.finish()


def tok_body(P, l, mode, last):
    kb, nc, dr = P.kb, P.nc, P.dr
    stq = kb.act if (MONO or QSEL) else kb.sp
    doA = (mode == "A") or (not last)
    x = kb.sb("x", [128, 8, TT], F32); x_r = Res("x")
    h = kb.sb("h", [128, 8, TT], BF16); h_r = Res("h")
    aT = kb.sb("aT", [128, NFC, TT], BF16); aT_r = [Res("aT%d" % i) for i in range(NFC)]
    sq = [kb.sb("sq%d" % i, [128, 512], F32) for i in range(2)]; sq_r = [Res() for i in range(2)]
    rstd = kb.sb("rstd", [128, TT], F32); rstd_r = Res()
    xsem = kb.new_sem("xsem")
    if QSEL:
        xstg2 = kb.sb("xstg", [128, 8 * TT], F32); xstg_r = Res("xstg")
        xstg = xstg2[:, :].rearrange("p (c n) -> p c n", n=TT)
    if mode == "CA":
        hsem = kb.new_sem("hsem"); osem = kb.new_sem("osem"); usem = kb.new_sem("usem")
        on = kb.sb("on", [128, 8, TT], BF16); on_r = Res("on")
        ue = kb.sb("ue", [128, 4, 528], F32); ue_r = Res("ue")
        sa = kb.sb("sa", [128, 528], F32); sa_r = Res("sa")
        sb_ = kb.sb("sbb", [128, 528], F32); sb_r = Res("sb")
        dl = kb.sb("dl", [128, 4, TT], BF16); dl_r = [Res() for _ in range(4)]
        opl = kb.sb("opl", [128, 4, TT], BF16); opl_r = Res("opl")
        mg = kb.sb("mg", [128, 8, TT], BF16); mg_r = [Res() for _ in range(8)]
        t1 = kb.sb("t1", [128, TT], F32); t1_r = Res()
        t2 = kb.sb("t2", [128, TT], F32); t2_r = Res()
        corr = kb.sb("corr", [128, 4, 16], F32); corr_r = Res()
        kb.dma(P.ldsem, corr[:], dr["corr"][:, :, :], writes=[corr_r])
    if doA:
        kvst = [kb.sb("kvst%d" % i, [128, TT], BF16) for i in range(2)]; kvst_r = [Res() for _ in range(2)]
        ust = [kb.sb("ust%d" % i, [128, TT], F32) for i in range(2)]; ust_r = [Res() for _ in range(2)]
        vst = kb.sb("vst", [128, 4, 256], BF16); vst_r = Res()
        osems = [kb.new_sem("kvo%d" % i) for i in range(2)]
        usems = [kb.new_sem("uo%d" % i) for i in range(2)]
        vsem = kb.new_sem("vo")
        hosem = kb.new_sem("ho")

    def colsl(ap, t0):
        return ap[:, t0:t0 + TT].rearrange("(c p) n -> p c n", p=128)

    for t in range(ntok() // TT):
        t0 = t * TT
        if QSEL:
            P.select_tile(x[:, :, :], x_r, [colsl(dr["xs_in"], (4 * t + c_) * 512) for c_ in range(4)], xstg[:, :, :], xstg_r, xsem, 0, 128)
        else:
            kb.dma(xsem, x[:, :, :], colsl(dr["xs_in"], t0), writes=[x_r], eng=stq)
        la = l
        if mode == "CA":
            vb = vbase(l)
            if QSEL:
                P.select_tile(h[:, :, :], h_r, [colsl(dr["h2T_in"], (4 * t + c_) * 512) for c_ in range(4)],
                              on[:, :, :], on_r, hsem, 0, 128)
            else:
                kb.dma(hsem, h[:, :, :], colsl(dr["h2T_in"], t0), writes=[h_r], eng=stq)
            kb.dma(osem, on[:, :, :], colsl(dr["onsaT"], t0), writes=[on_r], eng=stq)
            if QSEL:
                ustg = xstg2[:, 0:2112].rearrange("p (g n) -> p g n", n=528)
                cands = []
                for c_ in range(4):
                    a0 = (4 * t + c_) * 512
                    cands.append(dr["uT_in"][:, a0 - 16:a0 + 512].rearrange("(g p) n -> p g n", p=128) if a0 > 0 else None)
                if t == 0:
                    kb.op(kb.pool, lambda: nc.gpsimd.memset(ustg[:, :, 0:16], 0.0), writes=[xstg_r])
                    kb.dma(usem, ustg[:, :, 16:528], dr["uT_in"][:, 0:512].rearrange("(g p) n -> p g n", p=128), writes=[xstg_r])
                    kb.op(kb.dve, lambda: nc.vector.tensor_scalar(out=ue[:, :, :], in0=ustg, scalar1=P.selw[:, 0:1], scalar2=None, op0=ALU.mult),
                          reads=[xstg_r, P.selw_r], writes=[ue_r])
                    for c_ in range(1, 4):
                        kb.dma(usem, ustg, cands[c_], writes=[xstg_r])
                        kb.op(kb.dve, lambda: nc.vector.scalar_tensor_tensor(out=ue[:, :, :], in0=ustg, scalar=P.selw[:, c_:c_ + 1], in1=ue[:, :, :], op0=ALU.mult, op1=ALU.add),
                              reads=[xstg_r, P.selw_r, ue_r], writes=[ue_r])
                else:
                    P.select_tile(ue[:, :, :], ue_r, cands, ustg, xstg_r, usem, 0, 128)
            elif MONO:
                kb.dma(usem, ue[:, :, 16:528], dr["uT_in"][:, t0:t0 + 512].rearrange("(g p) n -> p g n", p=128), writes=[ue_r])
                if t == 0:
                    kb.op(kb.pool, lambda: nc.gpsimd.memset(ue[:, :, 0:16], 0.0), writes=[ue_r])
                else:
                    kb.dma(usem, ue[:, :, 0:16], dr["uT_in"][:, t0 - 16:t0].rearrange("(g p) n -> p g n", p=128), writes=[ue_r])
            else:
                kb.dma(usem, ue[:, :, :], dr["uext"][:, t, :].rearrange("(g p) n -> p g n", p=128), writes=[ue_r])
            for gi, w in enumerate(POOLW):
                cur, cur_r = None, None
                sh = 1
                src = ue[:, gi, :]
                src_r = ue_r
                bufs = [(sa, sa_r), (sb_, sb_r)]
                bi = 0
                while sh < w:
                    dst, dst_r = bufs[bi]
                    bi ^= 1
                    lo = 2 * sh - 1
                    kb.op(kb.dve, lambda src=src, dst=dst, lo=lo, sh=sh: nc.vector.tensor_tensor(
                        out=dst[:, lo:528], in0=src[:, lo:528], in1=src[:, lo - sh:528 - sh], op=ALU.add),
                        reads=[src_r], writes=[dst_r])
                    src, src_r = dst, dst_r
                    sh *= 2
                kb.op(kb.dve, lambda src=src, w=w: nc.vector.tensor_scalar(out=src[:, 16:528], in0=src[:, 16:528], scalar1=1.0 / w, scalar2=None, op0=ALU.mult),
                      reads=[src_r], writes=[src_r])
                if t == 0:
                    kb.op(kb.dve, lambda src=src, gi=gi: nc.vector.tensor_tensor(out=src[:, 16:32], in0=src[:, 16:32], in1=corr[:, gi, :], op=ALU.mult),
                          reads=[src_r, corr_r], writes=[src_r])
                kb.op(kb.dve, lambda src=src, gi=gi: nc.vector.tensor_tensor(out=dl[:, gi, :], in0=src[:, 16:528], in1=ue[:, gi, 16:528], op=ALU.subtract),
                      reads=[src_r, ue_r], writes=[dl_r[gi]])
            for gi in range(4):
                (pw,), w_r = P.load_w([dr["poolw"][gi]])
                ps, ps_r = P.bank()
                kb.op(kb.pe, lambda: nc.tensor.matmul(ps[:, :], lhsT=pw[:, 0, :], rhs=dl[:, gi, :], start=True, stop=True),
                      reads=[w_r, dl_r[gi]], writes=[ps_r])
                kb.op(kb.dve, lambda: nc.vector.tensor_scalar(out=opl[:, gi, :], in0=ps[:, :], scalar1=P.vcol(vb + 24 + gi), scalar2=None, op0=ALU.mult),
                      reads=[ps_r, P.vecs_r], writes=[opl_r])
            for dc in range(8):
                (wpa, wnb, wgp, wga), w_r = P.load_w([dr["wpa"][dc], dr["wnb"][dc],
                                                      dr["win_c"][WIN_IDX[C_GM + dc * 128]],
                                                      dr["win_c"][WIN_IDX[C_GM + 1024 + dc * 128]]])
                pa, pa_r = P.bank(); pb, pb_r = P.bank(); pgp, pgp_r = P.bank(); pga, pga_r = P.bank()
                for c in range(4):
                    kb.op(kb.pe, lambda c=c: nc.tensor.matmul(pa[:, :], lhsT=wpa[:, c, :], rhs=opl[:, c, :], start=(c == 0), stop=(c == 3)),
                          reads=[w_r, opl_r], writes=[pa_r], sig=(c == 3))
                for c in range(8):
                    kb.op(kb.pe, lambda c=c: nc.tensor.matmul(pb[:, :], lhsT=wnb[:, c, :], rhs=on[:, c, :], start=(c == 0), stop=(c == 7)),
                          reads=[w_r, on_r], writes=[pb_r], sig=(c == 7))
                for c in range(8):
                    kb.op(kb.pe, lambda c=c: nc.tensor.matmul(pgp[:, :], lhsT=wgp[:, c, :], rhs=h[:, c, :], start=(c == 0), stop=(c == 7)),
                          reads=[w_r, h_r], writes=[pgp_r], sig=(c == 7))
                for c in range(8):
                    kb.op(kb.pe, lambda c=c: nc.tensor.matmul(pga[:, :], lhsT=wga[:, c, :], rhs=h[:, c, :], start=(c == 0), stop=(c == 7)),
                          reads=[w_r, h_r], writes=[pga_r], sig=(c == 7))
                kb.op(kb.act, lambda: nc.scalar.activation(out=t1[:, :], in_=pgp[:, :], func=AF.Sigmoid), reads=[pgp_r], writes=[t1_r])
                kb.op(kb.act, lambda: nc.scalar.activation(out=t2[:, :], in_=pga[:, :], func=AF.Sigmoid), reads=[pga_r], writes=[t2_r])
                kb.op(kb.dve, lambda: nc.vector.tensor_tensor(out=t1[:, :], in0=t1[:, :], in1=pa[:, :], op=ALU.mult), reads=[t1_r, pa_r], writes=[t1_r])
                kb.op(kb.dve, lambda: nc.vector.tensor_tensor(out=t2[:, :], in0=t2[:, :], in1=pb[:, :], op=ALU.mult), reads=[t2_r, pb_r], writes=[t2_r])
                kb.op(kb.dve, lambda: nc.vector.tensor_tensor(out=mg[:, dc, :], in0=t1[:, :], in1=t2[:, :], op=ALU.add), reads=[t1_r, t2_r], writes=[mg_r[dc]])
            for dc in range(8):
                (wo,), w_r = P.load_w([dr["wo"][dc]])
                pz, pz_r = P.bank()
                for c in range(8):
                    kb.op(kb.pe, lambda c=c: nc.tensor.matmul(pz[:, :], lhsT=wo[:, c, :], rhs=mg[:, c, :], start=(c == 0), stop=(c == 7)),
                          reads=[w_r, mg_r[c]], writes=[pz_r], sig=(c == 7))
                kb.op(kb.dve, lambda: nc.vector.tensor_tensor(out=x[:, dc, :], in0=x[:, dc, :], in1=pz[:, :], op=ALU.add), reads=[x_r, pz_r], writes=[x_r])
            P.rmsnorm(x, x_r, h, h_r, TT, vb + 16, sq, sq_r, rstd, rstd_r)
            P.ffn(x, x_r, h, h_r, TT, dr["f2_wg"], dr["f2_wu"], dr["f2_wd"], aT, aT_r, sq, sq_r)
            la = l + 1
            if last:
                P.rmsnorm(x, x_r, None, None, TT, 56, sq, sq_r, rstd, rstd_r, out_f32=x, out_r=x_r)
                kb.dma(P.stsem, colsl(dr["out"], t0), x[:, :, :], reads=[x_r], eng=stq)
                continue
        vb = vbase(la)
        P.rmsnorm(x, x_r, h, h_r, TT, vb + 0, sq, sq_r, rstd, rstd_r)
        P.ffn(x, x_r, h, h_r, TT, dr["f1_wg"], dr["f1_wu"], dr["f1_wd"], aT, aT_r, sq, sq_r)
        kb.dma(P.stsem, colsl(dr["xs_out"], t0), x[:, :, :], reads=[x_r], eng=stq)
        P.rmsnorm(x, x_r, h, h_r, TT, vb + 8, sq, sq_r, rstd, rstd_r)
        kb.dma(hosem, colsl(dr["h2T_out"], t0), h[:, :, :], reads=[h_r], eng=stq)
        W = dr["win_a"]
        for j, c0 in enumerate((C_KC, C_VC, C_KSL, C_KWN)):
            (wc,), w_r = P.load_w([W[WIN_IDX[c0]]])
            ps, ps_r = P.bank()
            for c in range(8):
                kb.op(kb.pe, lambda c=c: nc.tensor.matmul(ps[:, :], lhsT=wc[:, c, :], rhs=h[:, c, :], start=(c == 0), stop=(c == 7)),
                      reads=[w_r, h_r], writes=[ps_r], sig=(c == 7))
            k = j % 2
            kb.op(kb.act, lambda k=k: nc.scalar.copy(out=kvst[k][:, :], in_=ps[:, :]), reads=[ps_r], writes=[kvst_r[k]])
            kb.dma(osems[k], dr["kvT_out"][j, :, t0:t0 + TT], kvst[k][:, :], reads=[kvst_r[k]], eng=stq)
        for j in range(4):
            (wc,), w_r = P.load_w([W[WIN_IDX[C_U + j * 128]]])
            ps, ps_r = P.bank()
            for c in range(8):
                kb.op(kb.pe, lambda c=c: nc.tensor.matmul(ps[:, :], lhsT=wc[:, c, :], rhs=h[:, c, :], start=(c == 0), stop=(c == 7)),
                      reads=[w_r, h_r], writes=[ps_r], sig=(c == 7))
            k = j % 2
            kb.op(kb.act, lambda k=k: nc.scalar.copy(out=ust[k][:, :], in_=ps[:, :]), reads=[ps_r], writes=[ust_r[k]])
            kb.dma(usems[k], dr["uT_out"][j * 128:(j + 1) * 128, t0:t0 + TT], ust[k][:, :], reads=[ust_r[k]], eng=stq)
        (wv1, wv2), w_r = P.load_w([W[WIN_IDX[C_VSL]], W[WIN_IDX[C_VWN]]])
        for tb in range(TT // 128):
            ps, ps_r = P.bank()
            for wi, wv in enumerate((wv1, wv2)):
                for c in range(8):
                    kb.op(kb.pe, lambda c=c, wv=wv, wi=wi: nc.tensor.matmul(ps[:, wi * 128:(wi + 1) * 128], lhsT=h[:, c, tb * 128:(tb + 1) * 128], rhs=wv[:, c, :],
                                                                         start=(c == 0), stop=(c == 7)),
                          reads=[w_r, h_r], writes=[ps_r], sig=(c == 7))
            kb.op(kb.act, lambda: nc.scalar.copy(out=vst[:, tb, :], in_=ps[:, 0:256]), reads=[ps_r], writes=[vst_r])
        kb.dma(vsem, dr["vtok_out"][t0:t0 + TT, :].rearrange("(tb p) c -> p tb c", p=128), vst[:, :, :], reads=[vst_r], eng=stq)


def attn_specs():
    BI = "ExternalInput"
    specs = {
        "vecs": ((128, 64), F32, BI),
        "h2T": ((D, NTOK), BF16, BI),
        "kvall": ((4, 128, SEQ + 32), BF16, BI),
        "vall": ((SEQ, 256), BF16, BI),
        "kwin": ((128, 4, 1024), BF16, BI),
        "vwin": ((4, 1024, 128), BF16, BI),
        "win": ((len(WIN_STARTS), 128, 8, 128), F32, BI), "win_gn": ((D, 48), F32, BI),
        "ck_w1": ((2, 128, 16, 128), F32, BI), "ck_w2": ((256, 64), F32, BI),
        "cv_w1": ((2, 128, 16, 128), F32, BI), "cv_w2": ((256, 64), F32, BI),
        "pecol": ((128, 16), F32, BI),
        "qal": ((3, 16, NTOK), BF16, BI),
        "kbs": ((128, 16 * 64), F32, BI), "kbc": ((128, 64), F32, BI), "kbw": ((128, 4 * 8 * 16), F32, BI),
        "cmpm": ((128, 2, 512), BF16, BI), "cm": ((128, 16, 512), BF16, BI), "wm": ((128, 8, 512), BF16, BI),
        "addm": ((128, 16, 128), BF16, BI), "vneg": ((128, 16, 128), BF16, BI),
        "karows": ((64, SEQ), BF16, BI), "ones3": ((3, 1024), BF16, BI), "ovl": ((128, 4, 128), BF16, BI), "identb": ((128, 128), BF16, BI),
        "onsaT": ((D, NTOK), BF16, "ExternalOutput"),
    }
    return specs


def build_attn():
    P = Prog(attn_specs(), WST=2048)
    attn_body(P)
    return P.finish()


def attn_body(P):
    kb, nc, dr = P.kb, P.nc, P.dr
    ld = P.ldsem

    def const(name, shape, dt, src):
        t = kb.sb(name, shape, dt)
        r = Res(name)
        kb.dma(ld, t[:], src, writes=[r])
        return t, r

    CM, CM_r = const("CM", [128, 4 if MONO else 16, 512], BF16, dr["cm"][:, :, :])
    WM, WM_r = const("WM", [128, 8, 512], BF16, dr["wm"][:, :, :])
    CPM, CPM_r = const("CPM", [128, 5 if MONO else 2, 512], BF16, dr["cmpm"][:, :, :])
    if MONO:
        ADM = kb.sb("ADM", [128, 4, 128], BF16); ADM_r = Res("ADM")
        VNG = kb.sb("VNG", [128, 4, 128], BF16); VNG_r = Res("VNG")
        KBW = kb.sb("KBW", [128, 128], F32); KBW_r = Res("KBW")
        pisem = kb.new_sem("pisem")
    else:
        ADM, ADM_r = const("ADM", [128, 16, 128], BF16, dr["addm"][:, :, :])
        VNG, VNG_r = const("VNG", [128, 16, 128], BF16, dr["vneg"][:, :, :])
        KBW, KBW_r = const("KBW", [128, 512], F32, dr["kbw"][:, :])
    KBS, KBS_r = const("KBS", [128, 1024], F32, dr["kbs"][:, :])
    KBC, KBC_r = const("KBC", [128, 64], F32, dr["kbc"][:, :])
    IDB, IDB_r = const("IDB", [128, 128], BF16, dr["identb"][:, :])
    PEC, PEC_r = const("PEC", [128, 16], F32, dr["pecol"][:, :])

    KA = kb.sb("KA", [128, SEQ], BF16); KA_r = Res("KA")
    VA = kb.sb("VA", [128, 64, 65], BF16); VA_r = Res("VA")
    KW = kb.sb("KW", [128, 1024], BF16); KW_r = Res("KW")
    VW = kb.sb("VW", [128, 8, 65], BF16); VW_r = Res("VW")
    KC = kb.sb("KC", [128, 2, 512], BF16); KC_r = Res("KC")
    VC = kb.sb("VC", [128, 4, 2, 193], BF16); VC_r = Res("VC")
    kb.op(kb.pool, lambda: nc.gpsimd.memset(KW[0:64, :], 0.0), writes=[KW_r])
    kb.op(kb.pool, lambda: nc.gpsimd.memset(VW[:, :, 0:64], 0.0), writes=[VW_r])
    kb.dma(ld, KA[64:128, :], dr["karows"][:, :], writes=[KA_r])
    kb.op(kb.pool, lambda: nc.gpsimd.memset(KW[64:128, :], 0.0), writes=[KW_r])
    kb.op(kb.pool, lambda: nc.gpsimd.memset(KC[64:128, :, :], 0.0), writes=[KC_r])
    kb.dma(ld, KW[124:127, :], dr["ones3"][:, :], writes=[KW_r])
    kb.dma(ld, KC[124:127, :, :], dr["ones3"][:, :].rearrange("r (g n) -> r g n", n=512), writes=[KC_r])
    kb.op(kb.pool, lambda: nc.gpsimd.memset(VA[:, :, 64:65], 1.0), writes=[VA_r])
    kb.op(kb.pool, lambda: nc.gpsimd.memset(VW[:, :, 64:65], 1.0), writes=[VW_r])
    kb.op(kb.pool, lambda: nc.gpsimd.memset(VC[:, :, :, 64:65], 1.0), writes=[VC_r])
    for g in range(2):
        kb.dma(ld, VC[:, :, g, 65:193], dr["ovl"][:, :, :], writes=[VC_r])

    ces = ExitStack()
    KC2 = kb.sb("KC2", [128, SEQ + 16], BF16, es=ces); KC2_r = Res("KC2")
    zb = kb.sb("zb", [128, 512], F32, es=ces); zb_r = Res()
    s2 = kb.sb("s2", [128, 512], F32, es=ces); s2_r = Res()
    hid = kb.sb("hid", [128, 2, 512], BF16, es=ces); hid_r = [Res(), Res()]
    pecb = kb.sb("pecb", [128, 16], BF16, es=ces); pecb_r = Res()
    bj = kb.sb("bj", [128, 1], F32, es=ces); bj_r = Res()
    kb.op(kb.dve, lambda: nc.vector.tensor_copy(out=pecb[:, :], in_=PEC[:, :]), reads=[PEC_r], writes=[pecb_r])
    c2sem = kb.new_sem("c2sem")
    if MONO or QSEL:
        kb.op(kb.pool, lambda: nc.gpsimd.memset(KC2[0:64, SEQ:SEQ + 16], 0.0), writes=[KC2_r])
        kb.op(kb.pool, lambda: nc.gpsimd.memset(KC2[64:128, SEQ - 1:SEQ + 16], 0.0), writes=[KC2_r])
    for kv in range(2):
        w1 = dr["ck_w1" if kv == 0 else "cv_w1"]
        w2 = dr["ck_w2" if kv == 0 else "cv_w2"]
        for g in range(2):
            if MONO or QSEL:
                kb.dma(c2sem, KC2[0:64, 0:SEQ], dr["kvall"][kv, g * 64:(g + 1) * 64, 0:SEQ], writes=[KC2_r])
                kb.dma(c2sem, KC2[64:128, 0:SEQ - 1], dr["kvall"][kv, g * 64:(g + 1) * 64, 1:SEQ], writes=[KC2_r])
            else:
                kb.dma(c2sem, KC2[0:64, :], dr["kvall"][kv, g * 64:(g + 1) * 64, 0:SEQ + 16], writes=[KC2_r])
                kb.dma(c2sem, KC2[64:128, :], dr["kvall"][kv, g * 64:(g + 1) * 64, 1:SEQ + 17], writes=[KC2_r])
            for jc in range(2):
                (w1v,), w_r = P.load_w([w1[jc]])
                pb_, pb_r = P.bank()
                for lp in range(16):
                    kb.op(kb.pe, lambda lp=lp: nc.tensor.matmul(pb_[:, 0:1], lhsT=w1v[:, lp, :], rhs=pecb[:, lp:lp + 1], start=(lp == 0), stop=(lp == 15)),
                          reads=[w_r, pecb_r], writes=[pb_r], sig=(lp == 15))
                kb.op(kb.act, lambda: nc.scalar.copy(out=bj[:, :], in_=pb_[:, 0:1]), reads=[pb_r], writes=[bj_r])
                ps, ps_r = P.bank()
                for lp in range(16):
                    kb.op(kb.pe, lambda lp=lp: nc.tensor.matmul(ps[:, :], lhsT=w1v[:, lp, :], rhs=KC2[:, 2 * lp:2 * lp + 16 * 511 + 1:16], start=(lp == 0), stop=(lp == 15)),
                          reads=[w_r, KC2_r], writes=[ps_r], sig=(lp == 15))
                kb.op(kb.act, lambda: nc.scalar.activation(out=zb[:, :], in_=ps[:, :], func=AF.Identity, bias=bj[:, 0:1]), reads=[ps_r, bj_r], writes=[zb_r])
                kb.op(kb.act, lambda: nc.scalar.activation(out=s2[:, :], in_=zb[:, :], func=AF.Square), reads=[zb_r], writes=[s2_r])
                kb.op(kb.dve, lambda: nc.vector.tensor_scalar(out=s2[:, :], in0=s2[:, :], scalar1=0.044715, scalar2=1.0, op0=ALU.mult, op1=ALU.add), reads=[s2_r], writes=[s2_r])
                kb.op(kb.dve, lambda: nc.vector.tensor_tensor(out=s2[:, :], in0=s2[:, :], in1=zb[:, :], op=ALU.mult), reads=[s2_r, zb_r], writes=[s2_r])
                kb.op(kb.act, lambda: nc.scalar.activation(out=s2[:, :], in_=s2[:, :], func=AF.Sigmoid, scale=1.5957691216057308), reads=[s2_r], writes=[s2_r])
                kb.op(kb.dve, lambda jc=jc: nc.vector.tensor_tensor(out=hid[:, jc, :], in0=s2[:, :], in1=zb[:, :], op=ALU.mult), reads=[s2_r, zb_r], writes=[hid_r[jc]])
            (w2v,), w_r = P.load_w([w2[:, :]])
            if kv == 0:
                ps, ps_r = P.bank()
                for jc in range(2):
                    kb.op(kb.pe, lambda jc=jc: nc.tensor.matmul(ps[0:64, :], lhsT=w2v[:, jc, :], rhs=hid[:, jc, :], start=(jc == 0), stop=(jc == 1)),
                          reads=[w_r, hid_r[jc]], writes=[ps_r], sig=(jc == 1))
                kb.op(kb.act, lambda g=g: nc.scalar.copy(out=KC[0:64, g, :], in_=ps[0:64, :]), reads=[ps_r], writes=[KC_r])
            else:
                for nt in range(4):
                    ps, ps_r = P.bank()
                    for jc in range(2):
                        kb.op(kb.pe, lambda jc=jc, nt=nt: nc.tensor.matmul(ps[:, 0:64], lhsT=hid[:, jc, nt * 128:(nt + 1) * 128], rhs=w2v[:, jc, :], start=(jc == 0), stop=(jc == 1)),
                              reads=[w_r, hid_r[jc]], writes=[ps_r], sig=(jc == 1))
                    kb.op(kb.act, lambda g=g, nt=nt: nc.scalar.copy(out=VC[:, nt, g, 0:64], in_=ps[:, 0:64]), reads=[ps_r], writes=[VC_r])
    kb.barrier()
    ces.close()

    Q = kb.sb("Q", [128, 3, 8, 512], BF16); Q_r = [Res("Q%d" % i) for i in range(8)]
    kb.op(kb.pool, lambda: nc.gpsimd.memset(Q[64:128, :, :, :], 0.0), writes=Q_r)
    hT = kb.sb("hT", [128, 8, 512], BF16); hT_r = Res("hT")
    if QSEL:
        qstg = kb.sb("qstg", [128, 8, 512], BF16); qstg_r = Res("qstg")
    gat = kb.sb("gat", [128, 4, 48], F32); gat_r = Res("gat")
    cst = kb.sb("cst", [128, 4, 8, 64], F32); cst_r = [Res() for _ in range(8)]
    crd = kb.sb("crd", [128, 4, 8], F32); crd_r = [Res() for _ in range(8)]
    sc = kb.sb("sc", [128, 4, 128], F32); sc_r = Res("sc")
    pT = [kb.sb("pT%d" % i, [128, 512], BF16) for i in range(4)]; pT_r = [Res() for _ in range(4)]
    tmp = [kb.sb("tmp%d" % i, [128, 512], F32) for i in range(2)]; tmp_r = [Res() for _ in range(2)]
    onb = kb.sb("onb", [128, 4, 1024], BF16); onb_r = Res("onb")
    ost = [kb.sb("ost%d" % i, [128, 512], BF16) for i in range(2)]; ost_r = [Res() for _ in range(2)]
    smb = kb.sb("smb", [128, 4, 128], F32); smb_r = [Res() for _ in range(4)]
    wa = kb.sb("wa", [128, 4, 128], F32); wa_r = [Res() for _ in range(4)]
    wb = kb.sb("wb", [128, 4, 128], F32); wb_r = [Res() for _ in range(4)]
    m8 = kb.sb("m8", [128, 4, 8], F32); m8_r = [Res() for _ in range(4)]
    m8b = kb.sb("m8b", [128, 4, 8], F32); m8b_r = [Res() for _ in range(4)]
    nqv = kb.sb("nqv", [128, 4, 3, 128], BF16); nq_r = Res()
    kb.op(kb.pool, lambda: nc.gpsimd.memset(nqv[:, :, :, :], 0.0), writes=[nq_r])
    dd = kb.sb("dd", [128, 2, 4], F32); dd_r = Res()
    cf = kb.sb("cf", [128, 3, 4], F32); cf_r = Res()
    oh = kb.sb("oh", [128, 64], F32); oh_r = Res()
    hsem = kb.new_sem("hsem"); ksem = kb.new_sem("ksem"); vsem = kb.new_sem("vsem")
    kwsem = kb.new_sem("kwsem"); vwsem = kb.new_sem("vwsem"); qsem = kb.new_sem("qsem")
    osems = [kb.new_sem("os%d" % i) for i in range(2)]
    W = dr["win"]
    st = {"s": 0, "p": 0, "t": 0}

    def sbank():
        k = st["s"] % 3
        st["s"] += 1
        return P.psf[k], P.psf_r[k]

    def pbuf():
        k = st["p"] % 4
        st["p"] += 1
        return pT[k], pT_r[k]

    def tbuf():
        k = st["t"] % 2
        st["t"] += 1
        return tmp[k], tmp_r[k]

    def exp_tile(ps, ps_r, bias_ap, bias_r, mask_ap, mask_r, qa=0, qb=512):
        p, p_r = pbuf()
        if mask_ap is not None:
            tm, tm_r = tbuf()
            kb.op(kb.dve, lambda: nc.vector.tensor_tensor(out=tm[:, qa:qb], in0=ps[:, qa:qb], in1=mask_ap[:, qa:qb], op=ALU.add), reads=[ps_r, mask_r], writes=[tm_r])
            kb.op(kb.act, lambda: nc.scalar.activation(out=p[:, qa:qb], in_=tm[:, qa:qb], func=AF.Exp, bias=bias_ap), reads=[tm_r, bias_r], writes=[p_r])
        else:
            kb.op(kb.act, lambda: nc.scalar.activation(out=p[:, qa:qb], in_=ps[:, qa:qb], func=AF.Exp, bias=bias_ap), reads=[ps_r, bias_r], writes=[p_r])
        return p, p_r

    NI = ntok() // 512
    KT0 = 4 if MONO else 16
    for g in range(2):
        kb.dma(ksem, KA[0:64, :], dr["kvall"][2, g * 64:(g + 1) * 64, 0:SEQ], writes=[KA_r])
        for kq in range(16):
            kb.dma(vsem, VA[:, kq * 4:(kq + 1) * 4, 0:64],
                   dr["vall"][kq * 512:(kq + 1) * 512, g * 64:(g + 1) * 64].rearrange("(kt p) d -> p kt d", p=128), writes=[VA_r])
        for i in range(NI):
            q0 = i * 512
            if MONO:
                kb.dma(pisem, ADM[:, :, :], dr["addm"][i], writes=[ADM_r])
                kb.dma(pisem, VNG[:, :, :], dr["vneg"][i], writes=[VNG_r])
                kb.dma(pisem, KBW[:, :], dr["kbw"][i], writes=[KBW_r])
                cmp_tiles = [(nt, (i - 4 * nt) if (i - 4 * nt) <= 4 else None) for nt in range((32 * (i + 1) - 2) // 128 + 1)]
            else:
                cmp_tiles = [(nt, (nt - i + 1) if nt - i >= -1 else None) for nt in range(i + 1)]
            if QSEL:
                P.select_tile(hT[:, :, :], hT_r, [dr["h2T"][:, (4 * i + c_) * 512:(4 * i + c_ + 1) * 512].rearrange("(c p) n -> p c n", p=128) for c_ in range(4)],
                              qstg[:, :, :], qstg_r, hsem, 0, 128)
            else:
                kb.dma(hsem, hT[:, :, :], dr["h2T"][:, q0:q0 + 512].rearrange("(c p) n -> p c n", p=128), writes=[hT_r])
            (wgn,), w_r = P.load_w([dr["win_gn"][:, :]])
            for qs in range(4):
                ps, ps_r = sbank()
                for c in range(8):
                    kb.op(kb.pe, lambda c=c: nc.tensor.matmul(ps[:, 0:48], lhsT=hT[:, c, qs * 128:(qs + 1) * 128], rhs=wgn[:, c, :], start=(c == 0), stop=(c == 7)),
                          reads=[w_r, hT_r], writes=[ps_r], sig=(c == 7))
                kb.op(kb.act, lambda: nc.scalar.activation(out=gat[:, qs, :], in_=ps[:, 0:48], func=AF.Sigmoid), reads=[ps_r], writes=[gat_r])
            if MONO:
                klo = max(q0 - 512, 0)
                kb.dma(kwsem, KW[0:64, 1024 - (q0 + 512 - klo):1024], dr["kvall"][3, g * 64:(g + 1) * 64, klo:q0 + 512], writes=[KW_r])
                w0 = 8 - (q0 + 512 - klo) // 128
                kb.dma(vwsem, VW[:, w0:8, 0:64], dr["vall"][klo:q0 + 512, 128 + g * 64:128 + (g + 1) * 64].rearrange("(w p) d -> p w d", p=128), writes=[VW_r])
            elif QSEL:
                for half in range(2):
                    sts = [4 * i + c_ - 1 + half for c_ in range(4)]
                    P.select_tile(KW[0:64, half * 512:(half + 1) * 512], KW_r,
                                  [dr["kvall"][3, g * 64:(g + 1) * 64, st_ * 512:(st_ + 1) * 512] if st_ >= 0 else None for st_ in sts],
                                  qstg[0:64, 0, :], qstg_r, kwsem, 0, 64)
                    P.select_tile(VW[:, half * 4:(half + 1) * 4, 0:64], VW_r,
                                  [dr["vall"][st_ * 512:(st_ + 1) * 512, 128 + g * 64:128 + (g + 1) * 64].rearrange("(w p) d -> p w d", p=128) if st_ >= 0 else None for st_ in sts],
                                  qstg[:, 1, 0:256].rearrange("p (w d) -> p w d", d=64), qstg_r, vwsem, 0, 128)
            else:
                kb.dma(kwsem, KW[0:64, :], dr["kwin"][g * 64:(g + 1) * 64, i, :], writes=[KW_r])
                kb.dma(vwsem, VW[:, :, 0:64], dr["vwin"][i, :, g * 64:(g + 1) * 64].rearrange("(w p) d -> p w d", p=128), writes=[VW_r])
            for hp in range(4):
                (wq,), w_r = P.load_w([W[g * 4 + hp]])
                for hh in range(2):
                    hl = hp * 2 + hh
                    ps, ps_r = sbank()
                    for c in range(8):
                        kb.op(kb.pe, lambda c=c: nc.tensor.matmul(ps[0:64, :], lhsT=wq[:, c, hh * 64:(hh + 1) * 64], rhs=hT[:, c, :], start=(c == 0), stop=(c == 7)),
                              reads=[w_r, hT_r], writes=[ps_r], sig=(c == 7))
                    kb.op(kb.dve, lambda: nc.vector.tensor_scalar(out=Q[0:64, 0, hl, :], in0=ps[0:64, :], scalar1=0.125, scalar2=None, op0=ALU.mult), reads=[ps_r], writes=[Q_r[hl]])
                    kb.op(kb.pool, lambda: nc.gpsimd.tensor_copy(out=Q[0:64, 1:3, hl, :], in_=Q[0:64, 0:1, hl, :].to_broadcast([64, 2, 512])),
                          reads=[Q_r[hl]], writes=[Q_r[hl]])
            for v_ in range(3):
                kb.dma(qsem, Q[124:127, v_, :, :], dr["qal"][:, g * 8:(g + 1) * 8, q0:q0 + 512], writes=Q_r)
            for hl in range(8):
                h = g * 8 + hl
                accs = [(P.psf[3 + 2 * (hl % 2)], P.psf_r[3 + 2 * (hl % 2)]), (P.psf[4 + 2 * (hl % 2)], P.psf_r[4 + 2 * (hl % 2)])]
                for a, a_r in accs:
                    kb.op(kb.dve, lambda: nc.vector.memset(a[:, 0:386], 0.0), writes=[a_r])
                cq_ = []
                for cu in list(cmp_tiles) + [None, None]:
                    if cu is not None:
                        nt, mi_ = cu
                        ps, ps_r = sbank()
                        kb.op(kb.pe, lambda: nc.tensor.matmul(ps[:, :], lhsT=KC[0:128, g, nt * 128:(nt + 1) * 128], rhs=Q[0:128, 0, hl, :], start=True, stop=True),
                              reads=[KC_r, Q_r[hl]], writes=[ps_r])
                        e, e_r = exp_tile(ps, ps_r, KBC[:, h * 4 + nt:h * 4 + nt + 1], KBC_r,
                                          CPM[:, mi_, :] if mi_ is not None else None, CPM_r)
                        cq_.append((nt, e, e_r))
                    if cq_ and (len(cq_) > 2 or cu is None):
                        nt_, e, e_r = cq_.pop(0)
                        for qs in range(4):
                            a, a_r = accs[qs // 2]
                            o0 = (qs % 2) * 193
                            kb.op(kb.pe, lambda: nc.tensor.matmul(a[:, o0:o0 + 193], lhsT=e[:, qs * 128:(qs + 1) * 128], rhs=VC[:, nt_, g, :], start=False, stop=(nt_ == cmp_tiles[-1][0]), skip_group_check=True),
                                  reads=[e_r, VC_r], writes=[a_r], sig=(qs == 3 or qs == 1))
                assert not cq_
                for half in range(2):
                    a, a_r = accs[half]
                    av = a[:, 0:386].rearrange("p (q c) -> p q c", c=193)
                    kb.op(kb.dve, lambda: nc.vector.tensor_scalar(out=crd[:, 2 * half:2 * half + 2, hl], in0=av[:, :, 64], scalar1=1e-30, scalar2=None, op0=ALU.max),
                          reads=[a_r], writes=[crd_r[hl]])
                    kb.op(kb.dve, lambda: nc.vector.tensor_copy(out=cst[:, 2 * half:2 * half + 2, hl, :], in_=av[:, :, 0:64]), reads=[a_r], writes=[cst_r[hl]])
                kb.op(kb.dve, lambda: nc.vector.reciprocal(out=crd[:, :, hl], in_=crd[:, :, hl]), reads=[crd_r[hl]], writes=[crd_r[hl]])
                for qs in range(4):
                    a, a_r = accs[qs // 2]
                    o0 = (qs % 2) * 193
                    if hl == 0:
                        kb.op(kb.dve, lambda: nc.vector.tensor_scalar(out=sc[:, qs, :], in0=a[:, o0 + 65:o0 + 193], scalar1=crd[:, qs, hl:hl + 1], scalar2=None, op0=ALU.mult),
                              reads=[a_r, crd_r[hl]], writes=[sc_r])
                    else:
                        kb.op(kb.dve, lambda: nc.vector.scalar_tensor_tensor(out=sc[:, qs, :], in0=a[:, o0 + 65:o0 + 193], scalar=crd[:, qs, hl:hl + 1], in1=sc[:, qs, :],
                                                                              op0=ALU.mult, op1=ALU.add), reads=[a_r, crd_r[hl], sc_r], writes=[sc_r])
            c0_ = 0 if MONO else i * 4
            kb.op(kb.dve, lambda: nc.vector.tensor_tensor(out=smb[:, :, :], in0=sc[:, :, :], in1=ADM[:, c0_:c0_ + 4, :], op=ALU.add), reads=[sc_r, ADM_r], writes=smb_r)
            for qs in range(4):
                kb.op(kb.dve, lambda: nc.vector.max(out=m8[:, qs, :], in_=smb[:, qs, :]), reads=[smb_r[qs]], writes=[m8_r[qs]])
            for qs in range(4):
                kb.op(kb.dve, lambda: nc.vector.match_replace(out=wa[:, qs, :], in_to_replace=m8[:, qs, :], in_values=smb[:, qs, :], imm_value=-1e30),
                      reads=[smb_r[qs], m8_r[qs]], writes=[wa_r[qs]])
            for qs in range(4):
                kb.op(kb.dve, lambda: nc.vector.max(out=m8b[:, qs, :], in_=wa[:, qs, :]), reads=[wa_r[qs]], writes=[m8b_r[qs]])
            for qs in range(4):
                kb.op(kb.dve, lambda: nc.vector.match_replace(out=wb[:, qs, :], in_to_replace=m8b[:, qs, :], in_values=wa[:, qs, :], imm_value=-1e30),
                      reads=[wa_r[qs], m8b_r[qs]], writes=[wb_r[qs]])
            kb.op(kb.dve, lambda: nc.vector.tensor_tensor(out=wa[:, :, :], in0=smb[:, :, :], in1=wb[:, :, :], op=ALU.subtract), reads=smb_r + wb_r, writes=wa_r)
            kb.op(kb.dve, lambda: nc.vector.tensor_scalar(out=wa[:, :, :], in0=wa[:, :, :], scalar1=1.0, scalar2=-NEG, op0=ALU.min, op1=ALU.mult), reads=wa_r, writes=wa_r)
            for v_ in range(3):
                nb_ = 60 if v_ < 2 else 8
                kb.op(kb.dve, lambda: nc.vector.scalar_tensor_tensor(out=nqv[:, :, v_, 64:64 + nb_], in0=wa[:, :, 60 * v_:60 * v_ + nb_], scalar=NEG,
                                                                      in1=VNG[:, c0_:c0_ + 4, 60 * v_:60 * v_ + nb_], op0=ALU.add, op1=ALU.min),
                      reads=wa_r + [VNG_r], writes=[nq_r])
            for qs in range(4):
                for v_ in range(3):
                    kb.op(kb.pe, lambda: nc.tensor.transpose(out=P.psb[:, v_ * 128:(v_ + 1) * 128], in_=nqv[:, qs, v_, :], identity=IDB[:, :]),
                          reads=[nq_r, IDB_r], writes=[P.psb_r])
                for v_ in range(3):
                    src_ = P.psb[64:124, v_ * 128:(v_ + 1) * 128].rearrange("p (o n) -> p o n", o=1).to_broadcast([60, 8, 128])
                    kb.op(kb.dve, lambda: nc.vector.tensor_copy(out=Q[64:124, v_, :, qs * 128:(qs + 1) * 128], in_=src_),
                          reads=[P.psb_r], writes=Q_r)
            for hl in range(8):
                h = g * 8 + hl
                aS, aS_r = P.psf[3 + 2 * (hl % 2)], P.psf_r[3 + 2 * (hl % 2)]
                aW, aW_r = P.psf[4 + 2 * (hl % 2)], P.psf_r[4 + 2 * (hl % 2)]
                nkt = KT0 * (i + 1)
                if hl == 0:
                    kb.op(kb.dve, lambda: nc.vector.memset(aS[:, 0:260], 0.0), writes=[aS_r])
                    kb.op(kb.dve, lambda: nc.vector.memset(aW[:, 0:260], 0.0), writes=[aW_r])
                units = [("s", kt) for kt in range(nkt)] + [("w", w) for w in range(8)]
                SKEW = 2
                pendq = []
                for u in units + [None] * SKEW:
                    cur = None
                    if u is not None:
                        kind, ix = u
                        ps, ps_r = sbank()
                        qa, qb = 0, 512
                        if kind == "s":
                            r_ = ix - KT0 * i
                            if MONO and r_ >= 0:
                                qa = 128 * r_
                            kb.op(kb.pe, lambda: nc.tensor.matmul(ps[:, qa:qb], lhsT=KA[0:128, ix * 128:(ix + 1) * 128], rhs=Q[0:128, (2 * ix) // 60, hl, qa:qb], start=True, stop=True),
                                  reads=[KA_r, Q_r[hl]], writes=[ps_r])
                            p, p_r = exp_tile(ps, ps_r, KBS[:, h * 64 + ix:h * 64 + ix + 1], KBS_r, CM[:, r_, :] if r_ >= 0 else None, CM_r, qa, qb)
                        else:
                            if ix < 4:
                                qb = 128 * (ix + 1)
                            else:
                                qa = 128 * (ix - 4)
                            kb.op(kb.pe, lambda: nc.tensor.matmul(ps[:, qa:qb], lhsT=KW[0:128, ix * 128:(ix + 1) * 128], rhs=Q[0:128, 0, hl, qa:qb], start=True, stop=True),
                                  reads=[KW_r, Q_r[hl]], writes=[ps_r])
                            cb = (ix * 16 + h) if MONO else ((i * 8 + ix) * 16 + h)
                            p, p_r = exp_tile(ps, ps_r, KBW[:, cb:cb + 1], KBW_r, WM[:, ix, :], WM_r, qa, qb)
                        cur = (kind, ix, p, p_r, qa, qb)
                        pendq.append(cur)
                    if pendq and (len(pendq) > SKEW or u is None):
                        kind_, ix_, pp, pp_r, qa_, qb_ = pendq.pop(0)
                        qss = list(range(qa_ // 128, qb_ // 128))
                        for qs in qss:
                            if kind_ == "s":
                                kb.op(kb.pe, lambda: nc.tensor.matmul(aS[:, qs * 65:(qs + 1) * 65], lhsT=pp[:, qs * 128:(qs + 1) * 128], rhs=VA[:, ix_, :], start=False, stop=(ix_ == nkt - 1), skip_group_check=True),
                                      reads=[pp_r, VA_r], writes=[aS_r], sig=(qs == qss[-1]))
                            else:
                                kb.op(kb.pe, lambda: nc.tensor.matmul(aW[:, qs * 65:(qs + 1) * 65], lhsT=pp[:, qs * 128:(qs + 1) * 128], rhs=VW[:, ix_, :], start=False, stop=(ix_ == 7), skip_group_check=True),
                                      reads=[pp_r, VW_r], writes=[aW_r], sig=(qs == qss[-1]))
                assert not pendq
                if hl < 7:
                    nS, nS_r = P.psf[3 + 2 * ((hl + 1) % 2)], P.psf_r[3 + 2 * ((hl + 1) % 2)]
                    nW, nW_r = P.psf[4 + 2 * ((hl + 1) % 2)], P.psf_r[4 + 2 * ((hl + 1) % 2)]
                    kb.op(kb.dve, lambda: nc.vector.memset(nS[:, 0:260], 0.0), writes=[nS_r])
                    kb.op(kb.dve, lambda: nc.vector.memset(nW[:, 0:260], 0.0), writes=[nW_r])
                aSv = aS[:, 0:260].rearrange("p (q c) -> p q c", c=65)
                aWv = aW[:, 0:260].rearrange("p (q c) -> p q c", c=65)
                kb.op(kb.dve, lambda: nc.vector.tensor_scalar(out=dd[:, 0, :], in0=aSv[:, :, 64], scalar1=1e-30, scalar2=None, op0=ALU.max), reads=[aS_r], writes=[dd_r])
                kb.op(kb.dve, lambda: nc.vector.tensor_scalar(out=dd[:, 1, :], in0=aWv[:, :, 64], scalar1=1e-30, scalar2=None, op0=ALU.max), reads=[aW_r], writes=[dd_r])
                kb.op(kb.dve, lambda: nc.vector.reciprocal(out=dd[:, :, :], in_=dd[:, :, :]), reads=[dd_r], writes=[dd_r])
                kb.op(kb.dve, lambda: nc.vector.tensor_tensor(out=cf[:, 0, :], in0=crd[:, :, hl], in1=gat[:, :, h * 3 + 0], op=ALU.mult), reads=[crd_r[hl], gat_r], writes=[cf_r])
                kb.op(kb.dve, lambda: nc.vector.tensor_tensor(out=cf[:, 1, :], in0=dd[:, 0, :], in1=gat[:, :, h * 3 + 1], op=ALU.mult), reads=[dd_r, gat_r], writes=[cf_r])
                kb.op(kb.dve, lambda: nc.vector.tensor_tensor(out=cf[:, 2, :], in0=dd[:, 1, :], in1=gat[:, :, h * 3 + 2], op=ALU.mult), reads=[dd_r, gat_r], writes=[cf_r])
                for qs in range(4):
                    kb.op(kb.dve, lambda: nc.vector.tensor_scalar(out=oh[:, :], in0=cst[:, qs, hl, :], scalar1=cf[:, 0, qs:qs + 1], scalar2=None, op0=ALU.mult),
                          reads=[cst_r[hl], cf_r], writes=[oh_r])
                    kb.op(kb.dve, lambda: nc.vector.scalar_tensor_tensor(out=oh[:, :], in0=aS[:, qs * 65:qs * 65 + 64], scalar=cf[:, 1, qs:qs + 1], in1=oh[:, :], op0=ALU.mult, op1=ALU.add),
                          reads=[aS_r, cf_r, oh_r], writes=[oh_r])
                    kb.op(kb.dve, lambda: nc.vector.scalar_tensor_tensor(out=onb[:, qs, h * 64:(h + 1) * 64], in0=aW[:, qs * 65:qs * 65 + 64], scalar=cf[:, 2, qs:qs + 1], in1=oh[:, :],
                                                                          op0=ALU.mult, op1=ALU.add), reads=[aW_r, cf_r, oh_r], writes=[onb_r])
            for fc in range(4 * g, 4 * g + 4):
                k = fc % 2
                for qs in range(4):
                    kb.op(kb.pe, lambda: nc.tensor.transpose(out=P.psb[:, 0:128], in_=onb[:, qs, fc * 128:(fc + 1) * 128], identity=IDB[:, :]),
                          reads=[onb_r, IDB_r], writes=[P.psb_r])
                    kb.op(kb.dve, lambda: nc.vector.tensor_copy(out=ost[k][:, qs * 128:(qs + 1) * 128], in_=P.psb[:, 0:128]), reads=[P.psb_r], writes=[ost_r[k]])
                kb.dma(osems[k], dr["onsaT"][fc * 128:(fc + 1) * 128, q0:q0 + 512], ost[k][:, :], reads=[ost_r[k]], eng=(kb.act if (MONO or QSEL) else kb.sp))


BF = ml_dtypes.bfloat16
_CACHE = {}
USE_MONO = True


def _slopes():
    hh = np.arange(1, 17, dtype=np.float32)
    return np.exp2(-8.0 * hh / 16.0).astype(np.float32)


def _split3(v):
    v = v.astype(np.float32)
    hi = v.astype(BF)
    r = v - hi.astype(np.float32)
    mid = r.astype(BF)
    r2 = r - mid.astype(np.float32)
    lo = r2.astype(BF)
    return hi, mid, lo


def core_consts(cc):
    sl = _slopes()
    p = np.arange(128)
    q = np.arange(512)
    c = {}
    tabs = np.concatenate([(4 * i + cc) * 512 + q for i in range(4)]).astype(np.float32)
    v = -(sl[:, None] * tabs[None, :])
    hi, mid, lo = _split3(v)
    c["qal"] = np.ascontiguousarray(np.stack([hi, mid, lo], 0))
    kbs = np.zeros((128, 16, 64), np.float32)
    for h in range(16):
        kbs[:, h, :] = sl[h] * (np.arange(64)[None, :] * 128 + p[:, None]).astype(np.float32)
    c["kbs"] = kbs.reshape(128, 1024)
    kbc = np.zeros((128, 16, 4), np.float32)
    for h in range(16):
        kbc[:, h, :] = sl[h] * (16 * (np.arange(4)[None, :] * 128 + p[:, None]) + 31).astype(np.float32)
    c["kbc"] = kbc.reshape(128, 64)
    kbw = np.zeros((128, 4, 8, 16), np.float32)
    for i in range(4):
        T0 = (4 * i + cc) * 512
        for w in range(8):
            ka = T0 - 512 + w * 128 + p
            for h in range(16):
                kbw[:, i, w, h] = np.where(ka >= 0, sl[h] * ka.astype(np.float32), -30000.0)
    c["kbw"] = kbw.reshape(128, 512)
    cmpm = np.zeros((128, 2, 512), np.float32)
    for d in (-1, 0):
        vis = (2048 * d + 16 * p[:, None] + 31 - 512 * cc) <= q[None, :]
        cmpm[:, d + 1, :] = np.where(vis, 0.0, NEG)
    c["cmpm"] = cmpm.astype(BF)
    cm = np.zeros((128, 16, 512), np.float32)
    for r in range(16):
        vis = (128 * r + p[:, None]) <= (512 * cc + q[None, :])
        cm[:, r, :] = np.where(vis, 0.0, NEG)
    c["cm"] = cm.astype(BF)
    wm = np.zeros((128, 8, 512), np.float32)
    for w in range(8):
        dist = 512 + q[None, :] - 128 * w - p[:, None]
        wm[:, w, :] = np.where((dist >= 0) & (dist < 512), 0.0, NEG)
    c["wm"] = wm.astype(BF)
    addm = np.zeros((128, 16, 128), np.float32)
    vneg = np.zeros((128, 16, 128), np.float32)
    j = np.arange(128)
    for i in range(4):
        for qs in range(4):
            t = (4 * i + cc) * 512 + qs * 128 + p
            valid = (j[None, :] * 64) <= t[:, None]
            cur = t // 64
            forced = valid & ((j[None, :] == 0) | (j[None, :] == cur[:, None]) | (j[None, :] == cur[:, None] - 1))
            addm[:, i * 4 + qs, :] = np.where(forced, 8192.0, np.where(valid, 0.0, -8192.0))
            vneg[:, i * 4 + qs, :] = np.where(valid, 0.0, NEG)
    c["addm"] = addm.astype(BF)
    c["vneg"] = vneg.astype(BF)
    return c


def shared_consts():
    c = {}
    cols = np.arange(SEQ)
    kar = np.zeros((64, SEQ), np.float32)
    kar[0:60] = ((cols[None, :] // 64) % 60 == np.arange(60)[:, None])
    kar[60:63] = 1.0
    c["karows"] = kar.astype(BF)
    c["ones3"] = np.ones((3, 1024), np.float32).astype(BF)
    n = np.arange(512)
    cs = n[:, None] * 16
    ss = np.arange(128)[None, :] * 64
    ov = np.clip(np.minimum(cs + 32, ss + 64) - np.maximum(cs, ss), 0, None) / 32.0
    c["ovl"] = np.ascontiguousarray(ov.reshape(4, 128, 128).transpose(1, 0, 2)).astype(np.float32).astype(BF)
    c["identb"] = np.eye(128, dtype=np.float32).astype(BF)
    return c


def tile_w(Wm, starts=None):
    K = Wm.shape[0]
    if starts is None:
        starts = list(range(0, Wm.shape[1], 128))
    out = np.empty((len(starts), 128, K // 128, 128), np.float32)
    for j, c0 in enumerate(starts):
        out[j] = Wm[:, c0:c0 + 128].reshape(K // 128, 128, 128).transpose(1, 0, 2)
    return out


def _prog(key, fn):
    if key not in _CACHE:
        _CACHE[key] = fn()
    return _CACHE[key]


def _tok_index(cc):
    return np.concatenate([np.arange((4 * i + cc) * 512, (4 * i + cc + 1) * 512) for i in range(4)])


def kernel(x, ffn1_norm, ffn1_w_gate, ffn1_w_up, ffn1_w_down, mix_norm, w_in, cmp_pos,
           cmp_k_w1, cmp_k_w2, cmp_v_w1, cmp_v_w2, pool_w, pool_scale, w_branch_pool,
           w_branch_nsa, w_out, ffn2_norm, ffn2_w_gate, ffn2_w_up, ffn2_w_down, final_norm):
    f32 = lambda a: np.ascontiguousarray(np.asarray(a, dtype=np.float32))
    x = f32(x)
    W = {k: f32(v) for k, v in dict(ffn1_norm=ffn1_norm, ffn1_w_gate=ffn1_w_gate, ffn1_w_up=ffn1_w_up, ffn1_w_down=ffn1_w_down,
                                    mix_norm=mix_norm, w_in=w_in, cmp_pos=cmp_pos, cmp_k_w1=cmp_k_w1, cmp_k_w2=cmp_k_w2,
                                    cmp_v_w1=cmp_v_w1, cmp_v_w2=cmp_v_w2, pool_w=pool_w, pool_scale=pool_scale,
                                    w_branch_pool=w_branch_pool, w_branch_nsa=w_branch_nsa, w_out=w_out, ffn2_norm=ffn2_norm,
                                    ffn2_w_gate=ffn2_w_gate, ffn2_w_up=ffn2_w_up, ffn2_w_down=ffn2_w_down, final_norm=final_norm).items()}
    cores = list(range(8))
    vecs = np.zeros((128, 64), np.float32)
    for l in range(NL):
        b0 = vbase(l)
        vecs[:, b0:b0 + 8] = gain_layout(W["ffn1_norm"][l])
        vecs[:, b0 + 8:b0 + 16] = gain_layout(W["mix_norm"][l])
        vecs[:, b0 + 16:b0 + 24] = gain_layout(W["ffn2_norm"][l])
        vecs[:, b0 + 24:b0 + 28] = gain_layout(W["pool_scale"][l])
    vecs[:, 56:64] = gain_layout(W["final_norm"])
    if USE_MONO:
        return kernel_mono(W, x, vecs)
    tix = [_tok_index(c % 4) for c in cores]
    cc_consts = [core_consts(cc) for cc in range(4)]
    sh = shared_consts()

    TW = {}
    for l in range(NL):
        TW["win", l] = tile_w(W["w_in"][l], WIN_STARTS)
        for nm in ("ffn1_w_gate", "ffn1_w_up", "ffn1_w_down", "ffn2_w_gate", "ffn2_w_up", "ffn2_w_down", "w_branch_pool", "w_branch_nsa", "w_out",
                   "cmp_k_w1", "cmp_v_w1"):
            TW[nm, l] = tile_w(W[nm][l])

    def a_weights(l):
        return {"f1_wg": TW["ffn1_w_gate", l], "f1_wu": TW["ffn1_w_up", l], "f1_wd": TW["ffn1_w_down", l], "win_a": TW["win", l]}

    progA = _prog("A", lambda: build_tok(0, "A", False))
    in_maps = []
    for c in cores:
        m = {"xs_in": np.ascontiguousarray(x[c // 4, tix[c], :].T), "vecs": vecs}
        m.update(a_weights(0))
        in_maps.append(m)
    res = run_bass_kernel_spmd(progA, in_maps, core_ids=cores).results

    out = np.zeros((2, SEQ, D), np.float32)
    for l in range(NL):
        last = (l == NL - 1)
        kvall = np.zeros((2, 4, 128, SEQ + 32), BF)
        vall = np.zeros((2, SEQ, 256), BF)
        uall = np.zeros((2, 512, SEQ), np.float32)
        for c in cores:
            b = c // 4
            kvall[b][:, :, tix[c]] = np.asarray(res[c]["kvT_out"]).view(BF) if np.asarray(res[c]["kvT_out"]).dtype != BF else res[c]["kvT_out"]
            vall[b][tix[c], :] = np.asarray(res[c]["vtok_out"])
            uall[b][:, tix[c]] = np.asarray(res[c]["uT_out"])
        prog = _prog(("BCA", l, last), lambda: build_bca(l, last))
        pecol = np.ascontiguousarray(W["cmp_pos"][l].reshape(16, 2, 64).transpose(1, 2, 0).reshape(128, 16))
        in_maps = []
        for c in cores:
            b, cc = c // 4, c % 4
            kwin = np.zeros((128, 4, 1024), BF)
            vwin = np.zeros((4, 1024, 128), BF)
            uext = np.zeros((512, 4, 528), np.float32)
            for i in range(4):
                T0 = (4 * i + cc) * 512
                lo = max(T0 - 512, 0)
                kwin[:, i, 1024 - (T0 + 512 - lo):] = kvall[b][3][:, lo:T0 + 512]
                vwin[i, 1024 - (T0 + 512 - lo):, :] = vall[b][lo:T0 + 512, 128:256]
                lo = max(T0 - 16, 0)
                uext[:, i, 528 - (T0 + 512 - lo):] = uall[b][:, lo:T0 + 512]
            corr = np.ones((128, 4, 16), np.float32)
            if cc == 0:
                for gi, w in enumerate(POOLW):
                    corr[:, gi, :] = (w / np.minimum(np.arange(16) + 1.0, float(w)))[None, :]
            m = {"vecs": vecs, "h2T": np.asarray(res[c]["h2T_out"]), "kvall": kvall[b], "vall": vall[b], "kwin": kwin, "vwin": vwin,
                 "win": TW["win", l], "win_gn": np.ascontiguousarray(W["w_in"][l][:, C_GN:C_GN + 48]),
                 "ck_w1": TW["cmp_k_w1", l], "ck_w2": W["cmp_k_w2"][l], "cv_w1": TW["cmp_v_w1", l], "cv_w2": W["cmp_v_w2"][l],
                 "pecol": pecol,
                 "xs_in": np.asarray(res[c]["xs_out"]),
                 "f2_wg": TW["ffn2_w_gate", l], "f2_wu": TW["ffn2_w_up", l], "f2_wd": TW["ffn2_w_down", l],
                 "wpa": TW["w_branch_pool", l], "wnb": TW["w_branch_nsa", l], "wo": TW["w_out", l],
                 "poolw": W["pool_w"][l], "uext": uext, "corr": corr}
            m.update(cc_consts[cc])
            m.update(sh)
            if not last:
                m.update(a_weights(l + 1))
            in_maps.append(m)
        res = run_bass_kernel_spmd(prog, in_maps, core_ids=cores).results
    for c in cores:
        out[c // 4, tix[c], :] = np.asarray(res[c]["out"]).T
    return out


def build_mono():
    global MONO
    MONO = True
    try:
        BI, IN_, BO = "ExternalInput", "Internal", "ExternalOutput"
        S = SEQ
        specs = {
            "x_in": ((D, S), F32, BI), "vecs": ((128, 64), F32, BI),
            "qal": ((3, 16, S), BF16, BI), "kbs": ((128, 1024), F32, BI), "kbc": ((128, 64), F32, BI),
            "kbw": ((16, 128, 128), F32, BI), "cmpm": ((128, 5, 512), BF16, BI), "cm": ((128, 4, 512), BF16, BI),
            "wm": ((128, 8, 512), BF16, BI), "addm": ((16, 128, 4, 128), BF16, BI), "vneg": ((16, 128, 4, 128), BF16, BI),
            "karows": ((64, S), BF16, BI), "ones3": ((3, 1024), BF16, BI), "ovl": ((128, 4, 128), BF16, BI), "identb": ((128, 128), BF16, BI),
            "corr": ((128, 4, 16), F32, BI),
            "xs": ((D, S), F32, IN_), "h2T": ((D, S), BF16, IN_), "kvT": ((4, 128, S), BF16, IN_), "vtok": ((S, 256), BF16, IN_),
            "uT0": ((512, S), F32, IN_), "uT1": ((512, S), F32, IN_), "onsaT": ((D, S), BF16, IN_),
            "out_q": ((D, NTOK), F32, BO), "onsaT_q": ((D, NTOK), BF16, IN_),
            "selw": ((128, 4), F32, BI), "corr_q": ((128, 4, 16), F32, BI),
            "qal_q": ((3, 16, NTOK), BF16, BI), "kbw_q": ((128, 512), F32, BI), "cmpm_q": ((128, 2, 512), BF16, BI),
            "cm_q": ((128, 16, 512), BF16, BI), "addm_q": ((128, 16, 128), BF16, BI), "vneg_q": ((128, 16, 128), BF16, BI),
        }
        for l in range(NL):
            for pre in ("f1_", "f2_"):
                specs["%swg_%d" % (pre, l)] = ((NFC, 128, 8, 128), F32, BI)
                specs["%swu_%d" % (pre, l)] = ((NFC, 128, 8, 128), F32, BI)
                specs["%swd_%d" % (pre, l)] = ((8, 128, NFC, 128), F32, BI)
            specs["win_%d" % l] = ((len(WIN_STARTS), 128, 8, 128), F32, BI)
            specs["win_gn_%d" % l] = ((D, 48), F32, BI)
            specs["wpa_%d" % l] = ((8, 128, 4, 128), F32, BI)
            specs["wnb_%d" % l] = ((8, 128, 8, 128), F32, BI)
            specs["wo_%d" % l] = ((8, 128, 8, 128), F32, BI)
            specs["poolw_%d" % l] = ((4, 128, 128), F32, BI)
            specs["ck_w1_%d" % l] = ((2, 128, 16, 128), F32, BI)
            specs["cv_w1_%d" % l] = ((2, 128, 16, 128), F32, BI)
            specs["ck_w2_%d" % l] = ((256, 64), F32, BI)
            specs["cv_w2_%d" % l] = ((256, 64), F32, BI)
            specs["pecol_%d" % l] = ((128, 16), F32, BI)
        conv = [n_ for n_, (sh_, dt_, k_) in specs.items() if k_ == BI and dt_ == F32 and len(sh_) == 4 and n_ != "kbw"]
        gu = [n_ for n_ in conv if n_[3:5] in ("wg", "wu")]
        conv = [n_ for n_ in conv if n_ not in gu]
        for n_ in conv:
            specs[n_ + "_b"] = (specs[n_][0], BF16, IN_)
        for l in range(NL):
            for pre in ("f1_", "f2_"):
                specs["%swgu_%d_b" % (pre, l)] = ((NFC, 128, 16, 128), BF16, IN_)
        P = Prog(specs, WST=None)
        kb, dr = P.kb, P.dr

        P.selw = kb.sb("selw", [128, 4], F32)
        P.selw_r = Res("selw")
        kb.dma(P.ldsem, P.selw[:, :], dr["selw"][:, :], writes=[P.selw_r])
        mono_tabs = {k_: dr[k_] for k_ in ("qal", "kbw", "cmpm", "cm", "addm", "vneg", "corr", "onsaT")}

        def do_convert():
            for n_ in conv:
                P.convert_w(dr[n_], dr[n_ + "_b"])
                dr[n_] = dr[n_ + "_b"]
            for l_ in range(NL):
                for pre in ("f1_", "f2_"):
                    d_ = dr["%swgu_%d_b" % (pre, l_)]
                    P.convert_w(dr["%swg_%d" % (pre, l_)], None, dst_fn=lambda j, d_=d_: d_[j, :, 0:8, :])
                    P.convert_w(dr["%swu_%d" % (pre, l_)], None, dst_fn=lambda j, d_=d_: d_[j, :, 8:16, :])

        def alias(l, mode):
            for nm in ("wpa", "wnb", "wo", "poolw", "ck_w1", "cv_w1", "ck_w2", "cv_w2", "pecol", "win_gn"):
                dr[nm] = dr["%s_%d" % (nm, l)]
            dr["win"] = dr["win_%d" % l]
            dr["win_c"] = dr["win_%d" % l]
            dr["f2_wd"] = dr["f2_wd_%d" % l]
            dr["f2_wg"] = dr["f2_wgu_%d_b" % l]
            dr["f2_wu"] = None
            la = l if mode == "A" else min(l + 1, NL - 1)
            dr["win_a"] = dr["win_%d" % la]
            dr["f1_wd"] = dr["f1_wd_%d" % la]
            dr["f1_wg"] = dr["f1_wgu_%d_b" % la]
            dr["f1_wu"] = None
            dr["kvall"] = dr["kvT"]
            dr["vall"] = dr["vtok"]
            dr["h2T_in"] = dr["h2T"]
            dr["h2T_out"] = dr["h2T"]
            dr["kvT_out"] = dr["kvT"]
            dr["vtok_out"] = dr["vtok"]
            dr["xs_out"] = dr["xs"]
            dr["xs_in"] = dr["x_in"] if (mode == "A" and l == 0) else dr["xs"]
            dr["uT_in"] = dr["uT%d" % (l % 2)]
            dr["uT_out"] = dr["uT%d" % (la % 2)]

        def phase(fn, wst, wstf=1024, nslot=4):
            with ExitStack() as pes:
                kb.cur_es = pes
                P.alloc_wstage(wst, wstf, nslot)
                fn()
                kb.barrier()
            kb.cur_es = None

        phase(do_convert, 3584, 3584, 2)
        alias(0, "A")
        phase(lambda: tok_body(P, 0, "A", False), 3584)
        global QSEL
        for l in range(NL):
            last = (l == NL - 1)
            alias(l, "CA")
            if last:
                MONO, QSEL = False, True
                for k_ in ("qal", "kbw", "cmpm", "cm", "addm", "vneg", "corr", "onsaT"):
                    dr[k_] = dr[k_ + "_q"]
                dr["out"] = dr["out_q"]
            phase(lambda: attn_body(P), 2048, 1024, 2)
            phase(lambda: tok_body(P, l, "CA", last), 3584)
        return P.finish()
    finally:
        MONO = False
        QSEL = False


def mono_consts():
    sl = _slopes()
    p = np.arange(128)
    q = np.arange(512)
    c = {}
    tabs = np.arange(SEQ).astype(np.float32)
    hi, mid, lo = _split3(-(sl[:, None] * tabs[None, :]))
    c["qal"] = np.ascontiguousarray(np.stack([hi, mid, lo], 0))
    kbs = np.zeros((128, 16, 64), np.float32)
    kbc = np.zeros((128, 16, 4), np.float32)
    for h in range(16):
        kbs[:, h, :] = sl[h] * (np.arange(64)[None, :] * 128 + p[:, None]).astype(np.float32)
        kbc[:, h, :] = sl[h] * (16 * (np.arange(4)[None, :] * 128 + p[:, None]) + 31).astype(np.float32)
    c["kbs"] = kbs.reshape(128, 1024)
    c["kbc"] = kbc.reshape(128, 64)
    kbw = np.zeros((16, 128, 8, 16), np.float32)
    for i in range(16):
        for w in range(8):
            ka = 512 * (i - 1) + w * 128 + p
            for h in range(16):
                kbw[i, :, w, h] = np.where(ka >= 0, sl[h] * ka.astype(np.float32), -30000.0)
    c["kbw"] = kbw.reshape(16, 128, 128)
    cmpm = np.zeros((128, 5, 512), np.float32)
    for d in range(5):
        cmpm[:, d, :] = np.where((16 * p[:, None] + 31 - 512 * d) <= q[None, :], 0.0, NEG)
    c["cmpm"] = cmpm.astype(BF)
    cm = np.zeros((128, 4, 512), np.float32)
    for r in range(4):
        cm[:, r, :] = np.where((128 * r + p[:, None]) <= q[None, :], 0.0, NEG)
    c["cm"] = cm.astype(BF)
    wm = np.zeros((128, 8, 512), np.float32)
    for w in range(8):
        dist = 512 + q[None, :] - 128 * w - p[:, None]
        wm[:, w, :] = np.where((dist >= 0) & (dist < 512), 0.0, NEG)
    c["wm"] = wm.astype(BF)
    addm = np.zeros((16, 128, 4, 128), np.float32)
    vneg = np.zeros((16, 128, 4, 128), np.float32)
    j = np.arange(128)
    for i in range(16):
        for qs in range(4):
            t = i * 512 + qs * 128 + p
            valid = (j[None, :] * 64) <= t[:, None]
            cur = t // 64
            forced = valid & ((j[None, :] == 0) | (j[None, :] == cur[:, None]) | (j[None, :] == cur[:, None] - 1))
            addm[i, :, qs, :] = np.where(forced, 8192.0, np.where(valid, 0.0, -8192.0))
            vneg[i, :, qs, :] = np.where(valid, 0.0, NEG)
    c["addm"] = addm.astype(BF)
    c["vneg"] = vneg.astype(BF)
    corr = np.ones((128, 4, 16), np.float32)
    for gi, w in enumerate(POOLW):
        corr[:, gi, :] = (w / np.minimum(np.arange(16) + 1.0, float(w)))[None, :]
    c["corr"] = corr
    c.update(shared_consts())
    return c


def kernel_mono(W, x, vecs):
    prog = _prog("MONO", build_mono)
    base = {"vecs": vecs}
    base.update(mono_consts())
    for l in range(NL):
        base["win_%d" % l] = tile_w(W["w_in"][l], WIN_STARTS)
        base["win_gn_%d" % l] = np.ascontiguousarray(W["w_in"][l][:, C_GN:C_GN + 48])
        for pre, a in (("f1_", "ffn1"), ("f2_", "ffn2")):
            base["%swg_%d" % (pre, l)] = tile_w(W[a + "_w_gate"][l])
            base["%swu_%d" % (pre, l)] = tile_w(W[a + "_w_up"][l])
            base["%swd_%d" % (pre, l)] = tile_w(W[a + "_w_down"][l])
        base["wpa_%d" % l] = tile_w(W["w_branch_pool"][l])
        base["wnb_%d" % l] = tile_w(W["w_branch_nsa"][l])
        base["wo_%d" % l] = tile_w(W["w_out"][l])
        base["poolw_%d" % l] = W["pool_w"][l]
        base["ck_w1_%d" % l] = tile_w(W["cmp_k_w1"][l])
        base["cv_w1_%d" % l] = tile_w(W["cmp_v_w1"][l])
        base["ck_w2_%d" % l] = W["cmp_k_w2"][l]
        base["cv_w2_%d" % l] = W["cmp_v_w2"][l]
        base["pecol_%d" % l] = np.ascontiguousarray(W["cmp_pos"][l].reshape(16, 2, 64).transpose(1, 2, 0).reshape(128, 16))
    cores = list(range(8))
    in_maps = []
    xT = [np.ascontiguousarray(x[b].T) for b in range(2)]
    for c in cores:
        b, cc = c % 2, c // 2
        m = dict(base)
        m["x_in"] = xT[b]
        cq = core_consts(cc)
        for k_ in ("qal", "kbw", "cmpm", "cm", "addm", "vneg"):
            m[k_ + "_q"] = cq[k_]
        selw = np.zeros((128, 4), np.float32)
        selw[:, cc] = 1.0
        m["selw"] = selw
        corr = np.ones((128, 4, 16), np.float32)
        if cc == 0:
            corr = base["corr"]
        m["corr_q"] = corr
        in_maps.append(m)
    res = run_bass_kernel_spmd(prog, in_maps, core_ids=cores).results
    out = np.zeros((2, SEQ, D), np.float32)
    for c in cores:
        out[c % 2, _tok_index(c // 2), :] = np.asarray(res[c]["out_q"]).T
    return out
```

```python
import numpy as np
import ml_dtypes
from contextlib import ExitStack
import concourse.bass as bass
import concourse.mybir as mybir
from concourse.bass_utils import run_bass_kernel_spmd

F32 = mybir.dt.float32
BF16 = mybir.dt.bfloat16
AF = mybir.ActivationFunctionType
ALU = mybir.AluOpType

D = 1024
DFF = 2816
NFC = DFF // 128
SEQ = 8192
NL = 2
NTOK = 2048
MONO = False
QSEL = False


def ntok():
    return SEQ if MONO else NTOK
TT = 512
INW = 4400
C_Q, C_KC, C_VC, C_KSL, C_VSL, C_KWN, C_VWN, C_GN, C_U, C_GM = 0, 1024, 1152, 1280, 1408, 1536, 1664, 1792, 1840, 2352
EPS = 1e-6
NEG = -16384.0
WIN_STARTS = [j * 128 for j in range(8)] + [C_KC, C_VC, C_KSL, C_VSL, C_KWN, C_VWN] + [C_U + j * 128 for j in range(4)] + [C_GM + j * 128 for j in range(16)]
WIN_IDX = {c: i for i, c in enumerate(WIN_STARTS)}


class Sem:
    def __init__(self, h, name):
        self.h = h
        self.name = name
        self.count = 0
        self.group = False


class Tok:
    __slots__ = ("sem", "val")

    def __init__(self, sem, val):
        self.sem = sem
        self.val = val


class Res:
    __slots__ = ("name", "w", "r", "excl")

    def __init__(self, name="", excl=False):
        self.name = name
        self.w = None
        self.r = {}
        self.excl = excl


class Eng:
    def __init__(self, name, h, sem, same_sync):
        self.name = name
        self.h = h
        self.sem = sem
        self.waited = {}
        self.pending = []
        self.same_sync = same_sync


class KB:
    def __init__(self, nc, es):
        self.nc = nc
        self.es = es
        self.sems = []
        self.pe = self._eng("pe", nc.tensor, False)
        self.act = self._eng("act", nc.scalar, True)
        self.dve = self._eng("dve", nc.vector, True)
        self.pool = self._eng("pool", nc.gpsimd, True)
        self.sp = self._eng("sp", nc.sync, False)
        self.engs = [self.pe, self.act, self.dve, self.pool, self.sp]
        self.n_inst = 0

    def new_sem(self, name):
        name = "%s_%d" % (name, len(self.sems))
        h = self.es.enter_context(self.nc.semaphore(name))
        s = Sem(h, name)
        self.sems.append(s)
        return s

    def _eng(self, name, h, same_sync):
        return Eng(name, h, self.new_sem("s_" + name), same_sync)

    def sb(self, name, shape, dtype, es=None):
        self.nsb = getattr(self, "nsb", 0) + 1
        return (es or getattr(self, "cur_es", None) or self.es).enter_context(self.nc.sbuf_tensor("sb%d_%s" % (self.nsb, name), shape, dtype))

    def ps(self, name, shape, dtype):
        return self.es.enter_context(self.nc.psum_tensor("pp_" + name, shape, dtype))

    def _wait(self, eng, tok):
        if tok is None:
            return
        if tok.sem is eng.sem and not eng.same_sync:
            return
        assert tok.val is not None, "waiting on unresolved token (%s)" % tok.sem.name
        val = tok.val
        if tok.sem.group:
            val = max(val, tok.sem.count)
        if eng.waited.get(tok.sem, 0) >= val:
            return
        eng.h.wait_ge(tok.sem.h, val)
        eng.waited[tok.sem] = val

    def _deps(self, eng, reads, writes):
        for r in reads:
            self._wait(eng, r.w)
        for w in writes:
            self._wait(eng, w.w)
            for t in w.r.values():
                self._wait(eng, t)

    def _mark(self, tok, reads, writes):
        for r in reads:
            r.r[tok.sem] = tok
        for w in writes:
            w.w = tok
            w.r = {}

    def op(self, eng, fn, reads=(), writes=(), sig=True):
        xr = [r for r in reads if r.excl]
        if xr:
            writes = list(writes) + xr
            reads = [r for r in reads if not r.excl]
        self._deps(eng, reads, writes)
        inst = fn()
        self.n_inst += 1
        if sig:
            eng.sem.count += 1
            inst.then_inc(eng.sem.h, 1)
            tok = Tok(eng.sem, eng.sem.count)
            for t in eng.pending:
                t.val = eng.sem.count
            eng.pending = []
        else:
            tok = Tok(eng.sem, None)
            eng.pending.append(tok)
        self._mark(tok, reads, writes)
        return tok

    def dma(self, sem, out, in_, reads=(), writes=(), eng=None, **kw):
        eng = eng or self.sp
        self._deps(eng, reads, writes)
        if sem.count > 0:
            self._wait(eng, Tok(sem, sem.count))
        inst = eng.h.dma_start(out=out, in_=in_, **kw)
        self.n_inst += 1
        sem.count += 16
        inst.then_inc(sem.h, 16)
        tok = Tok(sem, sem.count)
        self._mark(tok, reads, writes)
        return tok

    def barrier(self):
        for e in self.engs:
            assert not e.pending
            for s in self.sems:
                if s.count > 0 and not (s is e.sem):
                    self._wait(e, Tok(s, s.count))


class Prog:
    def __init__(self, dram_specs, WST=3584):
        self.nc = bass.Bass("TRN2", target_bir_lowering=False)
        self.es = ExitStack()
        self.kb = KB(self.nc, self.es)
        self.dr = {}
        self.dres = {}
        for name, (shape, dt, kind) in dram_specs.items():
            self.dr[name] = self.nc.dram_tensor(name, list(shape), dt, kind=kind).ap()
            self.dres[name] = Res("dram_" + name)
        self.out_names = [n for n, (_, _, k) in dram_specs.items() if k == "ExternalOutput"]
        kb = self.kb
        self.psf = [kb.ps("psf%d" % i, [128, 512], F32) for i in range(7)]
        self.psf_r = [Res("psf%d" % i, excl=True) for i in range(7)]
        self.psb = kb.ps("psb", [128, 1024], BF16)
        self.psb_r = Res("psb", excl=True)
        self.ps_rr = 0
        self.ones = kb.sb("ones", [128, 128], F32)
        self.ones_r = Res("ones")
        kb.op(kb.dve, lambda: self.nc.vector.memset(self.ones[:], 1.0 / D), writes=[self.ones_r])
        self.epsc = kb.sb("epsc", [128, 1], F32)
        kb.op(kb.dve, lambda: self.nc.vector.memset(self.epsc[:], EPS), writes=[self.ones_r])
        self.vecs = kb.sb("vecs", [128, 64], F32)
        self.vecs_r = Res("vecs")
        self.ldsem = kb.new_sem("ld_misc")
        self.ldsem.group = True
        kb.dma(self.ldsem, self.vecs[:], self.dr["vecs"][:, :], writes=[self.vecs_r])
        self.wsem = [kb.new_sem("wsem%d" % i) for i in range(4)]
        self.stsem = kb.new_sem("st_misc")
        if WST:
            self.alloc_wstage(WST)

    def alloc_wstage(self, WST, WSTF=None, nslot=2):
        kb = self.kb
        self.WST = WST
        self.WSTF = WSTF or WST
        self.nslot = nslot
        self.wst = [kb.sb("wst%d" % i, [128, self.WSTF], F32) for i in range(nslot)]
        self.wst_r = [Res("wst%d" % i) for i in range(nslot)]
        self.wbf = [kb.sb("wbf%d" % i, [128, self.WST], BF16) for i in range(nslot)]
        self.wbf_r = [Res("wbf%d" % i) for i in range(nslot)]
        self.wslot = 0

    def bank(self):
        i = self.ps_rr % 7
        self.ps_rr += 1
        return self.psf[i], self.psf_r[i]

    def load_w(self, pieces):
        kb, nc = self.kb, self.nc
        s = self.wslot
        self.wslot = (self.wslot + 1) % self.nslot
        off = 0
        foff = 0
        views = []
        for ap in pieces:
            if len(ap.shape) == 3:
                _, kc, n = ap.shape
                src = ap
            else:
                K, n = ap.shape
                kc = K // 128
                src = ap.rearrange("(k p) n -> p k n", p=128)
            sz = kc * n
            bview = self.wbf[s][:, off:off + sz].rearrange("p (k n) -> p k n", n=n)
            if ap.dtype == BF16:
                kb.dma(self.wsem[s], bview, src, writes=[self.wbf_r[s]])
            else:
                assert foff + sz <= self.WSTF
                dst = self.wst[s][:, foff:foff + sz].rearrange("p (k n) -> p k n", n=n)
                kb.dma(self.wsem[s], dst, src, writes=[self.wst_r[s]])
                a, b, fa = off, off + sz, foff
                kb.op(kb.pool, lambda a=a, b=b, fa=fa: nc.gpsimd.tensor_copy(out=self.wbf[s][:, a:b], in_=self.wst[s][:, fa:fa + (b - a)]),
                      reads=[self.wst_r[s]], writes=[self.wbf_r[s]])
                foff += sz
            views.append(bview)
            off += sz
        assert off <= self.WST
        return views, self.wbf_r[s]

    def convert_w(self, src, dst, dst_fn=None):
        kb, nc = self.kb, self.nc
        nch, _, kc, n = src.shape
        sz = kc * n
        if not hasattr(self, "cvsem"):
            self.cvsem = [kb.new_sem("cvs%d" % i) for i in range(4)]
            self.cv_rr = 0
        for j in range(nch):
            s = self.wslot
            self.wslot = (self.wslot + 1) % self.nslot
            kb.dma(self.wsem[s], self.wst[s][:, 0:sz].rearrange("p (k n) -> p k n", n=n), src[j], writes=[self.wst_r[s]])
            e = self.cv_rr % 3
            self.cv_rr += 1
            if e == 0:
                kb.op(kb.pool, lambda: nc.gpsimd.tensor_copy(out=self.wbf[s][:, 0:sz], in_=self.wst[s][:, 0:sz]), reads=[self.wst_r[s]], writes=[self.wbf_r[s]])
            elif e == 1:
                kb.op(kb.dve, lambda: nc.vector.tensor_copy(out=self.wbf[s][:, 0:sz], in_=self.wst[s][:, 0:sz]), reads=[self.wst_r[s]], writes=[self.wbf_r[s]])
            else:
                kb.op(kb.act, lambda: nc.scalar.copy(out=self.wbf[s][:, 0:sz], in_=self.wst[s][:, 0:sz]), reads=[self.wst_r[s]], writes=[self.wbf_r[s]])
            kb.dma(self.cvsem[s], dst_fn(j) if dst_fn else dst[j], self.wbf[s][:, 0:sz].rearrange("p (k n) -> p k n", n=n), reads=[self.wbf_r[s]])

    def select_tile(self, dst, dst_r, cands, stage, stage_r, sem, p0, p1):
        kb, nc = self.kb, self.nc
        first = True
        for c, src in enumerate(cands):
            if src is None:
                continue
            kb.dma(sem, stage, src, writes=[stage_r])
            sc_ = self.selw[p0:p1, c:c + 1]
            if first:
                kb.op(kb.dve, lambda: nc.vector.tensor_scalar(out=dst, in0=stage, scalar1=sc_, scalar2=None, op0=ALU.mult),
                      reads=[stage_r, self.selw_r], writes=[dst_r])
                first = False
            else:
                kb.op(kb.dve, lambda: nc.vector.scalar_tensor_tensor(out=dst, in0=stage, scalar=sc_, in1=dst, op0=ALU.mult, op1=ALU.add),
                      reads=[stage_r, self.selw_r, dst_r], writes=[dst_r])

    def vcol(self, c):
        return self.vecs[:, c:c + 1]

    def rmsnorm(self, x, x_r, h, h_r, n, gcol, sq, sq_r, rstd, rstd_r, out_f32=None, out_r=None):
        kb, nc = self.kb, self.nc
        for s0 in range(0, n, 512):
            ps, ps_r = self.bank()
            for c in range(8):
                k = c % 2
                kb.op(kb.act, lambda c=c, k=k: nc.scalar.activation(out=sq[k][:, :], in_=x[:, c, s0:s0 + 512], func=AF.Square),
                      reads=[x_r], writes=[sq_r[k]])
                kb.op(kb.pe, lambda c=c, k=k: nc.tensor.matmul(ps[:, :], lhsT=self.ones[:, :], rhs=sq[k][:, :], start=(c == 0), stop=(c == 7)),
                      reads=[sq_r[k], self.ones_r], writes=[ps_r], sig=True)
            kb.op(kb.act, lambda: nc.scalar.activation(out=rstd[:, s0:s0 + 512], in_=ps[:, :], func=AF.Ln, bias=self.epsc[:, 0:1]),
                  reads=[ps_r, self.ones_r], writes=[rstd_r])
            kb.op(kb.act, lambda: nc.scalar.activation(out=rstd[:, s0:s0 + 512], in_=rstd[:, s0:s0 + 512], func=AF.Exp, scale=-0.5),
                  reads=[rstd_r], writes=[rstd_r])
            for c in range(8):
                tgt = h if out_f32 is None else out_f32
                tgt_r = h_r if out_f32 is None else out_r
                kb.op(kb.dve, lambda c=c, tgt=tgt: nc.vector.scalar_tensor_tensor(
                    out=tgt[:, c, s0:s0 + 512], in0=x[:, c, s0:s0 + 512], scalar=self.vcol(gcol + c), in1=rstd[:, s0:s0 + 512],
                    op0=ALU.mult, op1=ALU.mult), reads=[x_r, rstd_r, self.vecs_r], writes=[tgt_r])

    def ffn(self, x, x_r, h, h_r, n, wg, wu, wd, aT, aT_r, sg, sg_r):
        kb, nc = self.kb, self.nc
        nsub = n // 512
        for fc in range(NFC):
            if wu is None:
                (wgu,), w_r = self.load_w([wg[fc]])
                wgb, wub = wgu[:, 0:8, :], wgu[:, 8:16, :]
            else:
                (wgb, wub), w_r = self.load_w([wg[fc], wu[fc]])
            for sub in range(nsub):
                s0 = sub * 512
                pg, pg_r = self.bank()
                pu, pu_r = self.bank()
                for c in range(8):
                    kb.op(kb.pe, lambda c=c: nc.tensor.matmul(pg[:, :], lhsT=wgb[:, c, :], rhs=h[:, c, s0:s0 + 512], start=(c == 0), stop=(c == 7)),
                          reads=[w_r, h_r], writes=[pg_r], sig=(c == 7))
                for c in range(8):
                    kb.op(kb.pe, lambda c=c: nc.tensor.matmul(pu[:, :], lhsT=wub[:, c, :], rhs=h[:, c, s0:s0 + 512], start=(c == 0), stop=(c == 7)),
                          reads=[w_r, h_r], writes=[pu_r], sig=(c == 7))
                k = (fc * nsub + sub) % 2
                kb.op(kb.act, lambda k=k: nc.scalar.activation(out=sg[k][:, :], in_=pg[:, :], func=AF.Silu), reads=[pg_r], writes=[sg_r[k]])
                kb.op(kb.dve, lambda k=k: nc.vector.tensor_tensor(out=aT[:, fc, s0:s0 + 512], in0=sg[k][:, :], in1=pu[:, :], op=ALU.mult),
                      reads=[sg_r[k], pu_r], writes=[aT_r[fc]])
        for dc in range(8):
            (wdb,), w_r = self.load_w([wd[dc]])
            for sub in range(nsub):
                s0 = sub * 512
                py, py_r = self.bank()
                for fc in range(NFC):
                    kb.op(kb.pe, lambda fc=fc: nc.tensor.matmul(py[:, :], lhsT=wdb[:, fc, :], rhs=aT[:, fc, s0:s0 + 512], start=(fc == 0), stop=(fc == NFC - 1)),
                          reads=[w_r, aT_r[fc]], writes=[py_r], sig=(fc == NFC - 1))
                kb.op(kb.dve, lambda: nc.vector.scalar_tensor_tensor(out=x[:, dc, s0:s0 + 512], in0=py[:, :], scalar=0.5, in1=x[:, dc, s0:s0 + 512],
                                                                      op0=ALU.mult, op1=ALU.add), reads=[py_r, x_r], writes=[x_r])

    def finish(self):
        kb = self.kb
        kb.barrier()
        self.es.close()
        return self.nc


def gain_layout(v):
    return np.ascontiguousarray(np.asarray(v, np.float32).reshape(-1, 128).T)


def vbase(l):
    return 28 * l


POOLW = (2, 4, 8, 16)


def tok_specs(l, mode, last):
    specs = {"xs_in": ((D, NTOK), F32, "ExternalInput"), "vecs": ((128, 64), F32, "ExternalInput")}

    def wspec(li, pre):
        specs[pre + "wg"] = ((NFC, 128, 8, 128), F32, "ExternalInput")
        specs[pre + "wu"] = ((NFC, 128, 8, 128), F32, "ExternalInput")
        specs[pre + "wd"] = ((8, 128, NFC, 128), F32, "ExternalInput")

    doA = (mode == "A") or (not last)
    if mode == "CA":
        wspec(l, "f2_")
        specs["win_c"] = ((len(WIN_STARTS), 128, 8, 128), F32, "ExternalInput")
        specs["wpa"] = ((8, 128, 4, 128), F32, "ExternalInput")
        specs["wnb"] = ((8, 128, 8, 128), F32, "ExternalInput")
        specs["wo"] = ((8, 128, 8, 128), F32, "ExternalInput")
        specs["poolw"] = ((4, 128, 128), F32, "ExternalInput")
        specs["h2T_in"] = ((D, NTOK), BF16, "ExternalInput")
        specs["onsaT"] = ((D, NTOK), BF16, "ExternalInput")
        specs["uext"] = ((512, 4, 528), F32, "ExternalInput")
        specs["corr"] = ((128, 4, 16), F32, "ExternalInput")
    if doA:
        wspec(l, "f1_")
        specs["win_a"] = ((len(WIN_STARTS), 128, 8, 128), F32, "ExternalInput")
        specs["h2T_out"] = ((D, NTOK), BF16, "ExternalOutput")
        specs["kvT_out"] = ((4, 128, NTOK), BF16, "ExternalOutput")
        specs["uT_out"] = ((512, NTOK), F32, "ExternalOutput")
        specs["vtok_out"] = ((NTOK, 256), BF16, "ExternalOutput")
        specs["xs_out"] = ((D, NTOK), F32, "ExternalOutput")
    else:
        specs["out"] = ((D, NTOK), F32, "ExternalOutput")
    return specs


def build_tok(l, mode, last):
    P = Prog(tok_specs(l, mode, last))
    tok_body(P, l, mode, last)
    return P.finish()


def build_bca(l, last):
    specs = attn_specs()
    ts = tok_specs(l, "CA", last)
    for k_ in ("h2T_in", "win_c", "onsaT", "vecs"):
        ts.pop(k_)
    specs.update(ts)
    specs["onsaT"] = ((D, NTOK), BF16, "Internal")
    P = Prog(specs, WST=None)
    P.dr["h2T_in"] = P.dr["h2T"]
    P.dr["win_c"] = P.dr["win"]
    kb = P.kb
    with ExitStack() as pes:
        kb.cur_es = pes
        P.alloc_wstage(2048)
        attn_body(P)
        kb.barrier()
    with ExitStack() as pes:
        kb.cur_es = pes
        P.alloc_wstage(3584)
        tok_body(P, l, "CA", last)
        kb.barrier()
    kb.cur_es = None
    return P.finish()


def tok_body(P, l, mode, last):
    kb, nc, dr = P.kb, P.nc, P.dr
    stq = kb.act if (MONO or QSEL) else kb.sp
    doA = (mode == "A") or (not last)
    x = kb.sb("x", [128, 8, TT], F32); x_r = Res("x")
    h = kb.sb("h", [128, 8, TT], BF16); h_r = Res("h")
    aT = kb.sb("aT", [128, NFC, TT], BF16); aT_r = [Res("aT%d" % i) for i in range(NFC)]
    sq = [kb.sb("sq%d" % i, [128, 512], F32) for i in range(2)]; sq_r = [Res() for i in range(2)]
    rstd = kb.sb("rstd", [128, TT], F32); rstd_r = Res()
    xsem = kb.new_sem("xsem")
    if QSEL:
        xstg2 = kb.sb("xstg", [128, 8 * TT], F32); xstg_r = Res("xstg")
        xstg = xstg2[:, :].rearrange("p (c n) -> p c n", n=TT)
    if mode == "CA":
        hsem = kb.new_sem("hsem"); osem = kb.new_sem("osem"); usem = kb.new_sem("usem")
        on = kb.sb("on", [128, 8, TT], BF16); on_r = Res("on")
        ue = kb.sb("ue", [128, 4, 528], F32); ue_r = Res("ue")
        sa = kb.sb("sa", [128, 528], F32); sa_r = Res("sa")
        sb_ = kb.sb("sbb", [128, 528], F32); sb_r = Res("sb")
        dl = kb.sb("dl", [128, 4, TT], BF16); dl_r = [Res() for _ in range(4)]
        opl = kb.sb("opl", [128, 4, TT], BF16); opl_r = Res("opl")
        mg = kb.sb("mg", [128, 8, TT], BF16); mg_r = [Res() for _ in range(8)]
        t1 = kb.sb("t1", [128, TT], F32); t1_r = Res()
        t2 = kb.sb("t2", [128, TT], F32); t2_r = Res()
        corr = kb.sb("corr", [128, 4, 16], F32); corr_r = Res()
        kb.dma(P.ldsem, corr[:], dr["corr"][:, :, :], writes=[corr_r])
    if doA:
        kvst = [kb.sb("kvst%d" % i, [128, TT], BF16) for i in range(2)]; kvst_r = [Res() for _ in range(2)]
        ust = [kb.sb("ust%d" % i, [128, TT], F32) for i in range(2)]; ust_r = [Res() for _ in range(2)]
        vst = kb.sb("vst", [128, 4, 256], BF16); vst_r = Res()
        osems = [kb.new_sem("kvo%d" % i) for i in range(2)]
        usems = [kb.new_sem("uo%d" % i) for i in range(2)]
        vsem = kb.new_sem("vo")
        hosem = kb.new_sem("ho")

    def colsl(ap, t0):
        return ap[:, t0:t0 + TT].rearrange("(c p) n -> p c n", p=128)

    for t in range(ntok() // TT):
        t0 = t * TT
        if QSEL:
            P.select_tile(x[:, :, :], x_r, [colsl(dr["xs_in"], (4 * t + c_) * 512) for c_ in range(4)], xstg[:, :, :], xstg_r, xsem, 0, 128)
        else:
            kb.dma(xsem, x[:, :, :], colsl(dr["xs_in"], t0), writes=[x_r], eng=stq)
        la = l
        if mode == "CA":
            vb = vbase(l)
            if QSEL:
                P.select_tile(h[:, :, :], h_r, [colsl(dr["h2T_in"], (4 * t + c_) * 512) for c_ in range(4)],
                              on[:, :, :], on_r, hsem, 0, 128)
            else:
                kb.dma(hsem, h[:, :, :], colsl(dr["h2T_in"], t0), writes=[h_r], eng=stq)
            kb.dma(osem, on[:, :, :], colsl(dr["onsaT"], t0), writes=[on_r], eng=stq)
            if QSEL:
                ustg = xstg2[:, 0:2112].rearrange("p (g n) -> p g n", n=528)
                cands = []
                for c_ in range(4):
                    a0 = (4 * t + c_) * 512
                    cands.append(dr["uT_in"][:, a0 - 16:a0 + 512].rearrange("(g p) n -> p g n", p=128) if a0 > 0 else None)
                if t == 0:
                    kb.op(kb.pool, lambda: nc.gpsimd.memset(ustg[:, :, 0:16], 0.0), writes=[xstg_r])
                    kb.dma(usem, ustg[:, :, 16:528], dr["uT_in"][:, 0:512].rearrange("(g p) n -> p g n", p=128), writes=[xstg_r])
                    kb.op(kb.dve, lambda: nc.vector.tensor_scalar(out=ue[:, :, :], in0=ustg, scalar1=P.selw[:, 0:1], scalar2=None, op0=ALU.mult),
                          reads=[xstg_r, P.selw_r], writes=[ue_r])
                    for c_ in range(1, 4):
                        kb.dma(usem, ustg, cands[c_], writes=[xstg_r])
                        kb.op(kb.dve, lambda: nc.vector.scalar_tensor_tensor(out=ue[:, :, :], in0=ustg, scalar=P.selw[:, c_:c_ + 1], in1=ue[:, :, :], op0=ALU.mult, op1=ALU.add),
                              reads=[xstg_r, P.selw_r, ue_r], writes=[ue_r])
                else:
                    P.select_tile(ue[:, :, :], ue_r, cands, ustg, xstg_r, usem, 0, 128)
            elif MONO:
                kb.dma(usem, ue[:, :, 16:528], dr["uT_in"][:, t0:t0 + 512].rearrange("(g p) n -> p g n", p=128), writes=[ue_r])
                if t == 0:
                    kb.op(kb.pool, lambda: nc.gpsimd.memset(ue[:, :, 0:16], 0.0), writes=[ue_r])
                else:
                    kb.dma(usem, ue[:, :, 0:16], dr["uT_in"][:, t0 - 16:t0].rearrange("(g p) n -> p g n", p=128), writes=[ue_r])
            else:
                kb.dma(usem, ue[:, :, :], dr["uext"][:, t, :].rearrange("(g p) n -> p g n", p=128), writes=[ue_r])
            for gi, w in enumerate(POOLW):
                cur, cur_r = None, None
                sh = 1
                src = ue[:, gi, :]
                src_r = ue_r
                bufs = [(sa, sa_r), (sb_, sb_r)]
                bi = 0
                while sh < w:
                    dst, dst_r = bufs[bi]
                    bi ^= 1
                    lo = 2 * sh - 1
                    kb.op(kb.dve, lambda src=src, dst=dst, lo=lo, sh=sh: nc.vector.tensor_tensor(
                        out=dst[:, lo:528], in0=src[:, lo:528], in1=src[:, lo - sh:528 - sh], op=ALU.add),
                        reads=[src_r], writes=[dst_r])
                    src, src_r = dst, dst_r
                    sh *= 2
                kb.op(kb.dve, lambda src=src, w=w: nc.vector.tensor_scalar(out=src[:, 16:528], in0=src[:, 16:528], scalar1=1.0 / w, scalar2=None, op0=ALU.mult),
                      reads=[src_r], writes=[src_r])
                if t == 0:
                    kb.op(kb.dve, lambda src=src, gi=gi: nc.vector.tensor_tensor(out=src[:, 16:32], in0=src[:, 16:32], in1=corr[:, gi, :], op=ALU.mult),
                          reads=[src_r, corr_r], writes=[src_r])
                kb.op(kb.dve, lambda src=src, gi=gi: nc.vector.tensor_tensor(out=dl[:, gi, :], in0=src[:, 16:528], in1=ue[:, gi, 16:528], op=ALU.subtract),
                      reads=[src_r, ue_r], writes=[dl_r[gi]])
            for gi in range(4):
                (pw,), w_r = P.load_w([dr["poolw"][gi]])
                ps, ps_r = P.bank()
                kb.op(kb.pe, lambda: nc.tensor.matmul(ps[:, :], lhsT=pw[:, 0, :], rhs=dl[:, gi, :], start=True, stop=True),
                      reads=[w_r, dl_r[gi]], writes=[ps_r])
                kb.op(kb.dve, lambda: nc.vector.tensor_scalar(out=opl[:, gi, :], in0=ps[:, :], scalar1=P.vcol(vb + 24 + gi), scalar2=None, op0=ALU.mult),
                      reads=[ps_r, P.vecs_r], writes=[opl_r])
            for dc in range(8):
                (wpa, wnb, wgp, wga), w_r = P.load_w([dr["wpa"][dc], dr["wnb"][dc],
                                                      dr["win_c"][WIN_IDX[C_GM + dc * 128]],
                                                      dr["win_c"][WIN_IDX[C_GM + 1024 + dc * 128]]])
                pa, pa_r = P.bank(); pb, pb_r = P.bank(); pgp, pgp_r = P.bank(); pga, pga_r = P.bank()
                for c in range(4):
                    kb.op(kb.pe, lambda c=c: nc.tensor.matmul(pa[:, :], lhsT=wpa[:, c, :], rhs=opl[:, c, :], start=(c == 0), stop=(c == 3)),
                          reads=[w_r, opl_r], writes=[pa_r], sig=(c == 3))
                for c in range(8):
                    kb.op(kb.pe, lambda c=c: nc.tensor.matmul(pb[:, :], lhsT=wnb[:, c, :], rhs=on[:, c, :], start=(c == 0), stop=(c == 7)),
                          reads=[w_r, on_r], writes=[pb_r], sig=(c == 7))
                for c in range(8):
                    kb.op(kb.pe, lambda c=c: nc.tensor.matmul(pgp[:, :], lhsT=wgp[:, c, :], rhs=h[:, c, :], start=(c == 0), stop=(c == 7)),
                          reads=[w_r, h_r], writes=[pgp_r], sig=(c == 7))
                for c in range(8):
                    kb.op(kb.pe, lambda c=c: nc.tensor.matmul(pga[:, :], lhsT=wga[:, c, :], rhs=h[:, c, :], start=(c == 0), stop=(c == 7)),
                          reads=[w_r, h_r], writes=[pga_r], sig=(c == 7))
                kb.op(kb.act, lambda: nc.scalar.activation(out=t1[:, :], in_=pgp[:, :], func=AF.Sigmoid), reads=[pgp_r], writes=[t1_r])
                kb.op(kb.act, lambda: nc.scalar.activation(out=t2[:, :], in_=pga[:, :], func=AF.Sigmoid), reads=[pga_r], writes=[t2_r])
                kb.op(kb.dve, lambda: nc.vector.tensor_tensor(out=t1[:, :], in0=t1[:, :], in1=pa[:, :], op=ALU.mult), reads=[t1_r, pa_r], writes=[t1_r])
                kb.op(kb.dve, lambda: nc.vector.tensor_tensor(out=t2[:, :], in0=t2[:, :], in1=pb[:, :], op=ALU.mult), reads=[t2_r, pb_r], writes=[t2_r])
                kb.op(kb.dve, lambda: nc.vector.tensor_tensor(out=mg[:, dc, :], in0=t1[:, :], in1=t2[:, :], op=ALU.add), reads=[t1_r, t2_r], writes=[mg_r[dc]])
            for dc in range(8):
                (wo,), w_r = P.load_w([dr["wo"][dc]])
                pz, pz_r = P.bank()
                for c in range(8):
                    kb.op(kb.pe, lambda c=c: nc.tensor.matmul(pz[:, :], lhsT=wo[:, c, :], rhs=mg[:, c, :], start=(c == 0), stop=(c == 7)),
                          reads=[w_r, mg_r[c]], writes=[pz_r], sig=(c == 7))
                kb.op(kb.dve, lambda: nc.vector.tensor_tensor(out=x[:, dc, :], in0=x[:, dc, :], in1=pz[:, :], op=ALU.add), reads=[x_r, pz_r], writes=[x_r])
            P.rmsnorm(x, x_r, h, h_r, TT, vb + 16, sq, sq_r, rstd, rstd_r)
            P.ffn(x, x_r, h, h_r, TT, dr["f2_wg"], dr["f2_wu"], dr["f2_wd"], aT, aT_r, sq, sq_r)
            la = l + 1
            if last:
                P.rmsnorm(x, x_r, None, None, TT, 56, sq, sq_r, rstd, rstd_r, out_f32=x, out_r=x_r)
                kb.dma(P.stsem, colsl(dr["out"], t0), x[:, :, :], reads=[x_r], eng=stq)
                continue
        vb = vbase(la)
        P.rmsnorm(x, x_r, h, h_r, TT, vb + 0, sq, sq_r, rstd, rstd_r)
        P.ffn(x, x_r, h, h_r, TT, dr["f1_wg"], dr["f1_wu"], dr["f1_wd"], aT, aT_r, sq, sq_r)
        kb.dma(P.stsem, colsl(dr["xs_out"], t0), x[:, :, :], reads=[x_r], eng=stq)
        P.rmsnorm(x, x_r, h, h_r, TT, vb + 8, sq, sq_r, rstd, rstd_r)
        kb.dma(hosem, colsl(dr["h2T_out"], t0), h[:, :, :], reads=[h_r], eng=stq)
        W = dr["win_a"]
        for j, c0 in enumerate((C_KC, C_VC, C_KSL, C_KWN)):
            (wc,), w_r = P.load_w([W[WIN_IDX[c0]]])
            ps, ps_r = P.bank()
            for c in range(8):
                kb.op(kb.pe, lambda c=c: nc.tensor.matmul(ps[:, :], lhsT=wc[:, c, :], rhs=h[:, c, :], start=(c == 0), stop=(c == 7)),
                      reads=[w_r, h_r], writes=[ps_r], sig=(c == 7))
            k = j % 2
            kb.op(kb.act, lambda k=k: nc.scalar.copy(out=kvst[k][:, :], in_=ps[:, :]), reads=[ps_r], writes=[kvst_r[k]])
            kb.dma(osems[k], dr["kvT_out"][j, :, t0:t0 + TT], kvst[k][:, :], reads=[kvst_r[k]], eng=stq)
        for j in range(4):
            (wc,), w_r = P.load_w([W[WIN_IDX[C_U + j * 128]]])
            ps, ps_r = P.bank()
            for c in range(8):
                kb.op(kb.pe, lambda c=c: nc.tensor.matmul(ps[:, :], lhsT=wc[:, c, :], rhs=h[:, c, :], start=(c == 0), stop=(c == 7)),
                      reads=[w_r, h_r], writes=[ps_r], sig=(c == 7))
            k = j % 2
            kb.op(kb.act, lambda k=k: nc.scalar.copy(out=ust[k][:, :], in_=ps[:, :]), reads=[ps_r], writes=[ust_r[k]])
            kb.dma(usems[k], dr["uT_out"][j * 128:(j + 1) * 128, t0:t0 + TT], ust[k][:, :], reads=[ust_r[k]], eng=stq)
        (wv1, wv2), w_r = P.load_w([W[WIN_IDX[C_VSL]], W[WIN_IDX[C_VWN]]])
        for tb in range(TT // 128):
            ps, ps_r = P.bank()
            for wi, wv in enumerate((wv1, wv2)):
                for c in range(8):
                    kb.op(kb.pe, lambda c=c, wv=wv, wi=wi: nc.tensor.matmul(ps[:, wi * 128:(wi + 1) * 128], lhsT=h[:, c, tb * 128:(tb + 1) * 128], rhs=wv[:, c, :],
                                                                         start=(c == 0), stop=(c == 7)),
                          reads=[w_r, h_r], writes=[ps_r], sig=(c == 7))
            kb.op(kb.act, lambda: nc.scalar.copy(out=vst[:, tb, :], in_=ps[:, 0:256]), reads=[ps_r], writes=[vst_r])
        kb.dma(vsem, dr["vtok_out"][t0:t0 + TT, :].rearrange("(tb p) c -> p tb c", p=128), vst[:, :, :], reads=[vst_r], eng=stq)


def attn_specs():
    BI = "ExternalInput"
    specs = {
        "vecs": ((128, 64), F32, BI),
        "h2T": ((D, NTOK), BF16, BI),
        "kvall": ((4, 128, SEQ + 32), BF16, BI),
        "vall": ((SEQ, 256), BF16, BI),
        "kwin": ((128, 4, 1024), BF16, BI),
        "vwin": ((4, 1024, 128), BF16, BI),
        "win": ((len(WIN_STARTS), 128, 8, 128), F32, BI), "win_gn": ((D, 48), F32, BI),
        "ck_w1": ((2, 128, 16, 128), F32, BI), "ck_w2": ((256, 64), F32, BI),
        "cv_w1": ((2, 128, 16, 128), F32, BI), "cv_w2": ((256, 64), F32, BI),
        "pecol": ((128, 16), F32, BI),
        "qal": ((3, 16, NTOK), BF16, BI),
        "kbs": ((128, 16 * 64), F32, BI), "kbc": ((128, 64), F32, BI), "kbw": ((128, 4 * 8 * 16), F32, BI),
        "cmpm": ((128, 2, 512), BF16, BI), "cm": ((128, 16, 512), BF16, BI), "wm": ((128, 8, 512), BF16, BI),
        "addm": ((128, 16, 128), BF16, BI), "vneg": ((128, 16, 128), BF16, BI),
        "karows": ((64, SEQ), BF16, BI), "ones3": ((3, 1024), BF16, BI), "ovl": ((128, 4, 128), BF16, BI), "identb": ((128, 128), BF16, BI),
        "onsaT": ((D, NTOK), BF16, "ExternalOutput"),
    }
    return specs


def build_attn():
    P = Prog(attn_specs(), WST=2048)
    attn_body(P)
    return P.finish()


def attn_body(P):
    kb, nc, dr = P.kb, P.nc, P.dr
    ld = P.ldsem

    def const(name, shape, dt, src):
        t = kb.sb(name, shape, dt)
        r = Res(name)
        kb.dma(ld, t[:], src, writes=[r])
        return t, r

    CM, CM_r = const("CM", [128, 4 if MONO else 16, 512], BF16, dr["cm"][:, :, :])
    WM, WM_r = const("WM", [128, 8, 512], BF16, dr["wm"][:, :, :])
    CPM, CPM_r = const("CPM", [128, 5 if MONO else 2, 512], BF16, dr["cmpm"][:, :, :])
    if MONO:
        ADM = kb.sb("ADM", [128, 4, 128], BF16); ADM_r = Res("ADM")
        VNG = kb.sb("VNG", [128, 4, 128], BF16); VNG_r = Res("VNG")
        KBW = kb.sb("KBW", [128, 128], F32); KBW_r = Res("KBW")
        pisem = kb.new_sem("pisem")
    else:
        ADM, ADM_r = const("ADM", [128, 16, 128], BF16, dr["addm"][:, :, :])
        VNG, VNG_r = const("VNG", [128, 16, 128], BF16, dr["vneg"][:, :, :])
        KBW, KBW_r = const("KBW", [128, 512], F32, dr["kbw"][:, :])
    KBS, KBS_r = const("KBS", [128, 1024], F32, dr["kbs"][:, :])
    KBC, KBC_r = const("KBC", [128, 64], F32, dr["kbc"][:, :])
    IDB, IDB_r = const("IDB", [128, 128], BF16, dr["identb"][:, :])
    PEC, PEC_r = const("PEC", [128, 16], F32, dr["pecol"][:, :])

    KA = kb.sb("KA", [128, SEQ], BF16); KA_r = Res("KA")
    VA = kb.sb("VA", [128, 64, 65], BF16); VA_r = Res("VA")
    KW = kb.sb("KW", [128, 1024], BF16); KW_r = Res("KW")
    VW = kb.sb("VW", [128, 8, 65], BF16); VW_r = Res("VW")
    KC = kb.sb("KC", [128, 2, 512], BF16); KC_r = Res("KC")
    VC = kb.sb("VC", [128, 4, 2, 193], BF16); VC_r = Res("VC")
    kb.op(kb.pool, lambda: nc.gpsimd.memset(KW[0:64, :], 0.0), writes=[KW_r])
    kb.op(kb.pool, lambda: nc.gpsimd.memset(VW[:, :, 0:64], 0.0), writes=[VW_r])
    kb.dma(ld, KA[64:128, :], dr["karows"][:, :], writes=[KA_r])
    kb.op(kb.pool, lambda: nc.gpsimd.memset(KW[64:128, :], 0.0), writes=[KW_r])
    kb.op(kb.pool, lambda: nc.gpsimd.memset(KC[64:128, :, :], 0.0), writes=[KC_r])
    kb.dma(ld, KW[124:127, :], dr["ones3"][:, :], writes=[KW_r])
    kb.dma(ld, KC[124:127, :, :], dr["ones3"][:, :].rearrange("r (g n) -> r g n", n=512), writes=[KC_r])
    kb.op(kb.pool, lambda: nc.gpsimd.memset(VA[:, :, 64:65], 1.0), writes=[VA_r])
    kb.op(kb.pool, lambda: nc.gpsimd.memset(VW[:, :, 64:65], 1.0), writes=[VW_r])
    kb.op(kb.pool, lambda: nc.gpsimd.memset(VC[:, :, :, 64:65], 1.0), writes=[VC_r])
    for g in range(2):
        kb.dma(ld, VC[:, :, g, 65:193], dr["ovl"][:, :, :], writes=[VC_r])

    ces = ExitStack()
    KC2 = kb.sb("KC2", [128, SEQ + 16], BF16, es=ces); KC2_r = Res("KC2")
    zb = kb.sb("zb", [128, 512], F32, es=ces); zb_r = Res()
    s2 = kb.sb("s2", [128, 512], F32, es=ces); s2_r = Res()
    hid = kb.sb("hid", [128, 2, 512], BF16, es=ces); hid_r = [Res(), Res()]
    pecb = kb.sb("pecb", [128, 16], BF16, es=ces); pecb_r = Res()
    bj = kb.sb("bj", [128, 1], F32, es=ces); bj_r = Res()
    kb.op(kb.dve, lambda: nc.vector.tensor_copy(out=pecb[:, :], in_=PEC[:, :]), reads=[PEC_r], writes=[pecb_r])
    c2sem = kb.new_sem("c2sem")
    if MONO or QSEL:
        kb.op(kb.pool, lambda: nc.gpsimd.memset(KC2[0:64, SEQ:SEQ + 16], 0.0), writes=[KC2_r])
        kb.op(kb.pool, lambda: nc.gpsimd.memset(KC2[64:128, SEQ - 1:SEQ + 16], 0.0), writes=[KC2_r])
    for kv in range(2):
        w1 = dr["ck_w1" if kv == 0 else "cv_w1"]
        w2 = dr["ck_w2" if kv == 0 else "cv_w2"]
        for g in range(2):
            if MONO or QSEL:
                kb.dma(c2sem, KC2[0:64, 0:SEQ], dr["kvall"][kv, g * 64:(g + 1) * 64, 0:SEQ], writes=[KC2_r])
                kb.dma(c2sem, KC2[64:128, 0:SEQ - 1], dr["kvall"][kv, g * 64:(g + 1) * 64, 1:SEQ], writes=[KC2_r])
            else:
                kb.dma(c2sem, KC2[0:64, :], dr["kvall"][kv, g * 64:(g + 1) * 64, 0:SEQ + 16], writes=[KC2_r])
                kb.dma(c2sem, KC2[64:128, :], dr["kvall"][kv, g * 64:(g + 1) * 64, 1:SEQ + 17], writes=[KC2_r])
            for jc in range(2):
                (w1v,), w_r = P.load_w([w1[jc]])
                pb_, pb_r = P.bank()
                for lp in range(16):
                    kb.op(kb.pe, lambda lp=lp: nc.tensor.matmul(pb_[:, 0:1], lhsT=w1v[:, lp, :], rhs=pecb[:, lp:lp + 1], start=(lp == 0), stop=(lp == 15)),
                          reads=[w_r, pecb_r], writes=[pb_r], sig=(lp == 15))
                kb.op(kb.act, lambda: nc.scalar.copy(out=bj[:, :], in_=pb_[:, 0:1]), reads=[pb_r], writes=[bj_r])
                ps, ps_r = P.bank()
                for lp in range(16):
                    kb.op(kb.pe, lambda lp=lp: nc.tensor.matmul(ps[:, :], lhsT=w1v[:, lp, :], rhs=KC2[:, 2 * lp:2 * lp + 16 * 511 + 1:16], start=(lp == 0), stop=(lp == 15)),
                          reads=[w_r, KC2_r], writes=[ps_r], sig=(lp == 15))
                kb.op(kb.act, lambda: nc.scalar.activation(out=zb[:, :], in_=ps[:, :], func=AF.Identity, bias=bj[:, 0:1]), reads=[ps_r, bj_r], writes=[zb_r])
                kb.op(kb.act, lambda: nc.scalar.activation(out=s2[:, :], in_=zb[:, :], func=AF.Square), reads=[zb_r], writes=[s2_r])
                kb.op(kb.dve, lambda: nc.vector.tensor_scalar(out=s2[:, :], in0=s2[:, :], scalar1=0.044715, scalar2=1.0, op0=ALU.mult, op1=ALU.add), reads=[s2_r], writes=[s2_r])
                kb.op(kb.dve, lambda: nc.vector.tensor_tensor(out=s2[:, :], in0=s2[:, :], in1=zb[:, :], op=ALU.mult), reads=[s2_r, zb_r], writes=[s2_r])
                kb.op(kb.act, lambda: nc.scalar.activation(out=s2[:, :], in_=s2[:, :], func=AF.Sigmoid, scale=1.5957691216057308), reads=[s2_r], writes=[s2_r])
                kb.op(kb.dve, lambda jc=jc: nc.vector.tensor_tensor(out=hid[:, jc, :], in0=s2[:, :], in1=zb[:, :], op=ALU.mult), reads=[s2_r, zb_r], writes=[hid_r[jc]])
            (w2v,), w_r = P.load_w([w2[:, :]])
            if kv == 0:
                ps, ps_r = P.bank()
                for jc in range(2):
                    kb.op(kb.pe, lambda jc=jc: nc.tensor.matmul(ps[0:64, :], lhsT=w2v[:, jc, :], rhs=hid[:, jc, :], start=(jc == 0), stop=(jc == 1)),
                          reads=[w_r, hid_r[jc]], writes=[ps_r], sig=(jc == 1))
                kb.op(kb.act, lambda g=g: nc.scalar.copy(out=KC[0:64, g, :], in_=ps[0:64, :]), reads=[ps_r], writes=[KC_r])
            else:
                for nt in range(4):
                    ps, ps_r = P.bank()
                    for jc in range(2):
                        kb.op(kb.pe, lambda jc=jc, nt=nt: nc.tensor.matmul(ps[:, 0:64], lhsT=hid[:, jc, nt * 128:(nt + 1) * 128], rhs=w2v[:, jc, :], start=(jc == 0), stop=(jc == 1)),
                              reads=[w_r, hid_r[jc]], writes=[ps_r], sig=(jc == 1))
                    kb.op(kb.act, lambda g=g, nt=nt: nc.scalar.copy(out=VC[:, nt, g, 0:64], in_=ps[:, 0:64]), reads=[ps_r], writes=[VC_r])
    kb.barrier()
    ces.close()

    Q = kb.sb("Q", [128, 3, 8, 512], BF16); Q_r = [Res("Q%d" % i) for i in range(8)]
    kb.op(kb.pool, lambda: nc.gpsimd.memset(Q[64:128, :, :, :], 0.0), writes=Q_r)
    hT = kb.sb("hT", [128, 8, 512], BF16); hT_r = Res("hT")
    if QSEL:
        qstg = kb.sb("qstg", [128, 8, 512], BF16); qstg_r = Res("qstg")
    gat = kb.sb("gat", [128, 4, 48], F32); gat_r = Res("gat")
    cst = kb.sb("cst", [128, 4, 8, 64], F32); cst_r = [Res() for _ in range(8)]
    crd = kb.sb("crd", [128, 4, 8], F32); crd_r = [Res() for _ in range(8)]
    sc = kb.sb("sc", [128, 4, 128], F32); sc_r = Res("sc")
    pT = [kb.sb("pT%d" % i, [128, 512], BF16) for i in range(4)]; pT_r = [Res() for _ in range(4)]
    tmp = [kb.sb("tmp%d" % i, [128, 512], F32) for i in range(2)]; tmp_r = [Res() for _ in range(2)]
    onb = kb.sb("onb", [128, 4, 1024], BF16); onb_r = Res("onb")
    ost = [kb.sb("ost%d" % i, [128, 512], BF16) for i in range(2)]; ost_r = [Res() for _ in range(2)]
    smb = kb.sb("smb", [128, 4, 128], F32); smb_r = [Res() for _ in range(4)]
    wa = kb.sb("wa", [128, 4, 128], F32); wa_r = [Res() for _ in range(4)]
    wb = kb.sb("wb", [128, 4, 128], F32); wb_r = [Res() for _ in range(4)]
    m8 = kb.sb("m8", [128, 4, 8], F32); m8_r = [Res() for _ in range(4)]
    m8b = kb.sb("m8b", [128, 4, 8], F32); m8b_r = [Res() for _ in range(4)]
    nqv = kb.sb("nqv", [128, 4, 3, 128], BF16); nq_r = Res()
    kb.op(kb.pool, lambda: nc.gpsimd.memset(nqv[:, :, :, :], 0.0), writes=[nq_r])
    dd = kb.sb("dd", [128, 2, 4], F32); dd_r = Res()
    cf = kb.sb("cf", [128, 3, 4], F32); cf_r = Res()
    oh = kb.sb("oh", [128, 64], F32); oh_r = Res()
    hsem = kb.new_sem("hsem"); ksem = kb.new_sem("ksem"); vsem = kb.new_sem("vsem")
    kwsem = kb.new_sem("kwsem"); vwsem = kb.new_sem("vwsem"); qsem = kb.new_sem("qsem")
    osems = [kb.new_sem("os%d" % i) for i in range(2)]
    W = dr["win"]
    st = {"s": 0, "p": 0, "t": 0}

    def sbank():
        k = st["s"] % 3
        st["s"] += 1
        return P.psf[k], P.psf_r[k]

    def pbuf():
        k = st["p"] % 4
        st["p"] += 1
        return pT[k], pT_r[k]

    def tbuf():
        k = st["t"] % 2
        st["t"] += 1
        return tmp[k], tmp_r[k]

    def exp_tile(ps, ps_r, bias_ap, bias_r, mask_ap, mask_r, qa=0, qb=512):
        p, p_r = pbuf()
        if mask_ap is not None:
            tm, tm_r = tbuf()
            kb.op(kb.dve, lambda: nc.vector.tensor_tensor(out=tm[:, qa:qb], in0=ps[:, qa:qb], in1=mask_ap[:, qa:qb], op=ALU.add), reads=[ps_r, mask_r], writes=[tm_r])
            kb.op(kb.act, lambda: nc.scalar.activation(out=p[:, qa:qb], in_=tm[:, qa:qb], func=AF.Exp, bias=bias_ap), reads=[tm_r, bias_r], writes=[p_r])
        else:
            kb.op(kb.act, lambda: nc.scalar.activation(out=p[:, qa:qb], in_=ps[:, qa:qb], func=AF.Exp, bias=bias_ap), reads=[ps_r, bias_r], writes=[p_r])
        return p, p_r

    NI = ntok() // 512
    KT0 = 4 if MONO else 16
    for g in range(2):
        kb.dma(ksem, KA[0:64, :], dr["kvall"][2, g * 64:(g + 1) * 64, 0:SEQ], writes=[KA_r])
        for kq in range(16):
            kb.dma(vsem, VA[:, kq * 4:(kq + 1) * 4, 0:64],
                   dr["vall"][kq * 512:(kq + 1) * 512, g * 64:(g + 1) * 64].rearrange("(kt p) d -> p kt d", p=128), writes=[VA_r])
        for i in range(NI):
            q0 = i * 512
            if MONO:
                kb.dma(pisem, ADM[:, :, :], dr["addm"][i], writes=[ADM_r])
                kb.dma(pisem, VNG[:, :, :], dr["vneg"][i], writes=[VNG_r])
                kb.dma(pisem, KBW[:, :], dr["kbw"][i], writes=[KBW_r])
                cmp_tiles = [(nt, (i - 4 * nt) if (i - 4 * nt) <= 4 else None) for nt in range((32 * (i + 1) - 2) // 128 + 1)]
            else:
                cmp_tiles = [(nt, (nt - i + 1) if nt - i >= -1 else None) for nt in range(i + 1)]
            if QSEL:
                P.select_tile(hT[:, :, :], hT_r, [dr["h2T"][:, (4 * i + c_) * 512:(4 * i + c_ + 1) * 512].rearrange("(c p) n -> p c n", p=128) for c_ in range(4)],
                              qstg[:, :, :], qstg_r, hsem, 0, 128)
            else:
                kb.dma(hsem, hT[:, :, :], dr["h2T"][:, q0:q0 + 512].rearrange("(c p) n -> p c n", p=128), writes=[hT_r])
            (wgn,), w_r = P.load_w([dr["win_gn"][:, :]])
            for qs in range(4):
                ps, ps_r = sbank()
                for c in range(8):
                    kb.op(kb.pe, lambda c=c: nc.tensor.matmul(ps[:, 0:48], lhsT=hT[:, c, qs * 128:(qs + 1) * 128], rhs=wgn[:, c, :], start=(c == 0), stop=(c == 7)),
                          reads=[w_r, hT_r], writes=[ps_r], sig=(c == 7))
                kb.op(kb.act, lambda: nc.scalar.activation(out=gat[:, qs, :], in_=ps[:, 0:48], func=AF.Sigmoid), reads=[ps_r], writes=[gat_r])
            for hp in range(4):
                (wq,), w_r = P.load_w([W[g * 4 + hp]])
                for hh in range(2):
                    hl = hp * 2 + hh
                    ps, ps_r = sbank()
                    for c in range(8):
                        kb.op(kb.pe, lambda c=c: nc.tensor.matmul(ps[0:64, :], lhsT=wq[:, c, hh * 64:(hh + 1) * 64], rhs=hT[:, c, :], start=(c == 0), stop=(c == 7)),
                              reads=[w_r, hT_r], writes=[ps_r], sig=(c == 7))
                    kb.op(kb.dve, lambda: nc.vector.tensor_scalar(out=Q[0:64, 0, hl, :], in0=ps[0:64, :], scalar1=0.125, scalar2=None, op0=ALU.mult), reads=[ps_r], writes=[Q_r[hl]])
                    kb.op(kb.pool, lambda: nc.gpsimd.tensor_copy(out=Q[0:64, 1:3, hl, :], in_=Q[0:64, 0:1, hl, :].to_broadcast([64, 2, 512])),
                          reads=[Q_r[hl]], writes=[Q_r[hl]])
            if MONO:
                klo = max(q0 - 512, 0)
                kb.dma(kwsem, KW[0:64, 1024 - (q0 + 512 - klo):1024], dr["kvall"][3, g * 64:(g + 1) * 64, klo:q0 + 512], writes=[KW_r])
                w0 = 8 - (q0 + 512 - klo) // 128
                kb.dma(vwsem, VW[:, w0:8, 0:64], dr["vall"][klo:q0 + 512, 128 + g * 64:128 + (g + 1) * 64].rearrange("(w p) d -> p w d", p=128), writes=[VW_r])
            elif QSEL:
                for half in range(2):
                    sts = [4 * i + c_ - 1 + half for c_ in range(4)]
                    P.select_tile(KW[0:64, half * 512:(half + 1) * 512], KW_r,
                                  [dr["kvall"][3, g * 64:(g + 1) * 64, st_ * 512:(st_ + 1) * 512] if st_ >= 0 else None for st_ in sts],
                                  qstg[0:64, 0, :], qstg_r, kwsem, 0, 64)
                    P.select_tile(VW[:, half * 4:(half + 1) * 4, 0:64], VW_r,
                                  [dr["vall"][st_ * 512:(st_ + 1) * 512, 128 + g * 64:128 + (g + 1) * 64].rearrange("(w p) d -> p w d", p=128) if st_ >= 0 else None for st_ in sts],
                                  qstg[:, 1, 0:256].rearrange("p (w d) -> p w d", d=64), qstg_r, vwsem, 0, 128)
            else:
                kb.dma(kwsem, KW[0:64, :], dr["kwin"][g * 64:(g + 1) * 64, i, :], writes=[KW_r])
                kb.dma(vwsem, VW[:, :, 0:64], dr["vwin"][i, :, g * 64:(g + 1) * 64].rearrange("(w p) d -> p w d", p=128), writes=[VW_r])
            for v_ in range(3):
                kb.dma(qsem, Q[124:127, v_, :, :], dr["qal"][:, g * 8:(g + 1) * 8, q0:q0 + 512], writes=Q_r)
            for hl in range(8):
                h = g * 8 + hl
                accs = [(P.psf[3 + 2 * (hl % 2)], P.psf_r[3 + 2 * (hl % 2)]), (P.psf[4 + 2 * (hl % 2)], P.psf_r[4 + 2 * (hl % 2)])]
                for a, a_r in accs:
                    kb.op(kb.dve, lambda: nc.vector.memset(a[:, 0:386], 0.0), writes=[a_r])
                cq_ = []
                for cu in list(cmp_tiles) + [None, None]:
                    if cu is not None:
                        nt, mi_ = cu
                        ps, ps_r = sbank()
                        kb.op(kb.pe, lambda: nc.tensor.matmul(ps[:, :], lhsT=KC[0:128, g, nt * 128:(nt + 1) * 128], rhs=Q[0:128, 0, hl, :], start=True, stop=True),
                              reads=[KC_r, Q_r[hl]], writes=[ps_r])
                        e, e_r = exp_tile(ps, ps_r, KBC[:, h * 4 + nt:h * 4 + nt + 1], KBC_r,
                                          CPM[:, mi_, :] if mi_ is not None else None, CPM_r)
                        cq_.append((nt, e, e_r))
                    if cq_ and (len(cq_) > 2 or cu is None):
                        nt_, e, e_r = cq_.pop(0)
                        for qs in range(4):
                            a, a_r = accs[qs // 2]
                            o0 = (qs % 2) * 193
                            kb.op(kb.pe, lambda: nc.tensor.matmul(a[:, o0:o0 + 193], lhsT=e[:, qs * 128:(qs + 1) * 128], rhs=VC[:, nt_, g, :], start=False, stop=(nt_ == cmp_tiles[-1][0]), skip_group_check=True),
                                  reads=[e_r, VC_r], writes=[a_r], sig=(qs == 3 or qs == 1))
                assert not cq_
                for half in range(2):
                    a, a_r = accs[half]
                    av = a[:, 0:386].rearrange("p (q c) -> p q c", c=193)
                    kb.op(kb.dve, lambda: nc.vector.tensor_scalar(out=crd[:, 2 * half:2 * half + 2, hl], in0=av[:, :, 64], scalar1=1e-30, scalar2=None, op0=ALU.max),
                          reads=[a_r], writes=[crd_r[hl]])
                    kb.op(kb.dve, lambda: nc.vector.tensor_copy(out=cst[:, 2 * half:2 * half + 2, hl, :], in_=av[:, :, 0:64]), reads=[a_r], writes=[cst_r[hl]])
                kb.op(kb.dve, lambda: nc.vector.reciprocal(out=crd[:, :, hl], in_=crd[:, :, hl]), reads=[crd_r[hl]], writes=[crd_r[hl]])
                for qs in range(4):
                    a, a_r = accs[qs // 2]
                    o0 = (qs % 2) * 193
                    if hl == 0:
                        kb.op(kb.dve, lambda: nc.vector.tensor_scalar(out=sc[:, qs, :], in0=a[:, o0 + 65:o0 + 193], scalar1=crd[:, qs, hl:hl + 1], scalar2=None, op0=ALU.mult),
                              reads=[a_r, crd_r[hl]], writes=[sc_r])
                    else:
                        kb.op(kb.dve, lambda: nc.vector.scalar_tensor_tensor(out=sc[:, qs, :], in0=a[:, o0 + 65:o0 + 193], scalar=crd[:, qs, hl:hl + 1], in1=sc[:, qs, :],
                                                                              op0=ALU.mult, op1=ALU.add), reads=[a_r, crd_r[hl], sc_r], writes=[sc_r])
            c0_ = 0 if MONO else i * 4
            kb.op(kb.dve, lambda: nc.vector.tensor_tensor(out=smb[:, :, :], in0=sc[:, :, :], in1=ADM[:, c0_:c0_ + 4, :], op=ALU.add), reads=[sc_r, ADM_r], writes=smb_r)
            for qs in range(4):
                kb.op(kb.dve, lambda: nc.vector.max(out=m8[:, qs, :], in_=smb[:, qs, :]), reads=[smb_r[qs]], writes=[m8_r[qs]])
            for qs in range(4):
                kb.op(kb.dve, lambda: nc.vector.match_replace(out=wa[:, qs, :], in_to_replace=m8[:, qs, :], in_values=smb[:, qs, :], imm_value=-1e30),
                      reads=[smb_r[qs], m8_r[qs]], writes=[wa_r[qs]])
            for qs in range(4):
                kb.op(kb.dve, lambda: nc.vector.max(out=m8b[:, qs, :], in_=wa[:, qs, :]), reads=[wa_r[qs]], writes=[m8b_r[qs]])
            for qs in range(4):
                kb.op(kb.dve, lambda: nc.vector.match_replace(out=wb[:, qs, :], in_to_replace=m8b[:, qs, :], in_values=wa[:, qs, :], imm_value=-1e30),
                      reads=[wa_r[qs], m8b_r[qs]], writes=[wb_r[qs]])
            kb.op(kb.dve, lambda: nc.vector.tensor_tensor(out=wa[:, :, :], in0=smb[:, :, :], in1=wb[:, :, :], op=ALU.subtract), reads=smb_r + wb_r, writes=wa_r)
            kb.op(kb.dve, lambda: nc.vector.tensor_scalar(out=wa[:, :, :], in0=wa[:, :, :], scalar1=1.0, scalar2=-NEG, op0=ALU.min, op1=ALU.mult), reads=wa_r, writes=wa_r)
            for v_ in range(3):
                nb_ = 60 if v_ < 2 else 8
                kb.op(kb.dve, lambda: nc.vector.scalar_tensor_tensor(out=nqv[:, :, v_, 64:64 + nb_], in0=wa[:, :, 60 * v_:60 * v_ + nb_], scalar=NEG,
                                                                      in1=VNG[:, c0_:c0_ + 4, 60 * v_:60 * v_ + nb_], op0=ALU.add, op1=ALU.min),
                      reads=wa_r + [VNG_r], writes=[nq_r])
            for qs in range(4):
                for v_ in range(3):
                    kb.op(kb.pe, lambda: nc.tensor.transpose(out=P.psb[:, v_ * 128:(v_ + 1) * 128], in_=nqv[:, qs, v_, :], identity=IDB[:, :]),
                          reads=[nq_r, IDB_r], writes=[P.psb_r])
                for v_ in range(3):
                    src_ = P.psb[64:124, v_ * 128:(v_ + 1) * 128].rearrange("p (o n) -> p o n", o=1).to_broadcast([60, 8, 128])
                    kb.op(kb.dve, lambda: nc.vector.tensor_copy(out=Q[64:124, v_, :, qs * 128:(qs + 1) * 128], in_=src_),
                          reads=[P.psb_r], writes=Q_r)
            for hl in range(8):
                h = g * 8 + hl
                aS, aS_r = P.psf[3 + 2 * (hl % 2)], P.psf_r[3 + 2 * (hl % 2)]
                aW, aW_r = P.psf[4 + 2 * (hl % 2)], P.psf_r[4 + 2 * (hl % 2)]
                nkt = KT0 * (i + 1)
                if hl == 0:
                    kb.op(kb.dve, lambda: nc.vector.memset(aS[:, 0:260], 0.0), writes=[aS_r])
                    kb.op(kb.dve, lambda: nc.vector.memset(aW[:, 0:260], 0.0), writes=[aW_r])
                units = [("s", kt) for kt in range(nkt)] + [("w", w) for w in range(8)]
                SKEW = 2
                pendq = []
                for u in units + [None] * SKEW:
                    cur = None
                    if u is not None:
                        kind, ix = u
                        ps, ps_r = sbank()
                        qa, qb = 0, 512
                        if kind == "s":
                            r_ = ix - KT0 * i
                            if MONO and r_ >= 0:
                                qa = 128 * r_
                            kb.op(kb.pe, lambda: nc.tensor.matmul(ps[:, qa:qb], lhsT=KA[0:128, ix * 128:(ix + 1) * 128], rhs=Q[0:128, (2 * ix) // 60, hl, qa:qb], start=True, stop=True),
                                  reads=[KA_r, Q_r[hl]], writes=[ps_r])
                            p, p_r = exp_tile(ps, ps_r, KBS[:, h * 64 + ix:h * 64 + ix + 1], KBS_r, CM[:, r_, :] if r_ >= 0 else None, CM_r, qa, qb)
                        else:
                            if ix < 4:
                                qb = 128 * (ix + 1)
                            else:
                                qa = 128 * (ix - 4)
                            kb.op(kb.pe, lambda: nc.tensor.matmul(ps[:, qa:qb], lhsT=KW[0:128, ix * 128:(ix + 1) * 128], rhs=Q[0:128, 0, hl, qa:qb], start=True, stop=True),
                                  reads=[KW_r, Q_r[hl]], writes=[ps_r])
                            cb = (ix * 16 + h) if MONO else ((i * 8 + ix) * 16 + h)
                            p, p_r = exp_tile(ps, ps_r, KBW[:, cb:cb + 1], KBW_r, WM[:, ix, :], WM_r, qa, qb)
                        cur = (kind, ix, p, p_r, qa, qb)
                        pendq.append(cur)
                    if pendq and (len(pendq) > SKEW or u is None):
                        kind_, ix_, pp, pp_r, qa_, qb_ = pendq.pop(0)
                        qss = list(range(qa_ // 128, qb_ // 128))
                        for qs in qss:
                            if kind_ == "s":
                                kb.op(kb.pe, lambda: nc.tensor.matmul(aS[:, qs * 65:(qs + 1) * 65], lhsT=pp[:, qs * 128:(qs + 1) * 128], rhs=VA[:, ix_, :], start=False, stop=(ix_ == nkt - 1), skip_group_check=True),
                                      reads=[pp_r, VA_r], writes=[aS_r], sig=(qs == qss[-1]))
                            else:
                                kb.op(kb.pe, lambda: nc.tensor.matmul(aW[:, qs * 65:(qs + 1) * 65], lhsT=pp[:, qs * 128:(qs + 1) * 128], rhs=VW[:, ix_, :], start=False, stop=(ix_ == 7), skip_group_check=True),
                                      reads=[pp_r, VW_r], writes=[aW_r], sig=(qs == qss[-1]))
                assert not pendq
                if hl < 7:
                    nS, nS_r = P.psf[3 + 2 * ((hl + 1) % 2)], P.psf_r[3 + 2 * ((hl + 1) % 2)]
                    nW, nW_r = P.psf[4 + 2 * ((hl + 1) % 2)], P.psf_r[4 + 2 * ((hl + 1) % 2)]
                    kb.op(kb.dve, lambda: nc.vector.memset(nS[:, 0:260], 0.0), writes=[nS_r])
                    kb.op(kb.dve, lambda: nc.vector.memset(nW[:, 0:260], 0.0), writes=[nW_r])
                aSv = aS[:, 0:260].rearrange("p (q c) -> p q c", c=65)
                aWv = aW[:, 0:260].rearrange("p (q c) -> p q c", c=65)
                kb.op(kb.dve, lambda: nc.vector.tensor_scalar(out=dd[:, 0, :], in0=aSv[:, :, 64], scalar1=1e-30, scalar2=None, op0=ALU.max), reads=[aS_r], writes=[dd_r])
                kb.op(kb.dve, lambda: nc.vector.tensor_scalar(out=dd[:, 1, :], in0=aWv[:, :, 64], scalar1=1e-30, scalar2=None, op0=ALU.max), reads=[aW_r], writes=[dd_r])
                kb.op(kb.dve, lambda: nc.vector.reciprocal(out=dd[:, :, :], in_=dd[:, :, :]), reads=[dd_r], writes=[dd_r])
                kb.op(kb.dve, lambda: nc.vector.tensor_tensor(out=cf[:, 0, :], in0=crd[:, :, hl], in1=gat[:, :, h * 3 + 0], op=ALU.mult), reads=[crd_r[hl], gat_r], writes=[cf_r])
                kb.op(kb.dve, lambda: nc.vector.tensor_tensor(out=cf[:, 1, :], in0=dd[:, 0, :], in1=gat[:, :, h * 3 + 1], op=ALU.mult), reads=[dd_r, gat_r], writes=[cf_r])
                kb.op(kb.dve, lambda: nc.vector.tensor_tensor(out=cf[:, 2, :], in0=dd[:, 1, :], in1=gat[:, :, h * 3 + 2], op=ALU.mult), reads=[dd_r, gat_r], writes=[cf_r])
                for qs in range(4):
                    kb.op(kb.dve, lambda: nc.vector.tensor_scalar(out=oh[:, :], in0=cst[:, qs, hl, :], scalar1=cf[:, 0, qs:qs + 1], scalar2=None, op0=ALU.mult),
                          reads=[cst_r[hl], cf_r], writes=[oh_r])
                    kb.op(kb.dve, lambda: nc.vector.scalar_tensor_tensor(out=oh[:, :], in0=aS[:, qs * 65:qs * 65 + 64], scalar=cf[:, 1, qs:qs + 1], in1=oh[:, :], op0=ALU.mult, op1=ALU.add),
                          reads=[aS_r, cf_r, oh_r], writes=[oh_r])
                    kb.op(kb.dve, lambda: nc.vector.scalar_tensor_tensor(out=onb[:, qs, h * 64:(h + 1) * 64], in0=aW[:, qs * 65:qs * 65 + 64], scalar=cf[:, 2, qs:qs + 1], in1=oh[:, :],
                                                                          op0=ALU.mult, op1=ALU.add), reads=[aW_r, cf_r, oh_r], writes=[onb_r])
            for fc in range(4 * g, 4 * g + 4):
                k = fc % 2
                for qs in range(4):
                    kb.op(kb.pe, lambda: nc.tensor.transpose(out=P.psb[:, 0:128], in_=onb[:, qs, fc * 128:(fc + 1) * 128], identity=IDB[:, :]),
                          reads=[onb_r, IDB_r], writes=[P.psb_r])
                    kb.op(kb.dve, lambda: nc.vector.tensor_copy(out=ost[k][:, qs * 128:(qs + 1) * 128], in_=P.psb[:, 0:128]), reads=[P.psb_r], writes=[ost_r[k]])
                kb.dma(osems[k], dr["onsaT"][fc * 128:(fc + 1) * 128, q0:q0 + 512], ost[k][:, :], reads=[ost_r[k]], eng=(kb.act if (MONO or QSEL) else kb.sp))


BF = ml_dtypes.bfloat16
_CACHE = {}
USE_MONO = True


def _slopes():
    hh = np.arange(1, 17, dtype=np.float32)
    return np.exp2(-8.0 * hh / 16.0).astype(np.float32)


def _split3(v):
    v = v.astype(np.float32)
    hi = v.astype(BF)
    r = v - hi.astype(np.float32)
    mid = r.astype(BF)
    r2 = r - mid.astype(np.float32)
    lo = r2.astype(BF)
    return hi, mid, lo


def core_consts(cc):
    sl = _slopes()
    p = np.arange(128)
    q = np.arange(512)
    c = {}
    tabs = np.concatenate([(4 * i + cc) * 512 + q for i in range(4)]).astype(np.float32)
    v = -(sl[:, None] * tabs[None, :])
    hi, mid, lo = _split3(v)
    c["qal"] = np.ascontiguousarray(np.stack([hi, mid, lo], 0))
    kbs = np.zeros((128, 16, 64), np.float32)
    for h in range(16):
        kbs[:, h, :] = sl[h] * (np.arange(64)[None, :] * 128 + p[:, None]).astype(np.float32)
    c["kbs"] = kbs.reshape(128, 1024)
    kbc = np.zeros((128, 16, 4), np.float32)
    for h in range(16):
        kbc[:, h, :] = sl[h] * (16 * (np.arange(4)[None, :] * 128 + p[:, None]) + 31).astype(np.float32)
    c["kbc"] = kbc.reshape(128, 64)
    kbw = np.zeros((128, 4, 8, 16), np.float32)
    for i in range(4):
        T0 = (4 * i + cc) * 512
        for w in range(8):
            ka = T0 - 512 + w * 128 + p
            for h in range(16):
                kbw[:, i, w, h] = np.where(ka >= 0, sl[h] * ka.astype(np.float32), -30000.0)
    c["kbw"] = kbw.reshape(128, 512)
    cmpm = np.zeros((128, 2, 512), np.float32)
    for d in (-1, 0):
        vis = (2048 * d + 16 * p[:, None] + 31 - 512 * cc) <= q[None, :]
        cmpm[:, d + 1, :] = np.where(vis, 0.0, NEG)
    c["cmpm"] = cmpm.astype(BF)
    cm = np.zeros((128, 16, 512), np.float32)
    for r in range(16):
        vis = (128 * r + p[:, None]) <= (512 * cc + q[None, :])
        cm[:, r, :] = np.where(vis, 0.0, NEG)
    c["cm"] = cm.astype(BF)
    wm = np.zeros((128, 8, 512), np.float32)
    for w in range(8):
        dist = 512 + q[None, :] - 128 * w - p[:, None]
        wm[:, w, :] = np.where((dist >= 0) & (dist < 512), 0.0, NEG)
    c["wm"] = wm.astype(BF)
    addm = np.zeros((128, 16, 128), np.float32)
    vneg = np.zeros((128, 16, 128), np.float32)
    j = np.arange(128)
    for i in range(4):
        for qs in range(4):
            t = (4 * i + cc) * 512 + qs * 128 + p
            valid = (j[None, :] * 64) <= t[:, None]
            cur = t // 64
            forced = valid & ((j[None, :] == 0) | (j[None, :] == cur[:, None]) | (j[None, :] == cur[:, None] - 1))
            addm[:, i * 4 + qs, :] = np.where(forced, 8192.0, np.where(valid, 0.0, -8192.0))
            vneg[:, i * 4 + qs, :] = np.where(valid, 0.0, NEG)
    c["addm"] = addm.astype(BF)
    c["vneg"] = vneg.astype(BF)
    return c


def shared_consts():
    c = {}
    cols = np.arange(SEQ)
    kar = np.zeros((64, SEQ), np.float32)
    kar[0:60] = ((cols[None, :] // 64) % 60 == np.arange(60)[:, None])
    kar[60:63] = 1.0
    c["karows"] = kar.astype(BF)
    c["ones3"] = np.ones((3, 1024), np.float32).astype(BF)
    n = np.arange(512)
    cs = n[:, None] * 16
    ss = np.arange(128)[None, :] * 64
    ov = np.clip(np.minimum(cs + 32, ss + 64) - np.maximum(cs, ss), 0, None) / 32.0
    c["ovl"] = np.ascontiguousarray(ov.reshape(4, 128, 128).transpose(1, 0, 2)).astype(np.float32).astype(BF)
    c["identb"] = np.eye(128, dtype=np.float32).astype(BF)
    return c


def tile_w(Wm, starts=None):
    K = Wm.shape[0]
    if starts is None:
        starts = list(range(0, Wm.shape[1], 128))
    out = np.empty((len(starts), 128, K // 128, 128), np.float32)
    for j, c0 in enumerate(starts):
        out[j] = Wm[:, c0:c0 + 128].reshape(K // 128, 128, 128).transpose(1, 0, 2)
    return out


def _prog(key, fn):
    if key not in _CACHE:
        _CACHE[key] = fn()
    return _CACHE[key]


def _tok_index(cc):
    return np.concatenate([np.arange((4 * i + cc) * 512, (4 * i + cc + 1) * 512) for i in range(4)])


def kernel(x, ffn1_norm, ffn1_w_gate, ffn1_w_up, ffn1_w_down, mix_norm, w_in, cmp_pos,
           cmp_k_w1, cmp_k_w2, cmp_v_w1, cmp_v_w2, pool_w, pool_scale, w_branch_pool,
           w_branch_nsa, w_out, ffn2_norm, ffn2_w_gate, ffn2_w_up, ffn2_w_down, final_norm):
    f32 = lambda a: np.ascontiguousarray(np.asarray(a, dtype=np.float32))
    x = f32(x)
    W = {k: f32(v) for k, v in dict(ffn1_norm=ffn1_norm, ffn1_w_gate=ffn1_w_gate, ffn1_w_up=ffn1_w_up, ffn1_w_down=ffn1_w_down,
                                    mix_norm=mix_norm, w_in=w_in, cmp_pos=cmp_pos, cmp_k_w1=cmp_k_w1, cmp_k_w2=cmp_k_w2,
                                    cmp_v_w1=cmp_v_w1, cmp_v_w2=cmp_v_w2, pool_w=pool_w, pool_scale=pool_scale,
                                    w_branch_pool=w_branch_pool, w_branch_nsa=w_branch_nsa, w_out=w_out, ffn2_norm=ffn2_norm,
                                    ffn2_w_gate=ffn2_w_gate, ffn2_w_up=ffn2_w_up, ffn2_w_down=ffn2_w_down, final_norm=final_norm).items()}
    cores = list(range(8))
    vecs = np.zeros((128, 64), np.float32)
    for l in range(NL):
        b0 = vbase(l)
        vecs[:, b0:b0 + 8] = gain_layout(W["ffn1_norm"][l])
        vecs[:, b0 + 8:b0 + 16] = gain_layout(W["mix_norm"][l])
        vecs[:, b0 + 16:b0 + 24] = gain_layout(W["ffn2_norm"][l])
        vecs[:, b0 + 24:b0 + 28] = gain_layout(W["pool_scale"][l])
    vecs[:, 56:64] = gain_layout(W["final_norm"])
    if USE_MONO:
        return kernel_mono(W, x, vecs)
    tix = [_tok_index(c % 4) for c in cores]
    cc_consts = [core_consts(cc) for cc in range(4)]
    sh = shared_consts()

    TW = {}
    for l in range(NL):
        TW["win", l] = tile_w(W["w_in"][l], WIN_STARTS)
        for nm in ("ffn1_w_gate", "ffn1_w_up", "ffn1_w_down", "ffn2_w_gate", "ffn2_w_up", "ffn2_w_down", "w_branch_pool", "w_branch_nsa", "w_out",
                   "cmp_k_w1", "cmp_v_w1"):
            TW[nm, l] = tile_w(W[nm][l])

    def a_weights(l):
        return {"f1_wg": TW["ffn1_w_gate", l], "f1_wu": TW["ffn1_w_up", l], "f1_wd": TW["ffn1_w_down", l], "win_a": TW["win", l]}

    progA = _prog("A", lambda: build_tok(0, "A", False))
    in_maps = []
    for c in cores:
        m = {"xs_in": np.ascontiguousarray(x[c // 4, tix[c], :].T), "vecs": vecs}
        m.update(a_weights(0))
        in_maps.append(m)
    res = run_bass_kernel_spmd(progA, in_maps, core_ids=cores).results

    out = np.zeros((2, SEQ, D), np.float32)
    for l in range(NL):
        last = (l == NL - 1)
        kvall = np.zeros((2, 4, 128, SEQ + 32), BF)
        vall = np.zeros((2, SEQ, 256), BF)
        uall = np.zeros((2, 512, SEQ), np.float32)
        for c in cores:
            b = c // 4
            kvall[b][:, :, tix[c]] = np.asarray(res[c]["kvT_out"]).view(BF) if np.asarray(res[c]["kvT_out"]).dtype != BF else res[c]["kvT_out"]
            vall[b][tix[c], :] = np.asarray(res[c]["vtok_out"])
            uall[b][:, tix[c]] = np.asarray(res[c]["uT_out"])
        prog = _prog(("BCA", l, last), lambda: build_bca(l, last))
        pecol = np.ascontiguousarray(W["cmp_pos"][l].reshape(16, 2, 64).transpose(1, 2, 0).reshape(128, 16))
        in_maps = []
        for c in cores:
            b, cc = c // 4, c % 4
            kwin = np.zeros((128, 4, 1024), BF)
            vwin = np.zeros((4, 1024, 128), BF)
            uext = np.zeros((512, 4, 528), np.float32)
            for i in range(4):
                T0 = (4 * i + cc) * 512
                lo = max(T0 - 512, 0)
                kwin[:, i, 1024 - (T0 + 512 - lo):] = kvall[b][3][:, lo:T0 + 512]
                vwin[i, 1024 - (T0 + 512 - lo):, :] = vall[b][lo:T0 + 512, 128:256]
                lo = max(T0 - 16, 0)
                uext[:, i, 528 - (T0 + 512 - lo):] = uall[b][:, lo:T0 + 512]
            corr = np.ones((128, 4, 16), np.float32)
            if cc == 0:
                for gi, w in enumerate(POOLW):
                    corr[:, gi, :] = (w / np.minimum(np.arange(16) + 1.0, float(w)))[None, :]
            m = {"vecs": vecs, "h2T": np.asarray(res[c]["h2T_out"]), "kvall": kvall[b], "vall": vall[b], "kwin": kwin, "vwin": vwin,
                 "win": TW["win", l], "win_gn": np.ascontiguousarray(W["w_in"][l][:, C_GN:C_GN + 48]),
                 "ck_w1": TW["cmp_k_w1", l], "ck_w2": W["cmp_k_w2"][l], "cv_w1": TW["cmp_v_w1", l], "cv_w2": W["cmp_v_w2"][l],
                 "pecol": pecol,
                 "xs_in": np.asarray(res[c]["xs_out"]),
                 "f2_wg": TW["ffn2_w_gate", l], "f2_wu": TW["ffn2_w_up", l], "f2_wd": TW["ffn2_w_down", l],
                 "wpa": TW["w_branch_pool", l], "wnb": TW["w_branch_nsa", l], "wo": TW["w_out", l],
                 "poolw": W["pool_w"][l], "uext": uext, "corr": corr}
            m.update(cc_consts[cc])
            m.update(sh)
            if not last:
                m.update(a_weights(l + 1))
            in_maps.append(m)
        res = run_bass_kernel_spmd(prog, in_maps, core_ids=cores).results
    for c in cores:
        out[c // 4, tix[c], :] = np.asarray(res[c]["out"]).T
    return out


def build_mono():
    global MONO
    MONO = True
    try:
        BI, IN_, BO = "ExternalInput", "Internal", "ExternalOutput"
        S = SEQ
        specs = {
            "x_in": ((D, S), F32, BI), "vecs": ((128, 64), F32, BI),
            "qal": ((3, 16, S), BF16, BI), "kbs": ((128, 1024), F32, BI), "kbc": ((128, 64), F32, BI),
            "kbw": ((16, 128, 128), F32, BI), "cmpm": ((128, 5, 512), BF16, BI), "cm": ((128, 4, 512), BF16, BI),
            "wm": ((128, 8, 512), BF16, BI), "addm": ((16, 128, 4, 128), BF16, BI), "vneg": ((16, 128, 4, 128), BF16, BI),
            "karows": ((64, S), BF16, BI), "ones3": ((3, 1024), BF16, BI), "ovl": ((128, 4, 128), BF16, BI), "identb": ((128, 128), BF16, BI),
            "corr": ((128, 4, 16), F32, BI),
            "xs": ((D, S), F32, IN_), "h2T": ((D, S), BF16, IN_), "kvT": ((4, 128, S), BF16, IN_), "vtok": ((S, 256), BF16, IN_),
            "uT0": ((512, S), F32, IN_), "uT1": ((512, S), F32, IN_), "onsaT": ((D, S), BF16, IN_),
            "out_q": ((D, NTOK), F32, BO), "onsaT_q": ((D, NTOK), BF16, IN_),
            "selw": ((128, 4), F32, BI), "corr_q": ((128, 4, 16), F32, BI),
            "qal_q": ((3, 16, NTOK), BF16, BI), "kbw_q": ((128, 512), F32, BI), "cmpm_q": ((128, 2, 512), BF16, BI),
            "cm_q": ((128, 16, 512), BF16, BI), "addm_q": ((128, 16, 128), BF16, BI), "vneg_q": ((128, 16, 128), BF16, BI),
        }
        for l in range(NL):
            for pre in ("f1_", "f2_"):
                specs["%swg_%d" % (pre, l)] = ((NFC, 128, 8, 128), F32, BI)
                specs["%swu_%d" % (pre, l)] = ((NFC, 128, 8, 128), F32, BI)
                specs["%swd_%d" % (pre, l)] = ((8, 128, NFC, 128), F32, BI)
            specs["win_%d" % l] = ((len(WIN_STARTS), 128, 8, 128), F32, BI)
            specs["win_gn_%d" % l] = ((D, 48), F32, BI)
            specs["wpa_%d" % l] = ((8, 128, 4, 128), F32, BI)
            specs["wnb_%d" % l] = ((8, 128, 8, 128), F32, BI)
            specs["wo_%d" % l] = ((8, 128, 8, 128), F32, BI)
            specs["poolw_%d" % l] = ((4, 128, 128), F32, BI)
            specs["ck_w1_%d" % l] = ((2, 128, 16, 128), F32, BI)
            specs["cv_w1_%d" % l] = ((2, 128, 16, 128), F32, BI)
            specs["ck_w2_%d" % l] = ((256, 64), F32, BI)
            specs["cv_w2_%d" % l] = ((256, 64), F32, BI)
            specs["pecol_%d" % l] = ((128, 16), F32, BI)
        conv = [n_ for n_, (sh_, dt_, k_) in specs.items() if k_ == BI and dt_ == F32 and len(sh_) == 4 and n_ != "kbw"]
        gu = [n_ for n_ in conv if n_[3:5] in ("wg", "wu")]
        conv = [n_ for n_ in conv if n_ not in gu]
        for n_ in conv:
            specs[n_ + "_b"] = (specs[n_][0], BF16, IN_)
        for l in range(NL):
            for pre in ("f1_", "f2_"):
                specs["%swgu_%d_b" % (pre, l)] = ((NFC, 128, 16, 128), BF16, IN_)
        P = Prog(specs, WST=None)
        kb, dr = P.kb, P.dr

        P.selw = kb.sb("selw", [128, 4], F32)
        P.selw_r = Res("selw")
        kb.dma(P.ldsem, P.selw[:, :], dr["selw"][:, :], writes=[P.selw_r])
        mono_tabs = {k_: dr[k_] for k_ in ("qal", "kbw", "cmpm", "cm", "addm", "vneg", "corr", "onsaT")}

        def do_convert():
            for n_ in conv:
                P.convert_w(dr[n_], dr[n_ + "_b"])
                dr[n_] = dr[n_ + "_b"]
            for l_ in range(NL):
                for pre in ("f1_", "f2_"):
                    d_ = dr["%swgu_%d_b" % (pre, l_)]
                    P.convert_w(dr["%swg_%d" % (pre, l_)], None, dst_fn=lambda j, d_=d_: d_[j, :, 0:8, :])
                    P.convert_w(dr["%swu_%d" % (pre, l_)], None, dst_fn=lambda j, d_=d_: d_[j, :, 8:16, :])

        def alias(l, mode):
            for nm in ("wpa", "wnb", "wo", "poolw", "ck_w1", "cv_w1", "ck_w2", "cv_w2", "pecol", "win_gn"):
                dr[nm] = dr["%s_%d" % (nm, l)]
            dr["win"] = dr["win_%d" % l]
            dr["win_c"] = dr["win_%d" % l]
            dr["f2_wd"] = dr["f2_wd_%d" % l]
            dr["f2_wg"] = dr["f2_wgu_%d_b" % l]
            dr["f2_wu"] = None
            la = l if mode == "A" else min(l + 1, NL - 1)
            dr["win_a"] = dr["win_%d" % la]
            dr["f1_wd"] = dr["f1_wd_%d" % la]
            dr["f1_wg"] = dr["f1_wgu_%d_b" % la]
            dr["f1_wu"] = None
            dr["kvall"] = dr["kvT"]
            dr["vall"] = dr["vtok"]
            dr["h2T_in"] = dr["h2T"]
            dr["h2T_out"] = dr["h2T"]
            dr["kvT_out"] = dr["kvT"]
            dr["vtok_out"] = dr["vtok"]
            dr["xs_out"] = dr["xs"]
            dr["xs_in"] = dr["x_in"] if (mode == "A" and l == 0) else dr["xs"]
            dr["uT_in"] = dr["uT%d" % (l % 2)]
            dr["uT_out"] = dr["uT%d" % (la % 2)]

        def phase(fn, wst, wstf=1024, nslot=4):
            with ExitStack() as pes:
                kb.cur_es = pes
                P.alloc_wstage(wst, wstf, nslot)
                fn()
                kb.barrier()
            kb.cur_es = None

        phase(do_convert, 3584, 3584, 2)
        alias(0, "A")
        phase(lambda: tok_body(P, 0, "A", False), 3584)
        global QSEL
        for l in range(NL):
            last = (l == NL - 1)
            alias(l, "CA")
            if last:
                MONO, QSEL = False, True
                for k_ in ("qal", "kbw", "cmpm", "cm", "addm", "vneg", "corr", "onsaT"):
                    dr[k_] = dr[k_ + "_q"]
                dr["out"] = dr["out_q"]
            phase(lambda: attn_body(P), 2048, 1024, 2)
            phase(lambda: tok_body(P, l, "CA", last), 3584)
        return P.finish()
    finally:
        MONO = False
        QSEL = False


def mono_consts():
    sl = _slopes()
    p = np.arange(128)
    q = np.arange(512)
    c = {}
    tabs = np.arange(SEQ).astype(np.float32)
    hi, mid, lo = _split3(-(sl[:, None] * tabs[None, :]))
    c["qal"] = np.ascontiguousarray(np.stack([hi, mid, lo], 0))
    kbs = np.zeros((128, 16, 64), np.float32)
    kbc = np.zeros((128, 16, 4), np.float32)
    for h in range(16):
        kbs[:, h, :] = sl[h] * (np.arange(64)[None, :] * 128 + p[:, None]).astype(np.float32)
        kbc[:, h, :] = sl[h] * (16 * (np.arange(4)[None, :] * 128 + p[:, None]) + 31).astype(np.float32)
    c["kbs"] = kbs.reshape(128, 1024)
    c["kbc"] = kbc.reshape(128, 64)
    kbw = np.zeros((16, 128, 8, 16), np.float32)
    for i in range(16):
        for w in range(8):
            ka = 512 * (i - 1) + w * 128 + p
            for h in range(16):
                kbw[i, :, w, h] = np.where(ka >= 0, sl[h] * ka.astype(np.float32), -30000.0)
    c["kbw"] = kbw.reshape(16, 128, 128)
    cmpm = np.zeros((128, 5, 512), np.float32)
    for d in range(5):
        cmpm[:, d, :] = np.where((16 * p[:, None] + 31 - 512 * d) <= q[None, :], 0.0, NEG)
    c["cmpm"] = cmpm.astype(BF)
    cm = np.zeros((128, 4, 512), np.float32)
    for r in range(4):
        cm[:, r, :] = np.where((128 * r + p[:, None]) <= q[None, :], 0.0, NEG)
    c["cm"] = cm.astype(BF)
    wm = np.zeros((128, 8, 512), np.float32)
    for w in range(8):
        dist = 512 + q[None, :] - 128 * w - p[:, None]
        wm[:, w, :] = np.where((dist >= 0) & (dist < 512), 0.0, NEG)
    c["wm"] = wm.astype(BF)
    addm = np.zeros((16, 128, 4, 128), np.float32)
    vneg = np.zeros((16, 128, 4, 128), np.float32)
    j = np.arange(128)
    for i in range(16):
        for qs in range(4):
            t = i * 512 + qs * 128 + p
            valid = (j[None, :] * 64) <= t[:, None]
            cur = t // 64
            forced = valid & ((j[None, :] == 0) | (j[None, :] == cur[:, None]) | (j[None, :] == cur[:, None] - 1))
            addm[i, :, qs, :] = np.where(forced, 8192.0, np.where(valid, 0.0, -8192.0))
            vneg[i, :, qs, :] = np.where(valid, 0.0, NEG)
    c["addm"] = addm.astype(BF)
    c["vneg"] = vneg.astype(BF)
    corr = np.ones((128, 4, 16), np.float32)
    for gi, w in enumerate(POOLW):
        corr[:, gi, :] = (w / np.minimum(np.arange(16) + 1.0, float(w)))[None, :]
    c["corr"] = corr
    c.update(shared_consts())
    return c


def kernel_mono(W, x, vecs):
    prog = _prog("MONO", build_mono)
    base = {"vecs": vecs}
    base.update(mono_consts())
    for l in range(NL):
        base["win_%d" % l] = tile_w(W["w_in"][l], WIN_STARTS)
        base["win_gn_%d" % l] = np.ascontiguousarray(W["w_in"][l][:, C_GN:C_GN + 48])
        for pre, a in (("f1_", "ffn1"), ("f2_", "ffn2")):
            base["%swg_%d" % (pre, l)] = tile_w(W[a + "_w_gate"][l])
            base["%swu_%d" % (pre, l)] = tile_w(W[a + "_w_up"][l])
            base["%swd_%d" % (pre, l)] = tile_w(W[a + "_w_down"][l])
        base["wpa_%d" % l] = tile_w(W["w_branch_pool"][l])
        base["wnb_%d" % l] = tile_w(W["w_branch_nsa"][l])
        base["wo_%d" % l] = tile_w(W["w_out"][l])
        base["poolw_%d" % l] = W["pool_w"][l]
        base["ck_w1_%d" % l] = tile_w(W["cmp_k_w1"][l])
        base["cv_w1_%d" % l] = tile_w(W["cmp_v_w1"][l])
        base["ck_w2_%d" % l] = W["cmp_k_w2"][l]
        base["cv_w2_%d" % l] = W["cmp_v_w2"][l]
        base["pecol_%d" % l] = np.ascontiguousarray(W["cmp_pos"][l].reshape(16, 2, 64).transpose(1, 2, 0).reshape(128, 16))
    cores = list(range(8))
    in_maps = []
    xT = [np.ascontiguousarray(x[b].T) for b in range(2)]
    for c in cores:
        b, cc = c % 2, c // 2
        m = dict(base)
        m["x_in"] = xT[b]
        cq = core_consts(cc)
        for k_ in ("qal", "kbw", "cmpm", "cm", "addm", "vneg"):
            m[k_ + "_q"] = cq[k_]
        selw = np.zeros((128, 4), np.float32)
        selw[:, cc] = 1.0
        m["selw"] = selw
        corr = np.ones((128, 4, 16), np.float32)
        if cc == 0:
            corr = base["corr"]
        m["corr_q"] = corr
        in_maps.append(m)
    res = run_bass_kernel_spmd(prog, in_maps, core_ids=cores).results
    out = np.zeros((2, SEQ, D), np.float32)
    for c in cores:
        out[c % 2, _tok_index(c // 2), :] = np.asarray(res[c]["out_q"]).T
    return out
```

```python
import numpy as np
import ml_dtypes
from contextlib import ExitStack
import concourse.bass as bass
import concourse.mybir as mybir
from concourse.bass_utils import run_bass_kernel_spmd

F32 = mybir.dt.float32
BF16 = mybir.dt.bfloat16
AF = mybir.ActivationFunctionType
ALU = mybir.AluOpType

D = 1024
DFF = 2816
NFC = DFF // 128
SEQ = 8192
NL = 2
NTOK = 2048
MONO = False
QSEL = False


def ntok():
    return SEQ if MONO else NTOK
TT = 512
INW = 4400
C_Q, C_KC, C_VC, C_KSL, C_VSL, C_KWN, C_VWN, C_GN, C_U, C_GM = 0, 1024, 1152, 1280, 1408, 1536, 1664, 1792, 1840, 2352
EPS = 1e-6
NEG = -16384.0
WIN_STARTS = [j * 128 for j in range(8)] + [C_KC, C_VC, C_KSL, C_VSL, C_KWN, C_VWN] + [C_U + j * 128 for j in range(4)] + [C_GM + j * 128 for j in range(16)]
WIN_IDX = {c: i for i, c in enumerate(WIN_STARTS)}


class Sem:
    def __init__(self, h, name):
        self.h = h
        self.name = name
        self.count = 0
        self.group = False


class Tok:
    __slots__ = ("sem", "val")

    def __init__(self, sem, val):
        self.sem = sem
        self.val = val


class Res:
    __slots__ = ("name", "w", "r", "excl")

    def __init__(self, name="", excl=False):
        self.name = name
        self.w = None
        self.r = {}
        self.excl = excl


class Eng:
    def __init__(self, name, h, sem, same_sync):
        self.name = name
        self.h = h
        self.sem = sem
        self.waited = {}
        self.pending = []
        self.same_sync = same_sync


class KB:
    def __init__(self, nc, es):
        self.nc = nc
        self.es = es
        self.sems = []
        self.pe = self._eng("pe", nc.tensor, False)
        self.act = self._eng("act", nc.scalar, True)
        self.dve = self._eng("dve", nc.vector, True)
        self.pool = self._eng("pool", nc.gpsimd, True)
        self.sp = self._eng("sp", nc.sync, False)
        self.engs = [self.pe, self.act, self.dve, self.pool, self.sp]
        self.n_inst = 0

    def new_sem(self, name):
        name = "%s_%d" % (name, len(self.sems))
        h = self.es.enter_context(self.nc.semaphore(name))
        s = Sem(h, name)
        self.sems.append(s)
        return s

    def _eng(self, name, h, same_sync):
        return Eng(name, h, self.new_sem("s_" + name), same_sync)

    def sb(self, name, shape, dtype, es=None):
        self.nsb = getattr(self, "nsb", 0) + 1
        return (es or getattr(self, "cur_es", None) or self.es).enter_context(self.nc.sbuf_tensor("sb%d_%s" % (self.nsb, name), shape, dtype))

    def ps(self, name, shape, dtype):
        return self.es.enter_context(self.nc.psum_tensor("pp_" + name, shape, dtype))

    def _wait(self, eng, tok):
        if tok is None:
            return
        if tok.sem is eng.sem and not eng.same_sync:
            return
        assert tok.val is not None, "waiting on unresolved token (%s)" % tok.sem.name
        val = tok.val
        if tok.sem.group:
            val = max(val, tok.sem.count)
        if eng.waited.get(tok.sem, 0) >= val:
            return
        eng.h.wait_ge(tok.sem.h, val)
        eng.waited[tok.sem] = val

    def _deps(self, eng, reads, writes):
        for r in reads:
            self._wait(eng, r.w)
        for w in writes:
            self._wait(eng, w.w)
            for t in w.r.values():
                self._wait(eng, t)

    def _mark(self, tok, reads, writes):
        for r in reads:
            r.r[tok.sem] = tok
        for w in writes:
            w.w = tok
            w.r = {}

    def op(self, eng, fn, reads=(), writes=(), sig=True):
        xr = [r for r in reads if r.excl]
        if xr:
            writes = list(writes) + xr
            reads = [r for r in reads if not r.excl]
        self._deps(eng, reads, writes)
        inst = fn()
        self.n_inst += 1
        if sig:
            eng.sem.count += 1
            inst.then_inc(eng.sem.h, 1)
            tok = Tok(eng.sem, eng.sem.count)
            for t in eng.pending:
                t.val = eng.sem.count
            eng.pending = []
        else:
            tok = Tok(eng.sem, None)
            eng.pending.append(tok)
        self._mark(tok, reads, writes)
        return tok

    def dma(self, sem, out, in_, reads=(), writes=(), eng=None, **kw):
        eng = eng or self.sp
        self._deps(eng, reads, writes)
        if sem.count > 0:
            self._wait(eng, Tok(sem, sem.count))
        inst = eng.h.dma_start(out=out, in_=in_, **kw)
        self.n_inst += 1
        sem.count += 16
        inst.then_inc(sem.h, 16)
        tok = Tok(sem, sem.count)
        self._mark(tok, reads, writes)
        return tok

    def barrier(self):
        for e in self.engs:
            assert not e.pending
            for s in self.sems:
                if s.count > 0 and not (s is e.sem):
                    self._wait(e, Tok(s, s.count))


class Prog:
    def __init__(self, dram_specs, WST=3584):
        self.nc = bass.Bass("TRN2", target_bir_lowering=False)
        self.es = ExitStack()
        self.kb = KB(self.nc, self.es)
        self.dr = {}
        self.dres = {}
        for name, (shape, dt, kind) in dram_specs.items():
            self.dr[name] = self.nc.dram_tensor(name, list(shape), dt, kind=kind).ap()
            self.dres[name] = Res("dram_" + name)
        self.out_names = [n for n, (_, _, k) in dram_specs.items() if k == "ExternalOutput"]
        kb = self.kb
        self.psf = [kb.ps("psf%d" % i, [128, 512], F32) for i in range(7)]
        self.psf_r = [Res("psf%d" % i, excl=True) for i in range(7)]
        self.psb = kb.ps("psb", [128, 1024], BF16)
        self.psb_r = Res("psb", excl=True)
        self.ps_rr = 0
        self.ones = kb.sb("ones", [128, 128], F32)
        self.ones_r = Res("ones")
        kb.op(kb.dve, lambda: self.nc.vector.memset(self.ones[:], 1.0 / D), writes=[self.ones_r])
        self.epsc = kb.sb("epsc", [128, 1], F32)
        kb.op(kb.dve, lambda: self.nc.vector.memset(self.epsc[:], EPS), writes=[self.ones_r])
        self.vecs = kb.sb("vecs", [128, 64], F32)
        self.vecs_r = Res("vecs")
        self.ldsem = kb.new_sem("ld_misc")
        self.ldsem.group = True
        kb.dma(self.ldsem, self.vecs[:], self.dr["vecs"][:, :], writes=[self.vecs_r])
        self.wsem = [kb.new_sem("wsem%d" % i) for i in range(4)]
        self.stsem = kb.new_sem("st_misc")
        if WST:
            self.alloc_wstage(WST)

    def alloc_wstage(self, WST, WSTF=None, nslot=2):
        kb = self.kb
        self.WST = WST
        self.WSTF = WSTF or WST
        self.nslot = nslot
        self.wst = [kb.sb("wst%d" % i, [128, self.WSTF], F32) for i in range(nslot)]
        self.wst_r = [Res("wst%d" % i) for i in range(nslot)]
        self.wbf = [kb.sb("wbf%d" % i, [128, self.WST], BF16) for i in range(nslot)]
        self.wbf_r = [Res("wbf%d" % i) for i in range(nslot)]
        self.wslot = 0

    def bank(self):
        i = self.ps_rr % 7
        self.ps_rr += 1
        return self.psf[i], self.psf_r[i]

    def load_w(self, pieces):
        kb, nc = self.kb, self.nc
        s = self.wslot
        self.wslot = (self.wslot + 1) % self.nslot
        off = 0
        foff = 0
        views = []
        for ap in pieces:
            if len(ap.shape) == 3:
                _, kc, n = ap.shape
                src = ap
            else:
                K, n = ap.shape
                kc = K // 128
                src = ap.rearrange("(k p) n -> p k n", p=128)
            sz = kc * n
            bview = self.wbf[s][:, off:off + sz].rearrange("p (k n) -> p k n", n=n)
            if ap.dtype == BF16:
                kb.dma(self.wsem[s], bview, src, writes=[self.wbf_r[s]])
            else:
                assert foff + sz <= self.WSTF
                dst = self.wst[s][:, foff:foff + sz].rearrange("p (k n) -> p k n", n=n)
                kb.dma(self.wsem[s], dst, src, writes=[self.wst_r[s]])
                a, b, fa = off, off + sz, foff
                kb.op(kb.pool, lambda a=a, b=b, fa=fa: nc.gpsimd.tensor_copy(out=self.wbf[s][:, a:b], in_=self.wst[s][:, fa:fa + (b - a)]),
                      reads=[self.wst_r[s]], writes=[self.wbf_r[s]])
                foff += sz
            views.append(bview)
            off += sz
        assert off <= self.WST
        return views, self.wbf_r[s]

    def convert_w(self, src, dst, dst_fn=None):
        kb, nc = self.kb, self.nc
        nch, _, kc, n = src.shape
        sz = kc * n
        if not hasattr(self, "cvsem"):
            self.cvsem = [kb.new_sem("cvs%d" % i) for i in range(4)]
            self.cv_rr = 0
        for j in range(nch):
            s = self.wslot
            self.wslot = (self.wslot + 1) % self.nslot
            kb.dma(self.wsem[s], self.wst[s][:, 0:sz].rearrange("p (k n) -> p k n", n=n), src[j], writes=[self.wst_r[s]])
            e = self.cv_rr % 3
            self.cv_rr += 1
            if e == 0:
                kb.op(kb.pool, lambda: nc.gpsimd.tensor_copy(out=self.wbf[s][:, 0:sz], in_=self.wst[s][:, 0:sz]), reads=[self.wst_r[s]], writes=[self.wbf_r[s]])
            elif e == 1:
                kb.op(kb.dve, lambda: nc.vector.tensor_copy(out=self.wbf[s][:, 0:sz], in_=self.wst[s][:, 0:sz]), reads=[self.wst_r[s]], writes=[self.wbf_r[s]])
            else:
                kb.op(kb.act, lambda: nc.scalar.copy(out=self.wbf[s][:, 0:sz], in_=self.wst[s][:, 0:sz]), reads=[self.wst_r[s]], writes=[self.wbf_r[s]])
            kb.dma(self.cvsem[s], dst_fn(j) if dst_fn else dst[j], self.wbf[s][:, 0:sz].rearrange("p (k n) -> p k n", n=n), reads=[self.wbf_r[s]])

    def select_tile(self, dst, dst_r, cands, stage, stage_r, sem, p0, p1):
        kb, nc = self.kb, self.nc
        first = True
        for c, src in enumerate(cands):
            if src is None:
                continue
            kb.dma(sem, stage, src, writes=[stage_r])
            sc_ = self.selw[p0:p1, c:c + 1]
            if first:
                kb.op(kb.dve, lambda: nc.vector.tensor_scalar(out=dst, in0=stage, scalar1=sc_, scalar2=None, op0=ALU.mult),
                      reads=[stage_r, self.selw_r], writes=[dst_r])
                first = False
            else:
                kb.op(kb.dve, lambda: nc.vector.scalar_tensor_tensor(out=dst, in0=stage, scalar=sc_, in1=dst, op0=ALU.mult, op1=ALU.add),
                      reads=[stage_r, self.selw_r, dst_r], writes=[dst_r])

    def vcol(self, c):
        return self.vecs[:, c:c + 1]

    def rmsnorm(self, x, x_r, h, h_r, n, gcol, sq, sq_r, rstd, rstd_r, out_f32=None, out_r=None):
        kb, nc = self.kb, self.nc
        for s0 in range(0, n, 512):
            ps, ps_r = self.bank()
            for c in range(8):
                k = c % 2
                kb.op(kb.act, lambda c=c, k=k: nc.scalar.activation(out=sq[k][:, :], in_=x[:, c, s0:s0 + 512], func=AF.Square),
                      reads=[x_r], writes=[sq_r[k]])
                kb.op(kb.pe, lambda c=c, k=k: nc.tensor.matmul(ps[:, :], lhsT=self.ones[:, :], rhs=sq[k][:, :], start=(c == 0), stop=(c == 7)),
                      reads=[sq_r[k], self.ones_r], writes=[ps_r], sig=True)
            kb.op(kb.act, lambda: nc.scalar.activation(out=rstd[:, s0:s0 + 512], in_=ps[:, :], func=AF.Ln, bias=self.epsc[:, 0:1]),
                  reads=[ps_r, self.ones_r], writes=[rstd_r])
            kb.op(kb.act, lambda: nc.scalar.activation(out=rstd[:, s0:s0 + 512], in_=rstd[:, s0:s0 + 512], func=AF.Exp, scale=-0.5),
                  reads=[rstd_r], writes=[rstd_r])
            for c in range(8):
                tgt = h if out_f32 is None else out_f32
                tgt_r = h_r if out_f32 is None else out_r
                kb.op(kb.dve, lambda c=c, tgt=tgt: nc.vector.scalar_tensor_tensor(
                    out=tgt[:, c, s0:s0 + 512], in0=x[:, c, s0:s0 + 512], scalar=self.vcol(gcol + c), in1=rstd[:, s0:s0 + 512],
                    op0=ALU.mult, op1=ALU.mult), reads=[x_r, rstd_r, self.vecs_r], writes=[tgt_r])

    def ffn(self, x, x_r, h, h_r, n, wg, wu, wd, aT, aT_r, sg, sg_r):
        kb, nc = self.kb, self.nc
        nsub = n // 512
        for fc in range(NFC):
            if wu is None:
                (wgu,), w_r = self.load_w([wg[fc]])
                wgb, wub = wgu[:, 0:8, :], wgu[:, 8:16, :]
            else:
                (wgb, wub), w_r = self.load_w([wg[fc], wu[fc]])
            for sub in range(nsub):
                s0 = sub * 512
                pg, pg_r = self.bank()
                pu, pu_r = self.bank()
                for c in range(8):
                    kb.op(kb.pe, lambda c=c: nc.tensor.matmul(pg[:, :], lhsT=wgb[:, c, :], rhs=h[:, c, s0:s0 + 512], start=(c == 0), stop=(c == 7)),
                          reads=[w_r, h_r], writes=[pg_r], sig=(c == 7))
                for c in range(8):
                    kb.op(kb.pe, lambda c=c: nc.tensor.matmul(pu[:, :], lhsT=wub[:, c, :], rhs=h[:, c, s0:s0 + 512], start=(c == 0), stop=(c == 7)),
                          reads=[w_r, h_r], writes=[pu_r], sig=(c == 7))
                k = (fc * nsub + sub) % 2
                kb.op(kb.act, lambda k=k: nc.scalar.activation(out=sg[k][:, :], in_=pg[:, :], func=AF.Silu), reads=[pg_r], writes=[sg_r[k]])
                kb.op(kb.dve, lambda k=k: nc.vector.tensor_tensor(out=aT[:, fc, s0:s0 + 512], in0=sg[k][:, :], in1=pu[:, :], op=ALU.mult),
                      reads=[sg_r[k], pu_r], writes=[aT_r[fc]])
        for dc in range(8):
            (wdb,), w_r = self.load_w([wd[dc]])
            for sub in range(nsub):
                s0 = sub * 512
                py, py_r = self.bank()
                for fc in range(NFC):
                    kb.op(kb.pe, lambda fc=fc: nc.tensor.matmul(py[:, :], lhsT=wdb[:, fc, :], rhs=aT[:, fc, s0:s0 + 512], start=(fc == 0), stop=(fc == NFC - 1)),
                          reads=[w_r, aT_r[fc]], writes=[py_r], sig=(fc == NFC - 1))
                kb.op(kb.dve, lambda: nc.vector.scalar_tensor_tensor(out=x[:, dc, s0:s0 + 512], in0=py[:, :], scalar=0.5, in1=x[:, dc, s0:s0 + 512],
                                                                      op0=ALU.mult, op1=ALU.add), reads=[py_r, x_r], writes=[x_r])

    def finish(self):
        kb = self.kb
        kb.barrier()
        self.es.close()
        return self.nc


def gain_layout(v):
    return np.ascontiguousarray(np.asarray(v, np.float32).reshape(-1, 128).T)


def vbase(l):
    return 28 * l


POOLW = (2, 4, 8, 16)


def tok_specs(l, mode, last):
    specs = {"xs_in": ((D, NTOK), F32, "ExternalInput"), "vecs": ((128, 64), F32, "ExternalInput")}

    def wspec(li, pre):
        specs[pre + "wg"] = ((NFC, 128, 8, 128), F32, "ExternalInput")
        specs[pre + "wu"] = ((NFC, 128, 8, 128), F32, "ExternalInput")
        specs[pre + "wd"] = ((8, 128, NFC, 128), F32, "ExternalInput")

    doA = (mode == "A") or (not last)
    if mode == "CA":
        wspec(l, "f2_")
        specs["win_c"] = ((len(WIN_STARTS), 128, 8, 128), F32, "ExternalInput")
        specs["wpa"] = ((8, 128, 4, 128), F32, "ExternalInput")
        specs["wnb"] = ((8, 128, 8, 128), F32, "ExternalInput")
        specs["wo"] = ((8, 128, 8, 128), F32, "ExternalInput")
        specs["poolw"] = ((4, 128, 128), F32, "ExternalInput")
        specs["h2T_in"] = ((D, NTOK), BF16, "ExternalInput")
        specs["onsaT"] = ((D, NTOK), BF16, "ExternalInput")
        specs["uext"] = ((512, 4, 528), F32, "ExternalInput")
        specs["corr"] = ((128, 4, 16), F32, "ExternalInput")
    if doA:
        wspec(l, "f1_")
        specs["win_a"] = ((len(WIN_STARTS), 128, 8, 128), F32, "ExternalInput")
        specs["h2T_out"] = ((D, NTOK), BF16, "ExternalOutput")
        specs["kvT_out"] = ((4, 128, NTOK), BF16, "ExternalOutput")
        specs["uT_out"] = ((512, NTOK), F32, "ExternalOutput")
        specs["vtok_out"] = ((NTOK, 256), BF16, "ExternalOutput")
        specs["xs_out"] = ((D, NTOK), F32, "ExternalOutput")
    else:
        specs["out"] = ((D, NTOK), F32, "ExternalOutput")
    return specs


def build_tok(l, mode, last):
    P = Prog(tok_specs(l, mode, last))
    tok_body(P, l, mode, last)
    return P.finish()


def build_bca(l, last):
    specs = attn_specs()
    ts = tok_specs(l, "CA", last)
    for k_ in ("h2T_in", "win_c", "onsaT", "vecs"):
        ts.pop(k_)
    specs.update(ts)
    specs["onsaT"] = ((D, NTOK), BF16, "Internal")
    P = Prog(specs, WST=None)
    P.dr["h2T_in"] = P.dr["h2T"]
    P.dr["win_c"] = P.dr["win"]
    kb = P.kb
    with ExitStack() as pes:
        kb.cur_es = pes
        P.alloc_wstage(2048)
        attn_body(P)
        kb.barrier()
    with ExitStack() as pes:
        kb.cur_es = pes
        P.alloc_wstage(3584)
        tok_body(P, l, "CA", last)
        kb.barrier()
    kb.cur_es = None
    return P.finish()


def tok_body(P, l, mode, last):
    kb, nc, dr = P.kb, P.nc, P.dr
    stq = kb.act if (MONO or QSEL) else kb.sp
    doA = (mode == "A") or (not last)
    x = kb.sb("x", [128, 8, TT], F32); x_r = Res("x")
    h = kb.sb("h", [128, 8, TT], BF16); h_r = Res("h")
    aT = kb.sb("aT", [128, NFC, TT], BF16); aT_r = [Res("aT%d" % i) for i in range(NFC)]
    sq = [kb.sb("sq%d" % i, [128, 512], F32) for i in range(2)]; sq_r = [Res() for i in range(2)]
    rstd = kb.sb("rstd", [128, TT], F32); rstd_r = Res()
    xsem = kb.new_sem("xsem")
    if QSEL:
        xstg2 = kb.sb("xstg", [128, 8 * TT], F32); xstg_r = Res("xstg")
        xstg = xstg2[:, :].rearrange("p (c n) -> p c n", n=TT)
    if mode == "CA":
        hsem = kb.new_sem("hsem"); osem = kb.new_sem("osem"); usem = kb.new_sem("usem")
        on = kb.sb("on", [128, 8, TT], BF16); on_r = Res("on")
        ue = kb.sb("ue", [128, 4, 528], F32); ue_r = Res("ue")
        sa = kb.sb("sa", [128, 528], F32); sa_r = Res("sa")
        sb_ = kb.sb("sbb", [128, 528], F32); sb_r = Res("sb")
        dl = kb.sb("dl", [128, 4, TT], BF16); dl_r = [Res() for _ in range(4)]
        opl = kb.sb("opl", [128, 4, TT], BF16); opl_r = Res("opl")
        mg = kb.sb("mg", [128, 8, TT], BF16); mg_r = [Res() for _ in range(8)]
        t1 = kb.sb("t1", [128, TT], F32); t1_r = Res()
        t2 = kb.sb("t2", [128, TT], F32); t2_r = Res()
        corr = kb.sb("corr", [128, 4, 16], F32); corr_r = Res()
        kb.dma(P.ldsem, corr[:], dr["corr"][:, :, :], writes=[corr_r])
    if doA:
        kvst = [kb.sb("kvst%d" % i, [128, TT], BF16) for i in range(2)]; kvst_r = [Res() for _ in range(2)]
        ust = [kb.sb("ust%d" % i, [128, TT], F32) for i in range(2)]; ust_r = [Res() for _ in range(2)]
        vst = kb.sb("vst", [128, 4, 256], BF16); vst_r = Res()
        osems = [kb.new_sem("kvo%d" % i) for i in range(2)]
        usems = [kb.new_sem("uo%d" % i) for i in range(2)]
        vsem = kb.new_sem("vo")
        hosem = kb.new_sem("ho")

    def colsl(ap, t0):
        return ap[:, t0:t0 + TT].rearrange("(c p) n -> p c n", p=128)

    for t in range(ntok() // TT):
        t0 = t * TT
        if QSEL:
            P.select_tile(x[:, :, :], x_r, [colsl(dr["xs_in"], (4 * t + c_) * 512) for c_ in range(4)], xstg[:, :, :], xstg_r, xsem, 0, 128)
        else:
            kb.dma(xsem, x[:, :, :], colsl(dr["xs_in"], t0), writes=[x_r], eng=stq)
        la = l
        if mode == "CA":
            vb = vbase(l)
            if QSEL:
                P.select_tile(h[:, :, :], h_r, [colsl(dr["h2T_in"], (4 * t + c_) * 512) for c_ in range(4)],
                              on[:, :, :], on_r, hsem, 0, 128)
            else:
                kb.dma(hsem, h[:, :, :], colsl(dr["h2T_in"], t0), writes=[h_r], eng=stq)
            kb.dma(osem, on[:, :, :], colsl(dr["onsaT"], t0), writes=[on_r], eng=stq)
            if QSEL:
                ustg = xstg2[:, 0:2112].rearrange("p (g n) -> p g n", n=528)
                cands = []
                for c_ in range(4):
                    a0 = (4 * t + c_) * 512
                    cands.append(dr["uT_in"][:, a0 - 16:a0 + 512].rearrange("(g p) n -> p g n", p=128) if a0 > 0 else None)
                if t == 0:
                    kb.op(kb.pool, lambda: nc.gpsimd.memset(ustg[:, :, 0:16], 0.0), writes=[xstg_r])
                    kb.dma(usem, ustg[:, :, 16:528], dr["uT_in"][:, 0:512].rearrange("(g p) n -> p g n", p=128), writes=[xstg_r])
                    kb.op(kb.dve, lambda: nc.vector.tensor_scalar(out=ue[:, :, :], in0=ustg, scalar1=P.selw[:, 0:1], scalar2=None, op0=ALU.mult),
                          reads=[xstg_r, P.selw_r], writes=[ue_r])
                    for c_ in range(1, 4):
                        kb.dma(usem, ustg, cands[c_], writes=[xstg_r])
                        kb.op(kb.dve, lambda: nc.vector.scalar_tensor_tensor(out=ue[:, :, :], in0=ustg, scalar=P.selw[:, c_:c_ + 1], in1=ue[:, :, :], op0=ALU.mult, op1=ALU.add),
                              reads=[xstg_r, P.selw_r, ue_r], writes=[ue_r])
                else:
                    P.select_tile(ue[:, :, :], ue_r, cands, ustg, xstg_r, usem, 0, 128)
            elif MONO:
                kb.dma(usem, ue[:, :, 16:528], dr["uT_in"][:, t0:t0 + 512].rearrange("(g p) n -> p g n", p=128), writes=[ue_r])
                if t == 0:
                    kb.op(kb.pool, lambda: nc.gpsimd.memset(ue[:, :, 0:16], 0.0), writes=[ue_r])
                else:
                    kb.dma(usem, ue[:, :, 0:16], dr["uT_in"][:, t0 - 16:t0].rearrange("(g p) n -> p g n", p=128), writes=[ue_r])
            else:
                kb.dma(usem, ue[:, :, :], dr["uext"][:, t, :].rearrange("(g p) n -> p g n", p=128), writes=[ue_r])
            for gi, w in enumerate(POOLW):
                cur, cur_r = None, None
                sh = 1
                src = ue[:, gi, :]
                src_r = ue_r
                bufs = [(sa, sa_r), (sb_, sb_r)]
                bi = 0
                while sh < w:
                    dst, dst_r = bufs[bi]
                    bi ^= 1
                    lo = 2 * sh - 1
                    kb.op(kb.dve, lambda src=src, dst=dst, lo=lo, sh=sh: nc.vector.tensor_tensor(
                        out=dst[:, lo:528], in0=src[:, lo:528], in1=src[:, lo - sh:528 - sh], op=ALU.add),
                        reads=[src_r], writes=[dst_r])
                    src, src_r = dst, dst_r
                    sh *= 2
                kb.op(kb.dve, lambda src=src, w=w: nc.vector.tensor_scalar(out=src[:, 16:528], in0=src[:, 16:528], scalar1=1.0 / w, scalar2=None, op0=ALU.mult),
                      reads=[src_r], writes=[src_r])
                if t == 0:
                    kb.op(kb.dve, lambda src=src, gi=gi: nc.vector.tensor_tensor(out=src[:, 16:32], in0=src[:, 16:32], in1=corr[:, gi, :], op=ALU.mult),
                          reads=[src_r, corr_r], writes=[src_r])
                kb.op(kb.dve, lambda src=src, gi=gi: nc.vector.tensor_tensor(out=dl[:, gi, :], in0=src[:, 16:528], in1=ue[:, gi, 16:528], op=ALU.subtract),
                      reads=[src_r, ue_r], writes=[dl_r[gi]])
            for gi in range(4):
                (pw,), w_r = P.load_w([dr["poolw"][gi]])
                ps, ps_r = P.bank()
                kb.op(kb.pe, lambda: nc.tensor.matmul(ps[:, :], lhsT=pw[:, 0, :], rhs=dl[:, gi, :], start=True, stop=True),
                      reads=[w_r, dl_r[gi]], writes=[ps_r])
                kb.op(kb.dve, lambda: nc.vector.tensor_scalar(out=opl[:, gi, :], in0=ps[:, :], scalar1=P.vcol(vb + 24 + gi), scalar2=None, op0=ALU.mult),
                      reads=[ps_r, P.vecs_r], writes=[opl_r])
            for dc in range(8):
                (wpa, wnb, wgp, wga), w_r = P.load_w([dr["wpa"][dc], dr["wnb"][dc],
                                                      dr["win_c"][WIN_IDX[C_GM + dc * 128]],
                                                      dr["win_c"][WIN_IDX[C_GM + 1024 + dc * 128]]])
                pa, pa_r = P.bank(); pb, pb_r = P.bank(); pgp, pgp_r = P.bank(); pga, pga_r = P.bank()
                for c in range(4):
                    kb.op(kb.pe, lambda c=c: nc.tensor.matmul(pa[:, :], lhsT=wpa[:, c, :], rhs=opl[:, c, :], start=(c == 0), stop=(c == 3)),
                          reads=[w_r, opl_r], writes=[pa_r], sig=(c == 3))
                for c in range(8):
                    kb.op(kb.pe, lambda c=c: nc.tensor.matmul(pb[:, :], lhsT=wnb[:, c, :], rhs=on[:, c, :], start=(c == 0), stop=(c == 7)),
                          reads=[w_r, on_r], writes=[pb_r], sig=(c == 7))
                for c in range(8):
                    kb.op(kb.pe, lambda c=c: nc.tensor.matmul(pgp[:, :], lhsT=wgp[:, c, :], rhs=h[:, c, :], start=(c == 0), stop=(c == 7)),
                          reads=[w_r, h_r], writes=[pgp_r], sig=(c == 7))
                for c in range(8):
                    kb.op(kb.pe, lambda c=c: nc.tensor.matmul(pga[:, :], lhsT=wga[:, c, :], rhs=h[:, c, :], start=(c == 0), stop=(c == 7)),
                          reads=[w_r, h_r], writes=[pga_r], sig=(c == 7))
                kb.op(kb.act, lambda: nc.scalar.activation(out=t1[:, :], in_=pgp[:, :], func=AF.Sigmoid), reads=[pgp_r], writes=[t1_r])
                kb.op(kb.act, lambda: nc.scalar.activation(out=t2[:, :], in_=pga[:, :], func=AF.Sigmoid), reads=[pga_r], writes=[t2_r])
                kb.op(kb.dve, lambda: nc.vector.tensor_tensor(out=t1[:, :], in0=t1[:, :], in1=pa[:, :], op=ALU.mult), reads=[t1_r, pa_r], writes=[t1_r])
                kb.op(kb.dve, lambda: nc.vector.tensor_tensor(out=t2[:, :], in0=t2[:, :], in1=pb[:, :], op=ALU.mult), reads=[t2_r, pb_r], writes=[t2_r])
                kb.op(kb.dve, lambda: nc.vector.tensor_tensor(out=mg[:, dc, :], in0=t1[:, :], in1=t2[:, :], op=ALU.add), reads=[t1_r, t2_r], writes=[mg_r[dc]])
            for dc in range(8):
                (wo,), w_r = P.load_w([dr["wo"][dc]])
                pz, pz_r = P.bank()
                for c in range(8):
                    kb.op(kb.pe, lambda c=c: nc.tensor.matmul(pz[:, :], lhsT=wo[:, c, :], rhs=mg[:, c, :], start=(c == 0), stop=(c == 7)),
                          reads=[w_r, mg_r[c]], writes=[pz_r], sig=(c == 7))
                kb.op(kb.dve, lambda: nc.vector.tensor_tensor(out=x[:, dc, :], in0=x[:, dc, :], in1=pz[:, :], op=ALU.add), reads=[x_r, pz_r], writes=[x_r])
            P.rmsnorm(x, x_r, h, h_r, TT, vb + 16, sq, sq_r, rstd, rstd_r)
            P.ffn(x, x_r, h, h_r, TT, dr["f2_wg"], dr["f2_wu"], dr["f2_wd"], aT, aT_r, sq, sq_r)
            la = l + 1
            if last:
                P.rmsnorm(x, x_r, None, None, TT, 56, sq, sq_r, rstd, rstd_r, out_f32=x, out_r=x_r)
                kb.dma(P.stsem, colsl(dr["out"], t0), x[:, :, :], reads=[x_r], eng=stq)
                continue
        vb = vbase(la)
        P.rmsnorm(x, x_r, h, h_r, TT, vb + 0, sq, sq_r, rstd, rstd_r)
        P.ffn(x, x_r, h, h_r, TT, dr["f1_wg"], dr["f1_wu"], dr["f1_wd"], aT, aT_r, sq, sq_r)
        kb.dma(P.stsem, colsl(dr["xs_out"], t0), x[:, :, :], reads=[x_r], eng=stq)
        P.rmsnorm(x, x_r, h, h_r, TT, vb + 8, sq, sq_r, rstd, rstd_r)
        kb.dma(hosem, colsl(dr["h2T_out"], t0), h[:, :, :], reads=[h_r], eng=stq)
        W = dr["win_a"]
        for j, c0 in enumerate((C_KC, C_VC, C_KSL, C_KWN)):
            (wc,), w_r = P.load_w([W[WIN_IDX[c0]]])
            ps, ps_r = P.bank()
            for c in range(8):
                kb.op(kb.pe, lambda c=c: nc.tensor.matmul(ps[:, :], lhsT=wc[:, c, :], rhs=h[:, c, :], start=(c == 0), stop=(c == 7)),
                      reads=[w_r, h_r], writes=[ps_r], sig=(c == 7))
            k = j % 2
            kb.op(kb.act, lambda k=k: nc.scalar.copy(out=kvst[k][:, :], in_=ps[:, :]), reads=[ps_r], writes=[kvst_r[k]])
            kb.dma(osems[k], dr["kvT_out"][j, :, t0:t0 + TT], kvst[k][:, :], reads=[kvst_r[k]], eng=stq)
        for j in range(4):
            (wc,), w_r = P.load_w([W[WIN_IDX[C_U + j * 128]]])
            ps, ps_r = P.bank()
            for c in range(8):
                kb.op(kb.pe, lambda c=c: nc.tensor.matmul(ps[:, :], lhsT=wc[:, c, :], rhs=h[:, c, :], start=(c == 0), stop=(c == 7)),
                      reads=[w_r, h_r], writes=[ps_r], sig=(c == 7))
            k = j % 2
            kb.op(kb.act, lambda k=k: nc.scalar.copy(out=ust[k][:, :], in_=ps[:, :]), reads=[ps_r], writes=[ust_r[k]])
            kb.dma(usems[k], dr["uT_out"][j * 128:(j + 1) * 128, t0:t0 + TT], ust[k][:, :], reads=[ust_r[k]], eng=stq)
        (wv1, wv2), w_r = P.load_w([W[WIN_IDX[C_VSL]], W[WIN_IDX[C_VWN]]])
        for tb in range(TT // 128):
            ps, ps_r = P.bank()
            for wi, wv in enumerate((wv1, wv2)):
                for c in range(8):
                    kb.op(kb.pe, lambda c=c, wv=wv, wi=wi: nc.tensor.matmul(ps[:, wi * 128:(wi + 1) * 128], lhsT=h[:, c, tb * 128:(tb + 1) * 128], rhs=wv[:, c, :],
                                                                         start=(c == 0), stop=(c == 7)),
                          reads=[w_r, h_r], writes=[ps_r], sig=(c == 7))
            kb.op(kb.act, lambda: nc.scalar.copy(out=vst[:, tb, :], in_=ps[:, 0:256]), reads=[ps_r], writes=[vst_r])
        kb.dma(vsem, dr["vtok_out"][t0:t0 + TT, :].rearrange("(tb p) c -> p tb c", p=128), vst[:, :, :], reads=[vst_r], eng=stq)


def attn_specs():
    BI = "ExternalInput"
    specs = {
        "vecs": ((128, 64), F32, BI),
        "h2T": ((D, NTOK), BF16, BI),
        "kvall": ((4, 128, SEQ + 32), BF16, BI),
        "vall": ((SEQ, 256), BF16, BI),
        "kwin": ((128, 4, 1024), BF16, BI),
        "vwin": ((4, 1024, 128), BF16, BI),
        "win": ((len(WIN_STARTS), 128, 8, 128), F32, BI), "win_gn": ((D, 48), F32, BI),
        "ck_w1": ((2, 128, 16, 128), F32, BI), "ck_w2": ((256, 64), F32, BI),
        "cv_w1": ((2, 128, 16, 128), F32, BI), "cv_w2": ((256, 64), F32, BI),
        "pecol": ((128, 16), F32, BI),
        "qal": ((3, 16, NTOK), BF16, BI),
        "kbs": ((128, 16 * 64), F32, BI), "kbc": ((128, 64), F32, BI), "kbw": ((128, 4 * 8 * 16), F32, BI),
        "cmpm": ((128, 2, 512), BF16, BI), "cm": ((128, 16, 512), BF16, BI), "wm": ((128, 8, 512), BF16, BI),
        "addm": ((128, 16, 128), BF16, BI), "vneg": ((128, 16, 128), BF16, BI),
        "karows": ((64, SEQ), BF16, BI), "ones3": ((3, 1024), BF16, BI), "ovl": ((128, 4, 128), BF16, BI), "identb": ((128, 128), BF16, BI),
        "onsaT": ((D, NTOK), BF16, "ExternalOutput"),
    }
    return specs


def build_attn():
    P = Prog(attn_specs(), WST=2048)
    attn_body(P)
    return P.finish()


def attn_body(P):
    kb, nc, dr = P.kb, P.nc, P.dr
    ld = P.ldsem

    def const(name, shape, dt, src):
        t = kb.sb(name, shape, dt)
        r = Res(name)
        kb.dma(ld, t[:], src, writes=[r])
        return t, r

    CM, CM_r = const("CM", [128, 4 if MONO else 16, 512], BF16, dr["cm"][:, :, :])
    WM, WM_r = const("WM", [128, 8, 512], BF16, dr["wm"][:, :, :])
    CPM, CPM_r = const("CPM", [128, 5 if MONO else 2, 512], BF16, dr["cmpm"][:, :, :])
    if MONO:
        ADM = kb.sb("ADM", [128, 4, 128], BF16); ADM_r = Res("ADM")
        VNG = kb.sb("VNG", [128, 4, 128], BF16); VNG_r = Res("VNG")
        KBW = kb.sb("KBW", [128, 128], F32); KBW_r = Res("KBW")
        pisem = kb.new_sem("pisem")
    else:
        ADM, ADM_r = const("ADM", [128, 16, 128], BF16, dr["addm"][:, :, :])
        VNG, VNG_r = const("VNG", [128, 16, 128], BF16, dr["vneg"][:, :, :])
        KBW, KBW_r = const("KBW", [128, 512], F32, dr["kbw"][:, :])
    KBS, KBS_r = const("KBS", [128, 1024], F32, dr["kbs"][:, :])
    KBC, KBC_r = const("KBC", [128, 64], F32, dr["kbc"][:, :])
    IDB, IDB_r = const("IDB", [128, 128], BF16, dr["identb"][:, :])
    PEC, PEC_r = const("PEC", [128, 16], F32, dr["pecol"][:, :])

    KA = kb.sb("KA", [128, SEQ], BF16); KA_r = Res("KA")
    VA = kb.sb("VA", [128, 64, 65], BF16); VA_r = Res("VA")
    KW = kb.sb("KW", [128, 1024], BF16); KW_r = Res("KW")
    VW = kb.sb("VW", [128, 8, 65], BF16); VW_r = Res("VW")
    KC = kb.sb("KC", [128, 2, 512], BF16); KC_r = Res("KC")
    VC = kb.sb("VC", [128, 4, 2, 193], BF16); VC_r = Res("VC")
    kb.op(kb.pool, lambda: nc.gpsimd.memset(KW[0:64, :], 0.0), writes=[KW_r])
    kb.op(kb.pool, lambda: nc.gpsimd.memset(VW[:, :, 0:64], 0.0), writes=[VW_r])
    kb.dma(ld, KA[64:128, :], dr["karows"][:, :], writes=[KA_r])
    kb.op(kb.pool, lambda: nc.gpsimd.memset(KW[64:128, :], 0.0), writes=[KW_r])
    kb.op(kb.pool, lambda: nc.gpsimd.memset(KC[64:128, :, :], 0.0), writes=[KC_r])
    kb.dma(ld, KW[124:127, :], dr["ones3"][:, :], writes=[KW_r])
    kb.dma(ld, KC[124:127, :, :], dr["ones3"][:, :].rearrange("r (g n) -> r g n", n=512), writes=[KC_r])
    kb.op(kb.pool, lambda: nc.gpsimd.memset(VA[:, :, 64:65], 1.0), writes=[VA_r])
    kb.op(kb.pool, lambda: nc.gpsimd.memset(VW[:, :, 64:65], 1.0), writes=[VW_r])
    kb.op(kb.pool, lambda: nc.gpsimd.memset(VC[:, :, :, 64:65], 1.0), writes=[VC_r])
    for g in range(2):
        kb.dma(ld, VC[:, :, g, 65:193], dr["ovl"][:, :, :], writes=[VC_r])

    ces = ExitStack()
    KC2 = kb.sb("KC2", [128, SEQ + 16], BF16, es=ces); KC2_r = Res("KC2")
    zb = kb.sb("zb", [128, 512], F32, es=ces); zb_r = Res()
    s2 = kb.sb("s2", [128, 512], F32, es=ces); s2_r = Res()
    hid = kb.sb("hid", [128, 2, 512], BF16, es=ces); hid_r = [Res(), Res()]
    pecb = kb.sb("pecb", [128, 16], BF16, es=ces); pecb_r = Res()
    bj = kb.sb("bj", [128, 1], F32, es=ces); bj_r = Res()
    kb.op(kb.dve, lambda: nc.vector.tensor_copy(out=pecb[:, :], in_=PEC[:, :]), reads=[PEC_r], writes=[pecb_r])
    c2sem = kb.new_sem("c2sem")
    if MONO or QSEL:
        kb.op(kb.pool, lambda: nc.gpsimd.memset(KC2[0:64, SEQ:SEQ + 16], 0.0), writes=[KC2_r])
        kb.op(kb.pool, lambda: nc.gpsimd.memset(KC2[64:128, SEQ - 1:SEQ + 16], 0.0), writes=[KC2_r])
    for kv in range(2):
        w1 = dr["ck_w1" if kv == 0 else "cv_w1"]
        w2 = dr["ck_w2" if kv == 0 else "cv_w2"]
        for g in range(2):
            if MONO or QSEL:
                kb.dma(c2sem, KC2[0:64, 0:SEQ], dr["kvall"][kv, g * 64:(g + 1) * 64, 0:SEQ], writes=[KC2_r])
                kb.dma(c2sem, KC2[64:128, 0:SEQ - 1], dr["kvall"][kv, g * 64:(g + 1) * 64, 1:SEQ], writes=[KC2_r])
            else:
                kb.dma(c2sem, KC2[0:64, :], dr["kvall"][kv, g * 64:(g + 1) * 64, 0:SEQ + 16], writes=[KC2_r])
                kb.dma(c2sem, KC2[64:128, :], dr["kvall"][kv, g * 64:(g + 1) * 64, 1:SEQ + 17], writes=[KC2_r])
            for jc in range(2):
                (w1v,), w_r = P.load_w([w1[jc]])
                pb_, pb_r = P.bank()
                for lp in range(16):
                    kb.op(kb.pe, lambda lp=lp: nc.tensor.matmul(pb_[:, 0:1], lhsT=w1v[:, lp, :], rhs=pecb[:, lp:lp + 1], start=(lp == 0), stop=(lp == 15)),
                          reads=[w_r, pecb_r], writes=[pb_r], sig=(lp == 15))
                kb.op(kb.act, lambda: nc.scalar.copy(out=bj[:, :], in_=pb_[:, 0:1]), reads=[pb_r], writes=[bj_r])
                ps, ps_r = P.bank()
                for lp in range(16):
                    kb.op(kb.pe, lambda lp=lp: nc.tensor.matmul(ps[:, :], lhsT=w1v[:, lp, :], rhs=KC2[:, 2 * lp:2 * lp + 16 * 511 + 1:16], start=(lp == 0), stop=(lp == 15)),
                          reads=[w_r, KC2_r], writes=[ps_r], sig=(lp == 15))
                kb.op(kb.act, lambda: nc.scalar.activation(out=zb[:, :], in_=ps[:, :], func=AF.Identity, bias=bj[:, 0:1]), reads=[ps_r, bj_r], writes=[zb_r])
                kb.op(kb.act, lambda: nc.scalar.activation(out=s2[:, :], in_=zb[:, :], func=AF.Square), reads=[zb_r], writes=[s2_r])
                kb.op(kb.dve, lambda: nc.vector.tensor_scalar(out=s2[:, :], in0=s2[:, :], scalar1=0.044715, scalar2=1.0, op0=ALU.mult, op1=ALU.add), reads=[s2_r], writes=[s2_r])
                kb.op(kb.dve, lambda: nc.vector.tensor_tensor(out=s2[:, :], in0=s2[:, :], in1=zb[:, :], op=ALU.mult), reads=[s2_r, zb_r], writes=[s2_r])
                kb.op(kb.act, lambda: nc.scalar.activation(out=s2[:, :], in_=s2[:, :], func=AF.Sigmoid, scale=1.5957691216057308), reads=[s2_r], writes=[s2_r])
                kb.op(kb.dve, lambda jc=jc: nc.vector.tensor_tensor(out=hid[:, jc, :], in0=s2[:, :], in1=zb[:, :], op=ALU.mult), reads=[s2_r, zb_r], writes=[hid_r[jc]])
            (w2v,), w_r = P.load_w([w2[:, :]])
            if kv == 0:
                ps, ps_r = P.bank()
                for jc in range(2):
                    kb.op(kb.pe, lambda jc=jc: nc.tensor.matmul(ps[0:64, :], lhsT=w2v[:, jc, :], rhs=hid[:, jc, :], start=(jc == 0), stop=(jc == 1)),
                          reads=[w_r, hid_r[jc]], writes=[ps_r], sig=(jc == 1))
                kb.op(kb.act, lambda g=g: nc.scalar.copy(out=KC[0:64, g, :], in_=ps[0:64, :]), reads=[ps_r], writes=[KC_r])
            else:
                for nt in range(4):
                    ps, ps_r = P.bank()
                    for jc in range(2):
                        kb.op(kb.pe, lambda jc=jc, nt=nt: nc.tensor.matmul(ps[:, 0:64], lhsT=hid[:, jc, nt * 128:(nt + 1) * 128], rhs=w2v[:, jc, :], start=(jc == 0), stop=(jc == 1)),
                              reads=[w_r, hid_r[jc]], writes=[ps_r], sig=(jc == 1))
                    kb.op(kb.act, lambda g=g, nt=nt: nc.scalar.copy(out=VC[:, nt, g, 0:64], in_=ps[:, 0:64]), reads=[ps_r], writes=[VC_r])
    kb.barrier()
    ces.close()

    Q = kb.sb("Q", [128, 3, 8, 512], BF16); Q_r = [Res("Q%d" % i) for i in range(8)]
    kb.op(kb.pool, lambda: nc.gpsimd.memset(Q[64:128, :, :, :], 0.0), writes=Q_r)
    hT = kb.sb("hT", [128, 8, 512], BF16); hT_r = Res("hT")
    if QSEL:
        qstg = kb.sb("qstg", [128, 8, 512], BF16); qstg_r = Res("qstg")
    gat = kb.sb("gat", [128, 4, 48], F32); gat_r = Res("gat")
    cst = kb.sb("cst", [128, 4, 8, 64], F32); cst_r = [Res() for _ in range(8)]
    crd = kb.sb("crd", [128, 4, 8], F32); crd_r = [Res() for _ in range(8)]
    sc = kb.sb("sc", [128, 4, 128], F32); sc_r = Res("sc")
    pT = [kb.sb("pT%d" % i, [128, 512], BF16) for i in range(4)]; pT_r = [Res() for _ in range(4)]
    tmp = [kb.sb("tmp%d" % i, [128, 512], F32) for i in range(2)]; tmp_r = [Res() for _ in range(2)]
    onb = kb.sb("onb", [128, 4, 1024], BF16); onb_r = Res("onb")
    ost = [kb.sb("ost%d" % i, [128, 512], BF16) for i in range(2)]; ost_r = [Res() for _ in range(2)]
    smb = kb.sb("smb", [128, 4, 128], F32); smb_r = [Res() for _ in range(4)]
    wa = kb.sb("wa", [128, 4, 128], F32); wa_r = [Res() for _ in range(4)]
    wb = kb.sb("wb", [128, 4, 128], F32); wb_r = [Res() for _ in range(4)]
    m8 = kb.sb("m8", [128, 4, 8], F32); m8_r = [Res() for _ in range(4)]
    m8b = kb.sb("m8b", [128, 4, 8], F32); m8b_r = [Res() for _ in range(4)]
    nqv = kb.sb("nqv", [128, 4, 3, 128], BF16); nq_r = Res()
    kb.op(kb.pool, lambda: nc.gpsimd.memset(nqv[:, :, :, :], 0.0), writes=[nq_r])
    dd = kb.sb("dd", [128, 2, 4], F32); dd_r = Res()
    cf = kb.sb("cf", [128, 3, 4], F32); cf_r = Res()
    oh = kb.sb("oh", [128, 64], F32); oh_r = Res()
    hsem = kb.new_sem("hsem"); ksem = kb.new_sem("ksem"); vsem = kb.new_sem("vsem")
    kwsem = kb.new_sem("kwsem"); vwsem = kb.new_sem("vwsem"); qsem = kb.new_sem("qsem")
    osems = [kb.new_sem("os%d" % i) for i in range(2)]
    W = dr["win"]
    st = {"s": 0, "p": 0, "t": 0}

    def sbank():
        k = st["s"] % 3
        st["s"] += 1
        return P.psf[k], P.psf_r[k]

    def pbuf():
        k = st["p"] % 4
        st["p"] += 1
        return pT[k], pT_r[k]

    def tbuf():
        k = st["t"] % 2
        st["t"] += 1
        return tmp[k], tmp_r[k]

    def exp_tile(ps, ps_r, bias_ap, bias_r, mask_ap, mask_r, qa=0, qb=512):
        p, p_r = pbuf()
        if mask_ap is not None:
            tm, tm_r = tbuf()
            kb.op(kb.dve, lambda: nc.vector.tensor_tensor(out=tm[:, qa:qb], in0=ps[:, qa:qb], in1=mask_ap[:, qa:qb], op=ALU.add), reads=[ps_r, mask_r], writes=[tm_r])
            kb.op(kb.act, lambda: nc.scalar.activation(out=p[:, qa:qb], in_=tm[:, qa:qb], func=AF.Exp, bias=bias_ap), reads=[tm_r, bias_r], writes=[p_r])
        else:
            kb.op(kb.act, lambda: nc.scalar.activation(out=p[:, qa:qb], in_=ps[:, qa:qb], func=AF.Exp, bias=bias_ap), reads=[ps_r, bias_r], writes=[p_r])
        return p, p_r

    NI = ntok() // 512
    KT0 = 4 if MONO else 16
    for g in range(2):
        kb.dma(ksem, KA[0:64, :], dr["kvall"][2, g * 64:(g + 1) * 64, 0:SEQ], writes=[KA_r])
        for kq in range(16):
            kb.dma(vsem, VA[:, kq * 4:(kq + 1) * 4, 0:64],
                   dr["vall"][kq * 512:(kq + 1) * 512, g * 64:(g + 1) * 64].rearrange("(kt p) d -> p kt d", p=128), writes=[VA_r])
        for i in range(NI):
            q0 = i * 512
            if MONO:
                kb.dma(pisem, ADM[:, :, :], dr["addm"][i], writes=[ADM_r])
                kb.dma(pisem, VNG[:, :, :], dr["vneg"][i], writes=[VNG_r])
                kb.dma(pisem, KBW[:, :], dr["kbw"][i], writes=[KBW_r])
                cmp_tiles = [(nt, (i - 4 * nt) if (i - 4 * nt) <= 4 else None) for nt in range((32 * (i + 1) - 2) // 128 + 1)]
            else:
                cmp_tiles = [(nt, (nt - i + 1) if nt - i >= -1 else None) for nt in range(i + 1)]
            if QSEL:
                P.select_tile(hT[:, :, :], hT_r, [dr["h2T"][:, (4 * i + c_) * 512:(4 * i + c_ + 1) * 512].rearrange("(c p) n -> p c n", p=128) for c_ in range(4)],
                              qstg[:, :, :], qstg_r, hsem, 0, 128)
            else:
                kb.dma(hsem, hT[:, :, :], dr["h2T"][:, q0:q0 + 512].rearrange("(c p) n -> p c n", p=128), writes=[hT_r])
            (wgn,), w_r = P.load_w([dr["win_gn"][:, :]])
            for qs in range(4):
                ps, ps_r = sbank()
                for c in range(8):
                    kb.op(kb.pe, lambda c=c: nc.tensor.matmul(ps[:, 0:48], lhsT=hT[:, c, qs * 128:(qs + 1) * 128], rhs=wgn[:, c, :], start=(c == 0), stop=(c == 7)),
                          reads=[w_r, hT_r], writes=[ps_r], sig=(c == 7))
                kb.op(kb.act, lambda: nc.scalar.activation(out=gat[:, qs, :], in_=ps[:, 0:48], func=AF.Sigmoid), reads=[ps_r], writes=[gat_r])
            for hp in range(4):
                (wq,), w_r = P.load_w([W[g * 4 + hp]])
                for hh in range(2):
                    hl = hp * 2 + hh
                    ps, ps_r = sbank()
                    for c in range(8):
                        kb.op(kb.pe, lambda c=c: nc.tensor.matmul(ps[0:64, :], lhsT=wq[:, c, hh * 64:(hh + 1) * 64], rhs=hT[:, c, :], start=(c == 0), stop=(c == 7)),
                              reads=[w_r, hT_r], writes=[ps_r], sig=(c == 7))
                    kb.op(kb.dve, lambda: nc.vector.tensor_scalar(out=Q[0:64, 0, hl, :], in0=ps[0:64, :], scalar1=0.125, scalar2=None, op0=ALU.mult), reads=[ps_r], writes=[Q_r[hl]])
                    kb.op(kb.pool, lambda: nc.gpsimd.tensor_copy(out=Q[0:64, 1:3, hl, :], in_=Q[0:64, 0:1, hl, :].to_broadcast([64, 2, 512])),
                          reads=[Q_r[hl]], writes=[Q_r[hl]])
            if MONO:
                klo = max(q0 - 512, 0)
                kb.dma(kwsem, KW[0:64, 1024 - (q0 + 512 - klo):1024], dr["kvall"][3, g * 64:(g + 1) * 64, klo:q0 + 512], writes=[KW_r])
                w0 = 8 - (q0 + 512 - klo) // 128
                kb.dma(vwsem, VW[:, w0:8, 0:64], dr["vall"][klo:q0 + 512, 128 + g * 64:128 + (g + 1) * 64].rearrange("(w p) d -> p w d", p=128), writes=[VW_r])
            elif QSEL:
                for half in range(2):
                    sts = [4 * i + c_ - 1 + half for c_ in range(4)]
                    P.select_tile(KW[0:64, half * 512:(half + 1) * 512], KW_r,
                                  [dr["kvall"][3, g * 64:(g + 1) * 64, st_ * 512:(st_ + 1) * 512] if st_ >= 0 else None for st_ in sts],
                                  qstg[0:64, 0, :], qstg_r, kwsem, 0, 64)
                    P.select_tile(VW[:, half * 4:(half + 1) * 4, 0:64], VW_r,
                                  [dr["vall"][st_ * 512:(st_ + 1) * 512, 128 + g * 64:128 + (g + 1) * 64].rearrange("(w p) d -> p w d", p=128) if st_ >= 0 else None for st_ in sts],
                                  qstg[:, 1, 0:256].rearrange("p (w d) -> p w d", d=64), qstg_r, vwsem, 0, 128)
            else:
                kb.dma(kwsem, KW[0:64, :], dr["kwin"][g * 64:(g + 1) * 64, i, :], writes=[KW_r])
                kb.dma(vwsem, VW[:, :, 0:64], dr["vwin"][i, :, g * 64:(g + 1) * 64].rearrange("(w p) d -> p w d", p=128), writes=[VW_r])
            for v_ in range(3):
                kb.dma(qsem, Q[124:127, v_, :, :], dr["qal"][:, g * 8:(g + 1) * 8, q0:q0 + 512], writes=Q_r)
            for hl in range(8):
                h = g * 8 + hl
                accs = [(P.psf[3 + 2 * (hl % 2)], P.psf_r[3 + 2 * (hl % 2)]), (P.psf[4 + 2 * (hl % 2)], P.psf_r[4 + 2 * (hl % 2)])]
                for a, a_r in accs:
                    kb.op(kb.dve, lambda: nc.vector.memset(a[:, 0:386], 0.0), writes=[a_r])
                cq_ = []
                for cu in list(cmp_tiles) + [None, None]:
                    if cu is not None:
                        nt, mi_ = cu
                        ps, ps_r = sbank()
                        kb.op(kb.pe, lambda: nc.tensor.matmul(ps[:, :], lhsT=KC[0:128, g, nt * 128:(nt + 1) * 128], rhs=Q[0:128, 0, hl, :], start=True, stop=True),
                              reads=[KC_r, Q_r[hl]], writes=[ps_r])
                        e, e_r = exp_tile(ps, ps_r, KBC[:, h * 4 + nt:h * 4 + nt + 1], KBC_r,
                                          CPM[:, mi_, :] if mi_ is not None else None, CPM_r)
                        cq_.append((nt, e, e_r))
                    if cq_ and (len(cq_) > 2 or cu is None):
                        nt_, e, e_r = cq_.pop(0)
                        for qs in range(4):
                            a, a_r = accs[qs // 2]
                            o0 = (qs % 2) * 193
                            kb.op(kb.pe, lambda: nc.tensor.matmul(a[:, o0:o0 + 193], lhsT=e[:, qs * 128:(qs + 1) * 128], rhs=VC[:, nt_, g, :], start=False, stop=(nt_ == cmp_tiles[-1][0]), skip_group_check=True),
                                  reads=[e_r, VC_r], writes=[a_r], sig=(qs == 3 or qs == 1))
                assert not cq_
                for half in range(2):
                    a, a_r = accs[half]
                    av = a[:, 0:386].rearrange("p (q c) -> p q c", c=193)
                    kb.op(kb.dve, lambda: nc.vector.tensor_scalar(out=crd[:, 2 * half:2 * half + 2, hl], in0=av[:, :, 64], scalar1=1e-30, scalar2=None, op0=ALU.max),
                          reads=[a_r], writes=[crd_r[hl]])
                    kb.op(kb.dve, lambda: nc.vector.tensor_copy(out=cst[:, 2 * half:2 * half + 2, hl, :], in_=av[:, :, 0:64]), reads=[a_r], writes=[cst_r[hl]])
                kb.op(kb.dve, lambda: nc.vector.reciprocal(out=crd[:, :, hl], in_=crd[:, :, hl]), reads=[crd_r[hl]], writes=[crd_r[hl]])
                for qs in range(4):
                    a, a_r = accs[qs // 2]
                    o0 = (qs % 2) * 193
                    if hl == 0:
                        kb.op(kb.dve, lambda: nc.vector.tensor_scalar(out=sc[:, qs, :], in0=a[:, o0 + 65:o0 + 193], scalar1=crd[:, qs, hl:hl + 1], scalar2=None, op0=ALU.mult),
                              reads=[a_r, crd_r[hl]], writes=[sc_r])
                    else:
                        kb.op(kb.dve, lambda: nc.vector.scalar_tensor_tensor(out=sc[:, qs, :], in0=a[:, o0 + 65:o0 + 193], scalar=crd[:, qs, hl:hl + 1], in1=sc[:, qs, :],
                                                                              op0=ALU.mult, op1=ALU.add), reads=[a_r, crd_r[hl], sc_r], writes=[sc_r])
            c0_ = 0 if MONO else i * 4
            kb.op(kb.dve, lambda: nc.vector.tensor_tensor(out=smb[:, :, :], in0=sc[:, :, :], in1=ADM[:, c0_:c0_ + 4, :], op=ALU.add), reads=[sc_r, ADM_r], writes=smb_r)
            for qs in range(4):
                kb.op(kb.dve, lambda: nc.vector.max(out=m8[:, qs, :], in_=smb[:, qs, :]), reads=[smb_r[qs]], writes=[m8_r[qs]])
            for qs in range(4):
                kb.op(kb.dve, lambda: nc.vector.match_replace(out=wa[:, qs, :], in_to_replace=m8[:, qs, :], in_values=smb[:, qs, :], imm_value=-1e30),
                      reads=[smb_r[qs], m8_r[qs]], writes=[wa_r[qs]])
            for qs in range(4):
                kb.op(kb.dve, lambda: nc.vector.max(out=m8b[:, qs, :], in_=wa[:, qs, :]), reads=[wa_r[qs]], writes=[m8b_r[qs]])
            for qs in range(4):
                kb.op(kb.dve, lambda: nc.vector.match_replace(out=wb[:, qs, :], in_to_replace=m8b[:, qs, :], in_values=wa[:, qs, :], imm_value=-1e30),
                      reads=[wa_r[qs], m8b_r[qs]], writes=[wb_r[qs]])
            kb.op(kb.dve, lambda: nc.vector.tensor_tensor(out=wa[:, :, :], in0=smb[:, :, :], in1=wb[:, :, :], op=ALU.subtract), reads=smb_r + wb_r, writes=wa_r)
            kb.op(kb.dve, lambda: nc.vector.tensor_scalar(out=wa[:, :, :], in0=wa[:, :, :], scalar1=1.0, scalar2=-NEG, op0=ALU.min, op1=ALU.mult), reads=wa_r, writes=wa_r)
            for v_ in range(3):
                nb_ = 60 if v_ < 2 else 8
                kb.op(kb.dve, lambda: nc.vector.scalar_tensor_tensor(out=nqv[:, :, v_, 64:64 + nb_], in0=wa[:, :, 60 * v_:60 * v_ + nb_], scalar=NEG,
                                                                      in1=VNG[:, c0_:c0_ + 4, 60 * v_:60 * v_ + nb_], op0=ALU.add, op1=ALU.min),
                      reads=wa_r + [VNG_r], writes=[nq_r])
            for qs in range(4):
                for v_ in range(3):
                    kb.op(kb.pe, lambda: nc.tensor.transpose(out=P.psb[:, v_ * 128:(v_ + 1) * 128], in_=nqv[:, qs, v_, :], identity=IDB[:, :]),
                          reads=[nq_r, IDB_r], writes=[P.psb_r])
                for v_ in range(3):
                    src_ = P.psb[64:124, v_ * 128:(v_ + 1) * 128].rearrange("p (o n) -> p o n", o=1).to_broadcast([60, 8, 128])
                    kb.op(kb.dve, lambda: nc.vector.tensor_copy(out=Q[64:124, v_, :, qs * 128:(qs + 1) * 128], in_=src_),
                          reads=[P.psb_r], writes=Q_r)
            for hl in range(8):
                h = g * 8 + hl
                aS, aS_r = P.psf[3 + 2 * (hl % 2)], P.psf_r[3 + 2 * (hl % 2)]
                aW, aW_r = P.psf[4 + 2 * (hl % 2)], P.psf_r[4 + 2 * (hl % 2)]
                nkt = KT0 * (i + 1)
                if hl == 0:
                    kb.op(kb.dve, lambda: nc.vector.memset(aS[:, 0:260], 0.0), writes=[aS_r])
                    kb.op(kb.dve, lambda: nc.vector.memset(aW[:, 0:260], 0.0), writes=[aW_r])
                units = [("s", kt) for kt in range(nkt)] + [("w", w) for w in range(8)]
                SKEW = 2
                pendq = []
                for u in units + [None] * SKEW:
                    cur = None
                    if u is not None:
                        kind, ix = u
                        ps, ps_r = sbank()
                        qa, qb = 0, 512
                        if kind == "s":
                            r_ = ix - KT0 * i
                            if MONO and r_ >= 0:
                                qa = 128 * r_
                            kb.op(kb.pe, lambda: nc.tensor.matmul(ps[:, qa:qb], lhsT=KA[0:128, ix * 128:(ix + 1) * 128], rhs=Q[0:128, (2 * ix) // 60, hl, qa:qb], start=True, stop=True),
                                  reads=[KA_r, Q_r[hl]], writes=[ps_r])
                            p, p_r = exp_tile(ps, ps_r, KBS[:, h * 64 + ix:h * 64 + ix + 1], KBS_r, CM[:, r_, :] if r_ >= 0 else None, CM_r, qa, qb)
                        else:
                            if ix < 4:
                                qb = 128 * (ix + 1)
                            else:
                                qa = 128 * (ix - 4)
                            kb.op(kb.pe, lambda: nc.tensor.matmul(ps[:, qa:qb], lhsT=KW[0:128, ix * 128:(ix + 1) * 128], rhs=Q[0:128, 0, hl, qa:qb], start=True, stop=True),
                                  reads=[KW_r, Q_r[hl]], writes=[ps_r])
                            cb = (ix * 16 + h) if MONO else ((i * 8 + ix) * 16 + h)
                            p, p_r = exp_tile(ps, ps_r, KBW[:, cb:cb + 1], KBW_r, WM[:, ix, :], WM_r, qa, qb)
                        cur = (kind, ix, p, p_r, qa, qb)
                        pendq.append(cur)
                    if pendq and (len(pendq) > SKEW or u is None):
                        kind_, ix_, pp, pp_r, qa_, qb_ = pendq.pop(0)
                        qss = list(range(qa_ // 128, qb_ // 128))
                        for qs in qss:
                            if kind_ == "s":
                                kb.op(kb.pe, lambda: nc.tensor.matmul(aS[:, qs * 65:(qs + 1) * 65], lhsT=pp[:, qs * 128:(qs + 1) * 128], rhs=VA[:, ix_, :], start=False, stop=(ix_ == nkt - 1), skip_group_check=True),
                                      reads=[pp_r, VA_r], writes=[aS_r], sig=(qs == qss[-1]))
                            else:
                                kb.op(kb.pe, lambda: nc.tensor.matmul(aW[:, qs * 65:(qs + 1) * 65], lhsT=pp[:, qs * 128:(qs + 1) * 128], rhs=VW[:, ix_, :], start=False, stop=(ix_ == 7), skip_group_check=True),
                                      reads=[pp_r, VW_r], writes=[aW_r], sig=(qs == qss[-1]))
                assert not pendq
                if hl < 7:
                    nS, nS_r = P.psf[3 + 2 * ((hl + 1) % 2)], P.psf_r[3 + 2 * ((hl + 1) % 2)]
                    nW, nW_r = P.psf[4 + 2 * ((hl + 1) % 2)], P.psf_r[4 + 2 * ((hl + 1) % 2)]
                    kb.op(kb.dve, lambda: nc.vector.memset(nS[:, 0:260], 0.0), writes=[nS_r])
                    kb.op(kb.dve, lambda: nc.vector.memset(nW[:, 0:260], 0.0), writes=[nW_r])
                aSv = aS[:, 0:260].rearrange("p (q c) -> p q c", c=65)
                aWv = aW[:, 0:260].rearrange("p (q c) -> p q c", c=65)
                kb.op(kb.dve, lambda: nc.vector.tensor_scalar(out=dd[:, 0, :], in0=aSv[:, :, 64], scalar1=1e-30, scalar2=None, op0=ALU.max), reads=[aS_r], writes=[dd_r])
                kb.op(kb.dve, lambda: nc.vector.tensor_scalar(out=dd[:, 1, :], in0=aWv[:, :, 64], scalar1=1e-30, scalar2=None, op0=ALU.max), reads=[aW_r], writes=[dd_r])
                kb.op(kb.dve, lambda: nc.vector.reciprocal(out=dd[:, :, :], in_=dd[:, :, :]), reads=[dd_r], writes=[dd_r])
                kb.op(kb.dve, lambda: nc.vector.tensor_tensor(out=cf[:, 0, :], in0=crd[:, :, hl], in1=gat[:, :, h * 3 + 0], op=ALU.mult), reads=[crd_r[hl], gat_r], writes=[cf_r])
                kb.op(kb.dve, lambda: nc.vector.tensor_tensor(out=cf[:, 1, :], in0=dd[:, 0, :], in1=gat[:, :, h * 3 + 1], op=ALU.mult), reads=[dd_r, gat_r], writes=[cf_r])
                kb.op(kb.dve, lambda: nc.vector.tensor_tensor(out=cf[:, 2, :], in0=dd[:, 1, :], in1=gat[:, :, h * 3 + 2], op=ALU.mult), reads=[dd_r, gat_r], writes=[cf_r])
                for qs in range(4):
                    kb.op(kb.dve, lambda: nc.vector.tensor_scalar(out=oh[:, :], in0=cst[:, qs, hl, :], scalar1=cf[:, 0, qs:qs + 1], scalar2=None, op0=ALU.mult),
                          reads=[cst_r[hl], cf_r], writes=[oh_r])
                    kb.op(kb.dve, lambda: nc.vector.scalar_tensor_tensor(out=oh[:, :], in0=aS[:, qs * 65:qs * 65 + 64], scalar=cf[:, 1, qs:qs + 1], in1=oh[:, :], op0=ALU.mult, op1=ALU.add),
                          reads=[aS_r, cf_r, oh_r], writes=[oh_r])
                    kb.op(kb.dve, lambda: nc.vector.scalar_tensor_tensor(out=onb[:, qs, h * 64:(h + 1) * 64], in0=aW[:, qs * 65:qs * 65 + 64], scalar=cf[:, 2, qs:qs + 1], in1=oh[:, :],
                                                                          op0=ALU.mult, op1=ALU.add), reads=[aW_r, cf_r, oh_r], writes=[onb_r])
            for fc in range(4 * g, 4 * g + 4):
                k = fc % 2
                for qs in range(4):
                    kb.op(kb.pe, lambda: nc.tensor.transpose(out=P.psb[:, 0:128], in_=onb[:, qs, fc * 128:(fc + 1) * 128], identity=IDB[:, :]),
                          reads=[onb_r, IDB_r], writes=[P.psb_r])
                    kb.op(kb.dve, lambda: nc.vector.tensor_copy(out=ost[k][:, qs * 128:(qs + 1) * 128], in_=P.psb[:, 0:128]), reads=[P.psb_r], writes=[ost_r[k]])
                kb.dma(osems[k], dr["onsaT"][fc * 128:(fc + 1) * 128, q0:q0 + 512], ost[k][:, :], reads=[ost_r[k]], eng=(kb.act if (MONO or QSEL) else kb.sp))


BF = ml_dtypes.bfloat16
_CACHE = {}
USE_MONO = True


def _slopes():
    hh = np.arange(1, 17, dtype=np.float32)
    return np.exp2(-8.0 * hh / 16.0).astype(np.float32)


def _split3(v):
    v = v.astype(np.float32)
    hi = v.astype(BF)
    r = v - hi.astype(np.float32)
    mid = r.astype(BF)
    r2 = r - mid.astype(np.float32)
    lo = r2.astype(BF)
    return hi, mid, lo


def core_consts(cc):
    sl = _slopes()
    p = np.arange(128)
    q = np.arange(512)
    c = {}
    tabs = np.concatenate([(4 * i + cc) * 512 + q for i in range(4)]).astype(np.float32)
    v = -(sl[:, None] * tabs[None, :])
    hi, mid, lo = _split3(v)
    c["qal"] = np.ascontiguousarray(np.stack([hi, mid, lo], 0))
    kbs = np.zeros((128, 16, 64), np.float32)
    for h in range(16):
        kbs[:, h, :] = sl[h] * (np.arange(64)[None, :] * 128 + p[:, None]).astype(np.float32)
    c["kbs"] = kbs.reshape(128, 1024)
    kbc = np.zeros((128, 16, 4), np.float32)
    for h in range(16):
        kbc[:, h, :] = sl[h] * (16 * (np.arange(4)[None, :] * 128 + p[:, None]) + 31).astype(np.float32)
    c["kbc"] = kbc.reshape(128, 64)
    kbw = np.zeros((128, 4, 8, 16), np.float32)
    for i in range(4):
        T0 = (4 * i + cc) * 512
        for w in range(8):
            ka = T0 - 512 + w * 128 + p
            for h in range(16):
                kbw[:, i, w, h] = np.where(ka >= 0, sl[h] * ka.astype(np.float32), -30000.0)
    c["kbw"] = kbw.reshape(128, 512)
    cmpm = np.zeros((128, 2, 512), np.float32)
    for d in (-1, 0):
        vis = (2048 * d + 16 * p[:, None] + 31 - 512 * cc) <= q[None, :]
        cmpm[:, d + 1, :] = np.where(vis, 0.0, NEG)
    c["cmpm"] = cmpm.astype(BF)
    cm = np.zeros((128, 16, 512), np.float32)
    for r in range(16):
        vis = (128 * r + p[:, None]) <= (512 * cc + q[None, :])
        cm[:, r, :] = np.where(vis, 0.0, NEG)
    c["cm"] = cm.astype(BF)
    wm = np.zeros((128, 8, 512), np.float32)
    for w in range(8):
        dist = 512 + q[None, :] - 128 * w - p[:, None]
        wm[:, w, :] = np.where((dist >= 0) & (dist < 512), 0.0, NEG)
    c["wm"] = wm.astype(BF)
    addm = np.zeros((128, 16, 128), np.float32)
    vneg = np.zeros((128, 16, 128), np.float32)
    j = np.arange(128)
    for i in range(4):
        for qs in range(4):
            t = (4 * i + cc) * 512 + qs * 128 + p
            valid = (j[None, :] * 64) <= t[:, None]
            cur = t // 64
            forced = valid & ((j[None, :] == 0) | (j[None, :] == cur[:, None]) | (j[None, :] == cur[:, None] - 1))
            addm[:, i * 4 + qs, :] = np.where(forced, 8192.0, np.where(valid, 0.0, -8192.0))
            vneg[:, i * 4 + qs, :] = np.where(valid, 0.0, NEG)
    c["addm"] = addm.astype(BF)
    c["vneg"] = vneg.astype(BF)
    return c


def shared_consts():
    c = {}
    cols = np.arange(SEQ)
    kar = np.zeros((64, SEQ), np.float32)
    kar[0:60] = ((cols[None, :] // 64) % 60 == np.arange(60)[:, None])
    kar[60:63] = 1.0
    c["karows"] = kar.astype(BF)
    c["ones3"] = np.ones((3, 1024), np.float32).astype(BF)
    n = np.arange(512)
    cs = n[:, None] * 16
    ss = np.arange(128)[None, :] * 64
    ov = np.clip(np.minimum(cs + 32, ss + 64) - np.maximum(cs, ss), 0, None) / 32.0
    c["ovl"] = np.ascontiguousarray(ov.reshape(4, 128, 128).transpose(1, 0, 2)).astype(np.float32).astype(BF)
    c["identb"] = np.eye(128, dtype=np.float32).astype(BF)
    return c


def tile_w(Wm, starts=None):
    K = Wm.shape[0]
    if starts is None:
        starts = list(range(0, Wm.shape[1], 128))
    out = np.empty((len(starts), 128, K // 128, 128), np.float32)
    for j, c0 in enumerate(starts):
        out[j] = Wm[:, c0:c0 + 128].reshape(K // 128, 128, 128).transpose(1, 0, 2)
    return out


def _prog(key, fn):
    if key not in _CACHE:
        _CACHE[key] = fn()
    return _CACHE[key]


def _tok_index(cc):
    return np.concatenate([np.arange((4 * i + cc) * 512, (4 * i + cc + 1) * 512) for i in range(4)])


def kernel(x, ffn1_norm, ffn1_w_gate, ffn1_w_up, ffn1_w_down, mix_norm, w_in, cmp_pos,
           cmp_k_w1, cmp_k_w2, cmp_v_w1, cmp_v_w2, pool_w, pool_scale, w_branch_pool,
           w_branch_nsa, w_out, ffn2_norm, ffn2_w_gate, ffn2_w_up, ffn2_w_down, final_norm):
    f32 = lambda a: np.ascontiguousarray(np.asarray(a, dtype=np.float32))
    x = f32(x)
    W = {k: f32(v) for k, v in dict(ffn1_norm=ffn1_norm, ffn1_w_gate=ffn1_w_gate, ffn1_w_up=ffn1_w_up, ffn1_w_down=ffn1_w_down,
                                    mix_norm=mix_norm, w_in=w_in, cmp_pos=cmp_pos, cmp_k_w1=cmp_k_w1, cmp_k_w2=cmp_k_w2,
                                    cmp_v_w1=cmp_v_w1, cmp_v_w2=cmp_v_w2, pool_w=pool_w, pool_scale=pool_scale,
                                    w_branch_pool=w_branch_pool, w_branch_nsa=w_branch_nsa, w_out=w_out, ffn2_norm=ffn2_norm,
                                    ffn2_w_gate=ffn2_w_gate, ffn2_w_up=ffn2_w_up, ffn2_w_down=ffn2_w_down, final_norm=final_norm).items()}
    cores = list(range(8))
    vecs = np.zeros((128, 64), np.float32)
    for l in range(NL):
        b0 = vbase(l)
        vecs[:, b0:b0 + 8] = gain_layout(W["ffn1_norm"][l])
        vecs[:, b0 + 8:b0 + 16] = gain_layout(W["mix_norm"][l])
        vecs[:, b0 + 16:b0 + 24] = gain_layout(W["ffn2_norm"][l])
        vecs[:, b0 + 24:b0 + 28] = gain_layout(W["pool_scale"][l])
    vecs[:, 56:64] = gain_layout(W["final_norm"])
    if USE_MONO:
        return kernel_mono(W, x, vecs)
    tix = [_tok_index(c % 4) for c in cores]
    cc_consts = [core_consts(cc) for cc in range(4)]
    sh = shared_consts()

    TW = {}
    for l in range(NL):
        TW["win", l] = tile_w(W["w_in"][l], WIN_STARTS)
        for nm in ("ffn1_w_gate", "ffn1_w_up", "ffn1_w_down", "ffn2_w_gate", "ffn2_w_up", "ffn2_w_down", "w_branch_pool", "w_branch_nsa", "w_out",
                   "cmp_k_w1", "cmp_v_w1"):
            TW[nm, l] = tile_w(W[nm][l])

    def a_weights(l):
        return {"f1_wg": TW["ffn1_w_gate", l], "f1_wu": TW["ffn1_w_up", l], "f1_wd": TW["ffn1_w_down", l], "win_a": TW["win", l]}

    progA = _prog("A", lambda: build_tok(0, "A", False))
    in_maps = []
    for c in cores:
        m = {"xs_in": np.ascontiguousarray(x[c // 4, tix[c], :].T), "vecs": vecs}
        m.update(a_weights(0))
        in_maps.append(m)
    res = run_bass_kernel_spmd(progA, in_maps, core_ids=cores).results

    out = np.zeros((2, SEQ, D), np.float32)
    for l in range(NL):
        last = (l == NL - 1)
        kvall = np.zeros((2, 4, 128, SEQ + 32), BF)
        vall = np.zeros((2, SEQ, 256), BF)
        uall = np.zeros((2, 512, SEQ), np.float32)
        for c in cores:
            b = c // 4
            kvall[b][:, :, tix[c]] = np.asarray(res[c]["kvT_out"]).view(BF) if np.asarray(res[c]["kvT_out"]).dtype != BF else res[c]["kvT_out"]
            vall[b][tix[c], :] = np.asarray(res[c]["vtok_out"])
            uall[b][:, tix[c]] = np.asarray(res[c]["uT_out"])
        prog = _prog(("BCA", l, last), lambda: build_bca(l, last))
        pecol = np.ascontiguousarray(W["cmp_pos"][l].reshape(16, 2, 64).transpose(1, 2, 0).reshape(128, 16))
        in_maps = []
        for c in cores:
            b, cc = c // 4, c % 4
            kwin = np.zeros((128, 4, 1024), BF)
            vwin = np.zeros((4, 1024, 128), BF)
            uext = np.zeros((512, 4, 528), np.float32)
            for i in range(4):
                T0 = (4 * i + cc) * 512
                lo = max(T0 - 512, 0)
                kwin[:, i, 1024 - (T0 + 512 - lo):] = kvall[b][3][:, lo:T0 + 512]
                vwin[i, 1024 - (T0 + 512 - lo):, :] = vall[b][lo:T0 + 512, 128:256]
                lo = max(T0 - 16, 0)
                uext[:, i, 528 - (T0 + 512 - lo):] = uall[b][:, lo:T0 + 512]
            corr = np.ones((128, 4, 16), np.float32)
            if cc == 0:
                for gi, w in enumerate(POOLW):
                    corr[:, gi, :] = (w / np.minimum(np.arange(16) + 1.0, float(w)))[None, :]
            m = {"vecs": vecs, "h2T": np.asarray(res[c]["h2T_out"]), "kvall": kvall[b], "vall": vall[b], "kwin": kwin, "vwin": vwin,
                 "win": TW["win", l], "win_gn": np.ascontiguousarray(W["w_in"][l][:, C_GN:C_GN + 48]),
                 "ck_w1": TW["cmp_k_w1", l], "ck_w2": W["cmp_k_w2"][l], "cv_w1": TW["cmp_v_w1", l], "cv_w2": W["cmp_v_w2"][l],
                 "pecol": pecol,
                 "xs_in": np.asarray(res[c]["xs_out"]),
                 "f2_wg": TW["ffn2_w_gate", l], "f2_wu": TW["ffn2_w_up", l], "f2_wd": TW["ffn2_w_down", l],
                 "wpa": TW["w_branch_pool", l], "wnb": TW["w_branch_nsa", l], "wo": TW["w_out", l],
                 "poolw": W["pool_w"][l], "uext": uext, "corr": corr}
            m.update(cc_consts[cc])
            m.update(sh)
            if not last:
                m.update(a_weights(l + 1))
            in_maps.append(m)
        res = run_bass_kernel_spmd(prog, in_maps, core_ids=cores).results
    for c in cores:
        out[c // 4, tix[c], :] = np.asarray(res[c]["out"]).T
    return out


def build_mono():
    global MONO
    MONO = True
    try:
        BI, IN_, BO = "ExternalInput", "Internal", "ExternalOutput"
        S = SEQ
        specs = {
            "x_in": ((D, S), F32, BI), "vecs": ((128, 64), F32, BI),
            "qal": ((3, 16, S), BF16, BI), "kbs": ((128, 1024), F32, BI), "kbc": ((128, 64), F32, BI),
            "kbw": ((16, 128, 128), F32, BI), "cmpm": ((128, 5, 512), BF16, BI), "cm": ((128, 4, 512), BF16, BI),
            "wm": ((128, 8, 512), BF16, BI), "addm": ((16, 128, 4, 128), BF16, BI), "vneg": ((16, 128, 4, 128), BF16, BI),
            "karows": ((64, S), BF16, BI), "ones3": ((3, 1024), BF16, BI), "ovl": ((128, 4, 128), BF16, BI), "identb": ((128, 128), BF16, BI),
            "corr": ((128, 4, 16), F32, BI),
            "xs": ((D, S), F32, IN_), "h2T": ((D, S), BF16, IN_), "kvT": ((4, 128, S), BF16, IN_), "vtok": ((S, 256), BF16, IN_),
            "uT0": ((512, S), F32, IN_), "uT1": ((512, S), F32, IN_), "onsaT": ((D, S), BF16, IN_),
            "out_q": ((D, NTOK), F32, BO), "onsaT_q": ((D, NTOK), BF16, IN_),
            "selw": ((128, 4), F32, BI), "corr_q": ((128, 4, 16), F32, BI),
            "qal_q": ((3, 16, NTOK), BF16, BI), "kbw_q": ((128, 512), F32, BI), "cmpm_q": ((128, 2, 512), BF16, BI),
            "cm_q": ((128, 16, 512), BF16, BI), "addm_q": ((128, 16, 128), BF16, BI), "vneg_q": ((128, 16, 128), BF16, BI),
        }
        for l in range(NL):
            for pre in ("f1_", "f2_"):
                specs["%swg_%d" % (pre, l)] = ((NFC, 128, 8, 128), F32, BI)
                specs["%swu_%d" % (pre, l)] = ((NFC, 128, 8, 128), F32, BI)
                specs["%swd_%d" % (pre, l)] = ((8, 128, NFC, 128), F32, BI)
            specs["win_%d" % l] = ((len(WIN_STARTS), 128, 8, 128), F32, BI)
            specs["win_gn_%d" % l] = ((D, 48), F32, BI)
            specs["wpa_%d" % l] = ((8, 128, 4, 128), F32, BI)
            specs["wnb_%d" % l] = ((8, 128, 8, 128), F32, BI)
            specs["wo_%d" % l] = ((8, 128, 8, 128), F32, BI)
            specs["poolw_%d" % l] = ((4, 128, 128), F32, BI)
            specs["ck_w1_%d" % l] = ((2, 128, 16, 128), F32, BI)
            specs["cv_w1_%d" % l] = ((2, 128, 16, 128), F32, BI)
            specs["ck_w2_%d" % l] = ((256, 64), F32, BI)
            specs["cv_w2_%d" % l] = ((256, 64), F32, BI)
            specs["pecol_%d" % l] = ((128, 16), F32, BI)
        conv = [n_ for n_, (sh_, dt_, k_) in specs.items() if k_ == BI and dt_ == F32 and len(sh_) == 4 and n_ != "kbw"]
        gu = [n_ for n_ in conv if n_[3:5] in ("wg", "wu")]
        conv = [n_ for n_ in conv if n_ not in gu]
        for n_ in conv:
            specs[n_ + "_b"] = (specs[n_][0], BF16, IN_)
        for l in range(NL):
            for pre in ("f1_", "f2_"):
                specs["%swgu_%d_b" % (pre, l)] = ((NFC, 128, 16, 128), BF16, IN_)
        P = Prog(specs, WST=None)
        kb, dr = P.kb, P.dr

        P.selw = kb.sb("selw", [128, 4], F32)
        P.selw_r = Res("selw")
        kb.dma(P.ldsem, P.selw[:, :], dr["selw"][:, :], writes=[P.selw_r])
        mono_tabs = {k_: dr[k_] for k_ in ("qal", "kbw", "cmpm", "cm", "addm", "vneg", "corr", "onsaT")}

        def do_convert():
            for n_ in conv:
                P.convert_w(dr[n_], dr[n_ + "_b"])
                dr[n_] = dr[n_ + "_b"]
            for l_ in range(NL):
                for pre in ("f1_", "f2_"):
                    d_ = dr["%swgu_%d_b" % (pre, l_)]
                    P.convert_w(dr["%swg_%d" % (pre, l_)], None, dst_fn=lambda j, d_=d_: d_[j, :, 0:8, :])
                    P.convert_w(dr["%swu_%d" % (pre, l_)], None, dst_fn=lambda j, d_=d_: d_[j, :, 8:16, :])

        def alias(l, mode):
            for nm in ("wpa", "wnb", "wo", "poolw", "ck_w1", "cv_w1", "ck_w2", "cv_w2", "pecol", "win_gn"):
                dr[nm] = dr["%s_%d" % (nm, l)]
            dr["win"] = dr["win_%d" % l]
            dr["win_c"] = dr["win_%d" % l]
            dr["f2_wd"] = dr["f2_wd_%d" % l]
            dr["f2_wg"] = dr["f2_wgu_%d_b" % l]
            dr["f2_wu"] = None
            la = l if mode == "A" else min(l + 1, NL - 1)
            dr["win_a"] = dr["win_%d" % la]
            dr["f1_wd"] = dr["f1_wd_%d" % la]
            dr["f1_wg"] = dr["f1_wgu_%d_b" % la]
            dr["f1_wu"] = None
            dr["kvall"] = dr["kvT"]
            dr["vall"] = dr["vtok"]
            dr["h2T_in"] = dr["h2T"]
            dr["h2T_out"] = dr["h2T"]
            dr["kvT_out"] = dr["kvT"]
            dr["vtok_out"] = dr["vtok"]
            dr["xs_out"] = dr["xs"]
            dr["xs_in"] = dr["x_in"] if (mode == "A" and l == 0) else dr["xs"]
            dr["uT_in"] = dr["uT%d" % (l % 2)]
            dr["uT_out"] = dr["uT%d" % (la % 2)]

        def phase(fn, wst, wstf=1024, nslot=4):
            with ExitStack() as pes:
                kb.cur_es = pes
                P.alloc_wstage(wst, wstf, nslot)
                fn()
                kb.barrier()
            kb.cur_es = None

        phase(do_convert, 3584, 3584, 4)
        alias(0, "A")
        phase(lambda: tok_body(P, 0, "A", False), 3584)
        global QSEL
        for l in range(NL):
            last = (l == NL - 1)
            alias(l, "CA")
            if last:
                MONO, QSEL = False, True
                for k_ in ("qal", "kbw", "cmpm", "cm", "addm", "vneg", "corr", "onsaT"):
                    dr[k_] = dr[k_ + "_q"]
                dr["out"] = dr["out_q"]
            phase(lambda: attn_body(P), 2048, 1024, 2)
            phase(lambda: tok_body(P, l, "CA", last), 3584)
        return P.finish()
    finally:
        MONO = False
        QSEL = False


def mono_consts():
    sl = _slopes()
    p = np.arange(128)
    q = np.arange(512)
    c = {}
    tabs = np.arange(SEQ).astype(np.float32)
    hi, mid, lo = _split3(-(sl[:, None] * tabs[None, :]))
    c["qal"] = np.ascontiguousarray(np.stack([hi, mid, lo], 0))
    kbs = np.zeros((128, 16, 64), np.float32)
    kbc = np.zeros((128, 16, 4), np.float32)
    for h in range(16):
        kbs[:, h, :] = sl[h] * (np.arange(64)[None, :] * 128 + p[:, None]).astype(np.float32)
        kbc[:, h, :] = sl[h] * (16 * (np.arange(4)[None, :] * 128 + p[:, None]) + 31).astype(np.float32)
    c["kbs"] = kbs.reshape(128, 1024)
    c["kbc"] = kbc.reshape(128, 64)
    kbw = np.zeros((16, 128, 8, 16), np.float32)
    for i in range(16):
        for w in range(8):
            ka = 512 * (i - 1) + w * 128 + p
            for h in range(16):
                kbw[i, :, w, h] = np.where(ka >= 0, sl[h] * ka.astype(np.float32), -30000.0)
    c["kbw"] = kbw.reshape(16, 128, 128)
    cmpm = np.zeros((128, 5, 512), np.float32)
    for d in range(5):
        cmpm[:, d, :] = np.where((16 * p[:, None] + 31 - 512 * d) <= q[None, :], 0.0, NEG)
    c["cmpm"] = cmpm.astype(BF)
    cm = np.zeros((128, 4, 512), np.float32)
    for r in range(4):
        cm[:, r, :] = np.where((128 * r + p[:, None]) <= q[None, :], 0.0, NEG)
    c["cm"] = cm.astype(BF)
    wm = np.zeros((128, 8, 512), np.float32)
    for w in range(8):
        dist = 512 + q[None, :] - 128 * w - p[:, None]
        wm[:, w, :] = np.where((dist >= 0) & (dist < 512), 0.0, NEG)
    c["wm"] = wm.astype(BF)
    addm = np.zeros((16, 128, 4, 128), np.float32)
    vneg = np.zeros((16, 128, 4, 128), np.float32)
    j = np.arange(128)
    for i in range(16):
        for qs in range(4):
            t = i * 512 + qs * 128 + p
            valid = (j[None, :] * 64) <= t[:, None]
            cur = t // 64
            forced = valid & ((j[None, :] == 0) | (j[None, :] == cur[:, None]) | (j[None, :] == cur[:, None] - 1))
            addm[i, :, qs, :] = np.where(forced, 8192.0, np.where(valid, 0.0, -8192.0))
            vneg[i, :, qs, :] = np.where(valid, 0.0, NEG)
    c["addm"] = addm.astype(BF)
    c["vneg"] = vneg.astype(BF)
    corr = np.ones((128, 4, 16), np.float32)
    for gi, w in enumerate(POOLW):
        corr[:, gi, :] = (w / np.minimum(np.arange(16) + 1.0, float(w)))[None, :]
    c["corr"] = corr
    c.update(shared_consts())
    return c


def kernel_mono(W, x, vecs):
    prog = _prog("MONO", build_mono)
    base = {"vecs": vecs}
    base.update(mono_consts())
    for l in range(NL):
        base["win_%d" % l] = tile_w(W["w_in"][l], WIN_STARTS)
        base["win_gn_%d" % l] = np.ascontiguousarray(W["w_in"][l][:, C_GN:C_GN + 48])
        for pre, a in (("f1_", "ffn1"), ("f2_", "ffn2")):
            base["%swg_%d" % (pre, l)] = tile_w(W[a + "_w_gate"][l])
            base["%swu_%d" % (pre, l)] = tile_w(W[a + "_w_up"][l])
            base["%swd_%d" % (pre, l)] = tile_w(W[a + "_w_down"][l])
        base["wpa_%d" % l] = tile_w(W["w_branch_pool"][l])
        base["wnb_%d" % l] = tile_w(W["w_branch_nsa"][l])
        base["wo_%d" % l] = tile_w(W["w_out"][l])
        base["poolw_%d" % l] = W["pool_w"][l]
        base["ck_w1_%d" % l] = tile_w(W["cmp_k_w1"][l])
        base["cv_w1_%d" % l] = tile_w(W["cmp_v_w1"][l])
        base["ck_w2_%d" % l] = W["cmp_k_w2"][l]
        base["cv_w2_%d" % l] = W["cmp_v_w2"][l]
        base["pecol_%d" % l] = np.ascontiguousarray(W["cmp_pos"][l].reshape(16, 2, 64).transpose(1, 2, 0).reshape(128, 16))
    cores = list(range(8))
    in_maps = []
    xT = [np.ascontiguousarray(x[b].T) for b in range(2)]
    for c in cores:
        b, cc = c % 2, c // 2
        m = dict(base)
        m["x_in"] = xT[b]
        cq = core_consts(cc)
        for k_ in ("qal", "kbw", "cmpm", "cm", "addm", "vneg"):
            m[k_ + "_q"] = cq[k_]
        selw = np.zeros((128, 4), np.float32)
        selw[:, cc] = 1.0
        m["selw"] = selw
        corr = np.ones((128, 4, 16), np.float32)
        if cc == 0:
            corr = base["corr"]
        m["corr_q"] = corr
        in_maps.append(m)
    res = run_bass_kernel_spmd(prog, in_maps, core_ids=cores).results
    out = np.zeros((2, SEQ, D), np.float32)
    for c in cores:
        out[c % 2, _tok_index(c // 2), :] = np.asarray(res[c]["out_q"]).T
    return out
```

```python
import numpy as np
import ml_dtypes
from contextlib import ExitStack
import concourse.bass as bass
import concourse.mybir as mybir
from concourse.bass_utils import run_bass_kernel_spmd

F32 = mybir.dt.float32
BF16 = mybir.dt.bfloat16
AF = mybir.ActivationFunctionType
ALU = mybir.AluOpType

D = 1024
DFF = 2816
NFC = DFF // 128
SEQ = 8192
NL = 2
NTOK = 2048
MONO = False
QSEL = False


def ntok():
    return SEQ if MONO else NTOK
TT = 512
INW = 4400
C_Q, C_KC, C_VC, C_KSL, C_VSL, C_KWN, C_VWN, C_GN, C_U, C_GM = 0, 1024, 1152, 1280, 1408, 1536, 1664, 1792, 1840, 2352
EPS = 1e-6
NEG = -16384.0
WIN_STARTS = [j * 128 for j in range(8)] + [C_KC, C_VC, C_KSL, C_VSL, C_KWN, C_VWN] + [C_U + j * 128 for j in range(4)] + [C_GM + j * 128 for j in range(16)]
WIN_IDX = {c: i for i, c in enumerate(WIN_STARTS)}


class Sem:
    def __init__(self, h, name):
        self.h = h
        self.name = name
        self.count = 0
        self.group = False


class Tok:
    __slots__ = ("sem", "val")

    def __init__(self, sem, val):
        self.sem = sem
        self.val = val


class Res:
    __slots__ = ("name", "w", "r", "excl")

    def __init__(self, name="", excl=False):
        self.name = name
        self.w = None
        self.r = {}
        self.excl = excl


class Eng:
    def __init__(self, name, h, sem, same_sync):
        self.name = name
        self.h = h
        self.sem = sem
        self.waited = {}
        self.pending = []
        self.same_sync = same_sync


class KB:
    def __init__(self, nc, es):
        self.nc = nc
        self.es = es
        self.sems = []
        self.pe = self._eng("pe", nc.tensor, False)
        self.act = self._eng("act", nc.scalar, True)
        self.dve = self._eng("dve", nc.vector, True)
        self.pool = self._eng("pool", nc.gpsimd, True)
        self.sp = self._eng("sp", nc.sync, False)
        self.engs = [self.pe, self.act, self.dve, self.pool, self.sp]
        self.n_inst = 0

    def new_sem(self, name):
        name = "%s_%d" % (name, len(self.sems))
        h = self.es.enter_context(self.nc.semaphore(name))
        s = Sem(h, name)
        self.sems.append(s)
        return s

    def _eng(self, name, h, same_sync):
        return Eng(name, h, self.new_sem("s_" + name), same_sync)

    def sb(self, name, shape, dtype, es=None):
        self.nsb = getattr(self, "nsb", 0) + 1
        return (es or getattr(self, "cur_es", None) or self.es).enter_context(self.nc.sbuf_tensor("sb%d_%s" % (self.nsb, name), shape, dtype))

    def ps(self, name, shape, dtype):
        return self.es.enter_context(self.nc.psum_tensor("pp_" + name, shape, dtype))

    def _wait(self, eng, tok):
        if tok is None:
            return
        if tok.sem is eng.sem and not eng.same_sync:
            return
        assert tok.val is not None, "waiting on unresolved token (%s)" % tok.sem.name
        val = tok.val
        if tok.sem.group:
            val = max(val, tok.sem.count)
        if eng.waited.get(tok.sem, 0) >= val:
            return
        eng.h.wait_ge(tok.sem.h, val)
        eng.waited[tok.sem] = val

    def _deps(self, eng, reads, writes):
        for r in reads:
            self._wait(eng, r.w)
        for w in writes:
            self._wait(eng, w.w)
            for t in w.r.values():
                self._wait(eng, t)

    def _mark(self, tok, reads, writes):
        for r in reads:
            r.r[tok.sem] = tok
        for w in writes:
            w.w = tok
            w.r = {}

    def op(self, eng, fn, reads=(), writes=(), sig=True):
        xr = [r for r in reads if r.excl]
        if xr:
            writes = list(writes) + xr
            reads = [r for r in reads if not r.excl]
        self._deps(eng, reads, writes)
        inst = fn()
        self.n_inst += 1
        if sig:
            eng.sem.count += 1
            inst.then_inc(eng.sem.h, 1)
            tok = Tok(eng.sem, eng.sem.count)
            for t in eng.pending:
                t.val = eng.sem.count
            eng.pending = []
        else:
            tok = Tok(eng.sem, None)
            eng.pending.append(tok)
        self._mark(tok, reads, writes)
        return tok

    def dma(self, sem, out, in_, reads=(), writes=(), eng=None, **kw):
        eng = eng or self.sp
        self._deps(eng, reads, writes)
        if sem.count > 0:
            self._wait(eng, Tok(sem, sem.count))
        inst = eng.h.dma_start(out=out, in_=in_, **kw)
        self.n_inst += 1
        sem.count += 16
        inst.then_inc(sem.h, 16)
        tok = Tok(sem, sem.count)
        self._mark(tok, reads, writes)
        return tok

    def barrier(self):
        for e in self.engs:
            assert not e.pending
            for s in self.sems:
                if s.count > 0 and not (s is e.sem):
                    self._wait(e, Tok(s, s.count))


class Prog:
    def __init__(self, dram_specs, WST=3584):
        self.nc = bass.Bass("TRN2", target_bir_lowering=False)
        self.es = ExitStack()
        self.kb = KB(self.nc, self.es)
        self.dr = {}
        self.dres = {}
        for name, (shape, dt, kind) in dram_specs.items():
            self.dr[name] = self.nc.dram_tensor(name, list(shape), dt, kind=kind).ap()
            self.dres[name] = Res("dram_" + name)
        self.out_names = [n for n, (_, _, k) in dram_specs.items() if k == "ExternalOutput"]
        kb = self.kb
        self.psf = [kb.ps("psf%d" % i, [128, 512], F32) for i in range(7)]
        self.psf_r = [Res("psf%d" % i, excl=True) for i in range(7)]
        self.psb = kb.ps("psb", [128, 1024], BF16)
        self.psb_r = Res("psb", excl=True)
        self.ps_rr = 0
        self.ones = kb.sb("ones", [128, 128], F32)
        self.ones_r = Res("ones")
        kb.op(kb.dve, lambda: self.nc.vector.memset(self.ones[:], 1.0 / D), writes=[self.ones_r])
        self.epsc = kb.sb("epsc", [128, 1], F32)
        kb.op(kb.dve, lambda: self.nc.vector.memset(self.epsc[:], EPS), writes=[self.ones_r])
        self.vecs = kb.sb("vecs", [128, 64], F32)
        self.vecs_r = Res("vecs")
        self.ldsem = kb.new_sem("ld_misc")
        self.ldsem.group = True
        kb.dma(self.ldsem, self.vecs[:], self.dr["vecs"][:, :], writes=[self.vecs_r])
        self.wsem = [kb.new_sem("wsem%d" % i) for i in range(4)]
        self.stsem = kb.new_sem("st_misc")
        if WST:
            self.alloc_wstage(WST)

    def alloc_wstage(self, WST, WSTF=None, nslot=2):
        kb = self.kb
        self.WST = WST
        self.WSTF = WSTF or WST
        self.nslot = nslot
        self.wst = [kb.sb("wst%d" % i, [128, self.WSTF], F32) for i in range(nslot)]
        self.wst_r = [Res("wst%d" % i) for i in range(nslot)]
        self.wbf = [kb.sb("wbf%d" % i, [128, self.WST], BF16) for i in range(nslot)]
        self.wbf_r = [Res("wbf%d" % i) for i in range(nslot)]
        self.wslot = 0

    def bank(self):
        i = self.ps_rr % 7
        self.ps_rr += 1
        return self.psf[i], self.psf_r[i]

    def load_w(self, pieces):
        kb, nc = self.kb, self.nc
        s = self.wslot
        self.wslot = (self.wslot + 1) % self.nslot
        off = 0
        foff = 0
        views = []
        for ap in pieces:
            if len(ap.shape) == 3:
                _, kc, n = ap.shape
                src = ap
            else:
                K, n = ap.shape
                kc = K // 128
                src = ap.rearrange("(k p) n -> p k n", p=128)
            sz = kc * n
            bview = self.wbf[s][:, off:off + sz].rearrange("p (k n) -> p k n", n=n)
            if ap.dtype == BF16:
                kb.dma(self.wsem[s], bview, src, writes=[self.wbf_r[s]])
            else:
                assert foff + sz <= self.WSTF
                dst = self.wst[s][:, foff:foff + sz].rearrange("p (k n) -> p k n", n=n)
                kb.dma(self.wsem[s], dst, src, writes=[self.wst_r[s]])
                a, b, fa = off, off + sz, foff
                kb.op(kb.pool, lambda a=a, b=b, fa=fa: nc.gpsimd.tensor_copy(out=self.wbf[s][:, a:b], in_=self.wst[s][:, fa:fa + (b - a)]),
                      reads=[self.wst_r[s]], writes=[self.wbf_r[s]])
                foff += sz
            views.append(bview)
            off += sz
        assert off <= self.WST
        return views, self.wbf_r[s]

    def convert_w(self, src, dst, dst_fn=None):
        kb, nc = self.kb, self.nc
        nch, _, kc, n = src.shape
        sz = kc * n
        if not hasattr(self, "cvsem"):
            self.cvsem = [kb.new_sem("cvs%d" % i) for i in range(4)]
            self.cv_rr = 0
        for j in range(nch):
            s = self.wslot
            self.wslot = (self.wslot + 1) % self.nslot
            kb.dma(self.wsem[s], self.wst[s][:, 0:sz].rearrange("p (k n) -> p k n", n=n), src[j], writes=[self.wst_r[s]])
            e = self.cv_rr % 3
            self.cv_rr += 1
            if e == 0:
                kb.op(kb.pool, lambda: nc.gpsimd.tensor_copy(out=self.wbf[s][:, 0:sz], in_=self.wst[s][:, 0:sz]), reads=[self.wst_r[s]], writes=[self.wbf_r[s]])
            elif e == 1:
                kb.op(kb.dve, lambda: nc.vector.tensor_copy(out=self.wbf[s][:, 0:sz], in_=self.wst[s][:, 0:sz]), reads=[self.wst_r[s]], writes=[self.wbf_r[s]])
            else:
                kb.op(kb.act, lambda: nc.scalar.copy(out=self.wbf[s][:, 0:sz], in_=self.wst[s][:, 0:sz]), reads=[self.wst_r[s]], writes=[self.wbf_r[s]])
            kb.dma(self.cvsem[s], dst_fn(j) if dst_fn else dst[j], self.wbf[s][:, 0:sz].rearrange("p (k n) -> p k n", n=n), reads=[self.wbf_r[s]])

    def select_tile(self, dst, dst_r, cands, stage, stage_r, sem, p0, p1):
        kb, nc = self.kb, self.nc
        first = True
        for c, src in enumerate(cands):
            if src is None:
                continue
            kb.dma(sem, stage, src, writes=[stage_r], eng=kb.act)
            sc_ = self.selw[p0:p1, c:c + 1]
            if first:
                kb.op(kb.dve, lambda: nc.vector.tensor_scalar(out=dst, in0=stage, scalar1=sc_, scalar2=None, op0=ALU.mult),
                      reads=[stage_r, self.selw_r], writes=[dst_r])
                first = False
            else:
                kb.op(kb.dve, lambda: nc.vector.scalar_tensor_tensor(out=dst, in0=stage, scalar=sc_, in1=dst, op0=ALU.mult, op1=ALU.add),
                      reads=[stage_r, self.selw_r, dst_r], writes=[dst_r])

    def vcol(self, c):
        return self.vecs[:, c:c + 1]

    def rmsnorm(self, x, x_r, h, h_r, n, gcol, sq, sq_r, rstd, rstd_r, out_f32=None, out_r=None):
        kb, nc = self.kb, self.nc
        for s0 in range(0, n, 512):
            ps, ps_r = self.bank()
            for c in range(8):
                k = c % 2
                kb.op(kb.act, lambda c=c, k=k: nc.scalar.activation(out=sq[k][:, :], in_=x[:, c, s0:s0 + 512], func=AF.Square),
                      reads=[x_r], writes=[sq_r[k]])
                kb.op(kb.pe, lambda c=c, k=k: nc.tensor.matmul(ps[:, :], lhsT=self.ones[:, :], rhs=sq[k][:, :], start=(c == 0), stop=(c == 7)),
                      reads=[sq_r[k], self.ones_r], writes=[ps_r], sig=True)
            kb.op(kb.act, lambda: nc.scalar.activation(out=rstd[:, s0:s0 + 512], in_=ps[:, :], func=AF.Ln, bias=self.epsc[:, 0:1]),
                  reads=[ps_r, self.ones_r], writes=[rstd_r])
            kb.op(kb.act, lambda: nc.scalar.activation(out=rstd[:, s0:s0 + 512], in_=rstd[:, s0:s0 + 512], func=AF.Exp, scale=-0.5),
                  reads=[rstd_r], writes=[rstd_r])
            for c in range(8):
                tgt = h if out_f32 is None else out_f32
                tgt_r = h_r if out_f32 is None else out_r
                kb.op(kb.dve, lambda c=c, tgt=tgt: nc.vector.scalar_tensor_tensor(
                    out=tgt[:, c, s0:s0 + 512], in0=x[:, c, s0:s0 + 512], scalar=self.vcol(gcol + c), in1=rstd[:, s0:s0 + 512],
                    op0=ALU.mult, op1=ALU.mult), reads=[x_r, rstd_r, self.vecs_r], writes=[tgt_r])

    def ffn(self, x, x_r, h, h_r, n, wg, wu, wd, aT, aT_r, sg, sg_r):
        kb, nc = self.kb, self.nc
        nsub = n // 512
        for fc in range(NFC):
            if wu is None:
                (wgu,), w_r = self.load_w([wg[fc]])
                wgb, wub = wgu[:, 0:8, :], wgu[:, 8:16, :]
            else:
                (wgb, wub), w_r = self.load_w([wg[fc], wu[fc]])
            for sub in range(nsub):
                s0 = sub * 512
                pg, pg_r = self.bank()
                pu, pu_r = self.bank()
                for c in range(8):
                    kb.op(kb.pe, lambda c=c: nc.tensor.matmul(pg[:, :], lhsT=wgb[:, c, :], rhs=h[:, c, s0:s0 + 512], start=(c == 0), stop=(c == 7)),
                          reads=[w_r, h_r], writes=[pg_r], sig=(c == 7))
                for c in range(8):
                    kb.op(kb.pe, lambda c=c: nc.tensor.matmul(pu[:, :], lhsT=wub[:, c, :], rhs=h[:, c, s0:s0 + 512], start=(c == 0), stop=(c == 7)),
                          reads=[w_r, h_r], writes=[pu_r], sig=(c == 7))
                k = (fc * nsub + sub) % 2
                kb.op(kb.act, lambda k=k: nc.scalar.activation(out=sg[k][:, :], in_=pg[:, :], func=AF.Silu), reads=[pg_r], writes=[sg_r[k]])
                kb.op(kb.dve, lambda k=k: nc.vector.tensor_tensor(out=aT[:, fc, s0:s0 + 512], in0=sg[k][:, :], in1=pu[:, :], op=ALU.mult),
                      reads=[sg_r[k], pu_r], writes=[aT_r[fc]])
        for dc in range(8):
            (wdb,), w_r = self.load_w([wd[dc]])
            for sub in range(nsub):
                s0 = sub * 512
                py, py_r = self.bank()
                for fc in range(NFC):
                    kb.op(kb.pe, lambda fc=fc: nc.tensor.matmul(py[:, :], lhsT=wdb[:, fc, :], rhs=aT[:, fc, s0:s0 + 512], start=(fc == 0), stop=(fc == NFC - 1)),
                          reads=[w_r, aT_r[fc]], writes=[py_r], sig=(fc == NFC - 1))
                kb.op(kb.dve, lambda: nc.vector.scalar_tensor_tensor(out=x[:, dc, s0:s0 + 512], in0=py[:, :], scalar=0.5, in1=x[:, dc, s0:s0 + 512],
                                                                      op0=ALU.mult, op1=ALU.add), reads=[py_r, x_r], writes=[x_r])

    def finish(self):
        kb = self.kb
        kb.barrier()
        self.es.close()
        return self.nc


def gain_layout(v):
    return np.ascontiguousarray(np.asarray(v, np.float32).reshape(-1, 128).T)


def vbase(l):
    return 28 * l


POOLW = (2, 4, 8, 16)


def tok_specs(l, mode, last):
    specs = {"xs_in": ((D, NTOK), F32, "ExternalInput"), "vecs": ((128, 64), F32, "ExternalInput")}

    def wspec(li, pre):
        specs[pre + "wg"] = ((NFC, 128, 8, 128), F32, "ExternalInput")
        specs[pre + "wu"] = ((NFC, 128, 8, 128), F32, "ExternalInput")
        specs[pre + "wd"] = ((8, 128, NFC, 128), F32, "ExternalInput")

    doA = (mode == "A") or (not last)
    if mode == "CA":
        wspec(l, "f2_")
        specs["win_c"] = ((len(WIN_STARTS), 128, 8, 128), F32, "ExternalInput")
        specs["wpa"] = ((8, 128, 4, 128), F32, "ExternalInput")
        specs["wnb"] = ((8, 128, 8, 128), F32, "ExternalInput")
        specs["wo"] = ((8, 128, 8, 128), F32, "ExternalInput")
        specs["poolw"] = ((4, 128, 128), F32, "ExternalInput")
        specs["h2T_in"] = ((D, NTOK), BF16, "ExternalInput")
        specs["onsaT"] = ((D, NTOK), BF16, "ExternalInput")
        specs["uext"] = ((512, 4, 528), F32, "ExternalInput")
        specs["corr"] = ((128, 4, 16), F32, "ExternalInput")
    if doA:
        wspec(l, "f1_")
        specs["win_a"] = ((len(WIN_STARTS), 128, 8, 128), F32, "ExternalInput")
        specs["h2T_out"] = ((D, NTOK), BF16, "ExternalOutput")
        specs["kvT_out"] = ((4, 128, NTOK), BF16, "ExternalOutput")
        specs["uT_out"] = ((512, NTOK), F32, "ExternalOutput")
        specs["vtok_out"] = ((NTOK, 256), BF16, "ExternalOutput")
        specs["xs_out"] = ((D, NTOK), F32, "ExternalOutput")
    else:
        specs["out"] = ((D, NTOK), F32, "ExternalOutput")
    return specs


def build_tok(l, mode, last):
    P = Prog(tok_specs(l, mode, last))
    tok_body(P, l, mode, last)
    return P.finish()


def build_bca(l, last):
    specs = attn_specs()
    ts = tok_specs(l, "CA", last)
    for k_ in ("h2T_in", "win_c", "onsaT", "vecs"):
        ts.pop(k_)
    specs.update(ts)
    specs["onsaT"] = ((D, NTOK), BF16, "Internal")
    P = Prog(specs, WST=None)
    P.dr["h2T_in"] = P.dr["h2T"]
    P.dr["win_c"] = P.dr["win"]
    kb = P.kb
    with ExitStack() as pes:
        kb.cur_es = pes
        P.alloc_wstage(2048)
        attn_body(P)
        kb.barrier()
    with ExitStack() as pes:
        kb.cur_es = pes
        P.alloc_wstage(3584)
        tok_body(P, l, "CA", last)
        kb.barrier()
    kb.cur_es = None
    return P.finish()


def tok_body(P, l, mode, last):
    kb, nc, dr = P.kb, P.nc, P.dr
    stq = kb.act if (MONO or QSEL) else kb.sp
    doA = (mode == "A") or (not last)
    x = kb.sb("x", [128, 8, TT], F32); x_r = Res("x")
    h = kb.sb("h", [128, 8, TT], BF16); h_r = Res("h")
    aT = kb.sb("aT", [128, NFC, TT], BF16); aT_r = [Res("aT%d" % i) for i in range(NFC)]
    sq = [kb.sb("sq%d" % i, [128, 512], F32) for i in range(2)]; sq_r = [Res() for i in range(2)]
    rstd = kb.sb("rstd", [128, TT], F32); rstd_r = Res()
    xsem = kb.new_sem("xsem")
    if QSEL:
        xstg2 = kb.sb("xstg", [128, 8 * TT], F32); xstg_r = Res("xstg")
        xstg = xstg2[:, :].rearrange("p (c n) -> p c n", n=TT)
    if mode == "CA":
        hsem = kb.new_sem("hsem"); osem = kb.new_sem("osem"); usem = kb.new_sem("usem")
        on = kb.sb("on", [128, 8, TT], BF16); on_r = Res("on")
        ue = kb.sb("ue", [128, 4, 528], F32); ue_r = Res("ue")
        sa = kb.sb("sa", [128, 528], F32); sa_r = Res("sa")
        sb_ = kb.sb("sbb", [128, 528], F32); sb_r = Res("sb")
        dl = kb.sb("dl", [128, 4, TT], BF16); dl_r = [Res() for _ in range(4)]
        opl = kb.sb("opl", [128, 4, TT], BF16); opl_r = Res("opl")
        mg = kb.sb("mg", [128, 8, TT], BF16); mg_r = [Res() for _ in range(8)]
        t1 = kb.sb("t1", [128, TT], F32); t1_r = Res()
        t2 = kb.sb("t2", [128, TT], F32); t2_r = Res()
        corr = kb.sb("corr", [128, 4, 16], F32); corr_r = Res()
        kb.dma(P.ldsem, corr[:], dr["corr"][:, :, :], writes=[corr_r])
    if doA:
        kvst = [kb.sb("kvst%d" % i, [128, TT], BF16) for i in range(2)]; kvst_r = [Res() for _ in range(2)]
        ust = [kb.sb("ust%d" % i, [128, TT], F32) for i in range(2)]; ust_r = [Res() for _ in range(2)]
        vst = kb.sb("vst", [128, 4, 256], BF16); vst_r = Res()
        osems = [kb.new_sem("kvo%d" % i) for i in range(2)]
        usems = [kb.new_sem("uo%d" % i) for i in range(2)]
        vsem = kb.new_sem("vo")
        hosem = kb.new_sem("ho")

    def colsl(ap, t0):
        return ap[:, t0:t0 + TT].rearrange("(c p) n -> p c n", p=128)

    for t in range(ntok() // TT):
        t0 = t * TT
        if QSEL:
            P.select_tile(x[:, :, :], x_r, [colsl(dr["xs_in"], (4 * t + c_) * 512) for c_ in range(4)], xstg[:, :, :], xstg_r, xsem, 0, 128)
        else:
            kb.dma(xsem, x[:, :, :], colsl(dr["xs_in"], t0), writes=[x_r], eng=stq)
        la = l
        if mode == "CA":
            vb = vbase(l)
            if QSEL:
                P.select_tile(h[:, :, :], h_r, [colsl(dr["h2T_in"], (4 * t + c_) * 512) for c_ in range(4)],
                              on[:, :, :], on_r, hsem, 0, 128)
            else:
                kb.dma(hsem, h[:, :, :], colsl(dr["h2T_in"], t0), writes=[h_r], eng=stq)
            kb.dma(osem, on[:, :, :], colsl(dr["onsaT"], t0), writes=[on_r], eng=stq)
            if QSEL:
                ustg = xstg2[:, 0:2112].rearrange("p (g n) -> p g n", n=528)
                cands = []
                for c_ in range(4):
                    a0 = (4 * t + c_) * 512
                    cands.append(dr["uT_in"][:, a0 - 16:a0 + 512].rearrange("(g p) n -> p g n", p=128) if a0 > 0 else None)
                if t == 0:
                    kb.op(kb.pool, lambda: nc.gpsimd.memset(ustg[:, :, 0:16], 0.0), writes=[xstg_r])
                    kb.dma(usem, ustg[:, :, 16:528], dr["uT_in"][:, 0:512].rearrange("(g p) n -> p g n", p=128), writes=[xstg_r])
                    kb.op(kb.dve, lambda: nc.vector.tensor_scalar(out=ue[:, :, :], in0=ustg, scalar1=P.selw[:, 0:1], scalar2=None, op0=ALU.mult),
                          reads=[xstg_r, P.selw_r], writes=[ue_r])
                    for c_ in range(1, 4):
                        kb.dma(usem, ustg, cands[c_], writes=[xstg_r])
                        kb.op(kb.dve, lambda: nc.vector.scalar_tensor_tensor(out=ue[:, :, :], in0=ustg, scalar=P.selw[:, c_:c_ + 1], in1=ue[:, :, :], op0=ALU.mult, op1=ALU.add),
                              reads=[xstg_r, P.selw_r, ue_r], writes=[ue_r])
                else:
                    P.select_tile(ue[:, :, :], ue_r, cands, ustg, xstg_r, usem, 0, 128)
            elif MONO:
                kb.dma(usem, ue[:, :, 16:528], dr["uT_in"][:, t0:t0 + 512].rearrange("(g p) n -> p g n", p=128), writes=[ue_r])
                if t == 0:
                    kb.op(kb.pool, lambda: nc.gpsimd.memset(ue[:, :, 0:16], 0.0), writes=[ue_r])
                else:
                    kb.dma(usem, ue[:, :, 0:16], dr["uT_in"][:, t0 - 16:t0].rearrange("(g p) n -> p g n", p=128), writes=[ue_r])
            else:
                kb.dma(usem, ue[:, :, :], dr["uext"][:, t, :].rearrange("(g p) n -> p g n", p=128), writes=[ue_r])
            for gi, w in enumerate(POOLW):
                cur, cur_r = None, None
                sh = 1
                src = ue[:, gi, :]
                src_r = ue_r
                bufs = [(sa, sa_r), (sb_, sb_r)]
                bi = 0
                while sh < w:
                    dst, dst_r = bufs[bi]
                    bi ^= 1
                    lo = 2 * sh - 1
                    kb.op(kb.dve, lambda src=src, dst=dst, lo=lo, sh=sh: nc.vector.tensor_tensor(
                        out=dst[:, lo:528], in0=src[:, lo:528], in1=src[:, lo - sh:528 - sh], op=ALU.add),
                        reads=[src_r], writes=[dst_r])
                    src, src_r = dst, dst_r
                    sh *= 2
                kb.op(kb.dve, lambda src=src, w=w: nc.vector.tensor_scalar(out=src[:, 16:528], in0=src[:, 16:528], scalar1=1.0 / w, scalar2=None, op0=ALU.mult),
                      reads=[src_r], writes=[src_r])
                if t == 0:
                    kb.op(kb.dve, lambda src=src, gi=gi: nc.vector.tensor_tensor(out=src[:, 16:32], in0=src[:, 16:32], in1=corr[:, gi, :], op=ALU.mult),
                          reads=[src_r, corr_r], writes=[src_r])
                kb.op(kb.dve, lambda src=src, gi=gi: nc.vector.tensor_tensor(out=dl[:, gi, :], in0=src[:, 16:528], in1=ue[:, gi, 16:528], op=ALU.subtract),
                      reads=[src_r, ue_r], writes=[dl_r[gi]])
            for gi in range(4):
                (pw,), w_r = P.load_w([dr["poolw"][gi]])
                ps, ps_r = P.bank()
                kb.op(kb.pe, lambda: nc.tensor.matmul(ps[:, :], lhsT=pw[:, 0, :], rhs=dl[:, gi, :], start=True, stop=True),
                      reads=[w_r, dl_r[gi]], writes=[ps_r])
                kb.op(kb.dve, lambda: nc.vector.tensor_scalar(out=opl[:, gi, :], in0=ps[:, :], scalar1=P.vcol(vb + 24 + gi), scalar2=None, op0=ALU.mult),
                      reads=[ps_r, P.vecs_r], writes=[opl_r])
            for dc in range(8):
                (wpa, wnb, wgp, wga), w_r = P.load_w([dr["wpa"][dc], dr["wnb"][dc],
                                                      dr["win_c"][WIN_IDX[C_GM + dc * 128]],
                                                      dr["win_c"][WIN_IDX[C_GM + 1024 + dc * 128]]])
                pa, pa_r = P.bank(); pb, pb_r = P.bank(); pgp, pgp_r = P.bank(); pga, pga_r = P.bank()
                for c in range(4):
                    kb.op(kb.pe, lambda c=c: nc.tensor.matmul(pa[:, :], lhsT=wpa[:, c, :], rhs=opl[:, c, :], start=(c == 0), stop=(c == 3)),
                          reads=[w_r, opl_r], writes=[pa_r], sig=(c == 3))
                for c in range(8):
                    kb.op(kb.pe, lambda c=c: nc.tensor.matmul(pb[:, :], lhsT=wnb[:, c, :], rhs=on[:, c, :], start=(c == 0), stop=(c == 7)),
                          reads=[w_r, on_r], writes=[pb_r], sig=(c == 7))
                for c in range(8):
                    kb.op(kb.pe, lambda c=c: nc.tensor.matmul(pgp[:, :], lhsT=wgp[:, c, :], rhs=h[:, c, :], start=(c == 0), stop=(c == 7)),
                          reads=[w_r, h_r], writes=[pgp_r], sig=(c == 7))
                for c in range(8):
                    kb.op(kb.pe, lambda c=c: nc.tensor.matmul(pga[:, :], lhsT=wga[:, c, :], rhs=h[:, c, :], start=(c == 0), stop=(c == 7)),
                          reads=[w_r, h_r], writes=[pga_r], sig=(c == 7))
                kb.op(kb.act, lambda: nc.scalar.activation(out=t1[:, :], in_=pgp[:, :], func=AF.Sigmoid), reads=[pgp_r], writes=[t1_r])
                kb.op(kb.act, lambda: nc.scalar.activation(out=t2[:, :], in_=pga[:, :], func=AF.Sigmoid), reads=[pga_r], writes=[t2_r])
                kb.op(kb.dve, lambda: nc.vector.tensor_tensor(out=t1[:, :], in0=t1[:, :], in1=pa[:, :], op=ALU.mult), reads=[t1_r, pa_r], writes=[t1_r])
                kb.op(kb.dve, lambda: nc.vector.tensor_tensor(out=t2[:, :], in0=t2[:, :], in1=pb[:, :], op=ALU.mult), reads=[t2_r, pb_r], writes=[t2_r])
                kb.op(kb.dve, lambda: nc.vector.tensor_tensor(out=mg[:, dc, :], in0=t1[:, :], in1=t2[:, :], op=ALU.add), reads=[t1_r, t2_r], writes=[mg_r[dc]])
            for dc in range(8):
                (wo,), w_r = P.load_w([dr["wo"][dc]])
                pz, pz_r = P.bank()
                for c in range(8):
                    kb.op(kb.pe, lambda c=c: nc.tensor.matmul(pz[:, :], lhsT=wo[:, c, :], rhs=mg[:, c, :], start=(c == 0), stop=(c == 7)),
                          reads=[w_r, mg_r[c]], writes=[pz_r], sig=(c == 7))
                kb.op(kb.dve, lambda: nc.vector.tensor_tensor(out=x[:, dc, :], in0=x[:, dc, :], in1=pz[:, :], op=ALU.add), reads=[x_r, pz_r], writes=[x_r])
            P.rmsnorm(x, x_r, h, h_r, TT, vb + 16, sq, sq_r, rstd, rstd_r)
            P.ffn(x, x_r, h, h_r, TT, dr["f2_wg"], dr["f2_wu"], dr["f2_wd"], aT, aT_r, sq, sq_r)
            la = l + 1
            if last:
                P.rmsnorm(x, x_r, None, None, TT, 56, sq, sq_r, rstd, rstd_r, out_f32=x, out_r=x_r)
                kb.dma(P.stsem, colsl(dr["out"], t0), x[:, :, :], reads=[x_r], eng=stq)
                continue
        vb = vbase(la)
        P.rmsnorm(x, x_r, h, h_r, TT, vb + 0, sq, sq_r, rstd, rstd_r)
        P.ffn(x, x_r, h, h_r, TT, dr["f1_wg"], dr["f1_wu"], dr["f1_wd"], aT, aT_r, sq, sq_r)
        kb.dma(P.stsem, colsl(dr["xs_out"], t0), x[:, :, :], reads=[x_r], eng=stq)
        P.rmsnorm(x, x_r, h, h_r, TT, vb + 8, sq, sq_r, rstd, rstd_r)
        kb.dma(hosem, colsl(dr["h2T_out"], t0), h[:, :, :], reads=[h_r], eng=stq)
        W = dr["win_a"]
        for j, c0 in enumerate((C_KC, C_VC, C_KSL, C_KWN)):
            (wc,), w_r = P.load_w([W[WIN_IDX[c0]]])
            ps, ps_r = P.bank()
            for c in range(8):
                kb.op(kb.pe, lambda c=c: nc.tensor.matmul(ps[:, :], lhsT=wc[:, c, :], rhs=h[:, c, :], start=(c == 0), stop=(c == 7)),
                      reads=[w_r, h_r], writes=[ps_r], sig=(c == 7))
            k = j % 2
            kb.op(kb.act, lambda k=k: nc.scalar.copy(out=kvst[k][:, :], in_=ps[:, :]), reads=[ps_r], writes=[kvst_r[k]])
            kb.dma(osems[k], dr["kvT_out"][j, :, t0:t0 + TT], kvst[k][:, :], reads=[kvst_r[k]], eng=stq)
        for j in range(4):
            (wc,), w_r = P.load_w([W[WIN_IDX[C_U + j * 128]]])
            ps, ps_r = P.bank()
            for c in range(8):
                kb.op(kb.pe, lambda c=c: nc.tensor.matmul(ps[:, :], lhsT=wc[:, c, :], rhs=h[:, c, :], start=(c == 0), stop=(c == 7)),
                      reads=[w_r, h_r], writes=[ps_r], sig=(c == 7))
            k = j % 2
            kb.op(kb.act, lambda k=k: nc.scalar.copy(out=ust[k][:, :], in_=ps[:, :]), reads=[ps_r], writes=[ust_r[k]])
            kb.dma(usems[k], dr["uT_out"][j * 128:(j + 1) * 128, t0:t0 + TT], ust[k][:, :], reads=[ust_r[k]], eng=stq)
        (wv1, wv2), w_r = P.load_w([W[WIN_IDX[C_VSL]], W[WIN_IDX[C_VWN]]])
        for tb in range(TT // 128):
            ps, ps_r = P.bank()
            for wi, wv in enumerate((wv1, wv2)):
                for c in range(8):
                    kb.op(kb.pe, lambda c=c, wv=wv, wi=wi: nc.tensor.matmul(ps[:, wi * 128:(wi + 1) * 128], lhsT=h[:, c, tb * 128:(tb + 1) * 128], rhs=wv[:, c, :],
                                                                         start=(c == 0), stop=(c == 7)),
                          reads=[w_r, h_r], writes=[ps_r], sig=(c == 7))
            kb.op(kb.act, lambda: nc.scalar.copy(out=vst[:, tb, :], in_=ps[:, 0:256]), reads=[ps_r], writes=[vst_r])
        kb.dma(vsem, dr["vtok_out"][t0:t0 + TT, :].rearrange("(tb p) c -> p tb c", p=128), vst[:, :, :], reads=[vst_r], eng=stq)


def attn_specs():
    BI = "ExternalInput"
    specs = {
        "vecs": ((128, 64), F32, BI),
        "h2T": ((D, NTOK), BF16, BI),
        "kvall": ((4, 128, SEQ + 32), BF16, BI),
        "vall": ((SEQ, 256), BF16, BI),
        "kwin": ((128, 4, 1024), BF16, BI),
        "vwin": ((4, 1024, 128), BF16, BI),
        "win": ((len(WIN_STARTS), 128, 8, 128), F32, BI), "win_gn": ((D, 48), F32, BI),
        "ck_w1": ((2, 128, 16, 128), F32, BI), "ck_w2": ((256, 64), F32, BI),
        "cv_w1": ((2, 128, 16, 128), F32, BI), "cv_w2": ((256, 64), F32, BI),
        "pecol": ((128, 16), F32, BI),
        "qal": ((3, 16, NTOK), BF16, BI),
        "kbs": ((128, 16 * 64), F32, BI), "kbc": ((128, 64), F32, BI), "kbw": ((128, 4 * 8 * 16), F32, BI),
        "cmpm": ((128, 2, 512), BF16, BI), "cm": ((128, 16, 512), BF16, BI), "wm": ((128, 8, 512), BF16, BI),
        "addm": ((128, 16, 128), BF16, BI), "vneg": ((128, 16, 128), BF16, BI),
        "karows": ((64, SEQ), BF16, BI), "ones3": ((3, 1024), BF16, BI), "ovl": ((128, 4, 128), BF16, BI), "identb": ((128, 128), BF16, BI),
        "onsaT": ((D, NTOK), BF16, "ExternalOutput"),
    }
    return specs


def build_attn():
    P = Prog(attn_specs(), WST=2048)
    attn_body(P)
    return P.finish()


def attn_body(P):
    kb, nc, dr = P.kb, P.nc, P.dr
    ld = P.ldsem

    def const(name, shape, dt, src):
        t = kb.sb(name, shape, dt)
        r = Res(name)
        kb.dma(ld, t[:], src, writes=[r])
        return t, r

    CM, CM_r = const("CM", [128, 4 if MONO else 16, 512], BF16, dr["cm"][:, :, :])
    WM, WM_r = const("WM", [128, 8, 512], BF16, dr["wm"][:, :, :])
    CPM, CPM_r = const("CPM", [128, 5 if MONO else 2, 512], BF16, dr["cmpm"][:, :, :])
    if MONO:
        ADM = kb.sb("ADM", [128, 4, 128], BF16); ADM_r = Res("ADM")
        VNG = kb.sb("VNG", [128, 4, 128], BF16); VNG_r = Res("VNG")
        KBW = kb.sb("KBW", [128, 128], F32); KBW_r = Res("KBW")
        pisem = kb.new_sem("pisem")
    else:
        ADM, ADM_r = const("ADM", [128, 16, 128], BF16, dr["addm"][:, :, :])
        VNG, VNG_r = const("VNG", [128, 16, 128], BF16, dr["vneg"][:, :, :])
        KBW, KBW_r = const("KBW", [128, 512], F32, dr["kbw"][:, :])
    KBS, KBS_r = const("KBS", [128, 1024], F32, dr["kbs"][:, :])
    KBC, KBC_r = const("KBC", [128, 64], F32, dr["kbc"][:, :])
    IDB, IDB_r = const("IDB", [128, 128], BF16, dr["identb"][:, :])
    PEC, PEC_r = const("PEC", [128, 16], F32, dr["pecol"][:, :])

    KA = kb.sb("KA", [128, SEQ], BF16); KA_r = Res("KA")
    VA = kb.sb("VA", [128, 64, 65], BF16); VA_r = Res("VA")
    KW = kb.sb("KW", [128, 1024], BF16); KW_r = Res("KW")
    VW = kb.sb("VW", [128, 8, 65], BF16); VW_r = Res("VW")
    KC = kb.sb("KC", [128, 2, 512], BF16); KC_r = Res("KC")
    VC = kb.sb("VC", [128, 4, 2, 193], BF16); VC_r = Res("VC")
    kb.op(kb.pool, lambda: nc.gpsimd.memset(KW[0:64, :], 0.0), writes=[KW_r])
    kb.op(kb.pool, lambda: nc.gpsimd.memset(VW[:, :, 0:64], 0.0), writes=[VW_r])
    kb.dma(ld, KA[64:128, :], dr["karows"][:, :], writes=[KA_r])
    kb.op(kb.pool, lambda: nc.gpsimd.memset(KW[64:128, :], 0.0), writes=[KW_r])
    kb.op(kb.pool, lambda: nc.gpsimd.memset(KC[64:128, :, :], 0.0), writes=[KC_r])
    kb.dma(ld, KW[124:127, :], dr["ones3"][:, :], writes=[KW_r])
    kb.dma(ld, KC[124:127, :, :], dr["ones3"][:, :].rearrange("r (g n) -> r g n", n=512), writes=[KC_r])
    kb.op(kb.pool, lambda: nc.gpsimd.memset(VA[:, :, 64:65], 1.0), writes=[VA_r])
    kb.op(kb.pool, lambda: nc.gpsimd.memset(VW[:, :, 64:65], 1.0), writes=[VW_r])
    kb.op(kb.pool, lambda: nc.gpsimd.memset(VC[:, :, :, 64:65], 1.0), writes=[VC_r])
    for g in range(2):
        kb.dma(ld, VC[:, :, g, 65:193], dr["ovl"][:, :, :], writes=[VC_r])

    ces = ExitStack()
    KC2 = kb.sb("KC2", [128, SEQ + 16], BF16, es=ces); KC2_r = Res("KC2")
    zb = kb.sb("zb", [128, 512], F32, es=ces); zb_r = Res()
    s2 = kb.sb("s2", [128, 512], F32, es=ces); s2_r = Res()
    hid = kb.sb("hid", [128, 2, 512], BF16, es=ces); hid_r = [Res(), Res()]
    pecb = kb.sb("pecb", [128, 16], BF16, es=ces); pecb_r = Res()
    bj = kb.sb("bj", [128, 1], F32, es=ces); bj_r = Res()
    kb.op(kb.dve, lambda: nc.vector.tensor_copy(out=pecb[:, :], in_=PEC[:, :]), reads=[PEC_r], writes=[pecb_r])
    c2sem = kb.new_sem("c2sem")
    if MONO or QSEL:
        kb.op(kb.pool, lambda: nc.gpsimd.memset(KC2[0:64, SEQ:SEQ + 16], 0.0), writes=[KC2_r])
        kb.op(kb.pool, lambda: nc.gpsimd.memset(KC2[64:128, SEQ - 1:SEQ + 16], 0.0), writes=[KC2_r])
    for kv in range(2):
        w1 = dr["ck_w1" if kv == 0 else "cv_w1"]
        w2 = dr["ck_w2" if kv == 0 else "cv_w2"]
        for g in range(2):
            if MONO or QSEL:
                kb.dma(c2sem, KC2[0:64, 0:SEQ], dr["kvall"][kv, g * 64:(g + 1) * 64, 0:SEQ], writes=[KC2_r])
                kb.dma(c2sem, KC2[64:128, 0:SEQ - 1], dr["kvall"][kv, g * 64:(g + 1) * 64, 1:SEQ], writes=[KC2_r])
            else:
                kb.dma(c2sem, KC2[0:64, :], dr["kvall"][kv, g * 64:(g + 1) * 64, 0:SEQ + 16], writes=[KC2_r])
                kb.dma(c2sem, KC2[64:128, :], dr["kvall"][kv, g * 64:(g + 1) * 64, 1:SEQ + 17], writes=[KC2_r])
            for jc in range(2):
                (w1v,), w_r = P.load_w([w1[jc]])
                pb_, pb_r = P.bank()
                for lp in range(16):
                    kb.op(kb.pe, lambda lp=lp: nc.tensor.matmul(pb_[:, 0:1], lhsT=w1v[:, lp, :], rhs=pecb[:, lp:lp + 1], start=(lp == 0), stop=(lp == 15)),
                          reads=[w_r, pecb_r], writes=[pb_r], sig=(lp == 15))
                kb.op(kb.act, lambda: nc.scalar.copy(out=bj[:, :], in_=pb_[:, 0:1]), reads=[pb_r], writes=[bj_r])
                ps, ps_r = P.bank()
                for lp in range(16):
                    kb.op(kb.pe, lambda lp=lp: nc.tensor.matmul(ps[:, :], lhsT=w1v[:, lp, :], rhs=KC2[:, 2 * lp:2 * lp + 16 * 511 + 1:16], start=(lp == 0), stop=(lp == 15)),
                          reads=[w_r, KC2_r], writes=[ps_r], sig=(lp == 15))
                kb.op(kb.act, lambda: nc.scalar.activation(out=zb[:, :], in_=ps[:, :], func=AF.Identity, bias=bj[:, 0:1]), reads=[ps_r, bj_r], writes=[zb_r])
                kb.op(kb.act, lambda: nc.scalar.activation(out=s2[:, :], in_=zb[:, :], func=AF.Square), reads=[zb_r], writes=[s2_r])
                kb.op(kb.dve, lambda: nc.vector.tensor_scalar(out=s2[:, :], in0=s2[:, :], scalar1=0.044715, scalar2=1.0, op0=ALU.mult, op1=ALU.add), reads=[s2_r], writes=[s2_r])
                kb.op(kb.dve, lambda: nc.vector.tensor_tensor(out=s2[:, :], in0=s2[:, :], in1=zb[:, :], op=ALU.mult), reads=[s2_r, zb_r], writes=[s2_r])
                kb.op(kb.act, lambda: nc.scalar.activation(out=s2[:, :], in_=s2[:, :], func=AF.Sigmoid, scale=1.5957691216057308), reads=[s2_r], writes=[s2_r])
                kb.op(kb.dve, lambda jc=jc: nc.vector.tensor_tensor(out=hid[:, jc, :], in0=s2[:, :], in1=zb[:, :], op=ALU.mult), reads=[s2_r, zb_r], writes=[hid_r[jc]])
            (w2v,), w_r = P.load_w([w2[:, :]])
            if kv == 0:
                ps, ps_r = P.bank()
                for jc in range(2):
                    kb.op(kb.pe, lambda jc=jc: nc.tensor.matmul(ps[0:64, :], lhsT=w2v[:, jc, :], rhs=hid[:, jc, :], start=(jc == 0), stop=(jc == 1)),
                          reads=[w_r, hid_r[jc]], writes=[ps_r], sig=(jc == 1))
                kb.op(kb.act, lambda g=g: nc.scalar.copy(out=KC[0:64, g, :], in_=ps[0:64, :]), reads=[ps_r], writes=[KC_r])
            else:
                for nt in range(4):
                    ps, ps_r = P.bank()
                    for jc in range(2):
                        kb.op(kb.pe, lambda jc=jc, nt=nt: nc.tensor.matmul(ps[:, 0:64], lhsT=hid[:, jc, nt * 128:(nt + 1) * 128], rhs=w2v[:, jc, :], start=(jc == 0), stop=(jc == 1)),
                              reads=[w_r, hid_r[jc]], writes=[ps_r], sig=(jc == 1))
                    kb.op(kb.act, lambda g=g, nt=nt: nc.scalar.copy(out=VC[:, nt, g, 0:64], in_=ps[:, 0:64]), reads=[ps_r], writes=[VC_r])
    kb.barrier()
    ces.close()

    Q = kb.sb("Q", [128, 3, 8, 512], BF16); Q_r = [Res("Q%d" % i) for i in range(8)]
    kb.op(kb.pool, lambda: nc.gpsimd.memset(Q[64:128, :, :, :], 0.0), writes=Q_r)
    hT = kb.sb("hT", [128, 8, 512], BF16); hT_r = Res("hT")
    if QSEL:
        qstg = kb.sb("qstg", [128, 8, 512], BF16); qstg_r = Res("qstg")
    gat = kb.sb("gat", [128, 4, 48], F32); gat_r = Res("gat")
    cst = kb.sb("cst", [128, 4, 8, 64], F32); cst_r = [Res() for _ in range(8)]
    crd = kb.sb("crd", [128, 4, 8], F32); crd_r = [Res() for _ in range(8)]
    sc = kb.sb("sc", [128, 4, 128], F32); sc_r = Res("sc")
    pT = [kb.sb("pT%d" % i, [128, 512], BF16) for i in range(4)]; pT_r = [Res() for _ in range(4)]
    tmp = [kb.sb("tmp%d" % i, [128, 512], F32) for i in range(3)]; tmp_r = [Res() for _ in range(3)]
    onb = kb.sb("onb", [128, 4, 1024], BF16); onb_r = Res("onb")
    ost = [kb.sb("ost%d" % i, [128, 512], BF16) for i in range(2)]; ost_r = [Res() for _ in range(2)]
    smb = kb.sb("smb", [128, 4, 128], F32); smb_r = [Res() for _ in range(4)]
    wa = kb.sb("wa", [128, 4, 128], F32); wa_r = [Res() for _ in range(4)]
    wb = kb.sb("wb", [128, 4, 128], F32); wb_r = [Res() for _ in range(4)]
    m8 = kb.sb("m8", [128, 4, 8], F32); m8_r = [Res() for _ in range(4)]
    m8b = kb.sb("m8b", [128, 4, 8], F32); m8b_r = [Res() for _ in range(4)]
    nqv = kb.sb("nqv", [128, 4, 3, 128], BF16); nq_r = Res()
    kb.op(kb.pool, lambda: nc.gpsimd.memset(nqv[:, :, :, :], 0.0), writes=[nq_r])
    dd = kb.sb("dd", [128, 2, 4], F32); dd_r = Res()
    cf = kb.sb("cf", [128, 3, 4], F32); cf_r = Res()
    oh = kb.sb("oh", [128, 64], F32); oh_r = Res()
    hsem = kb.new_sem("hsem"); ksem = kb.new_sem("ksem"); vsem = kb.new_sem("vsem")
    kwsem = kb.new_sem("kwsem"); vwsem = kb.new_sem("vwsem"); qsem = kb.new_sem("qsem")
    osems = [kb.new_sem("os%d" % i) for i in range(2)]
    W = dr["win"]
    st = {"s": 0, "p": 0, "t": 0}

    def sbank():
        k = st["s"] % 3
        st["s"] += 1
        return P.psf[k], P.psf_r[k]

    def pbuf():
        k = st["p"] % 4
        st["p"] += 1
        return pT[k], pT_r[k]

    def tbuf():
        k = st["t"] % 3
        st["t"] += 1
        return tmp[k], tmp_r[k]

    def exp_tile(ps, ps_r, bias_ap, bias_r, mask_ap, mask_r, qa=0, qb=512):
        p, p_r = pbuf()
        if mask_ap is not None:
            tm, tm_r = tbuf()
            kb.op(kb.dve, lambda: nc.vector.tensor_tensor(out=tm[:, qa:qb], in0=ps[:, qa:qb], in1=mask_ap[:, qa:qb], op=ALU.add), reads=[ps_r, mask_r], writes=[tm_r])
            kb.op(kb.act, lambda: nc.scalar.activation(out=p[:, qa:qb], in_=tm[:, qa:qb], func=AF.Exp, bias=bias_ap), reads=[tm_r, bias_r], writes=[p_r])
        else:
            kb.op(kb.act, lambda: nc.scalar.activation(out=p[:, qa:qb], in_=ps[:, qa:qb], func=AF.Exp, bias=bias_ap), reads=[ps_r, bias_r], writes=[p_r])
        return p, p_r

    NI = ntok() // 512
    KT0 = 4 if MONO else 16
    for g in range(2):
        kb.dma(ksem, KA[0:64, :], dr["kvall"][2, g * 64:(g + 1) * 64, 0:SEQ], writes=[KA_r])
        for kq in range(16):
            kb.dma(vsem, VA[:, kq * 4:(kq + 1) * 4, 0:64],
                   dr["vall"][kq * 512:(kq + 1) * 512, g * 64:(g + 1) * 64].rearrange("(kt p) d -> p kt d", p=128), writes=[VA_r])
        for i in range(NI):
            q0 = i * 512
            if MONO:
                kb.dma(pisem, ADM[:, :, :], dr["addm"][i], writes=[ADM_r])
                kb.dma(pisem, VNG[:, :, :], dr["vneg"][i], writes=[VNG_r])
                kb.dma(pisem, KBW[:, :], dr["kbw"][i], writes=[KBW_r])
                cmp_tiles = [(nt, (i - 4 * nt) if (i - 4 * nt) <= 4 else None) for nt in range((32 * (i + 1) - 2) // 128 + 1)]
            else:
                cmp_tiles = [(nt, (nt - i + 1) if nt - i >= -1 else None) for nt in range(i + 1)]
            if QSEL:
                P.select_tile(hT[:, :, :], hT_r, [dr["h2T"][:, (4 * i + c_) * 512:(4 * i + c_ + 1) * 512].rearrange("(c p) n -> p c n", p=128) for c_ in range(4)],
                              qstg[:, :, :], qstg_r, hsem, 0, 128)
            else:
                kb.dma(hsem, hT[:, :, :], dr["h2T"][:, q0:q0 + 512].rearrange("(c p) n -> p c n", p=128), writes=[hT_r])
            (wgn,), w_r = P.load_w([dr["win_gn"][:, :]])
            for qs in range(4):
                ps, ps_r = sbank()
                for c in range(8):
                    kb.op(kb.pe, lambda c=c: nc.tensor.matmul(ps[:, 0:48], lhsT=hT[:, c, qs * 128:(qs + 1) * 128], rhs=wgn[:, c, :], start=(c == 0), stop=(c == 7)),
                          reads=[w_r, hT_r], writes=[ps_r], sig=(c == 7))
                kb.op(kb.act, lambda: nc.scalar.activation(out=gat[:, qs, :], in_=ps[:, 0:48], func=AF.Sigmoid), reads=[ps_r], writes=[gat_r])
            for hp in range(4):
                (wq,), w_r = P.load_w([W[g * 4 + hp]])
                for hh in range(2):
                    hl = hp * 2 + hh
                    ps, ps_r = sbank()
                    for c in range(8):
                        kb.op(kb.pe, lambda c=c: nc.tensor.matmul(ps[0:64, :], lhsT=wq[:, c, hh * 64:(hh + 1) * 64], rhs=hT[:, c, :], start=(c == 0), stop=(c == 7)),
                              reads=[w_r, hT_r], writes=[ps_r], sig=(c == 7))
                    kb.op(kb.dve, lambda: nc.vector.tensor_scalar(out=Q[0:64, 0, hl, :], in0=ps[0:64, :], scalar1=0.125, scalar2=None, op0=ALU.mult), reads=[ps_r], writes=[Q_r[hl]])
                    kb.op(kb.pool, lambda: nc.gpsimd.tensor_copy(out=Q[0:64, 1:3, hl, :], in_=Q[0:64, 0:1, hl, :].to_broadcast([64, 2, 512])),
                          reads=[Q_r[hl]], writes=[Q_r[hl]])
            if MONO:
                klo = max(q0 - 512, 0)
                kb.dma(kwsem, KW[0:64, 1024 - (q0 + 512 - klo):1024], dr["kvall"][3, g * 64:(g + 1) * 64, klo:q0 + 512], writes=[KW_r])
                w0 = 8 - (q0 + 512 - klo) // 128
                kb.dma(vwsem, VW[:, w0:8, 0:64], dr["vall"][klo:q0 + 512, 128 + g * 64:128 + (g + 1) * 64].rearrange("(w p) d -> p w d", p=128), writes=[VW_r])
            elif QSEL:
                for half in range(2):
                    sts = [4 * i + c_ - 1 + half for c_ in range(4)]
                    P.select_tile(KW[0:64, half * 512:(half + 1) * 512], KW_r,
                                  [dr["kvall"][3, g * 64:(g + 1) * 64, st_ * 512:(st_ + 1) * 512] if st_ >= 0 else None for st_ in sts],
                                  qstg[0:64, 0, :], qstg_r, kwsem, 0, 64)
                    P.select_tile(VW[:, half * 4:(half + 1) * 4, 0:64], VW_r,
                                  [dr["vall"][st_ * 512:(st_ + 1) * 512, 128 + g * 64:128 + (g + 1) * 64].rearrange("(w p) d -> p w d", p=128) if st_ >= 0 else None for st_ in sts],
                                  qstg[:, 1, 0:256].rearrange("p (w d) -> p w d", d=64), qstg_r, vwsem, 0, 128)
            else:
                kb.dma(kwsem, KW[0:64, :], dr["kwin"][g * 64:(g + 1) * 64, i, :], writes=[KW_r])
                kb.dma(vwsem, VW[:, :, 0:64], dr["vwin"][i, :, g * 64:(g + 1) * 64].rearrange("(w p) d -> p w d", p=128), writes=[VW_r])
            for v_ in range(3):
                kb.dma(qsem, Q[124:127, v_, :, :], dr["qal"][:, g * 8:(g + 1) * 8, q0:q0 + 512], writes=Q_r)
            for hl in range(8):
                h = g * 8 + hl
                accs = [(P.psf[3 + 2 * (hl % 2)], P.psf_r[3 + 2 * (hl % 2)]), (P.psf[4 + 2 * (hl % 2)], P.psf_r[4 + 2 * (hl % 2)])]
                for a, a_r in accs:
                    kb.op(kb.dve, lambda: nc.vector.memset(a[:, 0:386], 0.0), writes=[a_r])
                cq_ = []
                for cu in list(cmp_tiles) + [None, None]:
                    if cu is not None:
                        nt, mi_ = cu
                        ps, ps_r = sbank()
                        kb.op(kb.pe, lambda: nc.tensor.matmul(ps[:, :], lhsT=KC[0:128, g, nt * 128:(nt + 1) * 128], rhs=Q[0:128, 0, hl, :], start=True, stop=True),
                              reads=[KC_r, Q_r[hl]], writes=[ps_r])
                        e, e_r = exp_tile(ps, ps_r, KBC[:, h * 4 + nt:h * 4 + nt + 1], KBC_r,
                                          CPM[:, mi_, :] if mi_ is not None else None, CPM_r)
                        cq_.append((nt, e, e_r))
                    if cq_ and (len(cq_) > 2 or cu is None):
                        nt_, e, e_r = cq_.pop(0)
                        for qs in range(4):
                            a, a_r = accs[qs // 2]
                            o0 = (qs % 2) * 193
                            kb.op(kb.pe, lambda: nc.tensor.matmul(a[:, o0:o0 + 193], lhsT=e[:, qs * 128:(qs + 1) * 128], rhs=VC[:, nt_, g, :], start=False, stop=(nt_ == cmp_tiles[-1][0]), skip_group_check=True),
                                  reads=[e_r, VC_r], writes=[a_r], sig=(qs == 3 or qs == 1))
                assert not cq_
                for half in range(2):
                    a, a_r = accs[half]
                    av = a[:, 0:386].rearrange("p (q c) -> p q c", c=193)
                    kb.op(kb.dve, lambda: nc.vector.tensor_scalar(out=crd[:, 2 * half:2 * half + 2, hl], in0=av[:, :, 64], scalar1=1e-30, scalar2=None, op0=ALU.max),
                          reads=[a_r], writes=[crd_r[hl]])
                    kb.op(kb.dve, lambda: nc.vector.tensor_copy(out=cst[:, 2 * half:2 * half + 2, hl, :], in_=av[:, :, 0:64]), reads=[a_r], writes=[cst_r[hl]])
                kb.op(kb.dve, lambda: nc.vector.reciprocal(out=crd[:, :, hl], in_=crd[:, :, hl]), reads=[crd_r[hl]], writes=[crd_r[hl]])
                for qs in range(4):
                    a, a_r = accs[qs // 2]
                    o0 = (qs % 2) * 193
                    if hl == 0:
                        kb.op(kb.dve, lambda: nc.vector.tensor_scalar(out=sc[:, qs, :], in0=a[:, o0 + 65:o0 + 193], scalar1=crd[:, qs, hl:hl + 1], scalar2=None, op0=ALU.mult),
                              reads=[a_r, crd_r[hl]], writes=[sc_r])
                    else:
                        kb.op(kb.dve, lambda: nc.vector.scalar_tensor_tensor(out=sc[:, qs, :], in0=a[:, o0 + 65:o0 + 193], scalar=crd[:, qs, hl:hl + 1], in1=sc[:, qs, :],
                                                                              op0=ALU.mult, op1=ALU.add), reads=[a_r, crd_r[hl], sc_r], writes=[sc_r])
            c0_ = 0 if MONO else i * 4
            kb.op(kb.dve, lambda: nc.vector.tensor_tensor(out=smb[:, :, :], in0=sc[:, :, :], in1=ADM[:, c0_:c0_ + 4, :], op=ALU.add), reads=[sc_r, ADM_r], writes=smb_r)
            for qs in range(4):
                kb.op(kb.dve, lambda: nc.vector.max(out=m8[:, qs, :], in_=smb[:, qs, :]), reads=[smb_r[qs]], writes=[m8_r[qs]])
            for qs in range(4):
                kb.op(kb.dve, lambda: nc.vector.match_replace(out=wa[:, qs, :], in_to_replace=m8[:, qs, :], in_values=smb[:, qs, :], imm_value=-1e30),
                      reads=[smb_r[qs], m8_r[qs]], writes=[wa_r[qs]])
            for qs in range(4):
                kb.op(kb.dve, lambda: nc.vector.max(out=m8b[:, qs, :], in_=wa[:, qs, :]), reads=[wa_r[qs]], writes=[m8b_r[qs]])
            for qs in range(4):
                kb.op(kb.dve, lambda: nc.vector.match_replace(out=wb[:, qs, :], in_to_replace=m8b[:, qs, :], in_values=wa[:, qs, :], imm_value=-1e30),
                      reads=[wa_r[qs], m8b_r[qs]], writes=[wb_r[qs]])
            kb.op(kb.dve, lambda: nc.vector.tensor_tensor(out=wa[:, :, :], in0=smb[:, :, :], in1=wb[:, :, :], op=ALU.subtract), reads=smb_r + wb_r, writes=wa_r)
            kb.op(kb.dve, lambda: nc.vector.tensor_scalar(out=wa[:, :, :], in0=wa[:, :, :], scalar1=1.0, scalar2=-NEG, op0=ALU.min, op1=ALU.mult), reads=wa_r, writes=wa_r)
            for v_ in range(3):
                nb_ = 60 if v_ < 2 else 8
                kb.op(kb.dve, lambda: nc.vector.scalar_tensor_tensor(out=nqv[:, :, v_, 64:64 + nb_], in0=wa[:, :, 60 * v_:60 * v_ + nb_], scalar=NEG,
                                                                      in1=VNG[:, c0_:c0_ + 4, 60 * v_:60 * v_ + nb_], op0=ALU.add, op1=ALU.min),
                      reads=wa_r + [VNG_r], writes=[nq_r])
            for qs in range(4):
                for v_ in range(3):
                    kb.op(kb.pe, lambda: nc.tensor.transpose(out=P.psb[:, v_ * 128:(v_ + 1) * 128], in_=nqv[:, qs, v_, :], identity=IDB[:, :]),
                          reads=[nq_r, IDB_r], writes=[P.psb_r])
                for v_ in range(3):
                    src_ = P.psb[64:124, v_ * 128:(v_ + 1) * 128].rearrange("p (o n) -> p o n", o=1).to_broadcast([60, 8, 128])
                    kb.op(kb.dve, lambda: nc.vector.tensor_copy(out=Q[64:124, v_, :, qs * 128:(qs + 1) * 128], in_=src_),
                          reads=[P.psb_r], writes=Q_r)
            for hl in range(8):
                h = g * 8 + hl
                aS, aS_r = P.psf[3 + 2 * (hl % 2)], P.psf_r[3 + 2 * (hl % 2)]
                aW, aW_r = P.psf[4 + 2 * (hl % 2)], P.psf_r[4 + 2 * (hl % 2)]
                nkt = KT0 * (i + 1)
                if hl == 0:
                    kb.op(kb.dve, lambda: nc.vector.memset(aS[:, 0:260], 0.0), writes=[aS_r])
                    kb.op(kb.dve, lambda: nc.vector.memset(aW[:, 0:260], 0.0), writes=[aW_r])
                units = [("s", kt) for kt in range(nkt)] + [("w", w) for w in range(8)]
                SKEW = 2
                pendq = []
                for u in units + [None] * SKEW:
                    cur = None
                    if u is not None:
                        kind, ix = u
                        ps, ps_r = sbank()
                        qa, qb = 0, 512
                        if kind == "s":
                            r_ = ix - KT0 * i
                            if MONO and r_ >= 0:
                                qa = 128 * r_
                            kb.op(kb.pe, lambda: nc.tensor.matmul(ps[:, qa:qb], lhsT=KA[0:128, ix * 128:(ix + 1) * 128], rhs=Q[0:128, (2 * ix) // 60, hl, qa:qb], start=True, stop=True),
                                  reads=[KA_r, Q_r[hl]], writes=[ps_r])
                            p, p_r = exp_tile(ps, ps_r, KBS[:, h * 64 + ix:h * 64 + ix + 1], KBS_r, CM[:, r_, :] if r_ >= 0 else None, CM_r, qa, qb)
                        else:
                            if ix < 4:
                                qb = 128 * (ix + 1)
                            else:
                                qa = 128 * (ix - 4)
                            kb.op(kb.pe, lambda: nc.tensor.matmul(ps[:, qa:qb], lhsT=KW[0:128, ix * 128:(ix + 1) * 128], rhs=Q[0:128, 0, hl, qa:qb], start=True, stop=True),
                                  reads=[KW_r, Q_r[hl]], writes=[ps_r])
                            cb = (ix * 16 + h) if MONO else ((i * 8 + ix) * 16 + h)
                            p, p_r = exp_tile(ps, ps_r, KBW[:, cb:cb + 1], KBW_r, WM[:, ix, :], WM_r, qa, qb)
                        cur = (kind, ix, p, p_r, qa, qb)
                        pendq.append(cur)
                    if pendq and (len(pendq) > SKEW or u is None):
                        kind_, ix_, pp, pp_r, qa_, qb_ = pendq.pop(0)
                        qss = list(range(qa_ // 128, qb_ // 128))
                        for qs in qss:
                            if kind_ == "s":
                                kb.op(kb.pe, lambda: nc.tensor.matmul(aS[:, qs * 65:(qs + 1) * 65], lhsT=pp[:, qs * 128:(qs + 1) * 128], rhs=VA[:, ix_, :], start=False, stop=(ix_ == nkt - 1), skip_group_check=True),
                                      reads=[pp_r, VA_r], writes=[aS_r], sig=(qs == qss[-1]))
                            else:
                                kb.op(kb.pe, lambda: nc.tensor.matmul(aW[:, qs * 65:(qs + 1) * 65], lhsT=pp[:, qs * 128:(qs + 1) * 128], rhs=VW[:, ix_, :], start=False, stop=(ix_ == 7), skip_group_check=True),
                                      reads=[pp_r, VW_r], writes=[aW_r], sig=(qs == qss[-1]))
                assert not pendq
                if hl < 7:
                    nS, nS_r = P.psf[3 + 2 * ((hl + 1) % 2)], P.psf_r[3 + 2 * ((hl + 1) % 2)]
                    nW, nW_r = P.psf[4 + 2 * ((hl + 1) % 2)], P.psf_r[4 + 2 * ((hl + 1) % 2)]
                    kb.op(kb.dve, lambda: nc.vector.memset(nS[:, 0:260], 0.0), writes=[nS_r])
                    kb.op(kb.dve, lambda: nc.vector.memset(nW[:, 0:260], 0.0), writes=[nW_r])
                aSv = aS[:, 0:260].rearrange("p (q c) -> p q c", c=65)
                aWv = aW[:, 0:260].rearrange("p (q c) -> p q c", c=65)
                kb.op(kb.dve, lambda: nc.vector.tensor_scalar(out=dd[:, 0, :], in0=aSv[:, :, 64], scalar1=1e-30, scalar2=None, op0=ALU.max), reads=[aS_r], writes=[dd_r])
                kb.op(kb.dve, lambda: nc.vector.tensor_scalar(out=dd[:, 1, :], in0=aWv[:, :, 64], scalar1=1e-30, scalar2=None, op0=ALU.max), reads=[aW_r], writes=[dd_r])
                kb.op(kb.dve, lambda: nc.vector.reciprocal(out=dd[:, :, :], in_=dd[:, :, :]), reads=[dd_r], writes=[dd_r])
                kb.op(kb.dve, lambda: nc.vector.tensor_tensor(out=cf[:, 0, :], in0=crd[:, :, hl], in1=gat[:, :, h * 3 + 0], op=ALU.mult), reads=[crd_r[hl], gat_r], writes=[cf_r])
                kb.op(kb.dve, lambda: nc.vector.tensor_tensor(out=cf[:, 1, :], in0=dd[:, 0, :], in1=gat[:, :, h * 3 + 1], op=ALU.mult), reads=[dd_r, gat_r], writes=[cf_r])
                kb.op(kb.dve, lambda: nc.vector.tensor_tensor(out=cf[:, 2, :], in0=dd[:, 1, :], in1=gat[:, :, h * 3 + 2], op=ALU.mult), reads=[dd_r, gat_r], writes=[cf_r])
                for qs in range(4):
                    kb.op(kb.dve, lambda: nc.vector.tensor_scalar(out=oh[:, :], in0=cst[:, qs, hl, :], scalar1=cf[:, 0, qs:qs + 1], scalar2=None, op0=ALU.mult),
                          reads=[cst_r[hl], cf_r], writes=[oh_r])
                    kb.op(kb.dve, lambda: nc.vector.scalar_tensor_tensor(out=oh[:, :], in0=aS[:, qs * 65:qs * 65 + 64], scalar=cf[:, 1, qs:qs + 1], in1=oh[:, :], op0=ALU.mult, op1=ALU.add),
                          reads=[aS_r, cf_r, oh_r], writes=[oh_r])
                    kb.op(kb.dve, lambda: nc.vector.scalar_tensor_tensor(out=onb[:, qs, h * 64:(h + 1) * 64], in0=aW[:, qs * 65:qs * 65 + 64], scalar=cf[:, 2, qs:qs + 1], in1=oh[:, :],
                                                                          op0=ALU.mult, op1=ALU.add), reads=[aW_r, cf_r, oh_r], writes=[onb_r])
            for fc in range(4 * g, 4 * g + 4):
                k = fc % 2
                for qs in range(4):
                    kb.op(kb.pe, lambda: nc.tensor.transpose(out=P.psb[:, 0:128], in_=onb[:, qs, fc * 128:(fc + 1) * 128], identity=IDB[:, :]),
                          reads=[onb_r, IDB_r], writes=[P.psb_r])
                    kb.op(kb.dve, lambda: nc.vector.tensor_copy(out=ost[k][:, qs * 128:(qs + 1) * 128], in_=P.psb[:, 0:128]), reads=[P.psb_r], writes=[ost_r[k]])
                kb.dma(osems[k], dr["onsaT"][fc * 128:(fc + 1) * 128, q0:q0 + 512], ost[k][:, :], reads=[ost_r[k]], eng=(kb.act if (MONO or QSEL) else kb.sp))


BF = ml_dtypes.bfloat16
_CACHE = {}
USE_MONO = True


def _slopes():
    hh = np.arange(1, 17, dtype=np.float32)
    return np.exp2(-8.0 * hh / 16.0).astype(np.float32)


def _split3(v):
    v = v.astype(np.float32)
    hi = v.astype(BF)
    r = v - hi.astype(np.float32)
    mid = r.astype(BF)
    r2 = r - mid.astype(np.float32)
    lo = r2.astype(BF)
    return hi, mid, lo


def core_consts(cc):
    sl = _slopes()
    p = np.arange(128)
    q = np.arange(512)
    c = {}
    tabs = np.concatenate([(4 * i + cc) * 512 + q for i in range(4)]).astype(np.float32)
    v = -(sl[:, None] * tabs[None, :])
    hi, mid, lo = _split3(v)
    c["qal"] = np.ascontiguousarray(np.stack([hi, mid, lo], 0))
    kbs = np.zeros((128, 16, 64), np.float32)
    for h in range(16):
        kbs[:, h, :] = sl[h] * (np.arange(64)[None, :] * 128 + p[:, None]).astype(np.float32)
    c["kbs"] = kbs.reshape(128, 1024)
    kbc = np.zeros((128, 16, 4), np.float32)
    for h in range(16):
        kbc[:, h, :] = sl[h] * (16 * (np.arange(4)[None, :] * 128 + p[:, None]) + 31).astype(np.float32)
    c["kbc"] = kbc.reshape(128, 64)
    kbw = np.zeros((128, 4, 8, 16), np.float32)
    for i in range(4):
        T0 = (4 * i + cc) * 512
        for w in range(8):
            ka = T0 - 512 + w * 128 + p
            for h in range(16):
                kbw[:, i, w, h] = np.where(ka >= 0, sl[h] * ka.astype(np.float32), -30000.0)
    c["kbw"] = kbw.reshape(128, 512)
    cmpm = np.zeros((128, 2, 512), np.float32)
    for d in (-1, 0):
        vis = (2048 * d + 16 * p[:, None] + 31 - 512 * cc) <= q[None, :]
        cmpm[:, d + 1, :] = np.where(vis, 0.0, NEG)
    c["cmpm"] = cmpm.astype(BF)
    cm = np.zeros((128, 16, 512), np.float32)
    for r in range(16):
        vis = (128 * r + p[:, None]) <= (512 * cc + q[None, :])
        cm[:, r, :] = np.where(vis, 0.0, NEG)
    c["cm"] = cm.astype(BF)
    wm = np.zeros((128, 8, 512), np.float32)
    for w in range(8):
        dist = 512 + q[None, :] - 128 * w - p[:, None]
        wm[:, w, :] = np.where((dist >= 0) & (dist < 512), 0.0, NEG)
    c["wm"] = wm.astype(BF)
    addm = np.zeros((128, 16, 128), np.float32)
    vneg = np.zeros((128, 16, 128), np.float32)
    j = np.arange(128)
    for i in range(4):
        for qs in range(4):
            t = (4 * i + cc) * 512 + qs * 128 + p
            valid = (j[None, :] * 64) <= t[:, None]
            cur = t // 64
            forced = valid & ((j[None, :] == 0) | (j[None, :] == cur[:, None]) | (j[None, :] == cur[:, None] - 1))
            addm[:, i * 4 + qs, :] = np.where(forced, 8192.0, np.where(valid, 0.0, -8192.0))
            vneg[:, i * 4 + qs, :] = np.where(valid, 0.0, NEG)
    c["addm"] = addm.astype(BF)
    c["vneg"] = vneg.astype(BF)
    return c


def shared_consts():
    c = {}
    cols = np.arange(SEQ)
    kar = np.zeros((64, SEQ), np.float32)
    kar[0:60] = ((cols[None, :] // 64) % 60 == np.arange(60)[:, None])
    kar[60:63] = 1.0
    c["karows"] = kar.astype(BF)
    c["ones3"] = np.ones((3, 1024), np.float32).astype(BF)
    n = np.arange(512)
    cs = n[:, None] * 16
    ss = np.arange(128)[None, :] * 64
    ov = np.clip(np.minimum(cs + 32, ss + 64) - np.maximum(cs, ss), 0, None) / 32.0
    c["ovl"] = np.ascontiguousarray(ov.reshape(4, 128, 128).transpose(1, 0, 2)).astype(np.float32).astype(BF)
    c["identb"] = np.eye(128, dtype=np.float32).astype(BF)
    return c


def tile_w(Wm, starts=None):
    K = Wm.shape[0]
    if starts is None:
        starts = list(range(0, Wm.shape[1], 128))
    out = np.empty((len(starts), 128, K // 128, 128), np.float32)
    for j, c0 in enumerate(starts):
        out[j] = Wm[:, c0:c0 + 128].reshape(K // 128, 128, 128).transpose(1, 0, 2)
    return out


def _prog(key, fn):
    if key not in _CACHE:
        _CACHE[key] = fn()
    return _CACHE[key]


def _tok_index(cc):
    return np.concatenate([np.arange((4 * i + cc) * 512, (4 * i + cc + 1) * 512) for i in range(4)])


def kernel(x, ffn1_norm, ffn1_w_gate, ffn1_w_up, ffn1_w_down, mix_norm, w_in, cmp_pos,
           cmp_k_w1, cmp_k_w2, cmp_v_w1, cmp_v_w2, pool_w, pool_scale, w_branch_pool,
           w_branch_nsa, w_out, ffn2_norm, ffn2_w_gate, ffn2_w_up, ffn2_w_down, final_norm):
    f32 = lambda a: np.ascontiguousarray(np.asarray(a, dtype=np.float32))
    x = f32(x)
    W = {k: f32(v) for k, v in dict(ffn1_norm=ffn1_norm, ffn1_w_gate=ffn1_w_gate, ffn1_w_up=ffn1_w_up, ffn1_w_down=ffn1_w_down,
                                    mix_norm=mix_norm, w_in=w_in, cmp_pos=cmp_pos, cmp_k_w1=cmp_k_w1, cmp_k_w2=cmp_k_w2,
                                    cmp_v_w1=cmp_v_w1, cmp_v_w2=cmp_v_w2, pool_w=pool_w, pool_scale=pool_scale,
                                    w_branch_pool=w_branch_pool, w_branch_nsa=w_branch_nsa, w_out=w_out, ffn2_norm=ffn2_norm,
                                    ffn2_w_gate=ffn2_w_gate, ffn2_w_up=ffn2_w_up, ffn2_w_down=ffn2_w_down, final_norm=final_norm).items()}
    cores = list(range(8))
    vecs = np.zeros((128, 64), np.float32)
    for l in range(NL):
        b0 = vbase(l)
        vecs[:, b0:b0 + 8] = gain_layout(W["ffn1_norm"][l])
        vecs[:, b0 + 8:b0 + 16] = gain_layout(W["mix_norm"][l])
        vecs[:, b0 + 16:b0 + 24] = gain_layout(W["ffn2_norm"][l])
        vecs[:, b0 + 24:b0 + 28] = gain_layout(W["pool_scale"][l])
    vecs[:, 56:64] = gain_layout(W["final_norm"])
    if USE_MONO:
        return kernel_mono(W, x, vecs)
    tix = [_tok_index(c % 4) for c in cores]
    cc_consts = [core_consts(cc) for cc in range(4)]
    sh = shared_consts()

    TW = {}
    for l in range(NL):
        TW["win", l] = tile_w(W["w_in"][l], WIN_STARTS)
        for nm in ("ffn1_w_gate", "ffn1_w_up", "ffn1_w_down", "ffn2_w_gate", "ffn2_w_up", "ffn2_w_down", "w_branch_pool", "w_branch_nsa", "w_out",
                   "cmp_k_w1", "cmp_v_w1"):
            TW[nm, l] = tile_w(W[nm][l])

    def a_weights(l):
        return {"f1_wg": TW["ffn1_w_gate", l], "f1_wu": TW["ffn1_w_up", l], "f1_wd": TW["ffn1_w_down", l], "win_a": TW["win", l]}

    progA = _prog("A", lambda: build_tok(0, "A", False))
    in_maps = []
    for c in cores:
        m = {"xs_in": np.ascontiguousarray(x[c // 4, tix[c], :].T), "vecs": vecs}
        m.update(a_weights(0))
        in_maps.append(m)
    res = run_bass_kernel_spmd(progA, in_maps, core_ids=cores).results

    out = np.zeros((2, SEQ, D), np.float32)
    for l in range(NL):
        last = (l == NL - 1)
        kvall = np.zeros((2, 4, 128, SEQ + 32), BF)
        vall = np.zeros((2, SEQ, 256), BF)
        uall = np.zeros((2, 512, SEQ), np.float32)
        for c in cores:
            b = c // 4
            kvall[b][:, :, tix[c]] = np.asarray(res[c]["kvT_out"]).view(BF) if np.asarray(res[c]["kvT_out"]).dtype != BF else res[c]["kvT_out"]
            vall[b][tix[c], :] = np.asarray(res[c]["vtok_out"])
            uall[b][:, tix[c]] = np.asarray(res[c]["uT_out"])
        prog = _prog(("BCA", l, last), lambda: build_bca(l, last))
        pecol = np.ascontiguousarray(W["cmp_pos"][l].reshape(16, 2, 64).transpose(1, 2, 0).reshape(128, 16))
        in_maps = []
        for c in cores:
            b, cc = c // 4, c % 4
            kwin = np.zeros((128, 4, 1024), BF)
            vwin = np.zeros((4, 1024, 128), BF)
            uext = np.zeros((512, 4, 528), np.float32)
            for i in range(4):
                T0 = (4 * i + cc) * 512
                lo = max(T0 - 512, 0)
                kwin[:, i, 1024 - (T0 + 512 - lo):] = kvall[b][3][:, lo:T0 + 512]
                vwin[i, 1024 - (T0 + 512 - lo):, :] = vall[b][lo:T0 + 512, 128:256]
                lo = max(T0 - 16, 0)
                uext[:, i, 528 - (T0 + 512 - lo):] = uall[b][:, lo:T0 + 512]
            corr = np.ones((128, 4, 16), np.float32)
            if cc == 0:
                for gi, w in enumerate(POOLW):
                    corr[:, gi, :] = (w / np.minimum(np.arange(16) + 1.0, float(w)))[None, :]
            m = {"vecs": vecs, "h2T": np.asarray(res[c]["h2T_out"]), "kvall": kvall[b], "vall": vall[b], "kwin": kwin, "vwin": vwin,
                 "win": TW["win", l], "win_gn": np.ascontiguousarray(W["w_in"][l][:, C_GN:C_GN + 48]),
                 "ck_w1": TW["cmp_k_w1", l], "ck_w2": W["cmp_k_w2"][l], "cv_w1": TW["cmp_v_w1", l], "cv_w2": W["cmp_v_w2"][l],
                 "pecol": pecol,
                 "xs_in": np.asarray(res[c]["xs_out"]),
                 "f2_wg": TW["ffn2_w_gate", l], "f2_wu": TW["ffn2_w_up", l], "f2_wd": TW["ffn2_w_down", l],
                 "wpa": TW["w_branch_pool", l], "wnb": TW["w_branch_nsa", l], "wo": TW["w_out", l],
                 "poolw": W["pool_w"][l], "uext": uext, "corr": corr}
            m.update(cc_consts[cc])
            m.update(sh)
            if not last:
                m.update(a_weights(l + 1))
            in_maps.append(m)
        res = run_bass_kernel_spmd(prog, in_maps, core_ids=cores).results
    for c in cores:
        out[c // 4, tix[c], :] = np.asarray(res[c]["out"]).T
    return out


def build_mono():
    global MONO
    MONO = True
    try:
        BI, IN_, BO = "ExternalInput", "Internal", "ExternalOutput"
        S = SEQ
        specs = {
            "x_in": ((D, S), F32, BI), "vecs": ((128, 64), F32, BI),
            "qal": ((3, 16, S), BF16, BI), "kbs": ((128, 1024), F32, BI), "kbc": ((128, 64), F32, BI),
            "kbw": ((16, 128, 128), F32, BI), "cmpm": ((128, 5, 512), BF16, BI), "cm": ((128, 4, 512), BF16, BI),
            "wm": ((128, 8, 512), BF16, BI), "addm": ((16, 128, 4, 128), BF16, BI), "vneg": ((16, 128, 4, 128), BF16, BI),
            "karows": ((64, S), BF16, BI), "ones3": ((3, 1024), BF16, BI), "ovl": ((128, 4, 128), BF16, BI), "identb": ((128, 128), BF16, BI),
            "corr": ((128, 4, 16), F32, BI),
            "xs": ((D, S), F32, IN_), "h2T": ((D, S), BF16, IN_), "kvT": ((4, 128, S), BF16, IN_), "vtok": ((S, 256), BF16, IN_),
            "uT0": ((512, S), F32, IN_), "uT1": ((512, S), F32, IN_), "onsaT": ((D, S), BF16, IN_),
            "out_q": ((D, NTOK), F32, BO), "onsaT_q": ((D, NTOK), BF16, IN_),
            "selw": ((128, 4), F32, BI), "corr_q": ((128, 4, 16), F32, BI),
            "qal_q": ((3, 16, NTOK), BF16, BI), "kbw_q": ((128, 512), F32, BI), "cmpm_q": ((128, 2, 512), BF16, BI),
            "cm_q": ((128, 16, 512), BF16, BI), "addm_q": ((128, 16, 128), BF16, BI), "vneg_q": ((128, 16, 128), BF16, BI),
        }
        for l in range(NL):
            for pre in ("f1_", "f2_"):
                specs["%swg_%d" % (pre, l)] = ((NFC, 128, 8, 128), F32, BI)
                specs["%swu_%d" % (pre, l)] = ((NFC, 128, 8, 128), F32, BI)
                specs["%swd_%d" % (pre, l)] = ((8, 128, NFC, 128), F32, BI)
            specs["win_%d" % l] = ((len(WIN_STARTS), 128, 8, 128), F32, BI)
            specs["win_gn_%d" % l] = ((D, 48), F32, BI)
            specs["wpa_%d" % l] = ((8, 128, 4, 128), F32, BI)
            specs["wnb_%d" % l] = ((8, 128, 8, 128), F32, BI)
            specs["wo_%d" % l] = ((8, 128, 8, 128), F32, BI)
            specs["poolw_%d" % l] = ((4, 128, 128), F32, BI)
            specs["ck_w1_%d" % l] = ((2, 128, 16, 128), F32, BI)
            specs["cv_w1_%d" % l] = ((2, 128, 16, 128), F32, BI)
            specs["ck_w2_%d" % l] = ((256, 64), F32, BI)
            specs["cv_w2_%d" % l] = ((256, 64), F32, BI)
            specs["pecol_%d" % l] = ((128, 16), F32, BI)
        conv = [n_ for n_, (sh_, dt_, k_) in specs.items() if k_ == BI and dt_ == F32 and len(sh_) == 4 and n_ != "kbw"]
        gu = [n_ for n_ in conv if n_[3:5] in ("wg", "wu")]
        conv = [n_ for n_ in conv if n_ not in gu]
        for n_ in conv:
            specs[n_ + "_b"] = (specs[n_][0], BF16, IN_)
        for l in range(NL):
            for pre in ("f1_", "f2_"):
                specs["%swgu_%d_b" % (pre, l)] = ((NFC, 128, 16, 128), BF16, IN_)
        P = Prog(specs, WST=None)
        kb, dr = P.kb, P.dr

        P.selw = kb.sb("selw", [128, 4], F32)
        P.selw_r = Res("selw")
        kb.dma(P.ldsem, P.selw[:, :], dr["selw"][:, :], writes=[P.selw_r])
        mono_tabs = {k_: dr[k_] for k_ in ("qal", "kbw", "cmpm", "cm", "addm", "vneg", "corr", "onsaT")}

        def do_convert():
            for n_ in conv:
                P.convert_w(dr[n_], dr[n_ + "_b"])
                dr[n_] = dr[n_ + "_b"]
            for l_ in range(NL):
                for pre in ("f1_", "f2_"):
                    d_ = dr["%swgu_%d_b" % (pre, l_)]
                    P.convert_w(dr["%swg_%d" % (pre, l_)], None, dst_fn=lambda j, d_=d_: d_[j, :, 0:8, :])
                    P.convert_w(dr["%swu_%d" % (pre, l_)], None, dst_fn=lambda j, d_=d_: d_[j, :, 8:16, :])

        def alias(l, mode):
            for nm in ("wpa", "wnb", "wo", "poolw", "ck_w1", "cv_w1", "ck_w2", "cv_w2", "pecol", "win_gn"):
                dr[nm] = dr["%s_%d" % (nm, l)]
            dr["win"] = dr["win_%d" % l]
            dr["win_c"] = dr["win_%d" % l]
            dr["f2_wd"] = dr["f2_wd_%d" % l]
            dr["f2_wg"] = dr["f2_wgu_%d_b" % l]
            dr["f2_wu"] = None
            la = l if mode == "A" else min(l + 1, NL - 1)
            dr["win_a"] = dr["win_%d" % la]
            dr["f1_wd"] = dr["f1_wd_%d" % la]
            dr["f1_wg"] = dr["f1_wgu_%d_b" % la]
            dr["f1_wu"] = None
            dr["kvall"] = dr["kvT"]
            dr["vall"] = dr["vtok"]
            dr["h2T_in"] = dr["h2T"]
            dr["h2T_out"] = dr["h2T"]
            dr["kvT_out"] = dr["kvT"]
            dr["vtok_out"] = dr["vtok"]
            dr["xs_out"] = dr["xs"]
            dr["xs_in"] = dr["x_in"] if (mode == "A" and l == 0) else dr["xs"]
            dr["uT_in"] = dr["uT%d" % (l % 2)]
            dr["uT_out"] = dr["uT%d" % (la % 2)]

        def phase(fn, wst, wstf=1024, nslot=4):
            with ExitStack() as pes:
                kb.cur_es = pes
                P.alloc_wstage(wst, wstf, nslot)
                fn()
                kb.barrier()
            kb.cur_es = None

        phase(do_convert, 3584, 3584, 2)
        alias(0, "A")
        phase(lambda: tok_body(P, 0, "A", False), 3584)
        global QSEL
        for l in range(NL):
            last = (l == NL - 1)
            alias(l, "CA")
            if last:
                MONO, QSEL = False, True
                for k_ in ("qal", "kbw", "cmpm", "cm", "addm", "vneg", "corr", "onsaT"):
                    dr[k_] = dr[k_ + "_q"]
                dr["out"] = dr["out_q"]
            phase(lambda: attn_body(P), 2048, 1024, 2)
            phase(lambda: tok_body(P, l, "CA", last), 3584)
        return P.finish()
    finally:
        MONO = False
        QSEL = False


def mono_consts():
    sl = _slopes()
    p = np.arange(128)
    q = np.arange(512)
    c = {}
    tabs = np.arange(SEQ).astype(np.float32)
    hi, mid, lo = _split3(-(sl[:, None] * tabs[None, :]))
    c["qal"] = np.ascontiguousarray(np.stack([hi, mid, lo], 0))
    kbs = np.zeros((128, 16, 64), np.float32)
    kbc = np.zeros((128, 16, 4), np.float32)
    for h in range(16):
        kbs[:, h, :] = sl[h] * (np.arange(64)[None, :] * 128 + p[:, None]).astype(np.float32)
        kbc[:, h, :] = sl[h] * (16 * (np.arange(4)[None, :] * 128 + p[:, None]) + 31).astype(np.float32)
    c["kbs"] = kbs.reshape(128, 1024)
    c["kbc"] = kbc.reshape(128, 64)
    kbw = np.zeros((16, 128, 8, 16), np.float32)
    for i in range(16):
        for w in range(8):
            ka = 512 * (i - 1) + w * 128 + p
            for h in range(16):
                kbw[i, :, w, h] = np.where(ka >= 0, sl[h] * ka.astype(np.float32), -30000.0)
    c["kbw"] = kbw.reshape(16, 128, 128)
    cmpm = np.zeros((128, 5, 512), np.float32)
    for d in range(5):
        cmpm[:, d, :] = np.where((16 * p[:, None] + 31 - 512 * d) <= q[None, :], 0.0, NEG)
    c["cmpm"] = cmpm.astype(BF)
    cm = np.zeros((128, 4, 512), np.float32)
    for r in range(4):
        cm[:, r, :] = np.where((128 * r + p[:, None]) <= q[None, :], 0.0, NEG)
    c["cm"] = cm.astype(BF)
    wm = np.zeros((128, 8, 512), np.float32)
    for w in range(8):
        dist = 512 + q[None, :] - 128 * w - p[:, None]
        wm[:, w, :] = np.where((dist >= 0) & (dist < 512), 0.0, NEG)
    c["wm"] = wm.astype(BF)
    addm = np.zeros((16, 128, 4, 128), np.float32)
    vneg = np.zeros((16, 128, 4, 128), np.float32)
    j = np.arange(128)
    for i in range(16):
        for qs in range(4):
            t = i * 512 + qs * 128 + p
            valid = (j[None, :] * 64) <= t[:, None]
            cur = t // 64
            forced = valid & ((j[None, :] == 0) | (j[None, :] == cur[:, None]) | (j[None, :] == cur[:, None] - 1))
            addm[i, :, qs, :] = np.where(forced, 8192.0, np.where(valid, 0.0, -8192.0))
            vneg[i, :, qs, :] = np.where(valid, 0.0, NEG)
    c["addm"] = addm.astype(BF)
    c["vneg"] = vneg.astype(BF)
    corr = np.ones((128, 4, 16), np.float32)
    for gi, w in enumerate(POOLW):
        corr[:, gi, :] = (w / np.minimum(np.arange(16) + 1.0, float(w)))[None, :]
    c["corr"] = corr
    c.update(shared_consts())
    return c


def kernel_mono(W, x, vecs):
    prog = _prog("MONO", build_mono)
    base = {"vecs": vecs}
    base.update(mono_consts())
    for l in range(NL):
        base["win_%d" % l] = tile_w(W["w_in"][l], WIN_STARTS)
        base["win_gn_%d" % l] = np.ascontiguousarray(W["w_in"][l][:, C_GN:C_GN + 48])
        for pre, a in (("f1_", "ffn1"), ("f2_", "ffn2")):
            base["%swg_%d" % (pre, l)] = tile_w(W[a + "_w_gate"][l])
            base["%swu_%d" % (pre, l)] = tile_w(W[a + "_w_up"][l])
            base["%swd_%d" % (pre, l)] = tile_w(W[a + "_w_down"][l])
        base["wpa_%d" % l] = tile_w(W["w_branch_pool"][l])
        base["wnb_%d" % l] = tile_w(W["w_branch_nsa"][l])
        base["wo_%d" % l] = tile_w(W["w_out"][l])
        base["poolw_%d" % l] = W["pool_w"][l]
        base["ck_w1_%d" % l] = tile_w(W["cmp_k_w1"][l])
        base["cv_w1_%d" % l] = tile_w(W["cmp_v_w1"][l])
        base["ck_w2_%d" % l] = W["cmp_k_w2"][l]
        base["cv_w2_%d" % l] = W["cmp_v_w2"][l]
        base["pecol_%d" % l] = np.ascontiguousarray(W["cmp_pos"][l].reshape(16, 2, 64).transpose(1, 2, 0).reshape(128, 16))
    cores = list(range(8))
    in_maps = []
    xT = [np.ascontiguousarray(x[b].T) for b in range(2)]
    for c in cores:
        b, cc = c % 2, c // 2
        m = dict(base)
        m["x_in"] = xT[b]
        cq = core_consts(cc)
        for k_ in ("qal", "kbw", "cmpm", "cm", "addm", "vneg"):
            m[k_ + "_q"] = cq[k_]
        selw = np.zeros((128, 4), np.float32)
        selw[:, cc] = 1.0
        m["selw"] = selw
        corr = np.ones((128, 4, 16), np.float32)
        if cc == 0:
            corr = base["corr"]
        m["corr_q"] = corr
        in_maps.append(m)
    res = run_bass_kernel_spmd(prog, in_maps, core_ids=cores).results
    out = np.zeros((2, SEQ, D), np.float32)
    for c in cores:
        out[c % 2, _tok_index(c // 2), :] = np.asarray(res[c]["out_q"]).T
    return out
```

```python
import numpy as np
import ml_dtypes
from contextlib import ExitStack
import concourse.bass as bass
import concourse.mybir as mybir
from concourse.bass_utils import run_bass_kernel_spmd

F32 = mybir.dt.float32
BF16 = mybir.dt.bfloat16
AF = mybir.ActivationFunctionType
ALU = mybir.AluOpType

D = 1024
DFF = 2816
NFC = DFF // 128
SEQ = 8192
NL = 2
NTOK = 2048
MONO = False
QSEL = False


def ntok():
    return SEQ if MONO else NTOK
TT = 512
INW = 4400
C_Q, C_KC, C_VC, C_KSL, C_VSL, C_KWN, C_VWN, C_GN, C_U, C_GM = 0, 1024, 1152, 1280, 1408, 1536, 1664, 1792, 1840, 2352
EPS = 1e-6
NEG = -16384.0
WIN_STARTS = [j * 128 for j in range(8)] + [C_KC, C_VC, C_KSL, C_VSL, C_KWN, C_VWN] + [C_U + j * 128 for j in range(4)] + [C_GM + j * 128 for j in range(16)]
WIN_IDX = {c: i for i, c in enumerate(WIN_STARTS)}


class Sem:
    def __init__(self, h, name):
        self.h = h
        self.name = name
        self.count = 0
        self.group = False


class Tok:
    __slots__ = ("sem", "val")

    def __init__(self, sem, val):
        self.sem = sem
        self.val = val


class Res:
    __slots__ = ("name", "w", "r", "excl")

    def __init__(self, name="", excl=False):
        self.name = name
        self.w = None
        self.r = {}
        self.excl = excl


class Eng:
    def __init__(self, name, h, sem, same_sync):
        self.name = name
        self.h = h
        self.sem = sem
        self.waited = {}
        self.pending = []
        self.same_sync = same_sync


class KB:
    def __init__(self, nc, es):
        self.nc = nc
        self.es = es
        self.sems = []
        self.pe = self._eng("pe", nc.tensor, False)
        self.act = self._eng("act", nc.scalar, True)
        self.dve = self._eng("dve", nc.vector, True)
        self.pool = self._eng("pool", nc.gpsimd, True)
        self.sp = self._eng("sp", nc.sync, False)
        self.engs = [self.pe, self.act, self.dve, self.pool, self.sp]
        self.n_inst = 0

    def new_sem(self, name):
        name = "%s_%d" % (name, len(self.sems))
        h = self.es.enter_context(self.nc.semaphore(name))
        s = Sem(h, name)
        self.sems.append(s)
        return s

    def _eng(self, name, h, same_sync):
        return Eng(name, h, self.new_sem("s_" + name), same_sync)

    def sb(self, name, shape, dtype, es=None):
        self.nsb = getattr(self, "nsb", 0) + 1
        return (es or getattr(self, "cur_es", None) or self.es).enter_context(self.nc.sbuf_tensor("sb%d_%s" % (self.nsb, name), shape, dtype))

    def ps(self, name, shape, dtype):
        return self.es.enter_context(self.nc.psum_tensor("pp_" + name, shape, dtype))

    def _wait(self, eng, tok):
        if tok is None:
            return
        if tok.sem is eng.sem and not eng.same_sync:
            return
        assert tok.val is not None, "waiting on unresolved token (%s)" % tok.sem.name
        val = tok.val
        if tok.sem.group:
            val = max(val, tok.sem.count)
        if eng.waited.get(tok.sem, 0) >= val:
            return
        eng.h.wait_ge(tok.sem.h, val)
        eng.waited[tok.sem] = val

    def _deps(self, eng, reads, writes):
        for r in reads:
            self._wait(eng, r.w)
        for w in writes:
            self._wait(eng, w.w)
            for t in w.r.values():
                self._wait(eng, t)

    def _mark(self, tok, reads, writes):
        for r in reads:
            r.r[tok.sem] = tok
        for w in writes:
            w.w = tok
            w.r = {}

    def op(self, eng, fn, reads=(), writes=(), sig=True):
        xr = [r for r in reads if r.excl]
        if xr:
            writes = list(writes) + xr
            reads = [r for r in reads if not r.excl]
        self._deps(eng, reads, writes)
        inst = fn()
        self.n_inst += 1
        if sig:
            eng.sem.count += 1
            inst.then_inc(eng.sem.h, 1)
            tok = Tok(eng.sem, eng.sem.count)
            for t in eng.pending:
                t.val = eng.sem.count
            eng.pending = []
        else:
            tok = Tok(eng.sem, None)
            eng.pending.append(tok)
        self._mark(tok, reads, writes)
        return tok

    def dma(self, sem, out, in_, reads=(), writes=(), eng=None, **kw):
        eng = eng or self.sp
        self._deps(eng, reads, writes)
        if sem.count > 0:
            self._wait(eng, Tok(sem, sem.count))
        inst = eng.h.dma_start(out=out, in_=in_, **kw)
        self.n_inst += 1
        sem.count += 16
        inst.then_inc(sem.h, 16)
        tok = Tok(sem, sem.count)
        self._mark(tok, reads, writes)
        return tok

    def barrier(self):
        for e in self.engs:
            assert not e.pending
            for s in self.sems:
                if s.count > 0 and not (s is e.sem):
                    self._wait(e, Tok(s, s.count))


class Prog:
    def __init__(self, dram_specs, WST=3584):
        self.nc = bass.Bass("TRN2", target_bir_lowering=False)
        self.es = ExitStack()
        self.kb = KB(self.nc, self.es)
        self.dr = {}
        self.dres = {}
        for name, (shape, dt, kind) in dram_specs.items():
            self.dr[name] = self.nc.dram_tensor(name, list(shape), dt, kind=kind).ap()
            self.dres[name] = Res("dram_" + name)
        self.out_names = [n for n, (_, _, k) in dram_specs.items() if k == "ExternalOutput"]
        kb = self.kb
        self.psf = [kb.ps("psf%d" % i, [128, 512], F32) for i in range(7)]
        self.psf_r = [Res("psf%d" % i, excl=True) for i in range(7)]
        self.psb = kb.ps("psb", [128, 1024], BF16)
        self.psb_r = Res("psb", excl=True)
        self.ps_rr = 0
        self.ones = kb.sb("ones", [128, 128], F32)
        self.ones_r = Res("ones")
        kb.op(kb.dve, lambda: self.nc.vector.memset(self.ones[:], 1.0 / D), writes=[self.ones_r])
        self.epsc = kb.sb("epsc", [128, 1], F32)
        kb.op(kb.dve, lambda: self.nc.vector.memset(self.epsc[:], EPS), writes=[self.ones_r])
        self.vecs = kb.sb("vecs", [128, 64], F32)
        self.vecs_r = Res("vecs")
        self.ldsem = kb.new_sem("ld_misc")
        self.ldsem.group = True
        kb.dma(self.ldsem, self.vecs[:], self.dr["vecs"][:, :], writes=[self.vecs_r])
        self.wsem = [kb.new_sem("wsem%d" % i) for i in range(4)]
        self.stsem = kb.new_sem("st_misc")
        if WST:
            self.alloc_wstage(WST)

    def alloc_wstage(self, WST, WSTF=None, nslot=2):
        kb = self.kb
        self.WST = WST
        self.WSTF = WSTF or WST
        self.nslot = nslot
        self.wst = [kb.sb("wst%d" % i, [128, self.WSTF], F32) for i in range(nslot)]
        self.wst_r = [Res("wst%d" % i) for i in range(nslot)]
        self.wbf = [kb.sb("wbf%d" % i, [128, self.WST], BF16) for i in range(nslot)]
        self.wbf_r = [Res("wbf%d" % i) for i in range(nslot)]
        self.wslot = 0

    def bank(self):
        i = self.ps_rr % 7
        self.ps_rr += 1
        return self.psf[i], self.psf_r[i]

    def load_w(self, pieces):
        kb, nc = self.kb, self.nc
        s = self.wslot
        self.wslot = (self.wslot + 1) % self.nslot
        off = 0
        foff = 0
        views = []
        for ap in pieces:
            if len(ap.shape) == 3:
                _, kc, n = ap.shape
                src = ap
            else:
                K, n = ap.shape
                kc = K // 128
                src = ap.rearrange("(k p) n -> p k n", p=128)
            sz = kc * n
            bview = self.wbf[s][:, off:off + sz].rearrange("p (k n) -> p k n", n=n)
            if ap.dtype == BF16:
                kb.dma(self.wsem[s], bview, src, writes=[self.wbf_r[s]])
            else:
                assert foff + sz <= self.WSTF
                dst = self.wst[s][:, foff:foff + sz].rearrange("p (k n) -> p k n", n=n)
                kb.dma(self.wsem[s], dst, src, writes=[self.wst_r[s]])
                a, b, fa = off, off + sz, foff
                kb.op(kb.pool, lambda a=a, b=b, fa=fa: nc.gpsimd.tensor_copy(out=self.wbf[s][:, a:b], in_=self.wst[s][:, fa:fa + (b - a)]),
                      reads=[self.wst_r[s]], writes=[self.wbf_r[s]])
                foff += sz
            views.append(bview)
            off += sz
        assert off <= self.WST
        return views, self.wbf_r[s]

    def convert_w(self, src, dst, dst_fn=None):
        kb, nc = self.kb, self.nc
        nch, _, kc, n = src.shape
        sz = kc * n
        if not hasattr(self, "cvsem"):
            self.cvsem = [kb.new_sem("cvs%d" % i) for i in range(4)]
            self.cv_rr = 0
        for j in range(nch):
            s = self.wslot
            self.wslot = (self.wslot + 1) % self.nslot
            kb.dma(self.wsem[s], self.wst[s][:, 0:sz].rearrange("p (k n) -> p k n", n=n), src[j], writes=[self.wst_r[s]])
            e = self.cv_rr % 3
            self.cv_rr += 1
            if e == 0:
                kb.op(kb.pool, lambda: nc.gpsimd.tensor_copy(out=self.wbf[s][:, 0:sz], in_=self.wst[s][:, 0:sz]), reads=[self.wst_r[s]], writes=[self.wbf_r[s]])
            elif e == 1:
                kb.op(kb.dve, lambda: nc.vector.tensor_copy(out=self.wbf[s][:, 0:sz], in_=self.wst[s][:, 0:sz]), reads=[self.wst_r[s]], writes=[self.wbf_r[s]])
            else:
                kb.op(kb.act, lambda: nc.scalar.copy(out=self.wbf[s][:, 0:sz], in_=self.wst[s][:, 0:sz]), reads=[self.wst_r[s]], writes=[self.wbf_r[s]])
            kb.dma(self.cvsem[s], dst_fn(j) if dst_fn else dst[j], self.wbf[s][:, 0:sz].rearrange("p (k n) -> p k n", n=n), reads=[self.wbf_r[s]])

    def select_tile(self, dst, dst_r, cands, stage, stage_r, sem, p0, p1):
        kb, nc = self.kb, self.nc
        first = True
        for c, src in enumerate(cands):
            if src is None:
                continue
            kb.dma(sem, stage, src, writes=[stage_r], eng=kb.act)
            sc_ = self.selw[p0:p1, c:c + 1]
            if first:
                kb.op(kb.dve, lambda: nc.vector.tensor_scalar(out=dst, in0=stage, scalar1=sc_, scalar2=None, op0=ALU.mult),
                      reads=[stage_r, self.selw_r], writes=[dst_r])
                first = False
            else:
                kb.op(kb.dve, lambda: nc.vector.scalar_tensor_tensor(out=dst, in0=stage, scalar=sc_, in1=dst, op0=ALU.mult, op1=ALU.add),
                      reads=[stage_r, self.selw_r, dst_r], writes=[dst_r])

    def vcol(self, c):
        return self.vecs[:, c:c + 1]

    def rmsnorm(self, x, x_r, h, h_r, n, gcol, sq, sq_r, rstd, rstd_r, out_f32=None, out_r=None):
        kb, nc = self.kb, self.nc
        for s0 in range(0, n, 512):
            ps, ps_r = self.bank()
            for c in range(8):
                k = c % 2
                kb.op(kb.act, lambda c=c, k=k: nc.scalar.activation(out=sq[k][:, :], in_=x[:, c, s0:s0 + 512], func=AF.Square),
                      reads=[x_r], writes=[sq_r[k]])
                kb.op(kb.pe, lambda c=c, k=k: nc.tensor.matmul(ps[:, :], lhsT=self.ones[:, :], rhs=sq[k][:, :], start=(c == 0), stop=(c == 7)),
                      reads=[sq_r[k], self.ones_r], writes=[ps_r], sig=True)
            kb.op(kb.act, lambda: nc.scalar.activation(out=rstd[:, s0:s0 + 512], in_=ps[:, :], func=AF.Ln, bias=self.epsc[:, 0:1]),
                  reads=[ps_r, self.ones_r], writes=[rstd_r])
            kb.op(kb.act, lambda: nc.scalar.activation(out=rstd[:, s0:s0 + 512], in_=rstd[:, s0:s0 + 512], func=AF.Exp, scale=-0.5),
                  reads=[rstd_r], writes=[rstd_r])
            for c in range(8):
                tgt = h if out_f32 is None else out_f32
                tgt_r = h_r if out_f32 is None else out_r
                kb.op(kb.dve, lambda c=c, tgt=tgt: nc.vector.scalar_tensor_tensor(
                    out=tgt[:, c, s0:s0 + 512], in0=x[:, c, s0:s0 + 512], scalar=self.vcol(gcol + c), in1=rstd[:, s0:s0 + 512],
                    op0=ALU.mult, op1=ALU.mult), reads=[x_r, rstd_r, self.vecs_r], writes=[tgt_r])

    def ffn(self, x, x_r, h, h_r, n, wg, wu, wd, aT, aT_r, sg, sg_r):
        kb, nc = self.kb, self.nc
        nsub = n // 512
        for fc in range(NFC):
            if wu is None:
                (wgu,), w_r = self.load_w([wg[fc]])
                wgb, wub = wgu[:, 0:8, :], wgu[:, 8:16, :]
            else:
                (wgb, wub), w_r = self.load_w([wg[fc], wu[fc]])
            for sub in range(nsub):
                s0 = sub * 512
                pg, pg_r = self.bank()
                pu, pu_r = self.bank()
                for c in range(8):
                    kb.op(kb.pe, lambda c=c: nc.tensor.matmul(pg[:, :], lhsT=wgb[:, c, :], rhs=h[:, c, s0:s0 + 512], start=(c == 0), stop=(c == 7)),
                          reads=[w_r, h_r], writes=[pg_r], sig=(c == 7))
                for c in range(8):
                    kb.op(kb.pe, lambda c=c: nc.tensor.matmul(pu[:, :], lhsT=wub[:, c, :], rhs=h[:, c, s0:s0 + 512], start=(c == 0), stop=(c == 7)),
                          reads=[w_r, h_r], writes=[pu_r], sig=(c == 7))
                k = (fc * nsub + sub) % 2
                kb.op(kb.act, lambda k=k: nc.scalar.activation(out=sg[k][:, :], in_=pg[:, :], func=AF.Silu), reads=[pg_r], writes=[sg_r[k]])
                kb.op(kb.dve, lambda k=k: nc.vector.tensor_tensor(out=aT[:, fc, s0:s0 + 512], in0=sg[k][:, :], in1=pu[:, :], op=ALU.mult),
                      reads=[sg_r[k], pu_r], writes=[aT_r[fc]])
        for dc in range(8):
            (wdb,), w_r = self.load_w([wd[dc]])
            for sub in range(nsub):
                s0 = sub * 512
                py, py_r = self.bank()
                for fc in range(NFC):
                    kb.op(kb.pe, lambda fc=fc: nc.tensor.matmul(py[:, :], lhsT=wdb[:, fc, :], rhs=aT[:, fc, s0:s0 + 512], start=(fc == 0), stop=(fc == NFC - 1)),
                          reads=[w_r, aT_r[fc]], writes=[py_r], sig=(fc == NFC - 1))
                kb.op(kb.dve, lambda: nc.vector.scalar_tensor_tensor(out=x[:, dc, s0:s0 + 512], in0=py[:, :], scalar=0.5, in1=x[:, dc, s0:s0 + 512],
                                                                      op0=ALU.mult, op1=ALU.add), reads=[py_r, x_r], writes=[x_r])

    def finish(self):
        kb = self.kb
        kb.barrier()
        self.es.close()
        return self.nc


def gain_layout(v):
    return np.ascontiguousarray(np.asarray(v, np.float32).reshape(-1, 128).T)


def vbase(l):
    return 28 * l


POOLW = (2, 4, 8, 16)


def tok_specs(l, mode, last):
    specs = {"xs_in": ((D, NTOK), F32, "ExternalInput"), "vecs": ((128, 64), F32, "ExternalInput")}

    def wspec(li, pre):
        specs[pre + "wg"] = ((NFC, 128, 8, 128), F32, "ExternalInput")
        specs[pre + "wu"] = ((NFC, 128, 8, 128), F32, "ExternalInput")
        specs[pre + "wd"] = ((8, 128, NFC, 128), F32, "ExternalInput")

    doA = (mode == "A") or (not last)
    if mode == "CA":
        wspec(l, "f2_")
        specs["win_c"] = ((len(WIN_STARTS), 128, 8, 128), F32, "ExternalInput")
        specs["wpa"] = ((8, 128, 4, 128), F32, "ExternalInput")
        specs["wnb"] = ((8, 128, 8, 128), F32, "ExternalInput")
        specs["wo"] = ((8, 128, 8, 128), F32, "ExternalInput")
        specs["poolw"] = ((4, 128, 128), F32, "ExternalInput")
        specs["h2T_in"] = ((D, NTOK), BF16, "ExternalInput")
        specs["onsaT"] = ((D, NTOK), BF16, "ExternalInput")
        specs["uext"] = ((512, 4, 528), F32, "ExternalInput")
        specs["corr"] = ((128, 4, 16), F32, "ExternalInput")
    if doA:
        wspec(l, "f1_")
        specs["win_a"] = ((len(WIN_STARTS), 128, 8, 128), F32, "ExternalInput")
        specs["h2T_out"] = ((D, NTOK), BF16, "ExternalOutput")
        specs["kvT_out"] = ((4, 128, NTOK), BF16, "ExternalOutput")
        specs["uT_out"] = ((512, NTOK), F32, "ExternalOutput")
        specs["vtok_out"] = ((NTOK, 256), BF16, "ExternalOutput")
        specs["xs_out"] = ((D, NTOK), F32, "ExternalOutput")
    else:
        specs["out"] = ((D, NTOK), F32, "ExternalOutput")
    return specs


def build_tok(l, mode, last):
    P = Prog(tok_specs(l, mode, last))
    tok_body(P, l, mode, last)
    return P.finish()


def build_bca(l, last):
    specs = attn_specs()
    ts = tok_specs(l, "CA", last)
    for k_ in ("h2T_in", "win_c", "onsaT", "vecs"):
        ts.pop(k_)
    specs.update(ts)
    specs["onsaT"] = ((D, NTOK), BF16, "Internal")
    P = Prog(specs, WST=None)
    P.dr["h2T_in"] = P.dr["h2T"]
    P.dr["win_c"] = P.dr["win"]
    kb = P.kb
    with ExitStack() as pes:
        kb.cur_es = pes
        P.alloc_wstage(2048)
        attn_body(P)
        kb.barrier()
    with ExitStack() as pes:
        kb.cur_es = pes
        P.alloc_wstage(3584)
        tok_body(P, l, "CA", last)
        kb.barrier()
    kb.cur_es = None
    return P.finish()


def tok_body(P, l, mode, last):
    kb, nc, dr = P.kb, P.nc, P.dr
    stq = kb.act if (MONO or QSEL) else kb.sp
    doA = (mode == "A") or (not last)
    x = kb.sb("x", [128, 8, TT], F32); x_r = Res("x")
    h = kb.sb("h", [128, 8, TT], BF16); h_r = Res("h")
    aT = kb.sb("aT", [128, NFC, TT], BF16); aT_r = [Res("aT%d" % i) for i in range(NFC)]
    sq = [kb.sb("sq%d" % i, [128, 512], F32) for i in range(2)]; sq_r = [Res() for i in range(2)]
    rstd = kb.sb("rstd", [128, TT], F32); rstd_r = Res()
    xsem = kb.new_sem("xsem")
    if QSEL:
        xstg2 = kb.sb("xstg", [128, 8 * TT], F32); xstg_r = Res("xstg")
        xstg = xstg2[:, :].rearrange("p (c n) -> p c n", n=TT)
    if mode == "CA":
        hsem = kb.new_sem("hsem"); osem = kb.new_sem("osem"); usem = kb.new_sem("usem")
        on = kb.sb("on", [128, 8, TT], BF16); on_r = Res("on")
        ue = kb.sb("ue", [128, 4, 528], F32); ue_r = Res("ue")
        sa = kb.sb("sa", [128, 528], F32); sa_r = Res("sa")
        sb_ = kb.sb("sbb", [128, 528], F32); sb_r = Res("sb")
        dl = kb.sb("dl", [128, 4, TT], BF16); dl_r = [Res() for _ in range(4)]
        opl = kb.sb("opl", [128, 4, TT], BF16); opl_r = Res("opl")
        mg = kb.sb("mg", [128, 8, TT], BF16); mg_r = [Res() for _ in range(8)]
        t1 = kb.sb("t1", [128, TT], F32); t1_r = Res()
        t2 = kb.sb("t2", [128, TT], F32); t2_r = Res()
        corr = kb.sb("corr", [128, 4, 16], F32); corr_r = Res()
        kb.dma(P.ldsem, corr[:], dr["corr"][:, :, :], writes=[corr_r])
    if doA:
        kvst = [kb.sb("kvst%d" % i, [128, TT], BF16) for i in range(2)]; kvst_r = [Res() for _ in range(2)]
        ust = [kb.sb("ust%d" % i, [128, TT], F32) for i in range(2)]; ust_r = [Res() for _ in range(2)]
        vst = kb.sb("vst", [128, 4, 256], BF16); vst_r = Res()
        osems = [kb.new_sem("kvo%d" % i) for i in range(2)]
        usems = [kb.new_sem("uo%d" % i) for i in range(2)]
        vsem = kb.new_sem("vo")
        hosem = kb.new_sem("ho")

    def colsl(ap, t0):
        return ap[:, t0:t0 + TT].rearrange("(c p) n -> p c n", p=128)

    for t in range(ntok() // TT):
        t0 = t * TT
        if QSEL:
            P.select_tile(x[:, :, :], x_r, [colsl(dr["xs_in"], (4 * t + c_) * 512) for c_ in range(4)], xstg[:, :, :], xstg_r, xsem, 0, 128)
        else:
            kb.dma(xsem, x[:, :, :], colsl(dr["xs_in"], t0), writes=[x_r], eng=stq)
        la = l
        if mode == "CA":
            vb = vbase(l)
            if QSEL:
                P.select_tile(h[:, :, :], h_r, [colsl(dr["h2T_in"], (4 * t + c_) * 512) for c_ in range(4)],
                              on[:, :, :], on_r, hsem, 0, 128)
            else:
                kb.dma(hsem, h[:, :, :], colsl(dr["h2T_in"], t0), writes=[h_r], eng=stq)
            kb.dma(osem, on[:, :, :], colsl(dr["onsaT"], t0), writes=[on_r], eng=stq)
            if QSEL:
                ustg = xstg2[:, 0:2112].rearrange("p (g n) -> p g n", n=528)
                cands = []
                for c_ in range(4):
                    a0 = (4 * t + c_) * 512
                    cands.append(dr["uT_in"][:, a0 - 16:a0 + 512].rearrange("(g p) n -> p g n", p=128) if a0 > 0 else None)
                if t == 0:
                    kb.op(kb.pool, lambda: nc.gpsimd.memset(ustg[:, :, 0:16], 0.0), writes=[xstg_r])
                    kb.dma(usem, ustg[:, :, 16:528], dr["uT_in"][:, 0:512].rearrange("(g p) n -> p g n", p=128), writes=[xstg_r])
                    kb.op(kb.dve, lambda: nc.vector.tensor_scalar(out=ue[:, :, :], in0=ustg, scalar1=P.selw[:, 0:1], scalar2=None, op0=ALU.mult),
                          reads=[xstg_r, P.selw_r], writes=[ue_r])
                    for c_ in range(1, 4):
                        kb.dma(usem, ustg, cands[c_], writes=[xstg_r])
                        kb.op(kb.dve, lambda: nc.vector.scalar_tensor_tensor(out=ue[:, :, :], in0=ustg, scalar=P.selw[:, c_:c_ + 1], in1=ue[:, :, :], op0=ALU.mult, op1=ALU.add),
                              reads=[xstg_r, P.selw_r, ue_r], writes=[ue_r])
                else:
                    P.select_tile(ue[:, :, :], ue_r, cands, ustg, xstg_r, usem, 0, 128)
            elif MONO:
                kb.dma(usem, ue[:, :, 16:528], dr["uT_in"][:, t0:t0 + 512].rearrange("(g p) n -> p g n", p=128), writes=[ue_r])
                if t == 0:
                    kb.op(kb.pool, lambda: nc.gpsimd.memset(ue[:, :, 0:16], 0.0), writes=[ue_r])
                else:
                    kb.dma(usem, ue[:, :, 0:16], dr["uT_in"][:, t0 - 16:t0].rearrange("(g p) n -> p g n", p=128), writes=[ue_r])
            else:
                kb.dma(usem, ue[:, :, :], dr["uext"][:, t, :].rearrange("(g p) n -> p g n", p=128), writes=[ue_r])
            for gi, w in enumerate(POOLW):
                cur, cur_r = None, None
                sh = 1
                src = ue[:, gi, :]
                src_r = ue_r
                bufs = [(sa, sa_r), (sb_, sb_r)]
                bi = 0
                while sh < w:
                    dst, dst_r = bufs[bi]
                    bi ^= 1
                    lo = 2 * sh - 1
                    kb.op(kb.dve, lambda src=src, dst=dst, lo=lo, sh=sh: nc.vector.tensor_tensor(
                        out=dst[:, lo:528], in0=src[:, lo:528], in1=src[:, lo - sh:528 - sh], op=ALU.add),
                        reads=[src_r], writes=[dst_r])
                    src, src_r = dst, dst_r
                    sh *= 2
                kb.op(kb.dve, lambda src=src, w=w: nc.vector.tensor_scalar(out=src[:, 16:528], in0=src[:, 16:528], scalar1=1.0 / w, scalar2=None, op0=ALU.mult),
                      reads=[src_r], writes=[src_r])
                if t == 0:
                    kb.op(kb.dve, lambda src=src, gi=gi: nc.vector.tensor_tensor(out=src[:, 16:32], in0=src[:, 16:32], in1=corr[:, gi, :], op=ALU.mult),
                          reads=[src_r, corr_r], writes=[src_r])
                kb.op(kb.dve, lambda src=src, gi=gi: nc.vector.tensor_tensor(out=dl[:, gi, :], in0=src[:, 16:528], in1=ue[:, gi, 16:528], op=ALU.subtract),
                      reads=[src_r, ue_r], writes=[dl_r[gi]])
            for gi in range(4):
                (pw,), w_r = P.load_w([dr["poolw"][gi]])
                ps, ps_r = P.bank()
                kb.op(kb.pe, lambda: nc.tensor.matmul(ps[:, :], lhsT=pw[:, 0, :], rhs=dl[:, gi, :], start=True, stop=True),
                      reads=[w_r, dl_r[gi]], writes=[ps_r])
                kb.op(kb.dve, lambda: nc.vector.tensor_scalar(out=opl[:, gi, :], in0=ps[:, :], scalar1=P.vcol(vb + 24 + gi), scalar2=None, op0=ALU.mult),
                      reads=[ps_r, P.vecs_r], writes=[opl_r])
            for dc in range(8):
                (wpa, wnb, wgp, wga), w_r = P.load_w([dr["wpa"][dc], dr["wnb"][dc],
                                                      dr["win_c"][WIN_IDX[C_GM + dc * 128]],
                                                      dr["win_c"][WIN_IDX[C_GM + 1024 + dc * 128]]])
                pa, pa_r = P.bank(); pb, pb_r = P.bank(); pgp, pgp_r = P.bank(); pga, pga_r = P.bank()
                for c in range(4):
                    kb.op(kb.pe, lambda c=c: nc.tensor.matmul(pa[:, :], lhsT=wpa[:, c, :], rhs=opl[:, c, :], start=(c == 0), stop=(c == 3)),
                          reads=[w_r, opl_r], writes=[pa_r], sig=(c == 3))
                for c in range(8):
                    kb.op(kb.pe, lambda c=c: nc.tensor.matmul(pb[:, :], lhsT=wnb[:, c, :], rhs=on[:, c, :], start=(c == 0), stop=(c == 7)),
                          reads=[w_r, on_r], writes=[pb_r], sig=(c == 7))
                for c in range(8):
                    kb.op(kb.pe, lambda c=c: nc.tensor.matmul(pgp[:, :], lhsT=wgp[:, c, :], rhs=h[:, c, :], start=(c == 0), stop=(c == 7)),
                          reads=[w_r, h_r], writes=[pgp_r], sig=(c == 7))
                for c in range(8):
                    kb.op(kb.pe, lambda c=c: nc.tensor.matmul(pga[:, :], lhsT=wga[:, c, :], rhs=h[:, c, :], start=(c == 0), stop=(c == 7)),
                          reads=[w_r, h_r], writes=[pga_r], sig=(c == 7))
                kb.op(kb.act, lambda: nc.scalar.activation(out=t1[:, :], in_=pgp[:, :], func=AF.Sigmoid), reads=[pgp_r], writes=[t1_r])
                kb.op(kb.act, lambda: nc.scalar.activation(out=t2[:, :], in_=pga[:, :], func=AF.Sigmoid), reads=[pga_r], writes=[t2_r])
                kb.op(kb.dve, lambda: nc.vector.tensor_tensor(out=t1[:, :], in0=t1[:, :], in1=pa[:, :], op=ALU.mult), reads=[t1_r, pa_r], writes=[t1_r])
                kb.op(kb.dve, lambda: nc.vector.tensor_tensor(out=t2[:, :], in0=t2[:, :], in1=pb[:, :], op=ALU.mult), reads=[t2_r, pb_r], writes=[t2_r])
                kb.op(kb.dve, lambda: nc.vector.tensor_tensor(out=mg[:, dc, :], in0=t1[:, :], in1=t2[:, :], op=ALU.add), reads=[t1_r, t2_r], writes=[mg_r[dc]])
            for dc in range(8):
                (wo,), w_r = P.load_w([dr["wo"][dc]])
                pz, pz_r = P.bank()
                for c in range(8):
                    kb.op(kb.pe, lambda c=c: nc.tensor.matmul(pz[:, :], lhsT=wo[:, c, :], rhs=mg[:, c, :], start=(c == 0), stop=(c == 7)),
                          reads=[w_r, mg_r[c]], writes=[pz_r], sig=(c == 7))
                kb.op(kb.dve, lambda: nc.vector.tensor_tensor(out=x[:, dc, :], in0=x[:, dc, :], in1=pz[:, :], op=ALU.add), reads=[x_r, pz_r], writes=[x_r])
            P.rmsnorm(x, x_r, h, h_r, TT, vb + 16, sq, sq_r, rstd, rstd_r)
            P.ffn(x, x_r, h, h_r, TT, dr["f2_wg"], dr["f2_wu"], dr["f2_wd"], aT, aT_r, sq, sq_r)
            la = l + 1
            if last:
                P.rmsnorm(x, x_r, None, None, TT, 56, sq, sq_r, rstd, rstd_r, out_f32=x, out_r=x_r)
                kb.dma(P.stsem, colsl(dr["out"], t0), x[:, :, :], reads=[x_r], eng=stq)
                continue
        vb = vbase(la)
        P.rmsnorm(x, x_r, h, h_r, TT, vb + 0, sq, sq_r, rstd, rstd_r)
        P.ffn(x, x_r, h, h_r, TT, dr["f1_wg"], dr["f1_wu"], dr["f1_wd"], aT, aT_r, sq, sq_r)
        kb.dma(P.stsem, colsl(dr["xs_out"], t0), x[:, :, :], reads=[x_r], eng=stq)
        P.rmsnorm(x, x_r, h, h_r, TT, vb + 8, sq, sq_r, rstd, rstd_r)
        kb.dma(hosem, colsl(dr["h2T_out"], t0), h[:, :, :], reads=[h_r], eng=stq)
        W = dr["win_a"]
        for j, c0 in enumerate((C_KC, C_VC, C_KSL, C_KWN)):
            (wc,), w_r = P.load_w([W[WIN_IDX[c0]]])
            ps, ps_r = P.bank()
            for c in range(8):
                kb.op(kb.pe, lambda c=c: nc.tensor.matmul(ps[:, :], lhsT=wc[:, c, :], rhs=h[:, c, :], start=(c == 0), stop=(c == 7)),
                      reads=[w_r, h_r], writes=[ps_r], sig=(c == 7))
            k = j % 2
            kb.op(kb.act, lambda k=k: nc.scalar.copy(out=kvst[k][:, :], in_=ps[:, :]), reads=[ps_r], writes=[kvst_r[k]])
            kb.dma(osems[k], dr["kvT_out"][j, :, t0:t0 + TT], kvst[k][:, :], reads=[kvst_r[k]], eng=stq)
        for j in range(4):
            (wc,), w_r = P.load_w([W[WIN_IDX[C_U + j * 128]]])
            ps, ps_r = P.bank()
            for c in range(8):
                kb.op(kb.pe, lambda c=c: nc.tensor.matmul(ps[:, :], lhsT=wc[:, c, :], rhs=h[:, c, :], start=(c == 0), stop=(c == 7)),
                      reads=[w_r, h_r], writes=[ps_r], sig=(c == 7))
            k = j % 2
            kb.op(kb.act, lambda k=k: nc.scalar.copy(out=ust[k][:, :], in_=ps[:, :]), reads=[ps_r], writes=[ust_r[k]])
            kb.dma(usems[k], dr["uT_out"][j * 128:(j + 1) * 128, t0:t0 + TT], ust[k][:, :], reads=[ust_r[k]], eng=stq)
        (wv1, wv2), w_r = P.load_w([W[WIN_IDX[C_VSL]], W[WIN_IDX[C_VWN]]])
        for tb in range(TT // 128):
            ps, ps_r = P.bank()
            for wi, wv in enumerate((wv1, wv2)):
                for c in range(8):
                    kb.op(kb.pe, lambda c=c, wv=wv, wi=wi: nc.tensor.matmul(ps[:, wi * 128:(wi + 1) * 128], lhsT=h[:, c, tb * 128:(tb + 1) * 128], rhs=wv[:, c, :],
                                                                         start=(c == 0), stop=(c == 7)),
                          reads=[w_r, h_r], writes=[ps_r], sig=(c == 7))
            kb.op(kb.act, lambda: nc.scalar.copy(out=vst[:, tb, :], in_=ps[:, 0:256]), reads=[ps_r], writes=[vst_r])
        kb.dma(vsem, dr["vtok_out"][t0:t0 + TT, :].rearrange("(tb p) c -> p tb c", p=128), vst[:, :, :], reads=[vst_r], eng=stq)


def attn_specs():
    BI = "ExternalInput"
    specs = {
        "vecs": ((128, 64), F32, BI),
        "h2T": ((D, NTOK), BF16, BI),
        "kvall": ((4, 128, SEQ + 32), BF16, BI),
        "vall": ((SEQ, 256), BF16, BI),
        "kwin": ((128, 4, 1024), BF16, BI),
        "vwin": ((4, 1024, 128), BF16, BI),
        "win": ((len(WIN_STARTS), 128, 8, 128), F32, BI), "win_gn": ((D, 48), F32, BI),
        "ck_w1": ((2, 128, 16, 128), F32, BI), "ck_w2": ((256, 64), F32, BI),
        "cv_w1": ((2, 128, 16, 128), F32, BI), "cv_w2": ((256, 64), F32, BI),
        "pecol": ((128, 16), F32, BI),
        "qal": ((3, 16, NTOK), BF16, BI),
        "kbs": ((128, 16 * 64), F32, BI), "kbc": ((128, 64), F32, BI), "kbw": ((128, 4 * 8 * 16), F32, BI),
        "cmpm": ((128, 2, 512), BF16, BI), "cm": ((128, 16, 512), BF16, BI), "wm": ((128, 8, 512), BF16, BI),
        "addm": ((128, 16, 128), BF16, BI), "vneg": ((128, 16, 128), BF16, BI),
        "karows": ((64, SEQ), BF16, BI), "ones3": ((3, 1024), BF16, BI), "ovl": ((128, 4, 128), BF16, BI), "identb": ((128, 128), BF16, BI),
        "onsaT": ((D, NTOK), BF16, "ExternalOutput"),
    }
    return specs


def build_attn():
    P = Prog(attn_specs(), WST=2048)
    attn_body(P)
    return P.finish()


def attn_body(P):
    kb, nc, dr = P.kb, P.nc, P.dr
    ld = P.ldsem

    def const(name, shape, dt, src):
        t = kb.sb(name, shape, dt)
        r = Res(name)
        kb.dma(ld, t[:], src, writes=[r])
        return t, r

    CM, CM_r = const("CM", [128, 4 if MONO else 16, 512], BF16, dr["cm"][:, :, :])
    WM, WM_r = const("WM", [128, 8, 512], BF16, dr["wm"][:, :, :])
    CPM, CPM_r = const("CPM", [128, 5 if MONO else 2, 512], BF16, dr["cmpm"][:, :, :])
    if MONO:
        ADM = kb.sb("ADM", [128, 4, 128], BF16); ADM_r = Res("ADM")
        VNG = kb.sb("VNG", [128, 4, 128], BF16); VNG_r = Res("VNG")
        KBW = kb.sb("KBW", [128, 128], F32); KBW_r = Res("KBW")
        pisem = kb.new_sem("pisem")
    else:
        ADM, ADM_r = const("ADM", [128, 16, 128], BF16, dr["addm"][:, :, :])
        VNG, VNG_r = const("VNG", [128, 16, 128], BF16, dr["vneg"][:, :, :])
        KBW, KBW_r = const("KBW", [128, 512], F32, dr["kbw"][:, :])
    KBS, KBS_r = const("KBS", [128, 1024], F32, dr["kbs"][:, :])
    KBC, KBC_r = const("KBC", [128, 64], F32, dr["kbc"][:, :])
    IDB, IDB_r = const("IDB", [128, 128], BF16, dr["identb"][:, :])
    PEC, PEC_r = const("PEC", [128, 16], F32, dr["pecol"][:, :])

    KA = kb.sb("KA", [128, SEQ], BF16); KA_r = Res("KA")
    VA = kb.sb("VA", [128, 64, 65], BF16); VA_r = Res("VA")
    KW = kb.sb("KW", [128, 1024], BF16); KW_r = Res("KW")
    VW = kb.sb("VW", [128, 8, 65], BF16); VW_r = Res("VW")
    KC = kb.sb("KC", [128, 2, 512], BF16); KC_r = Res("KC")
    VC = kb.sb("VC", [128, 4, 2, 193], BF16); VC_r = Res("VC")
    kb.op(kb.pool, lambda: nc.gpsimd.memset(KW[0:64, :], 0.0), writes=[KW_r])
    kb.op(kb.pool, lambda: nc.gpsimd.memset(VW[:, :, 0:64], 0.0), writes=[VW_r])
    kb.dma(ld, KA[64:128, :], dr["karows"][:, :], writes=[KA_r])
    kb.op(kb.pool, lambda: nc.gpsimd.memset(KW[64:128, :], 0.0), writes=[KW_r])
    kb.op(kb.pool, lambda: nc.gpsimd.memset(KC[64:128, :, :], 0.0), writes=[KC_r])
    kb.dma(ld, KW[124:127, :], dr["ones3"][:, :], writes=[KW_r])
    kb.dma(ld, KC[124:127, :, :], dr["ones3"][:, :].rearrange("r (g n) -> r g n", n=512), writes=[KC_r])
    kb.op(kb.pool, lambda: nc.gpsimd.memset(VA[:, :, 64:65], 1.0), writes=[VA_r])
    kb.op(kb.pool, lambda: nc.gpsimd.memset(VW[:, :, 64:65], 1.0), writes=[VW_r])
    kb.op(kb.pool, lambda: nc.gpsimd.memset(VC[:, :, :, 64:65], 1.0), writes=[VC_r])
    for g in range(2):
        kb.dma(ld, VC[:, :, g, 65:193], dr["ovl"][:, :, :], writes=[VC_r])

    ces = ExitStack()
    KC2 = kb.sb("KC2", [128, SEQ + 16], BF16, es=ces); KC2_r = Res("KC2")
    zb = kb.sb("zb", [128, 512], F32, es=ces); zb_r = Res()
    s2 = kb.sb("s2", [128, 512], F32, es=ces); s2_r = Res()
    hid = kb.sb("hid", [128, 2, 512], BF16, es=ces); hid_r = [Res(), Res()]
    pecb = kb.sb("pecb", [128, 16], BF16, es=ces); pecb_r = Res()
    bj = kb.sb("bj", [128, 1], F32, es=ces); bj_r = Res()
    kb.op(kb.dve, lambda: nc.vector.tensor_copy(out=pecb[:, :], in_=PEC[:, :]), reads=[PEC_r], writes=[pecb_r])
    c2sem = kb.new_sem("c2sem")
    if MONO or QSEL:
        kb.op(kb.pool, lambda: nc.gpsimd.memset(KC2[0:64, SEQ:SEQ + 16], 0.0), writes=[KC2_r])
        kb.op(kb.pool, lambda: nc.gpsimd.memset(KC2[64:128, SEQ - 1:SEQ + 16], 0.0), writes=[KC2_r])
    for kv in range(2):
        w1 = dr["ck_w1" if kv == 0 else "cv_w1"]
        w2 = dr["ck_w2" if kv == 0 else "cv_w2"]
        for g in range(2):
            if MONO or QSEL:
                kb.dma(c2sem, KC2[0:64, 0:SEQ], dr["kvall"][kv, g * 64:(g + 1) * 64, 0:SEQ], writes=[KC2_r])
                kb.dma(c2sem, KC2[64:128, 0:SEQ - 1], dr["kvall"][kv, g * 64:(g + 1) * 64, 1:SEQ], writes=[KC2_r])
            else:
                kb.dma(c2sem, KC2[0:64, :], dr["kvall"][kv, g * 64:(g + 1) * 64, 0:SEQ + 16], writes=[KC2_r])
                kb.dma(c2sem, KC2[64:128, :], dr["kvall"][kv, g * 64:(g + 1) * 64, 1:SEQ + 17], writes=[KC2_r])
            for jc in range(2):
                (w1v,), w_r = P.load_w([w1[jc]])
                pb_, pb_r = P.bank()
                for lp in range(16):
                    kb.op(kb.pe, lambda lp=lp: nc.tensor.matmul(pb_[:, 0:1], lhsT=w1v[:, lp, :], rhs=pecb[:, lp:lp + 1], start=(lp == 0), stop=(lp == 15)),
                          reads=[w_r, pecb_r], writes=[pb_r], sig=(lp == 15))
                kb.op(kb.act, lambda: nc.scalar.copy(out=bj[:, :], in_=pb_[:, 0:1]), reads=[pb_r], writes=[bj_r])
                ps, ps_r = P.bank()
                for lp in range(16):
                    kb.op(kb.pe, lambda lp=lp: nc.tensor.matmul(ps[:, :], lhsT=w1v[:, lp, :], rhs=KC2[:, 2 * lp:2 * lp + 16 * 511 + 1:16], start=(lp == 0), stop=(lp == 15)),
                          reads=[w_r, KC2_r], writes=[ps_r], sig=(lp == 15))
                kb.op(kb.act, lambda: nc.scalar.activation(out=zb[:, :], in_=ps[:, :], func=AF.Identity, bias=bj[:, 0:1]), reads=[ps_r, bj_r], writes=[zb_r])
                kb.op(kb.act, lambda: nc.scalar.activation(out=s2[:, :], in_=zb[:, :], func=AF.Square), reads=[zb_r], writes=[s2_r])
                kb.op(kb.dve, lambda: nc.vector.tensor_scalar(out=s2[:, :], in0=s2[:, :], scalar1=0.044715, scalar2=1.0, op0=ALU.mult, op1=ALU.add), reads=[s2_r], writes=[s2_r])
                kb.op(kb.dve, lambda: nc.vector.tensor_tensor(out=s2[:, :], in0=s2[:, :], in1=zb[:, :], op=ALU.mult), reads=[s2_r, zb_r], writes=[s2_r])
                kb.op(kb.act, lambda: nc.scalar.activation(out=s2[:, :], in_=s2[:, :], func=AF.Sigmoid, scale=1.5957691216057308), reads=[s2_r], writes=[s2_r])
                kb.op(kb.dve, lambda jc=jc: nc.vector.tensor_tensor(out=hid[:, jc, :], in0=s2[:, :], in1=zb[:, :], op=ALU.mult), reads=[s2_r, zb_r], writes=[hid_r[jc]])
            (w2v,), w_r = P.load_w([w2[:, :]])
            if kv == 0:
                ps, ps_r = P.bank()
                for jc in range(2):
                    kb.op(kb.pe, lambda jc=jc: nc.tensor.matmul(ps[0:64, :], lhsT=w2v[:, jc, :], rhs=hid[:, jc, :], start=(jc == 0), stop=(jc == 1)),
                          reads=[w_r, hid_r[jc]], writes=[ps_r], sig=(jc == 1))
                kb.op(kb.act, lambda g=g: nc.scalar.copy(out=KC[0:64, g, :], in_=ps[0:64, :]), reads=[ps_r], writes=[KC_r])
            else:
                for nt in range(4):
                    ps, ps_r = P.bank()
                    for jc in range(2):
                        kb.op(kb.pe, lambda jc=jc, nt=nt: nc.tensor.matmul(ps[:, 0:64], lhsT=hid[:, jc, nt * 128:(nt + 1) * 128], rhs=w2v[:, jc, :], start=(jc == 0), stop=(jc == 1)),
                              reads=[w_r, hid_r[jc]], writes=[ps_r], sig=(jc == 1))
                    kb.op(kb.act, lambda g=g, nt=nt: nc.scalar.copy(out=VC[:, nt, g, 0:64], in_=ps[:, 0:64]), reads=[ps_r], writes=[VC_r])
    kb.barrier()
    ces.close()

    Q = kb.sb("Q", [128, 3, 8, 512], BF16); Q_r = [Res("Q%d" % i) for i in range(8)]
    kb.op(kb.pool, lambda: nc.gpsimd.memset(Q[64:128, :, :, :], 0.0), writes=Q_r)
    hT = kb.sb("hT", [128, 8, 512], BF16); hT_r = Res("hT")
    if QSEL:
        qstg = kb.sb("qstg", [128, 8, 512], BF16); qstg_r = Res("qstg")
    gat = kb.sb("gat", [128, 4, 48], F32); gat_r = Res("gat")
    cst = kb.sb("cst", [128, 4, 8, 64], F32); cst_r = [Res() for _ in range(8)]
    crd = kb.sb("crd", [128, 4, 8], F32); crd_r = [Res() for _ in range(8)]
    sc = kb.sb("sc", [128, 4, 128], F32); sc_r = Res("sc")
    pT = [kb.sb("pT%d" % i, [128, 512], BF16) for i in range(6)]; pT_r = [Res() for _ in range(6)]
    tmp = [kb.sb("tmp%d" % i, [128, 512], F32) for i in range(4)]; tmp_r = [Res() for _ in range(4)]
    onb = kb.sb("onb", [128, 4, 1024], BF16); onb_r = Res("onb")
    ost = [kb.sb("ost%d" % i, [128, 512], BF16) for i in range(2)]; ost_r = [Res() for _ in range(2)]
    smb = kb.sb("smb", [128, 4, 128], F32); smb_r = [Res() for _ in range(4)]
    wa = kb.sb("wa", [128, 4, 128], F32); wa_r = [Res() for _ in range(4)]
    wb = kb.sb("wb", [128, 4, 128], F32); wb_r = [Res() for _ in range(4)]
    m8 = kb.sb("m8", [128, 4, 8], F32); m8_r = [Res() for _ in range(4)]
    m8b = kb.sb("m8b", [128, 4, 8], F32); m8b_r = [Res() for _ in range(4)]
    nqv = kb.sb("nqv", [128, 4, 3, 128], BF16); nq_r = Res()
    kb.op(kb.pool, lambda: nc.gpsimd.memset(nqv[:, :, :, :], 0.0), writes=[nq_r])
    dd = kb.sb("dd", [128, 2, 4], F32); dd_r = Res()
    cf = kb.sb("cf", [128, 3, 4], F32); cf_r = Res()
    oh = kb.sb("oh", [128, 64], F32); oh_r = Res()
    hsem = kb.new_sem("hsem"); ksem = kb.new_sem("ksem"); vsem = kb.new_sem("vsem")
    kwsem = kb.new_sem("kwsem"); vwsem = kb.new_sem("vwsem"); qsem = kb.new_sem("qsem")
    osems = [kb.new_sem("os%d" % i) for i in range(2)]
    W = dr["win"]
    st = {"s": 0, "p": 0, "t": 0}

    def sbank():
        k = st["s"] % 3
        st["s"] += 1
        return P.psf[k], P.psf_r[k]

    def pbuf():
        k = st["p"] % 6
        st["p"] += 1
        return pT[k], pT_r[k]

    def tbuf():
        k = st["t"] % 4
        st["t"] += 1
        return tmp[k], tmp_r[k]

    def exp_tile(ps, ps_r, bias_ap, bias_r, mask_ap, mask_r, qa=0, qb=512):
        p, p_r = pbuf()
        if mask_ap is not None:
            tm, tm_r = tbuf()
            kb.op(kb.dve, lambda: nc.vector.tensor_tensor(out=tm[:, qa:qb], in0=ps[:, qa:qb], in1=mask_ap[:, qa:qb], op=ALU.add), reads=[ps_r, mask_r], writes=[tm_r])
            kb.op(kb.act, lambda: nc.scalar.activation(out=p[:, qa:qb], in_=tm[:, qa:qb], func=AF.Exp, bias=bias_ap), reads=[tm_r, bias_r], writes=[p_r])
        else:
            kb.op(kb.act, lambda: nc.scalar.activation(out=p[:, qa:qb], in_=ps[:, qa:qb], func=AF.Exp, bias=bias_ap), reads=[ps_r, bias_r], writes=[p_r])
        return p, p_r

    NI = ntok() // 512
    KT0 = 4 if MONO else 16
    for g in range(2):
        kb.dma(ksem, KA[0:64, :], dr["kvall"][2, g * 64:(g + 1) * 64, 0:SEQ], writes=[KA_r])
        for kq in range(16):
            kb.dma(vsem, VA[:, kq * 4:(kq + 1) * 4, 0:64],
                   dr["vall"][kq * 512:(kq + 1) * 512, g * 64:(g + 1) * 64].rearrange("(kt p) d -> p kt d", p=128), writes=[VA_r])
        for i in range(NI):
            q0 = i * 512
            if MONO:
                kb.dma(pisem, ADM[:, :, :], dr["addm"][i], writes=[ADM_r])
                kb.dma(pisem, VNG[:, :, :], dr["vneg"][i], writes=[VNG_r])
                kb.dma(pisem, KBW[:, :], dr["kbw"][i], writes=[KBW_r])
                cmp_tiles = [(nt, (i - 4 * nt) if (i - 4 * nt) <= 4 else None) for nt in range((32 * (i + 1) - 2) // 128 + 1)]
            else:
                cmp_tiles = [(nt, (nt - i + 1) if nt - i >= -1 else None) for nt in range(i + 1)]
            if QSEL:
                P.select_tile(hT[:, :, :], hT_r, [dr["h2T"][:, (4 * i + c_) * 512:(4 * i + c_ + 1) * 512].rearrange("(c p) n -> p c n", p=128) for c_ in range(4)],
                              qstg[:, :, :], qstg_r, hsem, 0, 128)
            else:
                kb.dma(hsem, hT[:, :, :], dr["h2T"][:, q0:q0 + 512].rearrange("(c p) n -> p c n", p=128), writes=[hT_r])
            (wgn,), w_r = P.load_w([dr["win_gn"][:, :]])
            for qs in range(4):
                ps, ps_r = sbank()
                for c in range(8):
                    kb.op(kb.pe, lambda c=c: nc.tensor.matmul(ps[:, 0:48], lhsT=hT[:, c, qs * 128:(qs + 1) * 128], rhs=wgn[:, c, :], start=(c == 0), stop=(c == 7)),
                          reads=[w_r, hT_r], writes=[ps_r], sig=(c == 7))
                kb.op(kb.act, lambda: nc.scalar.activation(out=gat[:, qs, :], in_=ps[:, 0:48], func=AF.Sigmoid), reads=[ps_r], writes=[gat_r])
            for hp in range(4):
                (wq,), w_r = P.load_w([W[g * 4 + hp]])
                for hh in range(2):
                    hl = hp * 2 + hh
                    ps, ps_r = sbank()
                    for c in range(8):
                        kb.op(kb.pe, lambda c=c: nc.tensor.matmul(ps[0:64, :], lhsT=wq[:, c, hh * 64:(hh + 1) * 64], rhs=hT[:, c, :], start=(c == 0), stop=(c == 7)),
                              reads=[w_r, hT_r], writes=[ps_r], sig=(c == 7))
                    kb.op(kb.dve, lambda: nc.vector.tensor_scalar(out=Q[0:64, 0, hl, :], in0=ps[0:64, :], scalar1=0.125, scalar2=None, op0=ALU.mult), reads=[ps_r], writes=[Q_r[hl]])
                    kb.op(kb.pool, lambda: nc.gpsimd.tensor_copy(out=Q[0:64, 1:3, hl, :], in_=Q[0:64, 0:1, hl, :].to_broadcast([64, 2, 512])),
                          reads=[Q_r[hl]], writes=[Q_r[hl]])
            if MONO:
                klo = max(q0 - 512, 0)
                kb.dma(kwsem, KW[0:64, 1024 - (q0 + 512 - klo):1024], dr["kvall"][3, g * 64:(g + 1) * 64, klo:q0 + 512], writes=[KW_r])
                w0 = 8 - (q0 + 512 - klo) // 128
                kb.dma(vwsem, VW[:, w0:8, 0:64], dr["vall"][klo:q0 + 512, 128 + g * 64:128 + (g + 1) * 64].rearrange("(w p) d -> p w d", p=128), writes=[VW_r])
            elif QSEL:
                for half in range(2):
                    sts = [4 * i + c_ - 1 + half for c_ in range(4)]
                    P.select_tile(KW[0:64, half * 512:(half + 1) * 512], KW_r,
                                  [dr["kvall"][3, g * 64:(g + 1) * 64, st_ * 512:(st_ + 1) * 512] if st_ >= 0 else None for st_ in sts],
                                  qstg[0:64, 0, :], qstg_r, kwsem, 0, 64)
                    P.select_tile(VW[:, half * 4:(half + 1) * 4, 0:64], VW_r,
                                  [dr["vall"][st_ * 512:(st_ + 1) * 512, 128 + g * 64:128 + (g + 1) * 64].rearrange("(w p) d -> p w d", p=128) if st_ >= 0 else None for st_ in sts],
                                  qstg[:, 1, 0:256].rearrange("p (w d) -> p w d", d=64), qstg_r, vwsem, 0, 128)
            else:
                kb.dma(kwsem, KW[0:64, :], dr["kwin"][g * 64:(g + 1) * 64, i, :], writes=[KW_r])
                kb.dma(vwsem, VW[:, :, 0:64], dr["vwin"][i, :, g * 64:(g + 1) * 64].rearrange("(w p) d -> p w d", p=128), writes=[VW_r])
            for v_ in range(3):
                kb.dma(qsem, Q[124:127, v_, :, :], dr["qal"][:, g * 8:(g + 1) * 8, q0:q0 + 512], writes=Q_r)
            for hl in range(8):
                h = g * 8 + hl
                accs = [(P.psf[3 + 2 * (hl % 2)], P.psf_r[3 + 2 * (hl % 2)]), (P.psf[4 + 2 * (hl % 2)], P.psf_r[4 + 2 * (hl % 2)])]
                for a, a_r in accs:
                    kb.op(kb.dve, lambda: nc.vector.memset(a[:, 0:386], 0.0), writes=[a_r])
                cq_ = []
                for cu in list(cmp_tiles) + [None, None]:
                    if cu is not None:
                        nt, mi_ = cu
                        ps, ps_r = sbank()
                        kb.op(kb.pe, lambda: nc.tensor.matmul(ps[:, :], lhsT=KC[0:128, g, nt * 128:(nt + 1) * 128], rhs=Q[0:128, 0, hl, :], start=True, stop=True),
                              reads=[KC_r, Q_r[hl]], writes=[ps_r])
                        e, e_r = exp_tile(ps, ps_r, KBC[:, h * 4 + nt:h * 4 + nt + 1], KBC_r,
                                          CPM[:, mi_, :] if mi_ is not None else None, CPM_r)
                        cq_.append((nt, e, e_r))
                    if cq_ and (len(cq_) > 2 or cu is None):
                        nt_, e, e_r = cq_.pop(0)
                        for qs in range(4):
                            a, a_r = accs[qs // 2]
                            o0 = (qs % 2) * 193
                            kb.op(kb.pe, lambda: nc.tensor.matmul(a[:, o0:o0 + 193], lhsT=e[:, qs * 128:(qs + 1) * 128], rhs=VC[:, nt_, g, :], start=False, stop=(nt_ == cmp_tiles[-1][0]), skip_group_check=True),
                                  reads=[e_r, VC_r], writes=[a_r], sig=(qs == 3 or qs == 1))
                assert not cq_
                for half in range(2):
                    a, a_r = accs[half]
                    av = a[:, 0:386].rearrange("p (q c) -> p q c", c=193)
                    kb.op(kb.dve, lambda: nc.vector.tensor_scalar(out=crd[:, 2 * half:2 * half + 2, hl], in0=av[:, :, 64], scalar1=1e-30, scalar2=None, op0=ALU.max),
                          reads=[a_r], writes=[crd_r[hl]])
                    kb.op(kb.dve, lambda: nc.vector.tensor_copy(out=cst[:, 2 * half:2 * half + 2, hl, :], in_=av[:, :, 0:64]), reads=[a_r], writes=[cst_r[hl]])
                kb.op(kb.dve, lambda: nc.vector.reciprocal(out=crd[:, :, hl], in_=crd[:, :, hl]), reads=[crd_r[hl]], writes=[crd_r[hl]])
                for qs in range(4):
                    a, a_r = accs[qs // 2]
                    o0 = (qs % 2) * 193
                    if hl == 0:
                        kb.op(kb.dve, lambda: nc.vector.tensor_scalar(out=sc[:, qs, :], in0=a[:, o0 + 65:o0 + 193], scalar1=crd[:, qs, hl:hl + 1], scalar2=None, op0=ALU.mult),
                              reads=[a_r, crd_r[hl]], writes=[sc_r])
                    else:
                        kb.op(kb.dve, lambda: nc.vector.scalar_tensor_tensor(out=sc[:, qs, :], in0=a[:, o0 + 65:o0 + 193], scalar=crd[:, qs, hl:hl + 1], in1=sc[:, qs, :],
                                                                              op0=ALU.mult, op1=ALU.add), reads=[a_r, crd_r[hl], sc_r], writes=[sc_r])
            c0_ = 0 if MONO else i * 4
            kb.op(kb.dve, lambda: nc.vector.tensor_tensor(out=smb[:, :, :], in0=sc[:, :, :], in1=ADM[:, c0_:c0_ + 4, :], op=ALU.add), reads=[sc_r, ADM_r], writes=smb_r)
            for qs in range(4):
                kb.op(kb.dve, lambda: nc.vector.max(out=m8[:, qs, :], in_=smb[:, qs, :]), reads=[smb_r[qs]], writes=[m8_r[qs]])
            for qs in range(4):
                kb.op(kb.dve, lambda: nc.vector.match_replace(out=wa[:, qs, :], in_to_replace=m8[:, qs, :], in_values=smb[:, qs, :], imm_value=-1e30),
                      reads=[smb_r[qs], m8_r[qs]], writes=[wa_r[qs]])
            for qs in range(4):
                kb.op(kb.dve, lambda: nc.vector.max(out=m8b[:, qs, :], in_=wa[:, qs, :]), reads=[wa_r[qs]], writes=[m8b_r[qs]])
            for qs in range(4):
                kb.op(kb.dve, lambda: nc.vector.match_replace(out=wb[:, qs, :], in_to_replace=m8b[:, qs, :], in_values=wa[:, qs, :], imm_value=-1e30),
                      reads=[wa_r[qs], m8b_r[qs]], writes=[wb_r[qs]])
            kb.op(kb.dve, lambda: nc.vector.tensor_tensor(out=wa[:, :, :], in0=smb[:, :, :], in1=wb[:, :, :], op=ALU.subtract), reads=smb_r + wb_r, writes=wa_r)
            kb.op(kb.dve, lambda: nc.vector.tensor_scalar(out=wa[:, :, :], in0=wa[:, :, :], scalar1=1.0, scalar2=-NEG, op0=ALU.min, op1=ALU.mult), reads=wa_r, writes=wa_r)
            for v_ in range(3):
                nb_ = 60 if v_ < 2 else 8
                kb.op(kb.dve, lambda: nc.vector.scalar_tensor_tensor(out=nqv[:, :, v_, 64:64 + nb_], in0=wa[:, :, 60 * v_:60 * v_ + nb_], scalar=NEG,
                                                                      in1=VNG[:, c0_:c0_ + 4, 60 * v_:60 * v_ + nb_], op0=ALU.add, op1=ALU.min),
                      reads=wa_r + [VNG_r], writes=[nq_r])
            for qs in range(4):
                for v_ in range(3):
                    kb.op(kb.pe, lambda: nc.tensor.transpose(out=P.psb[:, v_ * 128:(v_ + 1) * 128], in_=nqv[:, qs, v_, :], identity=IDB[:, :]),
                          reads=[nq_r, IDB_r], writes=[P.psb_r])
                for v_ in range(3):
                    src_ = P.psb[64:124, v_ * 128:(v_ + 1) * 128].rearrange("p (o n) -> p o n", o=1).to_broadcast([60, 8, 128])
                    kb.op(kb.dve, lambda: nc.vector.tensor_copy(out=Q[64:124, v_, :, qs * 128:(qs + 1) * 128], in_=src_),
                          reads=[P.psb_r], writes=Q_r)
            for hl in range(8):
                h = g * 8 + hl
                aS, aS_r = P.psf[3 + 2 * (hl % 2)], P.psf_r[3 + 2 * (hl % 2)]
                aW, aW_r = P.psf[4 + 2 * (hl % 2)], P.psf_r[4 + 2 * (hl % 2)]
                nkt = KT0 * (i + 1)
                if hl == 0:
                    kb.op(kb.dve, lambda: nc.vector.memset(aS[:, 0:260], 0.0), writes=[aS_r])
                    kb.op(kb.dve, lambda: nc.vector.memset(aW[:, 0:260], 0.0), writes=[aW_r])
                units = [("s", kt) for kt in range(nkt)] + [("w", w) for w in range(8)]
                SKEW = 2
                pendq = []
                for u in units + [None] * SKEW:
                    cur = None
                    if u is not None:
                        kind, ix = u
                        ps, ps_r = sbank()
                        qa, qb = 0, 512
                        if kind == "s":
                            r_ = ix - KT0 * i
                            if MONO and r_ >= 0:
                                qa = 128 * r_
                            kb.op(kb.pe, lambda: nc.tensor.matmul(ps[:, qa:qb], lhsT=KA[0:128, ix * 128:(ix + 1) * 128], rhs=Q[0:128, (2 * ix) // 60, hl, qa:qb], start=True, stop=True),
                                  reads=[KA_r, Q_r[hl]], writes=[ps_r])
                            p, p_r = exp_tile(ps, ps_r, KBS[:, h * 64 + ix:h * 64 + ix + 1], KBS_r, CM[:, r_, :] if r_ >= 0 else None, CM_r, qa, qb)
                        else:
                            if ix < 4:
                                qb = 128 * (ix + 1)
                            else:
                                qa = 128 * (ix - 4)
                            kb.op(kb.pe, lambda: nc.tensor.matmul(ps[:, qa:qb], lhsT=KW[0:128, ix * 128:(ix + 1) * 128], rhs=Q[0:128, 0, hl, qa:qb], start=True, stop=True),
                                  reads=[KW_r, Q_r[hl]], writes=[ps_r])
                            cb = (ix * 16 + h) if MONO else ((i * 8 + ix) * 16 + h)
                            p, p_r = exp_tile(ps, ps_r, KBW[:, cb:cb + 1], KBW_r, WM[:, ix, :], WM_r, qa, qb)
                        cur = (kind, ix, p, p_r, qa, qb)
                        pendq.append(cur)
                    if pendq and (len(pendq) > SKEW or u is None):
                        kind_, ix_, pp, pp_r, qa_, qb_ = pendq.pop(0)
                        qss = list(range(qa_ // 128, qb_ // 128))
                        for qs in qss:
                            if kind_ == "s":
                                kb.op(kb.pe, lambda: nc.tensor.matmul(aS[:, qs * 65:(qs + 1) * 65], lhsT=pp[:, qs * 128:(qs + 1) * 128], rhs=VA[:, ix_, :], start=False, stop=(ix_ == nkt - 1), skip_group_check=True),
                                      reads=[pp_r, VA_r], writes=[aS_r], sig=(qs == qss[-1]))
                            else:
                                kb.op(kb.pe, lambda: nc.tensor.matmul(aW[:, qs * 65:(qs + 1) * 65], lhsT=pp[:, qs * 128:(qs + 1) * 128], rhs=VW[:, ix_, :], start=False, stop=(ix_ == 7), skip_group_check=True),
                                      reads=[pp_r, VW_r], writes=[aW_r], sig=(qs == qss[-1]))
                assert not pendq
                if hl < 7:
                    nS, nS_r = P.psf[3 + 2 * ((hl + 1) % 2)], P.psf_r[3 + 2 * ((hl + 1) % 2)]
                    nW, nW_r = P.psf[4 + 2 * ((hl + 1) % 2)], P.psf_r[4 + 2 * ((hl + 1) % 2)]
                    kb.op(kb.dve, lambda: nc.vector.memset(nS[:, 0:260], 0.0), writes=[nS_r])
                    kb.op(kb.dve, lambda: nc.vector.memset(nW[:, 0:260], 0.0), writes=[nW_r])
                aSv = aS[:, 0:260].rearrange("p (q c) -> p q c", c=65)
                aWv = aW[:, 0:260].rearrange("p (q c) -> p q c", c=65)
                kb.op(kb.dve, lambda: nc.vector.tensor_scalar(out=dd[:, 0, :], in0=aSv[:, :, 64], scalar1=1e-30, scalar2=None, op0=ALU.max), reads=[aS_r], writes=[dd_r])
                kb.op(kb.dve, lambda: nc.vector.tensor_scalar(out=dd[:, 1, :], in0=aWv[:, :, 64], scalar1=1e-30, scalar2=None, op0=ALU.max), reads=[aW_r], writes=[dd_r])
                kb.op(kb.dve, lambda: nc.vector.reciprocal(out=dd[:, :, :], in_=dd[:, :, :]), reads=[dd_r], writes=[dd_r])
                kb.op(kb.dve, lambda: nc.vector.tensor_tensor(out=cf[:, 0, :], in0=crd[:, :, hl], in1=gat[:, :, h * 3 + 0], op=ALU.mult), reads=[crd_r[hl], gat_r], writes=[cf_r])
                kb.op(kb.dve, lambda: nc.vector.tensor_tensor(out=cf[:, 1, :], in0=dd[:, 0, :], in1=gat[:, :, h * 3 + 1], op=ALU.mult), reads=[dd_r, gat_r], writes=[cf_r])
                kb.op(kb.dve, lambda: nc.vector.tensor_tensor(out=cf[:, 2, :], in0=dd[:, 1, :], in1=gat[:, :, h * 3 + 2], op=ALU.mult), reads=[dd_r, gat_r], writes=[cf_r])
                for qs in range(4):
                    kb.op(kb.dve, lambda: nc.vector.tensor_scalar(out=oh[:, :], in0=cst[:, qs, hl, :], scalar1=cf[:, 0, qs:qs + 1], scalar2=None, op0=ALU.mult),
                          reads=[cst_r[hl], cf_r], writes=[oh_r])
                    kb.op(kb.dve, lambda: nc.vector.scalar_tensor_tensor(out=oh[:, :], in0=aS[:, qs * 65:qs * 65 + 64], scalar=cf[:, 1, qs:qs + 1], in1=oh[:, :], op0=ALU.mult, op1=ALU.add),
                          reads=[aS_r, cf_r, oh_r], writes=[oh_r])
                    kb.op(kb.dve, lambda: nc.vector.scalar_tensor_tensor(out=onb[:, qs, h * 64:(h + 1) * 64], in0=aW[:, qs * 65:qs * 65 + 64], scalar=cf[:, 2, qs:qs + 1], in1=oh[:, :],
                                                                          op0=ALU.mult, op1=ALU.add), reads=[aW_r, cf_r, oh_r], writes=[onb_r])
            for fc in range(4 * g, 4 * g + 4):
                k = fc % 2
                for qs in range(4):
                    kb.op(kb.pe, lambda: nc.tensor.transpose(out=P.psb[:, 0:128], in_=onb[:, qs, fc * 128:(fc + 1) * 128], identity=IDB[:, :]),
                          reads=[onb_r, IDB_r], writes=[P.psb_r])
                    kb.op(kb.dve, lambda: nc.vector.tensor_copy(out=ost[k][:, qs * 128:(qs + 1) * 128], in_=P.psb[:, 0:128]), reads=[P.psb_r], writes=[ost_r[k]])
                kb.dma(osems[k], dr["onsaT"][fc * 128:(fc + 1) * 128, q0:q0 + 512], ost[k][:, :], reads=[ost_r[k]], eng=(kb.act if (MONO or QSEL) else kb.sp))


BF = ml_dtypes.bfloat16
_CACHE = {}
USE_MONO = True


def _slopes():
    hh = np.arange(1, 17, dtype=np.float32)
    return np.exp2(-8.0 * hh / 16.0).astype(np.float32)


def _split3(v):
    v = v.astype(np.float32)
    hi = v.astype(BF)
    r = v - hi.astype(np.float32)
    mid = r.astype(BF)
    r2 = r - mid.astype(np.float32)
    lo = r2.astype(BF)
    return hi, mid, lo


def core_consts(cc):
    sl = _slopes()
    p = np.arange(128)
    q = np.arange(512)
    c = {}
    tabs = np.concatenate([(4 * i + cc) * 512 + q for i in range(4)]).astype(np.float32)
    v = -(sl[:, None] * tabs[None, :])
    hi, mid, lo = _split3(v)
    c["qal"] = np.ascontiguousarray(np.stack([hi, mid, lo], 0))
    kbs = np.zeros((128, 16, 64), np.float32)
    for h in range(16):
        kbs[:, h, :] = sl[h] * (np.arange(64)[None, :] * 128 + p[:, None]).astype(np.float32)
    c["kbs"] = kbs.reshape(128, 1024)
    kbc = np.zeros((128, 16, 4), np.float32)
    for h in range(16):
        kbc[:, h, :] = sl[h] * (16 * (np.arange(4)[None, :] * 128 + p[:, None]) + 31).astype(np.float32)
    c["kbc"] = kbc.reshape(128, 64)
    kbw = np.zeros((128, 4, 8, 16), np.float32)
    for i in range(4):
        T0 = (4 * i + cc) * 512
        for w in range(8):
            ka = T0 - 512 + w * 128 + p
            for h in range(16):
                kbw[:, i, w, h] = np.where(ka >= 0, sl[h] * ka.astype(np.float32), -30000.0)
    c["kbw"] = kbw.reshape(128, 512)
    cmpm = np.zeros((128, 2, 512), np.float32)
    for d in (-1, 0):
        vis = (2048 * d + 16 * p[:, None] + 31 - 512 * cc) <= q[None, :]
        cmpm[:, d + 1, :] = np.where(vis, 0.0, NEG)
    c["cmpm"] = cmpm.astype(BF)
    cm = np.zeros((128, 16, 512), np.float32)
    for r in range(16):
        vis = (128 * r + p[:, None]) <= (512 * cc + q[None, :])
        cm[:, r, :] = np.where(vis, 0.0, NEG)
    c["cm"] = cm.astype(BF)
    wm = np.zeros((128, 8, 512), np.float32)
    for w in range(8):
        dist = 512 + q[None, :] - 128 * w - p[:, None]
        wm[:, w, :] = np.where((dist >= 0) & (dist < 512), 0.0, NEG)
    c["wm"] = wm.astype(BF)
    addm = np.zeros((128, 16, 128), np.float32)
    vneg = np.zeros((128, 16, 128), np.float32)
    j = np.arange(128)
    for i in range(4):
        for qs in range(4):
            t = (4 * i + cc) * 512 + qs * 128 + p
            valid = (j[None, :] * 64) <= t[:, None]
            cur = t // 64
            forced = valid & ((j[None, :] == 0) | (j[None, :] == cur[:, None]) | (j[None, :] == cur[:, None] - 1))
            addm[:, i * 4 + qs, :] = np.where(forced, 8192.0, np.where(valid, 0.0, -8192.0))
            vneg[:, i * 4 + qs, :] = np.where(valid, 0.0, NEG)
    c["addm"] = addm.astype(BF)
    c["vneg"] = vneg.astype(BF)
    return c


def shared_consts():
    c = {}
    cols = np.arange(SEQ)
    kar = np.zeros((64, SEQ), np.float32)
    kar[0:60] = ((cols[None, :] // 64) % 60 == np.arange(60)[:, None])
    kar[60:63] = 1.0
    c["karows"] = kar.astype(BF)
    c["ones3"] = np.ones((3, 1024), np.float32).astype(BF)
    n = np.arange(512)
    cs = n[:, None] * 16
    ss = np.arange(128)[None, :] * 64
    ov = np.clip(np.minimum(cs + 32, ss + 64) - np.maximum(cs, ss), 0, None) / 32.0
    c["ovl"] = np.ascontiguousarray(ov.reshape(4, 128, 128).transpose(1, 0, 2)).astype(np.float32).astype(BF)
    c["identb"] = np.eye(128, dtype=np.float32).astype(BF)
    return c


def tile_w(Wm, starts=None):
    K = Wm.shape[0]
    if starts is None:
        starts = list(range(0, Wm.shape[1], 128))
    out = np.empty((len(starts), 128, K // 128, 128), np.float32)
    for j, c0 in enumerate(starts):
        out[j] = Wm[:, c0:c0 + 128].reshape(K // 128, 128, 128).transpose(1, 0, 2)
    return out


def _prog(key, fn):
    if key not in _CACHE:
        _CACHE[key] = fn()
    return _CACHE[key]


def _tok_index(cc):
    return np.concatenate([np.arange((4 * i + cc) * 512, (4 * i + cc + 1) * 512) for i in range(4)])


def kernel(x, ffn1_norm, ffn1_w_gate, ffn1_w_up, ffn1_w_down, mix_norm, w_in, cmp_pos,
           cmp_k_w1, cmp_k_w2, cmp_v_w1, cmp_v_w2, pool_w, pool_scale, w_branch_pool,
           w_branch_nsa, w_out, ffn2_norm, ffn2_w_gate, ffn2_w_up, ffn2_w_down, final_norm):
    f32 = lambda a: np.ascontiguousarray(np.asarray(a, dtype=np.float32))
    x = f32(x)
    W = {k: f32(v) for k, v in dict(ffn1_norm=ffn1_norm, ffn1_w_gate=ffn1_w_gate, ffn1_w_up=ffn1_w_up, ffn1_w_down=ffn1_w_down,
                                    mix_norm=mix_norm, w_in=w_in, cmp_pos=cmp_pos, cmp_k_w1=cmp_k_w1, cmp_k_w2=cmp_k_w2,
                                    cmp_v_w1=cmp_v_w1, cmp_v_w2=cmp_v_w2, pool_w=pool_w, pool_scale=pool_scale,
                                    w_branch_pool=w_branch_pool, w_branch_nsa=w_branch_nsa, w_out=w_out, ffn2_norm=ffn2_norm,
                                    ffn2_w_gate=ffn2_w_gate, ffn2_w_up=ffn2_w_up, ffn2_w_down=ffn2_w_down, final_norm=final_norm).items()}
    cores = list(range(8))
    vecs = np.zeros((128, 64), np.float32)
    for l in range(NL):
        b0 = vbase(l)
        vecs[:, b0:b0 + 8] = gain_layout(W["ffn1_norm"][l])
        vecs[:, b0 + 8:b0 + 16] = gain_layout(W["mix_norm"][l])
        vecs[:, b0 + 16:b0 + 24] = gain_layout(W["ffn2_norm"][l])
        vecs[:, b0 + 24:b0 + 28] = gain_layout(W["pool_scale"][l])
    vecs[:, 56:64] = gain_layout(W["final_norm"])
    if USE_MONO:
        return kernel_mono(W, x, vecs)
    tix = [_tok_index(c % 4) for c in cores]
    cc_consts = [core_consts(cc) for cc in range(4)]
    sh = shared_consts()

    TW = {}
    for l in range(NL):
        TW["win", l] = tile_w(W["w_in"][l], WIN_STARTS)
        for nm in ("ffn1_w_gate", "ffn1_w_up", "ffn1_w_down", "ffn2_w_gate", "ffn2_w_up", "ffn2_w_down", "w_branch_pool", "w_branch_nsa", "w_out",
                   "cmp_k_w1", "cmp_v_w1"):
            TW[nm, l] = tile_w(W[nm][l])

    def a_weights(l):
        return {"f1_wg": TW["ffn1_w_gate", l], "f1_wu": TW["ffn1_w_up", l], "f1_wd": TW["ffn1_w_down", l], "win_a": TW["win", l]}

    progA = _prog("A", lambda: build_tok(0, "A", False))
    in_maps = []
    for c in cores:
        m = {"xs_in": np.ascontiguousarray(x[c // 4, tix[c], :].T), "vecs": vecs}
        m.update(a_weights(0))
        in_maps.append(m)
    res = run_bass_kernel_spmd(progA, in_maps, core_ids=cores).results

    out = np.zeros((2, SEQ, D), np.float32)
    for l in range(NL):
        last = (l == NL - 1)
        kvall = np.zeros((2, 4, 128, SEQ + 32), BF)
        vall = np.zeros((2, SEQ, 256), BF)
        uall = np.zeros((2, 512, SEQ), np.float32)
        for c in cores:
            b = c // 4
            kvall[b][:, :, tix[c]] = np.asarray(res[c]["kvT_out"]).view(BF) if np.asarray(res[c]["kvT_out"]).dtype != BF else res[c]["kvT_out"]
            vall[b][tix[c], :] = np.asarray(res[c]["vtok_out"])
            uall[b][:, tix[c]] = np.asarray(res[c]["uT_out"])
        prog = _prog(("BCA", l, last), lambda: build_bca(l, last))
        pecol = np.ascontiguousarray(W["cmp_pos"][l].reshape(16, 2, 64).transpose(1, 2, 0).reshape(128, 16))
        in_maps = []
        for c in cores:
            b, cc = c // 4, c % 4
            kwin = np.zeros((128, 4, 1024), BF)
            vwin = np.zeros((4, 1024, 128), BF)
            uext = np.zeros((512, 4, 528), np.float32)
            for i in range(4):
                T0 = (4 * i + cc) * 512
                lo = max(T0 - 512, 0)
                kwin[:, i, 1024 - (T0 + 512 - lo):] = kvall[b][3][:, lo:T0 + 512]
                vwin[i, 1024 - (T0 + 512 - lo):, :] = vall[b][lo:T0 + 512, 128:256]
                lo = max(T0 - 16, 0)
                uext[:, i, 528 - (T0 + 512 - lo):] = uall[b][:, lo:T0 + 512]
            corr = np.ones((128, 4, 16), np.float32)
            if cc == 0:
                for gi, w in enumerate(POOLW):
                    corr[:, gi, :] = (w / np.minimum(np.arange(16) + 1.0, float(w)))[None, :]
            m = {"vecs": vecs, "h2T": np.asarray(res[c]["h2T_out"]), "kvall": kvall[b], "vall": vall[b], "kwin": kwin, "vwin": vwin,
                 "win": TW["win", l], "win_gn": np.ascontiguousarray(W["w_in"][l][:, C_GN:C_GN + 48]),
                 "ck_w1": TW["cmp_k_w1", l], "ck_w2": W["cmp_k_w2"][l], "cv_w1": TW["cmp_v_w1", l], "cv_w2": W["cmp_v_w2"][l],
                 "pecol": pecol,
                 "xs_in": np.asarray(res[c]["xs_out"]),
                 "f2_wg": TW["ffn2_w_gate", l], "f2_wu": TW["ffn2_w_up", l], "f2_wd": TW["ffn2_w_down", l],
                 "wpa": TW["w_branch_pool", l], "wnb": TW["w_branch_nsa", l], "wo": TW["w_out", l],
                 "poolw": W["pool_w"][l], "uext": uext, "corr": corr}
            m.update(cc_consts[cc])
            m.update(sh)
            if not last:
                m.update(a_weights(l + 1))
            in_maps.append(m)
        res = run_bass_kernel_spmd(prog, in_maps, core_ids=cores).results
    for c in cores:
        out[c // 4, tix[c], :] = np.asarray(res[c]["out"]).T
    return out


def build_mono():
    global MONO
    MONO = True
    try:
        BI, IN_, BO = "ExternalInput", "Internal", "ExternalOutput"
        S = SEQ
        specs = {
            "x_in": ((D, S), F32, BI), "vecs": ((128, 64), F32, BI),
            "qal": ((3, 16, S), BF16, BI), "kbs": ((128, 1024), F32, BI), "kbc": ((128, 64), F32, BI),
            "kbw": ((16, 128, 128), F32, BI), "cmpm": ((128, 5, 512), BF16, BI), "cm": ((128, 4, 512), BF16, BI),
            "wm": ((128, 8, 512), BF16, BI), "addm": ((16, 128, 4, 128), BF16, BI), "vneg": ((16, 128, 4, 128), BF16, BI),
            "karows": ((64, S), BF16, BI), "ones3": ((3, 1024), BF16, BI), "ovl": ((128, 4, 128), BF16, BI), "identb": ((128, 128), BF16, BI),
            "corr": ((128, 4, 16), F32, BI),
            "xs": ((D, S), F32, IN_), "h2T": ((D, S), BF16, IN_), "kvT": ((4, 128, S), BF16, IN_), "vtok": ((S, 256), BF16, IN_),
            "uT0": ((512, S), F32, IN_), "uT1": ((512, S), F32, IN_), "onsaT": ((D, S), BF16, IN_),
            "out_q": ((D, NTOK), F32, BO), "onsaT_q": ((D, NTOK), BF16, IN_),
            "selw": ((128, 4), F32, BI), "corr_q": ((128, 4, 16), F32, BI),
            "qal_q": ((3, 16, NTOK), BF16, BI), "kbw_q": ((128, 512), F32, BI), "cmpm_q": ((128, 2, 512), BF16, BI),
            "cm_q": ((128, 16, 512), BF16, BI), "addm_q": ((128, 16, 128), BF16, BI), "vneg_q": ((128, 16, 128), BF16, BI),
        }
        for l in range(NL):
            for pre in ("f1_", "f2_"):
                specs["%swg_%d" % (pre, l)] = ((NFC, 128, 8, 128), F32, BI)
                specs["%swu_%d" % (pre, l)] = ((NFC, 128, 8, 128), F32, BI)
                specs["%swd_%d" % (pre, l)] = ((8, 128, NFC, 128), F32, BI)
            specs["win_%d" % l] = ((len(WIN_STARTS), 128, 8, 128), F32, BI)
            specs["win_gn_%d" % l] = ((D, 48), F32, BI)
            specs["wpa_%d" % l] = ((8, 128, 4, 128), F32, BI)
            specs["wnb_%d" % l] = ((8, 128, 8, 128), F32, BI)
            specs["wo_%d" % l] = ((8, 128, 8, 128), F32, BI)
            specs["poolw_%d" % l] = ((4, 128, 128), F32, BI)
            specs["ck_w1_%d" % l] = ((2, 128, 16, 128), F32, BI)
            specs["cv_w1_%d" % l] = ((2, 128, 16, 128), F32, BI)
            specs["ck_w2_%d" % l] = ((256, 64), F32, BI)
            specs["cv_w2_%d" % l] = ((256, 64), F32, BI)
            specs["pecol_%d" % l] = ((128, 16), F32, BI)
        conv = [n_ for n_, (sh_, dt_, k_) in specs.items() if k_ == BI and dt_ == F32 and len(sh_) == 4 and n_ != "kbw"]
        gu = [n_ for n_ in conv if n_[3:5] in ("wg", "wu")]
        conv = [n_ for n_ in conv if n_ not in gu]
        for n_ in conv:
            specs[n_ + "_b"] = (specs[n_][0], BF16, IN_)
        for l in range(NL):
            for pre in ("f1_", "f2_"):
                specs["%swgu_%d_b" % (pre, l)] = ((NFC, 128, 16, 128), BF16, IN_)
        P = Prog(specs, WST=None)
        kb, dr = P.kb, P.dr

        P.selw = kb.sb("selw", [128, 4], F32)
        P.selw_r = Res("selw")
        kb.dma(P.ldsem, P.selw[:, :], dr["selw"][:, :], writes=[P.selw_r])
        mono_tabs = {k_: dr[k_] for k_ in ("qal", "kbw", "cmpm", "cm", "addm", "vneg", "corr", "onsaT")}

        def do_convert():
            for n_ in conv:
                P.convert_w(dr[n_], dr[n_ + "_b"])
                dr[n_] = dr[n_ + "_b"]
            for l_ in range(NL):
                for pre in ("f1_", "f2_"):
                    d_ = dr["%swgu_%d_b" % (pre, l_)]
                    P.convert_w(dr["%swg_%d" % (pre, l_)], None, dst_fn=lambda j, d_=d_: d_[j, :, 0:8, :])
                    P.convert_w(dr["%swu_%d" % (pre, l_)], None, dst_fn=lambda j, d_=d_: d_[j, :, 8:16, :])

        def alias(l, mode):
            for nm in ("wpa", "wnb", "wo", "poolw", "ck_w1", "cv_w1", "ck_w2", "cv_w2", "pecol", "win_gn"):
                dr[nm] = dr["%s_%d" % (nm, l)]
            dr["win"] = dr["win_%d" % l]
            dr["win_c"] = dr["win_%d" % l]
            dr["f2_wd"] = dr["f2_wd_%d" % l]
            dr["f2_wg"] = dr["f2_wgu_%d_b" % l]
            dr["f2_wu"] = None
            la = l if mode == "A" else min(l + 1, NL - 1)
            dr["win_a"] = dr["win_%d" % la]
            dr["f1_wd"] = dr["f1_wd_%d" % la]
            dr["f1_wg"] = dr["f1_wgu_%d_b" % la]
            dr["f1_wu"] = None
            dr["kvall"] = dr["kvT"]
            dr["vall"] = dr["vtok"]
            dr["h2T_in"] = dr["h2T"]
            dr["h2T_out"] = dr["h2T"]
            dr["kvT_out"] = dr["kvT"]
            dr["vtok_out"] = dr["vtok"]
            dr["xs_out"] = dr["xs"]
            dr["xs_in"] = dr["x_in"] if (mode == "A" and l == 0) else dr["xs"]
            dr["uT_in"] = dr["uT%d" % (l % 2)]
            dr["uT_out"] = dr["uT%d" % (la % 2)]

        def phase(fn, wst, wstf=1024, nslot=4):
            with ExitStack() as pes:
                kb.cur_es = pes
                P.alloc_wstage(wst, wstf, nslot)
                fn()
                kb.barrier()
            kb.cur_es = None

        phase(do_convert, 3584, 3584, 2)
        alias(0, "A")
        phase(lambda: tok_body(P, 0, "A", False), 3584)
        global QSEL
        for l in range(NL):
            last = (l == NL - 1)
            alias(l, "CA")
            if last:
                MONO, QSEL = False, True
                for k_ in ("qal", "kbw", "cmpm", "cm", "addm", "vneg", "corr", "onsaT"):
                    dr[k_] = dr[k_ + "_q"]
                dr["out"] = dr["out_q"]
            phase(lambda: attn_body(P), 2048, 1024, 2)
            phase(lambda: tok_body(P, l, "CA", last), 3584)
        return P.finish()
    finally:
        MONO = False
        QSEL = False


def mono_consts():
    sl = _slopes()
    p = np.arange(128)
    q = np.arange(512)
    c = {}
    tabs = np.arange(SEQ).astype(np.float32)
    hi, mid, lo = _split3(-(sl[:, None] * tabs[None, :]))
    c["qal"] = np.ascontiguousarray(np.stack([hi, mid, lo], 0))
    kbs = np.zeros((128, 16, 64), np.float32)
    kbc = np.zeros((128, 16, 4), np.float32)
    for h in range(16):
        kbs[:, h, :] = sl[h] * (np.arange(64)[None, :] * 128 + p[:, None]).astype(np.float32)
        kbc[:, h, :] = sl[h] * (16 * (np.arange(4)[None, :] * 128 + p[:, None]) + 31).astype(np.float32)
    c["kbs"] = kbs.reshape(128, 1024)
    c["kbc"] = kbc.reshape(128, 64)
    kbw = np.zeros((16, 128, 8, 16), np.float32)
    for i in range(16):
        for w in range(8):
            ka = 512 * (i - 1) + w * 128 + p
            for h in range(16):
                kbw[i, :, w, h] = np.where(ka >= 0, sl[h] * ka.astype(np.float32), -30000.0)
    c["kbw"] = kbw.reshape(16, 128, 128)
    cmpm = np.zeros((128, 5, 512), np.float32)
    for d in range(5):
        cmpm[:, d, :] = np.where((16 * p[:, None] + 31 - 512 * d) <= q[None, :], 0.0, NEG)
    c["cmpm"] = cmpm.astype(BF)
    cm = np.zeros((128, 4, 512), np.float32)
    for r in range(4):
        cm[:, r, :] = np.where((128 * r + p[:, None]) <= q[None, :], 0.0, NEG)
    c["cm"] = cm.astype(BF)
    wm = np.zeros((128, 8, 512), np.float32)
    for w in range(8):
        dist = 512 + q[None, :] - 128 * w - p[:, None]
        wm[:, w, :] = np.where((dist >= 0) & (dist < 512), 0.0, NEG)
    c["wm"] = wm.astype(BF)
    addm = np.zeros((16, 128, 4, 128), np.float32)
    vneg = np.zeros((16, 128, 4, 128), np.float32)
    j = np.arange(128)
    for i in range(16):
        for qs in range(4):
            t = i * 512 + qs * 128 + p
            valid = (j[None, :] * 64) <= t[:, None]
            cur = t // 64
            forced = valid & ((j[None, :] == 0) | (j[None, :] == cur[:, None]) | (j[None, :] == cur[:, None] - 1))
            addm[i, :, qs, :] = np.where(forced, 8192.0, np.where(valid, 0.0, -8192.0))
            vneg[i, :, qs, :] = np.where(valid, 0.0, NEG)
    c["addm"] = addm.astype(BF)
    c["vneg"] = vneg.astype(BF)
    corr = np.ones((128, 4, 16), np.float32)
    for gi, w in enumerate(POOLW):
        corr[:, gi, :] = (w / np.minimum(np.arange(16) + 1.0, float(w)))[None, :]
    c["corr"] = corr
    c.update(shared_consts())
    return c


def kernel_mono(W, x, vecs):
    prog = _prog("MONO", build_mono)
    base = {"vecs": vecs}
    base.update(mono_consts())
    for l in range(NL):
        base["win_%d" % l] = tile_w(W["w_in"][l], WIN_STARTS)
        base["win_gn_%d" % l] = np.ascontiguousarray(W["w_in"][l][:, C_GN:C_GN + 48])
        for pre, a in (("f1_", "ffn1"), ("f2_", "ffn2")):
            base["%swg_%d" % (pre, l)] = tile_w(W[a + "_w_gate"][l])
            base["%swu_%d" % (pre, l)] = tile_w(W[a + "_w_up"][l])
            base["%swd_%d" % (pre, l)] = tile_w(W[a + "_w_down"][l])
        base["wpa_%d" % l] = tile_w(W["w_branch_pool"][l])
        base["wnb_%d" % l] = tile_w(W["w_branch_nsa"][l])
        base["wo_%d" % l] = tile_w(W["w_out"][l])
        base["poolw_%d" % l] = W["pool_w"][l]
        base["ck_w1_%d" % l] = tile_w(W["cmp_k_w1"][l])
        base["cv_w1_%d" % l] = tile_w(W["cmp_v_w1"][l])
        base["ck_w2_%d" % l] = W["cmp_k_w2"][l]
        base["cv_w2_%d" % l] = W["cmp_v_w2"][l]
        base["pecol_%d" % l] = np.ascontiguousarray(W["cmp_pos"][l].reshape(16, 2, 64).transpose(1, 2, 0).reshape(128, 16))
    cores = list(range(8))
    in_maps = []
    xT = [np.ascontiguousarray(x[b].T) for b in range(2)]
    for c in cores:
        b, cc = c % 2, c // 2
        m = dict(base)
        m["x_in"] = xT[b]
        cq = core_consts(cc)
        for k_ in ("qal", "kbw", "cmpm", "cm", "addm", "vneg"):
            m[k_ + "_q"] = cq[k_]
        selw = np.zeros((128, 4), np.float32)
        selw[:, cc] = 1.0
        m["selw"] = selw
        corr = np.ones((128, 4, 16), np.float32)
        if cc == 0:
            corr = base["corr"]
        m["corr_q"] = corr
        in_maps.append(m)
    res = run_bass_kernel_spmd(prog, in_maps, core_ids=cores).results
    out = np.zeros((2, SEQ, D), np.float32)
    for c in cores:
        out[c % 2, _tok_index(c // 2), :] = np.asarray(res[c]["out_q"]).T
    return out
```
